# Optimizing a Trainium2 kernel written in Bass

```python
import jax, jax.numpy as jnp
from jax import lax
import numpy as np

D_MODEL = 1024
BATCH = 16
SEQ = 256
DEPTH = 2
DEC_BATCH = 4
DEC_SEQ = 4096
PAST_LEN = 256

GRID_W = 64
N_MIXERS = 2
N_FNET = (DEPTH + 1) // 2
N_HGRN = DEPTH // 2
FNET_GROUPS = 4
FNET_GROUP_DIM = D_MODEL // FNET_GROUPS
HGRN_EXPAND = 128
HGRN_HEADS = D_MODEL // HGRN_EXPAND
HGRN_DK = HGRN_EXPAND
HGRN_DV = D_MODEL // HGRN_HEADS
CHUNK = 32
D_FF = 4 * D_MODEL
N_MOD = 6
EPS = 1e-6
POS_BASE = 10000.0

kernel_name = "hybrid_fnet_hgrn2_diffusion_step"


def rmsnorm(x, g):
    xf = x.astype(jnp.float32)
    r = xf * lax.rsqrt(jnp.mean(xf * xf, axis=-1, keepdims=True) + EPS)
    return (r * g.astype(jnp.float32)).astype(x.dtype)


def grid_pos_embed(n_tok, d, dtype):
    rows = n_tok // GRID_W
    quarter = d // 4
    omega = 1.0 / (POS_BASE ** (jnp.arange(quarter, dtype=jnp.float32) / quarter))
    r = jnp.arange(rows, dtype=jnp.float32)[:, None] * omega[None, :]
    cl = jnp.arange(GRID_W, dtype=jnp.float32)[:, None] * omega[None, :]
    er = jnp.concatenate([jnp.sin(r), jnp.cos(r)], axis=-1)
    ec = jnp.concatenate([jnp.sin(cl), jnp.cos(cl)], axis=-1)
    emb = jnp.concatenate([jnp.broadcast_to(er[:, None, :], (rows, GRID_W, d // 2)),
                           jnp.broadcast_to(ec[None, :, :], (rows, GRID_W, d // 2))], axis=-1)
    return emb.reshape(n_tok, d).astype(dtype)


def fourier_mixer(h, w_o):
    b, l, d = h.shape
    hg = h.astype(jnp.float32).reshape(b, l, FNET_GROUPS, FNET_GROUP_DIM)
    mixed = jnp.fft.fft2(hg, axes=(1, 3), norm="ortho").real
    return mixed.reshape(b, l, d).astype(h.dtype) @ w_o


def gated_scan(q, k, v, logf, s0):
    dr, b, hh, l, dk = q.shape
    dv = v.shape[-1]
    n = l // CHUNK

    def to_chunks(a):
        return jnp.moveaxis(a.reshape(dr, b, hh, n, CHUNK, a.shape[-1]), 3, 0)

    tril = jnp.tril(jnp.ones((CHUNK, CHUNK), dtype=bool))

    def step(s, inp):
        qc, kc, vc, gc = inp
        bcum = jnp.cumsum(gc, axis=-2)
        o_inter = jnp.einsum('dbhtk,dbhkv->dbhtv', qc * jnp.exp(bcum), s)
        diff = bcum[..., :, None, :] - bcum[..., None, :, :]
        decay = jnp.exp(jnp.where(tril[:, :, None], diff, -jnp.inf))
        scores = jnp.einsum('dbhtk,dbhsk,dbhtsk->dbhts', qc, kc, decay)
        o_intra = jnp.einsum('dbhts,dbhsv->dbhtv', scores, vc)
        blast = bcum[..., -1:, :]
        s_new = jnp.exp(blast[..., 0, :])[..., None] * s + jnp.einsum(
            'dbhsk,dbhsv->dbhkv', kc * jnp.exp(blast - bcum), vc)
        return s_new, o_inter + o_intra

    s_fin, o = lax.scan(step, s0, (to_chunks(q), to_chunks(k), to_chunks(v), to_chunks(logf)))
    o = jnp.moveaxis(o, 0, 3).reshape(dr, b, hh, l, dv)
    return s_fin, o


def hgrn2_mixer(h, w_in, lb, g_norm, w_o, s0):
    b, l, d = h.shape
    proj = h @ w_in
    q, f_fw, f_bw, i_in, g = jnp.split(proj, 5, axis=-1)

    def heads(a, hd):
        return a.astype(jnp.float32).reshape(b, l, HGRN_HEADS, hd).transpose(0, 2, 1, 3)

    q = jax.nn.silu(heads(q, HGRN_DK))
    v = heads(i_in, HGRN_DV)
    lbh = lb.reshape(2, 1, HGRN_HEADS, 1, HGRN_DK)
    z = jnp.stack([heads(f_fw, HGRN_DK), heads(f_bw, HGRN_DK)])
    f = lbh + (1.0 - lbh) * jax.nn.sigmoid(z)
    k = 1.0 - f
    logf = jnp.log(f)
    flip = lambda a: jnp.flip(a, axis=2)
    q_s = jnp.stack([q, flip(q)])
    v_s = jnp.stack([v, flip(v)])
    k_s = jnp.stack([k[0], flip(k[1])])
    g_s = jnp.stack([logf[0], flip(logf[1])])
    s_fin, o = gated_scan(q_s, k_s, v_s, g_s, s0.astype(jnp.float32))
    o = o[0] + flip(o[1])
    o = o * lax.rsqrt(jnp.mean(o * o, axis=-1, keepdims=True) + EPS) * g_norm.astype(jnp.float32)
    o = o.transpose(0, 2, 1, 3).reshape(b, l, d) * jax.nn.silu(g.astype(jnp.float32))
    return o.astype(h.dtype) @ w_o, s_fin


def trunk(x, cond, states_in, ada_w, ada_b, norm_mix, norm_mlp, fnet_wo, hgrn_w_in, lb_all,
          hgrn_norm, hgrn_wo, mlp_w1, mlp_w2, norm_final):
    states_out = []
    sc = jax.nn.silu(cond)
    for i in range(DEPTH):
        mod = (sc @ ada_w[i] + ada_b[i]).reshape(-1, 1, N_MOD * D_MODEL)
        sh_m, sc_m, gt_m, sh_f, sc_f, gt_f = jnp.split(mod, N_MOD, axis=-1)
        h = rmsnorm(x, norm_mix[i]) * (1.0 + sc_m) + sh_m
        j = i // N_MIXERS
        if i % N_MIXERS == 0:
            y = fourier_mixer(h, fnet_wo[j])
        else:
            y, s_fin = hgrn2_mixer(h, hgrn_w_in[j], lb_all[:, i], hgrn_norm[j], hgrn_wo[j], states_in[j])
            states_out.append(s_fin)
        x = x + gt_m * y
        h = rmsnorm(x, norm_mlp[i]) * (1.0 + sc_f) + sh_f
        x = x + gt_f * (jnp.square(jax.nn.relu(h @ mlp_w1[i])) @ mlp_w2[i])
    return rmsnorm(x, norm_final), states_out


def setup_inputs(seed: int = 0) -> dict:
    key = jax.random.key(seed)
    ks = jax.random.split(key, 20)
    f32 = jnp.float32
    nrm = lambda k, s, sc: jax.random.normal(k, s, f32) * sc
    return {
        "x_prompt": nrm(ks[0], (BATCH, SEQ, D_MODEL), 1.0),
        "x_sample": nrm(ks[1], (DEC_BATCH, DEC_SEQ, D_MODEL), 1.0),
        "state_hgrn": nrm(ks[2], (DEC_BATCH, N_HGRN, 2, HGRN_HEADS, HGRN_DK, HGRN_DV), 0.5),
        "c": nrm(ks[3], (DEC_BATCH, D_MODEL), 1.0),
        "c_ctx": nrm(ks[4], (D_MODEL,), 1.0),
        "ada_w": nrm(ks[5], (DEPTH, D_MODEL, N_MOD * D_MODEL), 0.5 * D_MODEL ** -0.5),
        "ada_b": nrm(ks[6], (DEPTH, N_MOD * D_MODEL), 0.02),
        "norm_mix": 1.0 + nrm(ks[7], (DEPTH, D_MODEL), 0.05),
        "norm_mlp": 1.0 + nrm(ks[8], (DEPTH, D_MODEL), 0.05),
        "fnet_wo": nrm(ks[9], (N_FNET, D_MODEL, D_MODEL), D_MODEL ** -0.5),
        "hgrn_w_in": nrm(ks[10], (N_HGRN, D_MODEL, 5 * D_MODEL), D_MODEL ** -0.5),
        "hgrn_lb": nrm(ks[11], (2, DEPTH, D_MODEL), 1.0),
        "hgrn_norm": 1.0 + nrm(ks[12], (N_HGRN, HGRN_DV), 0.05),
        "hgrn_wo": nrm(ks[13], (N_HGRN, D_MODEL, D_MODEL), D_MODEL ** -0.5),
        "mlp_w1": nrm(ks[14], (DEPTH, D_MODEL, D_FF), D_MODEL ** -0.5),
        "mlp_w2": nrm(ks[15], (DEPTH, D_FF, D_MODEL), D_FF ** -0.5),
        "norm_final": 1.0 + nrm(ks[16], (D_MODEL,), 0.05),
    }


def reference(x_prompt, x_sample, state_hgrn, c, c_ctx, ada_w, ada_b, norm_mix, norm_mlp, fnet_wo,
              hgrn_w_in, hgrn_lb, hgrn_norm, hgrn_wo, mlp_w1, mlp_w2, norm_final):
    sm = jax.nn.softmax(hgrn_lb.astype(jnp.float32), axis=1)
    lb_all = jnp.cumsum(sm, axis=1) - sm[:, :1]
    weights = (ada_w, ada_b, norm_mix, norm_mlp, fnet_wo, hgrn_w_in, lb_all, hgrn_norm, hgrn_wo,
               mlp_w1, mlp_w2, norm_final)

    bp = x_prompt.shape[0]
    zero_states = [jnp.zeros((2, bp, HGRN_HEADS, HGRN_DK, HGRN_DV), jnp.float32) for _ in range(N_HGRN)]
    y_prompt, ctx_states = trunk(x_prompt, c_ctx, zero_states, *weights)
    new_state_hgrn = jnp.stack([s.transpose(1, 0, 2, 3, 4) for s in ctx_states], axis=1)

    xs = x_sample + grid_pos_embed(x_sample.shape[1], D_MODEL, x_sample.dtype)[None]
    cached = [state_hgrn[:, j].transpose(1, 0, 2, 3, 4) for j in range(N_HGRN)]
    y_sample, _ = trunk(xs, c, cached, *weights)
    return (y_prompt, y_sample, new_state_hgrn)
```

```python
import numpy as np
import ml_dtypes
from contextlib import ExitStack
import concourse.bass as bass
import concourse.mybir as mybir
from concourse.bass_utils import run_bass_kernel_spmd

F32 = mybir.dt.float32
BF16 = mybir.dt.bfloat16
AF = mybir.ActivationFunctionType
ALU = mybir.AluOpType

D = 1024
T = 2560
TS = 2048
NT = 5
EPS = 1e-6
NBLK = 20


class _Op:
    pass


class Sched:
    ENG = ['pe', 'act', 'dve', 'pool', 'sp']

    def __init__(s):
        s.q = {e: [] for e in s.ENG}
        s.lastw = {}
        s.readers = {}
        s.barrier = []
        s.dma_ring = {'sp': 12, 'act': 6, 'pool': 12}
        s.dma_next = {k: 0 for k in s.dma_ring}
        s.slot_last = {}

    def _deps(s, reads, writes):
        d = []
        for b in reads:
            if b in s.lastw:
                d.append(s.lastw[b])
        for b in writes:
            if b in s.lastw:
                d.append(s.lastw[b])
            d.extend(s.readers.get(b, ()))
        d.extend(s.barrier)
        return d

    def _commit(s, o, reads, writes):
        for b in reads:
            s.readers.setdefault(b, []).append(o)
        for b in writes:
            s.lastw[b] = o
            s.readers[b] = []

    def op(s, eng, fn, reads=(), writes=()):
        psr = [b for b in reads if isinstance(b, str) and b.startswith('ps')]
        if psr:
            reads = [b for b in reads if b not in psr]
            writes = list(writes) + psr
        o = _Op()
        o.eng = eng; o.fn = fn; o.is_dma = False; o.sig = False; o.sigidx = 0
        o.deps = s._deps(reads, writes)
        s.q[eng].append(o)
        s._commit(o, reads, writes)
        return o

    def dma(s, q, fn, reads=(), writes=()):
        o = _Op()
        o.eng = q; o.fn = fn; o.is_dma = True; o.sig = False
        o.deps = s._deps(reads, writes)
        slot = s.dma_next[q]
        s.dma_next[q] = (slot + 1) % s.dma_ring[q]
        o.slot = (q, slot)
        prev = s.slot_last.get(o.slot)
        o.slotval = (prev.slotval if prev else 0) + 16
        if prev is not None:
            o.deps.append(prev)
        s.slot_last[o.slot] = o
        s.q[q].append(o)
        s._commit(o, reads, writes)
        return o

    def special(s, q, fn, key, reads=(), writes=()):
        o = _Op()
        o.eng = q; o.fn = fn; o.is_dma = True; o.sig = False
        o.deps = s._deps(reads, writes)
        o.slot = key
        o.slotval = 1
        o.incval = 1
        s.slot_last[o.slot] = o
        s.q[q].append(o)
        s._commit(o, reads, writes)
        return o

    def set_barrier(s):
        b = []
        for e in s.ENG:
            for o in reversed(s.q[e]):
                if not o.is_dma:
                    b.append(o)
                    break
        for o in s.slot_last.values():
            b.append(o)
        s.barrier = b

    def finalize(s):
        for e in s.ENG:
            for o in s.q[e]:
                for d in o.deps:
                    if d.is_dma:
                        continue
                    if d.eng == 'pe' and e == 'pe' and not o.is_dma:
                        continue
                    d.sig = True
        for e in s.ENG:
            c = 0
            for o in s.q[e]:
                if (not o.is_dma) and o.sig:
                    c += 1
                    o.sigidx = c
        for e in s.ENG:
            waited = {}
            for o in s.q[e]:
                w = {}
                for d in o.deps:
                    if d.is_dma:
                        key = ('dma',) + d.slot
                        val = d.slotval
                    else:
                        if d.eng == 'pe' and e == 'pe' and not o.is_dma:
                            continue
                        key = d.eng
                        val = d.sigidx
                    if waited.get(key, 0) >= val:
                        continue
                    if w.get(key, 0) < val:
                        w[key] = val
                for k, v in w.items():
                    waited[k] = v
                o.waits = list(w.items())

    def emit(s, nc, block, sems):
        def run(e, eng):
            for o in s.q[e]:
                for k, v in o.waits:
                    eng.wait_ge(sems[k], v)
                ins = o.fn(eng)
                if o.is_dma:
                    ins.then_inc(sems[('dma',) + o.slot], getattr(o, 'incval', 16))
                elif o.sig:
                    ins.then_inc(sems[e], 1)
            if e == 'sp':
                for slot, o in s.slot_last.items():
                    eng.wait_ge(sems[('dma',) + slot], o.slotval)

        @block.tensor
        def _(eng):
            run('pe', eng)

        @block.scalar
        def _(eng):
            run('act', eng)

        @block.vector
        def _(eng):
            run('dve', eng)

        @block.gpsimd
        def _(eng):
            run('pool', eng)

        @block.sync
        def _(eng):
            run('sp', eng)


def build(stage=4):
    nc = bass.Bass("TRN2", target_bir_lowering=False)
    S = Sched()

    def din(name, shape, dt=F32):
        return nc.dram_tensor(name, list(shape), dt, kind="ExternalInput").ap()

    x_my = din("x_my", [T, D]); pos_my = din("pos_my", [TS, D])
    x_all = din("x_all", [4608, D]); pos_all = din("pos_all", [4096, D])
    flp = din("flp", [2, 128, 512], BF16)
    cs = din("cs", [128, 2, 2, 256], BF16)
    cond = din("cond", [128, 16]); ada_w = din("ada_w", [2, D, 6 * D]); adab = din("adab", [128, 2, 48])
    nrm = din("nrm", [128, 4, 8]); nfin = din("nfin", [1, D])
    fnet_wo = din("fnet_wo", [D, D]); w_in = din("w_in", [D, 5 * D]); lbraw = din("lbraw", [128, 2, 2, 8])
    gnorm = din("gnorm", [128, 1]); hgrn_wo = din("hgrn_wo", [D, D])
    w1 = din("w1", [2, D, 4 * D]); w2 = din("w2", [2, 4 * D, D])
    sA0 = din("sA0", [128, 8, 128]); selm = din("selm", [128, 2])
    maskAB = din("maskAB", [128, 2, 128], BF16)
    identf = din("identf", [128, 128]); identb = din("identb", [128, 128], BF16)
    rmask = din("rmask", [128, 512])
    da_in = din("da_in", [128, 2, 128], BF16); mc_in = din("mc_in", [128, 64, 64], BF16)
    y_my = nc.dram_tensor("y_my", [T, D], F32, kind="ExternalOutput").ap()
    st_out = nc.dram_tensor("st_out", [2, 2, 8, 128, 128], F32, kind="ExternalOutput").ap()
    scr_x = nc.dram_tensor("scr_x", [8, 128, T], F32, kind="Internal").ap()
    scr_vec = nc.dram_tensor("scr_vec", [4, D], F32, kind="Internal").ap()
    zs = nc.dram_tensor("zs", [64, 64, 2, D], BF16, kind="Internal").ap()
    scr_og = nc.dram_tensor("scr_og", [8, 128, T], BF16, kind="Internal").ap()
    cc_in = nc.dram_tensor("cc_in", [8, 128, 128], F32, kind="Internal").ap()
    cc_out = nc.dram_tensor("cc_out", [8, 256, 128], F32, kind="Internal", addr_space="Local").ap()

    es = ExitStack()
    with es:
        def sb(name, shape, dt=F32):
            return es.enter_context(nc.sbuf_tensor(name, list(shape), dt))

        BIG = sb("BIG", [128, 20480], F32)
        HB = sb("HB", [128, 20480], BF16)
        WB = sb("WB", [128, 18432], BF16)
        MISC = sb("MISC", [128, 6144], F32)
        SM = sb("SM", [128, 1024], F32)
        IDF = sb("IDF", [128, 128], F32)
        IDB = sb("IDB", [128, 128], BF16)
        MSK = sb("MSK", [128, 2, 128], BF16)
        ONES = sb("ONES", [128, 128], BF16)
        SCB = sb("SCB", [128, 16], BF16)
        CST = sb("CST", [128, 2, 2, 256], BF16)
        RM = sb("RM", [128, 512], F32)
        DAT = sb("DAT", [128, 2, 128], BF16)
        MCT = sb("MCT", [128, 64, 64], BF16)
        ps = [es.enter_context(nc.psum_tensor(f"ps{i}", [128, 512], F32)) for i in range(8)]
        sems = {}
        for e in Sched.ENG:
            sems[e] = es.enter_context(nc.semaphore(f"sem_{e}"))
        for q, n in S.dma_ring.items():
            for i in range(n):
                sems[('dma', q, i)] = es.enter_context(nc.semaphore(f"dsem_{q}_{i}"))
        for hh_ in range(8):
            sems[('dma', 'cc', hh_)] = es.enter_context(nc.semaphore(f"ccsem_{hh_}"))

        xT = BIG[:, :].rearrange("p (k t) -> p k t", k=8)
        hT = HB[:, :].rearrange("p (k t) -> p k t", k=8)
        BIGb = BIG[:, :].bitcast(BF16)

        MOD = SM[:, 0:192].rearrange("p (i j c) -> p i j c", i=2, j=48)
        AM = SM[:, 192:448].rearrange("p (n k c) -> p n k c", n=4, k=8)
        SSQ = SM[:, 448:512]
        RSTD = SM[:, 512:576]
        NRM = SM[:, 576:608].rearrange("p (n k) -> p n k", n=4)
        ADB = SM[:, 608:704].rearrange("p (i j) -> p i j", i=2)
        CND = SM[:, 704:720]
        EPST = SM[:, 720:721]
        LB = SM[:, 736:768].rearrange("p (d l h) -> p d l h", d=2, l=2)
        LBV = SM[:, 768:784].rearrange("p (d h) -> p d h", d=2)
        OML = SM[:, 784:800].rearrange("p (d h) -> p d h", d=2)
        GN = SM[:, 800:801]
        SEL = SM[:, 801:803]
        ACOL = SM[:, 816:976].rearrange("p (d c) -> p d c", d=2)

        S.dma('sp', lambda e: e.dma_start(out=IDF[:, :], in_=identf[:, :]), [], ['IDF'])
        S.dma('sp', lambda e: e.dma_start(out=IDB[:, :], in_=identb[:, :]), [], ['IDB'])
        S.dma('sp', lambda e: e.dma_start(out=MSK[:, :, :], in_=maskAB[:, :, :]), [], ['MSK'])
        S.dma('sp', lambda e: e.dma_start(out=CST[:, :, :, :], in_=cs[:, :, :, :]), [], ['CST'])
        S.dma('sp', lambda e: e.dma_start(out=RM[:, :], in_=rmask[:, :]), [], ['RM'])
        S.dma('sp', lambda e: e.dma_start(out=CND, in_=cond[:, :]), [], ['CND'])
        S.dma('sp', lambda e: e.dma_start(out=ADB, in_=adab[:, :, :]), [], ['ADB'])
        S.dma('sp', lambda e: e.dma_start(out=NRM, in_=nrm[:, :, :]), [], ['NRM'])
        S.dma('sp', lambda e: e.dma_start(out=LB, in_=lbraw[:, :, :, :]), [], ['LB'])
        S.dma('sp', lambda e: e.dma_start(out=GN, in_=gnorm[:, :]), [], ['GN'])
        S.dma('sp', lambda e: e.dma_start(out=SEL, in_=selm[:, :]), [], ['SEL'])
        S.op('dve', lambda e: e.memset(ONES[:, :], 1.0), [], ['ONES'])
        S.op('dve', lambda e: e.memset(EPST, EPS), [], ['EPST'])

        import os
        _skip = os.environ.get('KSKIP', '').split(',')
        S.op('act', lambda e: e.activation(out=SCB[:, :], in_=CND, func=AF.Silu), ['CND'], ['SCB'])
        scb3 = SCB[:, :].rearrange("p (k c) -> p k c", c=2)
        AWB = [WB[:, 10240:14336].rearrange("p (k n) -> p k n", k=8), WB[:, 14336:18432].rearrange("p (k n) -> p k n", k=8)]
        ada_blocks = [(0, 0, 0), (0, 0, 1), (0, 1, 0), (0, 1, 1)] + [(0, j, hh) for j in range(2, 6) for hh in range(2)] \
            + [(1, j, hh) for j in range(6) for hh in range(2)]
        ada_cnt = [0]

        def ada_block():
            if ada_cnt[0] >= len(ada_blocks):
                return
            i, j, hh = ada_blocks[ada_cnt[0]]
            b = ada_cnt[0] % 2
            ada_cnt[0] += 1
            buf = AWB[b]; key = f"AWB{b}"
            src = ada_w[i, :, j * 1024 + hh * 512:j * 1024 + (hh + 1) * 512].rearrange("(k p) n -> p k n", p=128)
            S.dma('pool', lambda e, buf=buf, src=src: e.dma_start(out=buf, in_=src), [], [key])
            for o4 in range(4):
                oc = hh * 4 + o4
                col = (j * 8 + oc) * 2
                for kc in range(8):
                    S.op('pe', lambda e, buf=buf, o4=o4, kc=kc, col=col, i=i: e.matmul(
                        ps[i][:, col:col + 2], buf[:, kc, o4 * 128:(o4 + 1) * 128], scb3[:, kc, :],
                        start=(kc == 0), stop=(kc == 7)), [key, 'SCB'], [f"ps{i}"])

        def mod_evac(i, j0, j1):
            for c in range(2):
                S.op('dve', lambda e, i=i, c=c, j0=j0, j1=j1: e.tensor_tensor(
                    out=MOD[:, i, j0 * 8:j1 * 8, c], in0=ps[i][:, j0 * 16 + c:j1 * 16:2], in1=ADB[:, i, j0 * 8:j1 * 8], op=ALU.add),
                    [f"ps{i}", 'ADB'], [f"MOD{i}_{j0}"])

        def am_derive(n):
            i = n // 2; j = 1 if n % 2 == 0 else 4
            for c in range(2):
                S.op('dve', lambda e, n=n, i=i, j=j, c=c: e.scalar_tensor_tensor(
                    out=AM[:, n, :, c], in0=MOD[:, i, j * 8:(j + 1) * 8, c], scalar=1.0, in1=NRM[:, n, :],
                    op0=ALU.add, op1=ALU.mult), [f"MOD{i}_0", f"MOD{i}_2", 'NRM'], [f"AM{n}"])

        for _ in range(4):
            ada_block()
        mod_evac(0, 0, 2)
        am_derive(0)

        ALLX = [f"xT{k_}_{j_}" for k_ in range(8) for j_ in range(NT)]
        ALLH = [f"hT{k_}_{j_}" for k_ in range(8) for j_ in range(NT)]

        def modv(i, j, kc, c):
            return MOD[:, i, j * 8 + kc, c:c + 1]

        if 'fnet' not in _skip:
            ABC = WB[:, 6144:10240].rearrange("p (a n) -> p a n", a=4)
            DV = MISC[:, 4096:4608].bitcast(BF16).rearrange("p (k j) -> p k j", k=8)
            for v in range(4):
                c = v % 2
                vec = AM[:, 0, :, c] if v < 2 else MOD[:, 0, 0:8, c]
                S.op('dve', lambda e, vec=vec: e.tensor_tensor(
                    out=DV, in0=IDB[:, :].unsqueeze(1).to_broadcast([128, 8, 128]), in1=vec.unsqueeze(2).to_broadcast([128, 8, 128]), op=ALU.mult),
                    ['IDB', 'AM0', 'MOD0_0'], ['JUNK'])
                for hh in range(2):
                    S.op('pe', lambda e, hh=hh: e.matmul(ps[2 + hh][:, :], ONES[:, :], DV[:, hh * 4:(hh + 1) * 4, :], start=True, stop=True),
                         ['JUNK', 'ONES'], [f"ps{2 + hh}"])
                    if hh == 0:
                        S.op('act', lambda e, v=v, hh=hh: e.activation(out=ABC[:, v, hh * 512:(hh + 1) * 512], in_=ps[2 + hh][:, :], func=AF.Copy),
                             [f"ps{2 + hh}"], ['ABC'])
                    else:
                        S.op('dve', lambda e, v=v, hh=hh: e.tensor_copy(out=ABC[:, v, hh * 512:(hh + 1) * 512], in_=ps[2 + hh][:, :]),
                             [f"ps{2 + hh}"], ['ABC'])

            hn = BIGb[:, 0:36864].rearrange("p (c n) -> p c n", c=36)
            XIN = [MISC[:, 0:1024], MISC[:, 1024:2048]]
            PIN = [MISC[:, 2048:3072], MISC[:, 2048:3072]]
            TMPF = MISC[:, 3072:4096]
            JUNK = MISC[:, 4096:4608].bitcast(BF16)
            RB = MISC[:, 4608:5120]
            SQB = [MISC[:, 5120:5376].bitcast(BF16), MISC[:, 5376:5632].bitcast(BF16)]
            RL = [MISC[:, 5632:5888].bitcast(BF16), MISC[:, 5888:6144].bitcast(BF16)]
            XR = [XIN[0], XIN[1], WB[:, 0:2048].bitcast(F32)]
            PR = [PIN[0], WB[:, 2048:4096].bitcast(F32)]
            TR = [TMPF, WB[:, 4096:6144].bitcast(F32)]
            for ci in range(36):
                b = ci % 3; pb_ = ci % 2; tb_ = ci % 2
                c = 0 if ci < 32 else 1
                xk = f"XR{b}"; pk = f"PR{pb_}"; tk = f"TR{tb_}"
                if ci < 32:
                    for hf in range(2):
                        S.dma('sp', lambda e, ci=ci, b=b, hf=hf: e.dma_start(
                            out=XR[b][hf * 64:(hf + 1) * 64, :],
                            in_=x_all[0:4096, :].rearrange("(t1 r) c -> r t1 c", r=64)[32 * hf + ci]), [], [xk])
                        S.dma('sp', lambda e, ci=ci, pb_=pb_, hf=hf: e.dma_start(
                            out=PR[pb_][hf * 64:(hf + 1) * 64, :],
                            in_=pos_all[0:4096, :].rearrange("(t1 r) c -> r t1 c", r=64)[32 * hf + ci]), [], [pk])
                    S.op('pool', lambda e, b=b, pb_=pb_: e.tensor_tensor(out=XR[b][:, :], in0=XR[b][:, :], in1=PR[pb_][:, :], op=ALU.add),
                         [xk, pk], [xk])
                else:
                    S.dma('sp', lambda e, ci=ci, b=b: e.dma_start(out=XR[b][:, :], in_=x_all[ci * 128:(ci + 1) * 128, :]), [], [xk])
                S.op('act', lambda e, ci=ci, b=b: e.activation(out=JUNK[:, :], in_=XR[b][:, :], func=AF.Square,
                                                               accum_out=SSQ[:, ci:ci + 1]), [xk], ['JUNK', f"SSQ{ci}"])
                S.op('act', lambda e, ci=ci: e.activation(out=RSTD[:, ci:ci + 1], in_=SSQ[:, ci:ci + 1], func=AF.Sqrt,
                                                          bias=EPST, scale=1.0 / D), [f"SSQ{ci}", 'EPST'], [f"RSTD{ci}"])
                S.op('dve', lambda e, ci=ci: e.reciprocal(out=RSTD[:, ci:ci + 1], in_=RSTD[:, ci:ci + 1]),
                     [f"RSTD{ci}"], [f"RSTD{ci}"])
                S.op('dve', lambda e, ci=ci, b=b, c=c, tb_=tb_: e.scalar_tensor_tensor(
                    out=TR[tb_][:, :], in0=XR[b][:, :], scalar=RSTD[:, ci:ci + 1], in1=ABC[:, c, :],
                    op0=ALU.mult, op1=ALU.mult), [xk, f"RSTD{ci}", 'ABC'], [tk])
                S.op('dve', lambda e, ci=ci, c=c, tb_=tb_: e.tensor_tensor(out=hn[:, ci, :], in0=TR[tb_][:, :], in1=ABC[:, 2 + c, :], op=ALU.add),
                     [tk, 'ABC'], [f"hn{ci}"])
                ada_block()
            while ada_cnt[0] < len(ada_blocks):
                ada_block()
            mod_evac(0, 2, 6)
            mod_evac(1, 0, 2)
            mod_evac(1, 2, 6)
            am_derive(1); am_derive(2); am_derive(3)
            S.set_barrier()

            ZST = [WB[:, z * 2048:(z + 1) * 2048].rearrange("p (r c) -> p r c", r=2) for z in range(8)]
            MIXF = HB[:, 4096:20480].rearrange("p (k t) -> p k t", k=8)
            MIXP = HB[:, 0:4096].rearrange("p (k t) -> p k t", k=8)
            YTGS = [[WB[:, (2 * y_ + r_) * 4096:(2 * y_ + r_ + 1) * 4096].rearrange("p (k t) -> p k t", k=2) for r_ in range(2)] for y_ in range(2)]
            YTP = [WB[:, 0:4096].rearrange("p (k t) -> p k t", k=8), WB[:, 4096:8192].rearrange("p (k t) -> p k t", k=8)]
            FTP = [WB[:, 8192:9216].rearrange("p (c n) -> p c n", c=2), WB[:, 9216:10240].rearrange("p (c n) -> p c n", c=2)]
            WO = WB[:, 10240:18432].rearrange("p (k n) -> p k n", k=8)
            Z1T = [BIGb[:, 0:16384].rearrange("p (f c) -> p f c", f=64), BIGb[:, 16384:32768].rearrange("p (f c) -> p f c", f=64)]
            XTILES = [BIG[:, 0:4096].rearrange("p (k t) -> p k t", k=8), BIG[:, 4096:8192].rearrange("p (k t) -> p k t", k=8)]
            S.dma('sp', lambda e: e.dma_start(out=DAT[:, :, :], in_=da_in[:, :, :]), [], ['DAT'])
            S.dma('sp', lambda e: e.dma_start(out=MCT[:, :, :], in_=mc_in[:, :, :]), [], ['MCT'])
            for t2 in range(32):
                zb = t2 % 8
                for ri in range(2):
                    for ch in range(2):
                        bank = (t2 % 2) * 4 + ri * 2 + ch
                        S.op('pe', lambda e, t2=t2, ri=ri, ch=ch, bank=bank: e.matmul(
                            ps[bank][:, :], DAT[:, ri, :], hn[:, t2, ch * 512:(ch + 1) * 512], start=True, stop=True),
                            ['DAT', f"hn{t2}"], [f"ps{bank}"])
                        if (ri + ch) % 2 == 0:
                            S.op('act', lambda e, zb=zb, ri=ri, ch=ch, bank=bank: e.activation(
                                out=ZST[zb][:, ri, ch * 512:(ch + 1) * 512], in_=ps[bank][:, :], func=AF.Copy), [f"ps{bank}"], [f"ZST{zb}_{ri}_{ch}"])
                        else:
                            S.op('dve', lambda e, zb=zb, ri=ri, ch=ch, bank=bank: e.tensor_copy(
                                out=ZST[zb][:, ri, ch * 512:(ch + 1) * 512], in_=ps[bank][:, :]), [f"ps{bank}"], [f"ZST{zb}_{ri}_{ch}"])
                for hf in range(2):
                    S.dma('sp' if hf == 0 else 'pool', lambda e, t2=t2, zb=zb, hf=hf: e.dma_start(
                        out=zs[:, 32 * hf + t2, :, :], in_=ZST[zb][hf * 64:(hf + 1) * 64, :, :]),
                        [f"ZST{zb}_{r_}_{c_}" for r_ in range(2) for c_ in range(2)], [f"zs{t2}_{hf}"])
            S.set_barrier()
            for g in range(4):
                zb = g % 2
                YTG = YTGS[g % 2]; yk = f"YTG{g % 2}"
                for ri in range(2):
                    q_ = 'sp' if ri == 0 else 'act'
                    S.dma(q_, lambda e, g=g, zb=zb, ri=ri: e.dma_start(
                        out=Z1T[zb][ri * 64:(ri + 1) * 64, :, :],
                        in_=zs[:, :, ri, g * 256:(g + 1) * 256].rearrange("f t c -> t f c")), [], [f"Z1T{zb}"])
                for cl in range(2):
                    for a8 in range(8):
                        bank = a8
                        for f1l in range(8):
                            f1s = a8 * 8 + f1l
                            S.op('pe', lambda e, zb=zb, cl=cl, bank=bank, f1l=f1l, f1s=f1s: e.matmul(
                                ps[bank][:, f1l * 64:(f1l + 1) * 64], Z1T[zb][:, f1s, cl * 128:(cl + 1) * 128], MCT[:, f1s, :],
                                start=True, stop=True), [f"Z1T{zb}", 'MCT'], [f"ps{bank}"])
                        for ri in range(2):
                            src = ps[bank][:, :].rearrange("p (f r t) -> p f r t", f=8, r=2)[:, :, ri, :]
                            dst = YTG[ri][:, cl, :].rearrange("p (t f) -> p f t", f=64)[:, a8 * 8:(a8 + 1) * 8, :]
                            if a8 % 2 == 0:
                                S.op('act', lambda e, src=src, dst=dst: e.activation(out=dst, in_=src, func=AF.Copy), [f"ps{bank}"], [f"{yk}_{ri}_{cl}_{a8}"])
                            else:
                                S.op('dve', lambda e, src=src, dst=dst: e.tensor_copy(out=dst, in_=src), [f"ps{bank}"], [f"{yk}_{ri}_{cl}_{a8}"])
                for half in range(2):
                    oc = 2 * g + half
                    for jt in range(4):
                        bank = (half * 4 + jt) % 8
                        n = 0
                        for kc in range(2):
                            for ri in range(2):
                                S.op('pe', lambda e, half=half, kc=kc, ri=ri, jt=jt, n=n, bank=bank, YTG=YTG: e.matmul(
                                    ps[bank][:, :], CST[:, kc, ri, half * 128:(half + 1) * 128], YTG[ri][:, kc, jt * 512:(jt + 1) * 512],
                                    start=(n == 0), stop=(n == 3)), ['CST'] + [f"{yk}_{ri}_{kc}_{a}" for a in range(8)], [f"ps{bank}"])
                                n += 1
                        if jt % 2 == 0:
                            S.op('act', lambda e, oc=oc, jt=jt, bank=bank: e.activation(out=MIXF[:, oc, jt * 512:(jt + 1) * 512], in_=ps[bank][:, :], func=AF.Copy),
                                 [f"ps{bank}"], [f"MIXF{oc}_{jt}"])
                        else:
                            S.op('dve', lambda e, oc=oc, jt=jt, bank=bank: e.tensor_copy(out=MIXF[:, oc, jt * 512:(jt + 1) * 512], in_=ps[bank][:, :]),
                                 [f"ps{bank}"], [f"MIXF{oc}_{jt}"])
            S.set_barrier()
            S.dma('pool', lambda e: e.dma_start(out=WO, in_=fnet_wo[:, :].rearrange("(k p) n -> p k n", p=128)), [], ['WO'])
            ftc = [0]
            for sq in range(2):
                fb = ftc[0] % 2; ftc[0] += 1
                S.dma('sp', lambda e, fb=fb: e.dma_start(out=FTP[fb], in_=flp[:, :, :].rearrange("c p n -> p c n")), [], [f"FTP{fb}"])
                for tc in range(2):
                    for cc in range(8):
                        S.op('pe', lambda e, fb=fb, tc=tc, cc=cc, sq=sq: e.matmul(
                            ps[cc][:, :], hn[:, 32 + 2 * sq + tc, cc * 128:(cc + 1) * 128], FTP[fb][:, tc, :],
                            start=(tc == 0), stop=(tc == 1)), [f"FTP{fb}", f"hn{32 + 2 * sq + tc}"], [f"ps{cc}"])
                for cc in range(8):
                    for ri in range(2):
                        if cc % 2 == 0:
                            S.op('act', lambda e, cc=cc, ri=ri, sq=sq: e.activation(
                                out=YTP[ri][:, cc, sq * 256:(sq + 1) * 256], in_=ps[cc][:, ri * 256:(ri + 1) * 256], func=AF.Copy),
                                [f"ps{cc}"], [f"YTP{ri}"])
                        else:
                            S.op('dve', lambda e, cc=cc, ri=ri, sq=sq: e.tensor_copy(
                                out=YTP[ri][:, cc, sq * 256:(sq + 1) * 256], in_=ps[cc][:, ri * 256:(ri + 1) * 256]),
                                [f"ps{cc}"], [f"YTP{ri}"])
            for oc in range(8):
                g = oc // 2; half = oc % 2
                n = 0
                for kc in range(2):
                    for ri in range(2):
                        S.op('pe', lambda e, oc=oc, g=g, half=half, kc=kc, ri=ri, n=n: e.matmul(
                            ps[oc][:, :], CST[:, kc, ri, half * 128:(half + 1) * 128], YTP[ri][:, g * 2 + kc, :],
                            start=(n == 0), stop=(n == 3)), ['CST', f"YTP{ri}"], [f"ps{oc}"])
                        n += 1
                if oc % 2 == 0:
                    S.op('act', lambda e, oc=oc: e.activation(out=MIXP[:, oc, :], in_=ps[oc][:, :], func=AF.Copy), [f"ps{oc}"], [f"MIXF{oc}_4"])
                else:
                    S.op('dve', lambda e, oc=oc: e.tensor_copy(out=MIXP[:, oc, :], in_=ps[oc][:, :]), [f"ps{oc}"], [f"MIXF{oc}_4"])
            S.set_barrier()
            XQ = [XIN[0], XIN[1], WB[:, 0:2048].bitcast(F32), WB[:, 2048:4096].bitcast(F32)]
            PQ = [PIN[0], WB[:, 4096:6144].bitcast(F32), WB[:, 6144:8192].bitcast(F32)]
            xq_i = [0]
            for jt in range(NT):
                XT = XTILES[jt % 2]; xtk = f"XTILE{jt % 2}"
                for q4 in range(4):
                    b = xq_i[0] % 4; pb_ = xq_i[0] % 3; xq_i[0] += 1
                    r0 = jt * 512 + q4 * 128
                    S.dma('sp', lambda e, b=b, r0=r0: e.dma_start(out=XQ[b][:, :], in_=x_my[r0:r0 + 128, :]), [], [f"XQ{b}"])
                    if jt < 4:
                        S.dma('sp', lambda e, pb_=pb_, r0=r0: e.dma_start(out=PQ[pb_][:, :], in_=pos_my[r0:r0 + 128, :]), [], [f"PQ{pb_}"])
                        S.op('pool', lambda e, b=b, pb_=pb_: e.tensor_tensor(out=XQ[b][:, :], in0=XQ[b][:, :], in1=PQ[pb_][:, :], op=ALU.add),
                             [f"XQ{b}", f"PQ{pb_}"], [f"XQ{b}"])
                    for cc in range(8):
                        S.op('pe', lambda e, b=b, cc=cc, q4=q4: e.transpose(
                            ps[cc][:, q4 * 128:(q4 + 1) * 128], XQ[b][:, cc * 128:(cc + 1) * 128], IDF[:, :]),
                            [f"XQ{b}", 'IDF'], [f"ps{cc}"])
                for cc in range(8):
                    if cc % 2 == 0:
                        S.op('act', lambda e, cc=cc, XT=XT: e.activation(out=XT[:, cc, :], in_=ps[cc][:, :], func=AF.Copy), [f"ps{cc}"], [f"{xtk}_{cc}"])
                    else:
                        S.op('dve', lambda e, cc=cc, XT=XT: e.tensor_copy(out=XT[:, cc, :], in_=ps[cc][:, :]), [f"ps{cc}"], [f"{xtk}_{cc}"])
                c = 0 if jt < 4 else 1
                for oc in range(8):
                    for kc in range(8):
                        rhs = MIXF[:, kc, jt * 512:(jt + 1) * 512] if jt < 4 else MIXP[:, kc, :]
                        S.op('pe', lambda e, oc=oc, kc=kc, rhs=rhs: e.matmul(ps[oc][:, :], WO[:, kc, oc * 128:(oc + 1) * 128], rhs,
                                                                              start=(kc == 0), stop=(kc == 7)), ['WO', f"MIXF{kc}_{jt}"], [f"ps{oc}"])
                    S.op('dve', lambda e, oc=oc, c=c, XT=XT: e.scalar_tensor_tensor(
                        out=XT[:, oc, :], in0=ps[oc][:, :], scalar=modv(0, 2, oc, c), in1=XT[:, oc, :],
                        op0=ALU.mult, op1=ALU.add), [f"ps{oc}", f"{xtk}_{oc}", 'MOD0_2'], [f"{xtk}_{oc}"])
                S.dma('sp', lambda e, jt=jt, XT=XT: e.dma_start(
                    out=scr_x[:, :, jt * 512:(jt + 1) * 512].rearrange("k p t -> p k t"), in_=XT[:, :, :]),
                    [f"{xtk}_{c_}" for c_ in range(8)], [f"scrx{jt}"])

        S.set_barrier()
        S.dma('sp', lambda e: e.dma_start(out=xT, in_=scr_x[:, :, :].rearrange("k p t -> p k t")), [], ALLX)
        S.set_barrier()


        def norm_mod(n, i, jsh):
            TM2 = [TMPF[:, 0:512], TMPF[:, 512:1024]]
            for jt in range(NT):
                c = 0 if jt < 4 else 1
                for kc in range(8):
                    b = kc % 2
                    S.op('act', lambda e, kc=kc, jt=jt, b=b: e.activation(
                        out=SQB[b][:, :], in_=xT[:, kc, jt * 512:(jt + 1) * 512], func=AF.Square), [f"xT{kc}_{jt}"], [f"SQB{b}"])
                    S.op('pe', lambda e, kc=kc, b=b: e.matmul(ps[7][:, :], ONES[:, :], SQB[b][:, :], start=(kc == 0), stop=(kc == 7)),
                         [f"SQB{b}", 'ONES'], ['ps7'])
                S.op('act', lambda e: e.activation(out=RB[:, :], in_=ps[7][:, :], func=AF.Ln, bias=EPST, scale=1.0 / D),
                     ['ps7', 'EPST'], ['RB'])
                S.op('act', lambda e: e.activation(out=RB[:, :], in_=RB[:, :], func=AF.Exp, scale=-0.5), ['RB'], ['RB'])
                for kc in range(8):
                    tb_ = kc % 2
                    S.op('dve', lambda e, kc=kc, jt=jt, c=c, tb_=tb_: e.scalar_tensor_tensor(
                        out=TM2[tb_], in0=xT[:, kc, jt * 512:(jt + 1) * 512], scalar=AM[:, n, kc, c:c + 1], in1=RB[:, :],
                        op0=ALU.mult, op1=ALU.mult), [f"xT{kc}_{jt}", f"AM{n}", 'RB'], [f"TM2{tb_}"])
                    S.op('act', lambda e, kc=kc, jt=jt, c=c, tb_=tb_: e.activation(
                        out=hT[:, kc, jt * 512:(jt + 1) * 512], in_=TM2[tb_], func=AF.Identity,
                        bias=modv(i, jsh, kc, c), scale=1.0), [f"TM2{tb_}", f"MOD{i}_0", f"MOD{i}_2"], [f"hT{kc}_{jt}"])

        W1B = [WB[:, k * 4096:(k + 1) * 4096].rearrange("p (k n) -> p k n", k=8) for k in range(2)]
        W2B = [WB[:, 8192 + k * 4096:8192 + (k + 1) * 4096].rearrange("p (k n) -> p k n", k=4) for k in range(2)]
        UT = [MISC[:, 0:1024].bitcast(BF16).rearrange("p (k n) -> p k n", k=4), MISC[:, 1024:2048].bitcast(BF16).rearrange("p (k n) -> p k n", k=4)]
        wcnt = [0]

        def mlp(i):
            jg = 5
            for hp in range(8):
                wb = wcnt[0] % 2; wcnt[0] += 1
                S.dma('pool', lambda e, wb=wb, hp=hp: e.dma_start(
                    out=W1B[wb], in_=w1[i, :, hp * 512:(hp + 1) * 512].rearrange("(k p) n -> p k n", p=128)), [], [f"W1B{wb}"])
                S.dma('pool', lambda e, wb=wb, hp=hp: e.dma_start(
                    out=W2B[wb], in_=w2[i, hp * 512:(hp + 1) * 512, :].rearrange("(k p) n -> p k n", p=128)), [], [f"W2B{wb}", 'WO'])
                for jt in range(NT):
                    c = 0 if jt < 4 else 1
                    ub = jt % 2
                    for fc in range(4):
                        pb = fc % 4
                        for kc in range(8):
                            S.op('pe', lambda e, wb=wb, fc=fc, kc=kc, jt=jt, pb=pb: e.matmul(
                                ps[pb][:, :], W1B[wb][:, kc, fc * 128:(fc + 1) * 128], hT[:, kc, jt * 512:(jt + 1) * 512],
                                start=(kc == 0), stop=(kc == 7)), [f"W1B{wb}", f"hT{kc}_{jt}"], [f"ps{pb}"])
                        rb = fc % 2
                        S.op('act', lambda e, pb=pb, rb=rb: e.activation(out=RL[rb][:, :], in_=ps[pb][:, :], func=AF.Relu),
                             [f"ps{pb}"], [f"RL{rb}"])
                        S.op('pool', lambda e, rb=rb, ub=ub, fc=fc: e.tensor_tensor(
                            out=UT[ub][:, fc, :], in0=RL[rb][:, :], in1=RL[rb][:, :], op=ALU.mult), [f"RL{rb}"], [f"UT{ub}"])
                    for oc in range(8):
                        pb = 4 + oc % 4
                        for fc in range(4):
                            S.op('pe', lambda e, wb=wb, fc=fc, oc=oc, ub=ub, pb=pb: e.matmul(
                                ps[pb][:, :], W2B[wb][:, fc, oc * 128:(oc + 1) * 128], UT[ub][:, fc, :],
                                start=(fc == 0), stop=(fc == 3)), [f"W2B{wb}", f"UT{ub}"], [f"ps{pb}"])
                        S.op('dve', lambda e, oc=oc, jt=jt, c=c, pb=pb: e.scalar_tensor_tensor(
                            out=xT[:, oc, jt * 512:(jt + 1) * 512], in0=ps[pb][:, :], scalar=modv(i, jg, oc, c),
                            in1=xT[:, oc, jt * 512:(jt + 1) * 512], op0=ALU.mult, op1=ALU.add),
                            [f"ps{pb}", f"xT{oc}_{jt}", 'MOD0_2', 'MOD1_2'], [f"xT{oc}_{jt}"])

        if stage >= 2:
            norm_mod(1, 0, 3)
            mlp(0)
            if stage < 3:
                S.set_barrier()


        def hgrn():
            i = 1
            norm_mod(2, 1, 0)
            S.set_barrier()
            S.dma('sp', lambda e: e.dma_start(out=scr_x[:, :, :].rearrange("k p t -> p k t"), in_=xT), ALLX, ['scrx'])
            S.set_barrier()
            S.op('dve', lambda e: e.tensor_tensor(out=LBV, in0=LB[:, :, 1, :], in1=LB[:, :, 0, :], op=ALU.subtract), ['LB'], ['LBV'])
            S.op('act', lambda e: e.activation(out=LBV, in_=LBV, func=AF.Sigmoid), ['LBV'], ['LBV'])
            S.op('dve', lambda e: e.tensor_scalar(out=OML, in0=LBV, scalar1=-1.0, scalar2=1.0, op0=ALU.mult, op1=ALU.add), ['LBV'], ['OML'])
            S.dma('pool', lambda e: e.dma_start(out=WO, in_=hgrn_wo[:, :].rearrange("(k p) n -> p k n", p=128)), [], ['WO'])
            WH = [WB[:, w * 5120:(w + 1) * 5120].rearrange("p (k j n) -> p k j n", k=8, j=5) for w in range(2)]
            QE = [BIGb[:, 0:2560], BIGb[:, 2560:5120]]
            KE = [BIGb[:, 5120:7680], BIGb[:, 7680:10240]]
            KDT = [BIGb[:, 10240:12800].rearrange("p (b k) -> p b k", b=20), BIGb[:, 12800:15360].rearrange("p (b k) -> p b k", b=20)]
            VT = BIGb[:, 15360:17920].rearrange("p (b k) -> p b k", b=20)
            GS = BIGb[:, 17920:20480]
            OA = BIGb[:, 20480:23040]
            def wt(n):
                return BIG[:, 12288 + n * 512:12288 + (n + 1) * 512]
            QS = wt(0)
            SG = [wt(1), wt(2)]; FT_ = [wt(3), wt(4)]; LG = [wt(5), wt(6)]; KT = [wt(7), wt(8)]
            BB = [wt(9), wt(10)]; E1 = [wt(11), wt(12)]; E2 = [wt(13), wt(14)]
            SST = [MISC[:, 0:128], MISC[:, 3584:3712], MISC[:, 3712:3840], MISC[:, 3840:3968]]
            SR = MISC[:, 128:384].rearrange("p (r n) -> p r n", r=2)
            OS = MISC[:, 384:896]; Ro = MISC[:, 896:1408]; T1 = MISC[:, 1408:1920]
            OSB = [MISC[:, 384:896], MISC[:, 4096:4608]]
            mb = MISC[:, 1920:3584].bitcast(BF16)
            SBR = [mb[:, r * 128:(r + 1) * 128] for r in range(8)]
            SCb = [mb[:, 1024:1152], mb[:, 1152:1280]]
            SQo = mb[:, 1280:1792]; OGh = mb[:, 1792:2304]
            KDt = [mb[:, 2304:2816], mb[:, 2816:3328]]
            psb5 = ps[6][:, :].bitcast(BF16)
            sbc = [0]
            def load_wh(h_):
                wh_ = WH[h_ % 2]
                for j in range(5):
                    S.dma('pool', lambda e, wh_=wh_, j=j, h_=h_: e.dma_start(
                        out=wh_[:, :, j, :], in_=w_in[:, j * 1024 + h_ * 128:j * 1024 + (h_ + 1) * 128].rearrange("(k p) n -> p k n", p=128)),
                        [], [f"WH{h_ % 2}"])
            load_wh(0)
            for h in range(8):
                wh = WH[h % 2]; whk = f"WH{h % 2}"
                if h + 1 < 8:
                    load_wh(h + 1)
                def elementwise(jt):
                    tsl = slice(jt * 512, (jt + 1) * 512)
                    def proj(pj, bank):
                        for kc in range(8):
                            S.op('pe', lambda e, pj=pj, bank=bank, kc=kc, wh=wh, tsl=tsl: e.matmul(
                                ps[bank][:, :], wh[:, kc, pj, :], hT[:, kc, tsl], start=(kc == 0), stop=(kc == 7)),
                                [whk, f"hT{kc}_{jt}"], [f"ps{bank}"])
                    proj(0, 4)
                    S.op('act', lambda e: e.activation(out=QS, in_=ps[4][:, :], func=AF.Silu), ['ps4'], ['QS'])
                    proj(1, 6)
                    S.op('act', lambda e: e.activation(out=SG[0], in_=ps[6][:, :], func=AF.Sigmoid), ['ps6'], ['SG0'])
                    proj(2, 4)
                    S.op('act', lambda e: e.activation(out=SG[1], in_=ps[4][:, :], func=AF.Sigmoid), ['ps4'], ['SG1'])
                    proj(4, 6)
                    S.op('act', lambda e, tsl=tsl: e.activation(out=GS[:, tsl], in_=ps[6][:, :], func=AF.Silu), ['ps6'], [f"GS{jt}"])
                    for q4 in range(4):
                        blk = jt * 4 + q4
                        for kc in range(8):
                            S.op('pe', lambda e, kc=kc, q4=q4, blk=blk, wh=wh: e.matmul(
                                ps[4][:, q4 * 128:(q4 + 1) * 128], hT[:, kc, blk * 128:(blk + 1) * 128], wh[:, kc, 3, :],
                                start=(kc == 0), stop=(kc == 7)), [whk, f"hT{kc}_{jt}"], ['ps4'])
                    S.op('dve', lambda e, jt=jt: e.tensor_copy(out=VT[:, jt * 4:(jt + 1) * 4, :], in_=ps[4][:, :].rearrange("p (b k) -> p b k", b=4)),
                         ['ps4'], [f"VT{jt}"])
                    def estep(k, d):
                        lastcol = 31 if d == 0 else 0
                        if k == 0:
                            S.op('dve', lambda e, d=d, h=h: e.tensor_scalar(out=FT_[d], in0=SG[d], scalar1=OML[:, d, h:h + 1], scalar2=LBV[:, d, h:h + 1],
                                                                            op0=ALU.mult, op1=ALU.add), [f"SG{d}", 'OML', 'LBV'], [f"F{d}"])
                        elif k == 1:
                            S.op('act', lambda e, d=d: e.activation(out=LG[d], in_=FT_[d], func=AF.Ln), [f"F{d}"], [f"LG{d}"])
                            S.op('pool', lambda e, d=d: e.tensor_scalar(out=KT[d], in0=FT_[d], scalar1=-1.0, scalar2=1.0, op0=ALU.mult, op1=ALU.add),
                                 [f"F{d}"], [f"KT{d}"])
                        elif k == 2:
                            if d == 0:
                                S.op('dve', lambda e, d=d: e.tensor_tensor_scan(out=BB[d], data0=RM[:, :], data1=LG[d], initial=0.0, op0=ALU.mult, op1=ALU.add),
                                     ['RM', f"LG{d}"], [f"BB{d}"])
                            else:
                                S.op('dve', lambda e, d=d: e.tensor_tensor_scan(out=BB[d][:, ::-1], data0=RM[:, :], data1=LG[d][:, ::-1], initial=0.0,
                                                                                op0=ALU.mult, op1=ALU.add), ['RM', f"LG{d}"], [f"BB{d}"])
                        elif k == 3:
                            S.op('act', lambda e, d=d: e.activation(out=E1[d], in_=BB[d], func=AF.Exp), [f"BB{d}"], [f"E1{d}"])
                            S.op('act', lambda e, d=d: e.activation(out=E2[d], in_=BB[d], func=AF.Exp, scale=-1.0), [f"BB{d}"], [f"E2{d}"])
                        elif k == 4:
                            S.op('dve', lambda e, d=d, tsl=tsl: e.tensor_tensor(out=QE[d][:, tsl], in0=QS, in1=E1[d], op=ALU.mult),
                                 ['QS', f"E1{d}"], [f"QE{d}_{jt}"])
                            S.op('pool', lambda e, d=d, jt=jt, lastcol=lastcol: e.tensor_copy(out=ACOL[:, d, jt * 16:(jt + 1) * 16], in_=E1[d][:, lastcol::32]),
                                 [f"E1{d}"], [f"ACOL{d}_{jt}"])
                            S.op('dve', lambda e, d=d, tsl=tsl: e.tensor_tensor(out=KE[d][:, tsl], in0=KT[d], in1=E2[d], op=ALU.mult),
                                 [f"KT{d}", f"E2{d}"], [f"KE{d}_{jt}"])
                        elif k == 5:
                            S.op('pool', lambda e, d=d, tsl=tsl, jt=jt: e.tensor_tensor(
                                out=KDt[d].rearrange("p (c j) -> p c j", j=32), in0=KE[d][:, tsl].rearrange("p (c j) -> p c j", j=32),
                                in1=ACOL[:, d, jt * 16:(jt + 1) * 16].unsqueeze(2).to_broadcast([128, 16, 32]), op=ALU.mult),
                                [f"KE{d}_{jt}", f"ACOL{d}_{jt}"], [f"KDt{d}"])
                        elif k == 6:
                            for q4 in range(4):
                                S.op('pe', lambda e, d=d, q4=q4: e.transpose(psb5[:, q4 * 128:(q4 + 1) * 128], KDt[d][:, q4 * 128:(q4 + 1) * 128], IDB[:, :]),
                                     [f"KDt{d}", 'IDB'], ['ps6'])
                            S.op('dve', lambda e, d=d, jt=jt: e.tensor_copy(out=KDT[d][:, jt * 4:(jt + 1) * 4, :], in_=psb5[:, 0:512].rearrange("p (b k) -> p b k", b=4)),
                                 ['ps6'], [f"KDT{d}_{jt}"])
                    for k in range(7):
                        for d in range(2):
                            estep(k, d)

                def seg_start(d, init):
                    st = {'d': d, 'cs': 0}
                    S0 = SST[0]
                    if init[0] == 'dram':
                        S.dma('sp', lambda e: e.dma_start(out=S0, in_=init[1]), [], ['Sst0'])
                    elif init[0] == 'zero':
                        S.op('dve', lambda e: e.memset(S0, 0.0), [], ['Sst0'])
                    else:
                        S.op('dve', lambda e: e.tensor_scalar(out=S0, in0=SR[:, 0, :], scalar1=SEL[:, 0:1], scalar2=None, op0=ALU.mult),
                             ['SR', 'SEL'], ['Sst0'])
                        S.op('dve', lambda e: e.scalar_tensor_tensor(out=S0, in0=SR[:, 1, :], scalar=SEL[:, 1:2], in1=S0, op0=ALU.mult, op1=ALU.add),
                             ['SR', 'SEL', 'Sst0'], ['Sst0'])
                    cur = sbc[0] % 8; sbc[0] += 1
                    S.op('act', lambda e, cur=cur: e.activation(out=SBR[cur], in_=S0, func=AF.Copy), ['Sst0'], [f"SBR{cur}"])
                    st['cur'] = cur
                    return st

                def seg_pre(st, blk, parts):
                    d = st['d']
                    bsl = slice(blk * 128, (blk + 1) * 128)
                    sc = blk % 2
                    tj = blk // 4
                    for pt in parts:
                        if pt == 's':
                            S.op('pe', lambda e, d=d, bsl=bsl: e.matmul(ps[7][:, 0:128], KE[d][:, bsl], QE[d][:, bsl], start=True, stop=True),
                                 [f"KE{d}_{tj}", f"QE{d}_{tj}"], ['ps7'])
                        elif pt == 'm':
                            S.op('dve', lambda e, d=d, sc=sc: e.tensor_tensor(out=SCb[sc], in0=ps[7][:, 0:128], in1=MSK[:, d, :], op=ALU.mult),
                                 ['ps7', 'MSK'], [f"SCb{sc}"])
                        elif pt == 'o':
                            S.op('pe', lambda e, blk=blk, sc=sc: e.matmul(ps[5][:, 0:128], VT[:, blk, :], SCb[sc], start=True, stop=False),
                                 [f"VT{tj}", f"SCb{sc}"], ['ps5'])
                        else:
                            c = int(pt[1])
                            kw = dict(tile_position=(96, 0)) if c == 3 else {}
                            S.op('pe', lambda e, d=d, c=c, blk=blk, kw=kw: e.matmul(
                                ps[c][:, 0:128], KDT[d][c * 32:(c + 1) * 32, blk, :], VT[c * 32:(c + 1) * 32, blk, :], start=True, stop=True, **kw),
                                [f"KDT{d}_{tj}", f"VT{tj}"], [f"ps{c}"])

                def seg_block(st, blk, nb=None):
                    d = st['d']
                    bsl = slice(blk * 128, (blk + 1) * 128)
                    po = 5
                    tj = blk // 4
                    if st.get('pre') != blk:
                        seg_pre(st, blk, ['s', 'm', 'u0', 'u1', 'u2', 'u3', 'o'])
                    corder = range(4) if d == 0 else range(3, -1, -1)
                    for n, c in enumerate(corder):
                        cur = st['cur']; cs = st['cs']
                        S.op('pe', lambda e, d=d, c=c, blk=blk, po=po, cur=cur, n=n: e.matmul(
                            ps[po][:, c * 32:(c + 1) * 32], SBR[cur], QE[d][:, blk * 128 + c * 32:blk * 128 + (c + 1) * 32],
                            start=False, stop=(n == 3)), [f"SBR{cur}", f"QE{d}_{tj}"], [f"ps{po}"])
                        ns = (cs + 1) % 4
                        S.op('dve', lambda e, d=d, c=c, blk=blk, cs=cs, ns=ns: e.scalar_tensor_tensor(
                            out=SST[ns], in0=SST[cs], scalar=ACOL[:, d, blk * 4 + c:blk * 4 + c + 1], in1=ps[c][:, 0:128], op0=ALU.mult, op1=ALU.add),
                            [f"Sst{cs}", f"ACOL{d}_{tj}", f"ps{c}"], [f"Sst{ns}"])
                        st['cs'] = ns
                        cur = sbc[0] % 8; sbc[0] += 1
                        st['cur'] = cur
                        if n % 2 == 0:
                            S.op('act', lambda e, cur=cur, ns=ns: e.activation(out=SBR[cur], in_=SST[ns], func=AF.Copy), [f"Sst{ns}"], [f"SBR{cur}"])
                        else:
                            S.op('pool', lambda e, cur=cur, ns=ns: e.tensor_copy(out=SBR[cur], in_=SST[ns]), [f"Sst{ns}"], [f"SBR{cur}"])
                        if nb is not None:
                            seg_pre(st, nb, [f"u{c}"])
                            if n == 0:
                                seg_pre(st, nb, ['s'])
                            if n == 3:
                                seg_pre(st, nb, ['m'])
                    if d == 0:
                        S.op('act', lambda e, bsl=bsl, po=po: e.activation(out=OA[:, bsl], in_=ps[po][:, 0:128], func=AF.Copy), [f"ps{po}"], [f"OA{blk}"])
                    else:
                        q4 = blk % 4
                        ob = (blk // 4) % 2
                        OSx = OSB[ob]
                        S.op('dve', lambda e, bsl=bsl, po=po, q4=q4, OSx=OSx: e.tensor_tensor(
                            out=OSx[:, q4 * 128:(q4 + 1) * 128], in0=ps[po][:, 0:128], in1=OA[:, bsl], op=ALU.add), [f"ps{po}", f"OA{blk}"], [f"OS{ob}_{q4}"])
                    if nb is not None:
                        seg_pre(st, nb, ['o'])
                        st['pre'] = nb
                    if d == 1:
                        if st.get('pending') is not None and st.get('pend_at') == blk:
                            st['pending'](); st['pending'] = None
                        q4 = blk % 4
                        if q4 == 0:
                            jt = blk // 4
                            tsl = slice(jt * 512, (jt + 1) * 512)
                            ob = jt % 2
                            OSx = OSB[ob]
                            oskeys = [f"OS{ob}_{q_}" for q_ in range(4)]

                            def onorm(jt=jt, tsl=tsl, OSx=OSx, oskeys=oskeys):
                                S.op('act', lambda e: e.activation(out=SQo, in_=OSx, func=AF.Square), oskeys, ['SQo'])
                                S.op('pe', lambda e: e.matmul(ps[6][:, :], ONES[:, :], SQo, start=True, stop=True), ['SQo', 'ONES'], ['ps6'])
                                S.op('act', lambda e: e.activation(out=Ro, in_=ps[6][:, :], func=AF.Ln, bias=EPST, scale=1.0 / 128.0),
                                     ['ps6', 'EPST'], ['Ro'])
                                S.op('act', lambda e: e.activation(out=Ro, in_=Ro, func=AF.Exp, scale=-0.5), ['Ro'], ['Ro'])
                                S.op('dve', lambda e: e.scalar_tensor_tensor(out=T1, in0=OSx, scalar=GN, in1=Ro, op0=ALU.mult, op1=ALU.mult),
                                     oskeys + ['GN', 'Ro'], ['T1'])
                                S.op('pool', lambda e: e.tensor_tensor(out=OGh, in0=T1, in1=GS[:, tsl], op=ALU.mult), ['T1', f"GS{jt}"], ['OGh'])
                                S.dma('sp', lambda e, h=h: e.dma_start(out=scr_og[h, :, tsl], in_=OGh), ['OGh'], [f"scrog{h}_{jt}"])
                            if nb is not None and (nb // 4) != jt and (nb % 4) == 3:
                                st['pending'] = onorm
                                st['pend_at'] = nb - 1
                            else:
                                onorm()

                def seg_end(st, fin):
                    if st.get('pending') is not None:
                        st['pending'](); st['pending'] = None
                    cs = st['cs']
                    if fin[0] == 'state':
                        S.dma('sp', lambda e: e.dma_start(out=fin[1], in_=SST[cs]), [f"Sst{cs}"], [])
                    elif fin[0] == 'cc':
                        S.dma('pool', lambda e, h=h: e.dma_start(out=cc_in[h, :, :], in_=SST[cs]), [f"Sst{cs}"], [f"ccin{h}"])
                        S.special('pool', lambda e, h=h: e.collective_compute(
                            "AllGather", ALU.bypass, replica_groups=[[0, 1], [2, 3], [4, 5], [6, 7]],
                            ins=[cc_in[h, :, :]], outs=[cc_out[h, :, :]]), ('cc', h), [f"ccin{h}"], [f"ccout{h}"])
                        S.dma('sp', lambda e, h=h: e.dma_start(out=SR, in_=cc_out[h, :, :].rearrange("(r p) n -> p r n", p=128)),
                              [f"ccout{h}"], ['SR'])

                def run_segment(d, blocks, init, fin):
                    st = seg_start(d, init)
                    for i_, blk in enumerate(blocks):
                        seg_block(st, blk, blocks[i_ + 1] if i_ + 1 < len(blocks) else None)
                    seg_end(st, fin)

                elementwise(0)
                elementwise(1)
                stA = seg_start(0, ('dram', sA0[:, h, :]))
                for jt in range(4):
                    for q4 in range(4):
                        blk = jt * 4 + q4
                        seg_block(stA, blk, blk + 1 if blk + 1 < 16 else None)
                    if jt == 3:
                        seg_end(stA, ('cc',))
                    if jt + 2 < NT:
                        elementwise(jt + 2)
                run_segment(0, [16, 17], ('zero',), ('state', st_out[0, 0, h, :, :]))
                run_segment(0, [18, 19], ('zero',), ('state', st_out[1, 0, h, :, :]))
                run_segment(1, [19, 18], ('zero',), ('state', st_out[1, 1, h, :, :]))
                run_segment(1, [17, 16], ('zero',), ('state', st_out[0, 1, h, :, :]))
                run_segment(1, list(range(15, -1, -1)), ('sel',), ('none',))
            S.set_barrier()
            OGT = hT
            S.dma('sp', lambda e: e.dma_start(out=OGT, in_=scr_og[:, :, :].rearrange("h p t -> p h t")), [], ALLH)
            S.dma('act', lambda e: e.dma_start(out=xT, in_=scr_x[:, :, :].rearrange("k p t -> p k t")), ['scrx'], ALLX)
            S.set_barrier()
            for jt in range(NT):
                c = 0 if jt < 4 else 1
                tsl = slice(jt * 512, (jt + 1) * 512)
                for oc in range(8):
                    pb = oc % 4
                    for kc in range(8):
                        S.op('pe', lambda e, oc=oc, kc=kc, tsl=tsl, pb=pb: e.matmul(ps[pb][:, :], WO[:, kc, oc * 128:(oc + 1) * 128], OGT[:, kc, tsl],
                                                                                   start=(kc == 0), stop=(kc == 7)), ['WO', f"hT{kc}_{jt}"], [f"ps{pb}"])
                    S.op('dve', lambda e, oc=oc, tsl=tsl, c=c, pb=pb: e.scalar_tensor_tensor(
                        out=xT[:, oc, tsl], in0=ps[pb][:, :], scalar=modv(1, 2, oc, c), in1=xT[:, oc, tsl], op0=ALU.mult, op1=ALU.add),
                        [f"ps{pb}", f"xT{oc}_{jt}", 'MOD0_2', 'MOD1_2'], [f"xT{oc}_{jt}"])
            if stage < 4:
                S.set_barrier()

        if stage >= 3:
            hgrn()
        if stage >= 4:
            norm_mod(3, 1, 3)
            mlp(1)

        OUTB = [WB[:, 16384:18432].bitcast(F32), MISC[:, 3072:4096]]
        NFB = MISC[:, 2048:3072]
        S.dma('sp', lambda e: e.dma_start(out=NFB[:, :], in_=nfin[:, :].partition_broadcast(128)), [], ['NFB'])
        final_norm = (stage >= 4)
        for bk in range(NBLK):
            ob = bk % 2
            pp = (bk % 2) * 4
            for kc in range(8):
                bank = pp + kc // 4
                S.op('pe', lambda e, kc=kc, bk=bk, bank=bank: e.transpose(
                    ps[bank][:, (kc % 4) * 128:(kc % 4 + 1) * 128], xT[:, kc, bk * 128:(bk + 1) * 128], IDF[:, :]),
                    [f"xT{kc}_{bk // 4}", 'IDF'], [f"ps{bank}"])
            for hh in range(2):
                bank = pp + hh
                if not final_norm:
                    S.op('act', lambda e, ob=ob, hh=hh, bank=bank: e.activation(
                        out=OUTB[ob][:, hh * 512:(hh + 1) * 512], in_=ps[bank][:, :], func=AF.Copy), [f"ps{bank}"], [f"OUTB{ob}"])
                else:
                    S.op('act', lambda e, hh=hh, bank=bank, bk=bk: e.activation(
                        out=JUNK[:, 0:512], in_=ps[bank][:, :], func=AF.Square, accum_out=SSQ[:, hh:hh + 1]),
                        [f"ps{bank}"], ['JUNK', f"SSQ{hh}"])
            if final_norm:
                S.op('dve', lambda e: e.tensor_tensor(out=SSQ[:, 2:3], in0=SSQ[:, 0:1], in1=SSQ[:, 1:2], op=ALU.add),
                     ['SSQ0', 'SSQ1'], ['SSQ2'])
                S.op('act', lambda e: e.activation(out=RSTD[:, 0:1], in_=SSQ[:, 2:3], func=AF.Sqrt, bias=EPST, scale=1.0 / D),
                     ['SSQ2', 'EPST'], ['RSTD0'])
                S.op('dve', lambda e: e.reciprocal(out=RSTD[:, 0:1], in_=RSTD[:, 0:1]), ['RSTD0'], ['RSTD0'])
                for hh in range(2):
                    bank = pp + hh
                    S.op('dve', lambda e, ob=ob, hh=hh, bank=bank: e.scalar_tensor_tensor(
                        out=OUTB[ob][:, hh * 512:(hh + 1) * 512], in0=ps[bank][:, :], scalar=RSTD[:, 0:1],
                        in1=NFB[:, hh * 512:(hh + 1) * 512], op0=ALU.mult, op1=ALU.mult),
                        [f"ps{bank}", 'RSTD0', 'NFB'], [f"OUTB{ob}"])
            S.dma('sp', lambda e, ob=ob, bk=bk: e.dma_start(out=y_my[bk * 128:(bk + 1) * 128, :], in_=OUTB[ob][:, :]),
                  [f"OUTB{ob}"], [])

        S.finalize()
        with nc.Block() as block:
            S.emit(nc, block, sems)
    return nc


GRID_W = 64


def _pos_embed(n_tok, d):
    rows = n_tok // GRID_W
    quarter = d // 4
    omega = (1.0 / (10000.0 ** (np.arange(quarter, dtype=np.float32) / quarter))).astype(np.float32)
    r = np.arange(rows, dtype=np.float32)[:, None] * omega[None, :]
    cl = np.arange(GRID_W, dtype=np.float32)[:, None] * omega[None, :]
    er = np.concatenate([np.sin(r), np.cos(r)], axis=-1)
    ec = np.concatenate([np.sin(cl), np.cos(cl)], axis=-1)
    emb = np.concatenate([np.broadcast_to(er[:, None, :], (rows, GRID_W, d // 2)),
                          np.broadcast_to(ec[None, :, :], (rows, GRID_W, d // 2))], axis=-1)
    return emb.reshape(n_tok, d).astype(np.float32)


def _tables(a):
    bf = ml_dtypes.bfloat16
    t1 = np.arange(64); f1s = np.arange(64)
    f1 = f1s if a == 0 else 63 - f1s
    angA = (2.0 * np.pi / 64.0) * ((t1[:, None] * f1[None, :]) % 64).astype(np.float64)
    da = np.zeros((128, 2, 128), np.float64)
    for hf in range(2):
        da[hf * 64:(hf + 1) * 64, 0, hf * 64:(hf + 1) * 64] = np.cos(angA) / 8.0
        da[hf * 64:(hf + 1) * 64, 1, hf * 64:(hf + 1) * 64] = -np.sin(angA) / 8.0
    t2 = np.arange(64); f2s = np.arange(32)
    f2 = f2s if a == 0 else 63 - f2s
    f = f1[:, None] + 64 * f2[None, :]
    th = (2.0 * np.pi / 4096.0) * ((t2[:, None, None].astype(np.int64) * f[None, :, :].astype(np.int64)) % 4096).astype(np.float64)
    co = np.cos(th) / 8.0; si = np.sin(th) / 8.0
    mc = np.zeros((2, 64, 64, 2, 32), np.float64)
    mc[0, :, :, 0, :] = co; mc[0, :, :, 1, :] = -si
    mc[1, :, :, 0, :] = si; mc[1, :, :, 1, :] = co
    mc = mc.reshape(128, 64, 64)
    tp = np.arange(256)
    fp = tp if a == 0 else 255 - tp
    angp = (2.0 * np.pi / 256.0) * ((tp[:, None] * fp[None, :]) % 256).astype(np.float64)
    flp = np.concatenate([np.cos(angp) / 16.0, -np.sin(angp) / 16.0], axis=1)
    flp = np.ascontiguousarray(flp.reshape(2, 128, 512)).astype(bf)
    c = np.arange(256)
    angc = (2.0 * np.pi / 256.0) * ((c[:, None] * c[None, :]) % 256).astype(np.float64)
    cc = np.cos(angc) / 16.0; sc = np.sin(angc) / 16.0
    cst = np.stack([cc, sc], axis=1)
    cst = cst.reshape(2, 128, 2, 256).transpose(1, 0, 2, 3)
    cst = np.ascontiguousarray(cst).astype(bf)
    return np.ascontiguousarray(da).astype(bf), np.ascontiguousarray(mc).astype(bf), flp, cst


_NC_CACHE = {}
STAGE = 4


def kernel(x_prompt, x_sample, state_hgrn, c, c_ctx, ada_w, ada_b, norm_mix, norm_mlp, fnet_wo,
           hgrn_w_in, hgrn_lb, hgrn_norm, hgrn_wo, mlp_w1, mlp_w2, norm_final, _stage=None):
    stage = STAGE if _stage is None else _stage
    f32 = np.float32
    bf = ml_dtypes.bfloat16
    x_prompt = np.asarray(x_prompt, f32); x_sample = np.asarray(x_sample, f32)
    state_hgrn = np.asarray(state_hgrn, f32); c = np.asarray(c, f32); c_ctx = np.asarray(c_ctx, f32)
    ada_w = np.ascontiguousarray(np.asarray(ada_w, f32)); ada_b = np.asarray(ada_b, f32)
    norm_mix = np.asarray(norm_mix, f32); norm_mlp = np.asarray(norm_mlp, f32)
    fnet_wo = np.ascontiguousarray(np.asarray(fnet_wo, f32)[0]); hgrn_w_in = np.asarray(hgrn_w_in, f32)[0]
    hgrn_lb = np.asarray(hgrn_lb, f32); hgrn_norm = np.asarray(hgrn_norm, f32)
    hgrn_wo = np.ascontiguousarray(np.asarray(hgrn_wo, f32)[0])
    mlp_w1 = np.ascontiguousarray(np.asarray(mlp_w1, f32)); mlp_w2 = np.ascontiguousarray(np.asarray(mlp_w2, f32))
    norm_final = np.asarray(norm_final, f32)

    if stage not in _NC_CACHE:
        _NC_CACHE[stage] = build(stage)
    nc = _NC_CACHE[stage]
    pos = _pos_embed(4096, D)
    tabs = [_tables(0), _tables(1)]
    adab = np.ascontiguousarray(ada_b.reshape(2, 48, 128).transpose(2, 0, 1))
    nrm = np.stack([norm_mix[0], norm_mlp[0], norm_mix[1], norm_mlp[1]], 0).reshape(4, 8, 128).transpose(2, 0, 1)
    nrm = np.ascontiguousarray(nrm)
    identf = np.eye(128, dtype=f32); identb = np.eye(128).astype(bf)
    s_i = np.arange(128)[:, None]; t_i = np.arange(128)[None, :]
    same = (s_i // 32) == (t_i // 32)
    maskAB = np.stack([(same & (s_i <= t_i)), (same & (s_i >= t_i))], axis=1).astype(f32).astype(bf)
    rmask = np.ones((128, 512), f32); rmask[:, ::32] = 0.0
    wq, wff, wfb, wi, wg = np.split(hgrn_w_in, 5, axis=1)
    in_maps = []
    for r in range(8):
        p, a = r // 2, r % 2
        xs = x_sample[p]
        if a == 0:
            loc = np.arange(2048)
        else:
            loc = 4095 - np.arange(2048)
        prs = [x_prompt[2 * r], x_prompt[2 * r + 1]]
        if a == 1:
            prs_loc = [pp[::-1] for pp in prs]
        else:
            prs_loc = prs
        x_my = np.concatenate([xs[loc]] + prs_loc, axis=0)
        x_all = np.concatenate([xs] + prs, axis=0)
        da, mc, flp, cst = tabs[a]
        cnd = np.stack([c[p], c_ctx], axis=-1).reshape(8, 128, 2).transpose(1, 0, 2).reshape(128, 16)
        if a == 0:
            w_in = np.concatenate([wq, wff, wfb, wi, wg], axis=1); lbr = hgrn_lb
        else:
            w_in = np.concatenate([wq, wfb, wff, wi, wg], axis=1); lbr = hgrn_lb[::-1]
        lbraw = np.ascontiguousarray(lbr.reshape(2, 2, 8, 128).transpose(3, 0, 1, 2))
        selm = np.zeros((128, 2), f32); selm[:, 1 - a] = 1.0
        sA0 = np.ascontiguousarray(state_hgrn[p, 0, a].transpose(1, 0, 2))
        in_maps.append({
            "x_my": np.ascontiguousarray(x_my), "pos_my": np.ascontiguousarray(pos[loc]),
            "x_all": np.ascontiguousarray(x_all), "pos_all": pos,
            "da_in": da, "mc_in": mc, "flp": flp, "cs": cst, "cond": np.ascontiguousarray(cnd),
            "ada_w": ada_w, "adab": adab, "nrm": nrm, "nfin": norm_final.reshape(1, D),
            "fnet_wo": fnet_wo, "w_in": np.ascontiguousarray(w_in), "lbraw": lbraw,
            "gnorm": np.ascontiguousarray(hgrn_norm.reshape(128, 1)), "hgrn_wo": hgrn_wo,
            "w1": mlp_w1, "w2": mlp_w2, "sA0": sA0, "selm": selm, "maskAB": np.ascontiguousarray(maskAB),
            "identf": identf, "identb": identb, "rmask": rmask,
        })
    res = run_bass_kernel_spmd(nc, in_maps, core_ids=list(range(8)))
    y_prompt = np.zeros((16, 256, D), f32); y_sample = np.zeros((4, 4096, D), f32)
    new_state = np.zeros((16, 1, 2, 8, 128, 128), f32)
    for r in range(8):
        p, a = r // 2, r % 2
        y = res.results[r]["y_my"]
        st = res.results[r]["st_out"]
        if a == 0:
            y_sample[p, 0:2048] = y[0:2048]
            y_prompt[2 * r] = y[2048:2304]; y_prompt[2 * r + 1] = y[2304:2560]
        else:
            y_sample[p, 2048:4096] = y[0:2048][::-1]
            y_prompt[2 * r] = y[2048:2304][::-1]; y_prompt[2 * r + 1] = y[2304:2560][::-1]
        for q in range(2):
            for dd in range(2):
                gd = dd if a == 0 else 1 - dd
                new_state[2 * r + q, 0, gd] = st[q, dd]
    return (y_prompt, y_sample, new_state)
```

```python
import numpy as np
import ml_dtypes
from contextlib import ExitStack
import concourse.bass as bass
import concourse.mybir as mybir
from concourse.bass_utils import run_bass_kernel_spmd

F32 = mybir.dt.float32
BF16 = mybir.dt.bfloat16
AF = mybir.ActivationFunctionType
ALU = mybir.AluOpType

D = 1024
T = 2560
TS = 2048
NT = 5
EPS = 1e-6
NBLK = 20


class _Op:
    pass


class Sched:
    ENG = ['pe', 'act', 'dve', 'pool', 'sp']

    def __init__(s):
        s.q = {e: [] for e in s.ENG}
        s.lastw = {}
        s.readers = {}
        s.barrier = []
        s.dma_ring = {'sp': 12, 'act': 6, 'pool': 12}
        s.dma_next = {k: 0 for k in s.dma_ring}
        s.slot_last = {}

    def _deps(s, reads, writes):
        d = []
        for b in reads:
            if b in s.lastw:
                d.append(s.lastw[b])
        for b in writes:
            if b in s.lastw:
                d.append(s.lastw[b])
            d.extend(s.readers.get(b, ()))
        d.extend(s.barrier)
        return d

    def _commit(s, o, reads, writes):
        for b in reads:
            s.readers.setdefault(b, []).append(o)
        for b in writes:
            s.lastw[b] = o
            s.readers[b] = []

    def op(s, eng, fn, reads=(), writes=()):
        psr = [b for b in reads if isinstance(b, str) and b.startswith('ps')]
        if psr:
            reads = [b for b in reads if b not in psr]
            writes = list(writes) + psr
        o = _Op()
        o.eng = eng; o.fn = fn; o.is_dma = False; o.sig = False; o.sigidx = 0
        o.deps = s._deps(reads, writes)
        s.q[eng].append(o)
        s._commit(o, reads, writes)
        return o

    def dma(s, q, fn, reads=(), writes=()):
        o = _Op()
        o.eng = q; o.fn = fn; o.is_dma = True; o.sig = False
        o.deps = s._deps(reads, writes)
        slot = s.dma_next[q]
        s.dma_next[q] = (slot + 1) % s.dma_ring[q]
        o.slot = (q, slot)
        prev = s.slot_last.get(o.slot)
        o.slotval = (prev.slotval if prev else 0) + 16
        if prev is not None:
            o.deps.append(prev)
        s.slot_last[o.slot] = o
        s.q[q].append(o)
        s._commit(o, reads, writes)
        return o

    def special(s, q, fn, key, reads=(), writes=()):
        o = _Op()
        o.eng = q; o.fn = fn; o.is_dma = True; o.sig = False
        o.deps = s._deps(reads, writes)
        o.slot = key
        o.slotval = 1
        o.incval = 1
        s.slot_last[o.slot] = o
        s.q[q].append(o)
        s._commit(o, reads, writes)
        return o

    def set_barrier(s):
        b = []
        for e in s.ENG:
            for o in reversed(s.q[e]):
                if not o.is_dma:
                    b.append(o)
                    break
        for o in s.slot_last.values():
            b.append(o)
        s.barrier = b

    def finalize(s):
        for e in s.ENG:
            for o in s.q[e]:
                for d in o.deps:
                    if d.is_dma:
                        continue
                    if d.eng == 'pe' and e == 'pe' and not o.is_dma:
                        continue
                    d.sig = True
        for e in s.ENG:
            c = 0
            for o in s.q[e]:
                if (not o.is_dma) and o.sig:
                    c += 1
                    o.sigidx = c
        for e in s.ENG:
            waited = {}
            for o in s.q[e]:
                w = {}
                for d in o.deps:
                    if d.is_dma:
                        key = ('dma',) + d.slot
                        val = d.slotval
                    else:
                        if d.eng == 'pe' and e == 'pe' and not o.is_dma:
                            continue
                        key = d.eng
                        val = d.sigidx
                    if waited.get(key, 0) >= val:
                        continue
                    if w.get(key, 0) < val:
                        w[key] = val
                for k, v in w.items():
                    waited[k] = v
                o.waits = list(w.items())

    def emit(s, nc, block, sems):
        def run(e, eng):
            for o in s.q[e]:
                for k, v in o.waits:
                    eng.wait_ge(sems[k], v)
                ins = o.fn(eng)
                if o.is_dma:
                    ins.then_inc(sems[('dma',) + o.slot], getattr(o, 'incval', 16))
                elif o.sig:
                    ins.then_inc(sems[e], 1)
            if e == 'sp':
                for slot, o in s.slot_last.items():
                    eng.wait_ge(sems[('dma',) + slot], o.slotval)

        @block.tensor
        def _(eng):
            run('pe', eng)

        @block.scalar
        def _(eng):
            run('act', eng)

        @block.vector
        def _(eng):
            run('dve', eng)

        @block.gpsimd
        def _(eng):
            run('pool', eng)

        @block.sync
        def _(eng):
            run('sp', eng)


def build(stage=4):
    nc = bass.Bass("TRN2", target_bir_lowering=False)
    S = Sched()

    def din(name, shape, dt=F32):
        return nc.dram_tensor(name, list(shape), dt, kind="ExternalInput").ap()

    x_my = din("x_my", [T, D]); pos_my = din("pos_my", [TS, D])
    x_all = din("x_all", [4608, D]); pos_all = din("pos_all", [4096, D])
    flp = din("flp", [2, 128, 512], BF16)
    cs = din("cs", [128, 2, 2, 256], BF16)
    cond = din("cond", [128, 16]); ada_w = din("ada_w", [2, D, 6 * D]); adab = din("adab", [128, 2, 48])
    nrm = din("nrm", [128, 4, 8]); nfin = din("nfin", [1, D])
    fnet_wo = din("fnet_wo", [D, D]); w_in = din("w_in", [D, 5 * D]); lbraw = din("lbraw", [128, 2, 2, 8])
    gnorm = din("gnorm", [128, 1]); hgrn_wo = din("hgrn_wo", [D, D])
    w1 = din("w1", [2, D, 4 * D]); w2 = din("w2", [2, 4 * D, D])
    sA0 = din("sA0", [128, 8, 128]); selm = din("selm", [128, 2])
    maskAB = din("maskAB", [128, 2, 128], BF16)
    identf = din("identf", [128, 128]); identb = din("identb", [128, 128], BF16)
    rmask = din("rmask", [128, 512])
    da_in = din("da_in", [128, 2, 128], BF16); mc_in = din("mc_in", [128, 64, 64], BF16)
    y_my = nc.dram_tensor("y_my", [T, D], F32, kind="ExternalOutput").ap()
    st_out = nc.dram_tensor("st_out", [2, 2, 8, 128, 128], F32, kind="ExternalOutput").ap()
    scr_x = nc.dram_tensor("scr_x", [8, 128, T], F32, kind="Internal").ap()
    scr_vec = nc.dram_tensor("scr_vec", [4, D], F32, kind="Internal").ap()
    zs = nc.dram_tensor("zs", [64, 64, 2, D], BF16, kind="Internal").ap()
    scr_og = nc.dram_tensor("scr_og", [8, 128, T], BF16, kind="Internal").ap()
    cc_in = nc.dram_tensor("cc_in", [8, 128, 128], F32, kind="Internal").ap()
    cc_out = nc.dram_tensor("cc_out", [8, 256, 128], F32, kind="Internal", addr_space="Local").ap()

    es = ExitStack()
    with es:
        def sb(name, shape, dt=F32):
            return es.enter_context(nc.sbuf_tensor(name, list(shape), dt))

        BIG = sb("BIG", [128, 20480], F32)
        HB = sb("HB", [128, 20480], BF16)
        WB = sb("WB", [128, 18432], BF16)
        MISC = sb("MISC", [128, 6144], F32)
        SM = sb("SM", [128, 1024], F32)
        IDF = sb("IDF", [128, 128], F32)
        IDB = sb("IDB", [128, 128], BF16)
        MSK = sb("MSK", [128, 2, 128], BF16)
        ONES = sb("ONES", [128, 128], BF16)
        SCB = sb("SCB", [128, 16], BF16)
        CST = sb("CST", [128, 2, 2, 256], BF16)
        RM = sb("RM", [128, 512], F32)
        DAT = sb("DAT", [128, 2, 128], BF16)
        MCT = sb("MCT", [128, 64, 64], BF16)
        ps = [es.enter_context(nc.psum_tensor(f"ps{i}", [128, 512], F32)) for i in range(8)]
        sems = {}
        for e in Sched.ENG:
            sems[e] = es.enter_context(nc.semaphore(f"sem_{e}"))
        for q, n in S.dma_ring.items():
            for i in range(n):
                sems[('dma', q, i)] = es.enter_context(nc.semaphore(f"dsem_{q}_{i}"))
        for hh_ in range(8):
            sems[('dma', 'cc', hh_)] = es.enter_context(nc.semaphore(f"ccsem_{hh_}"))

        xT = BIG[:, :].rearrange("p (k t) -> p k t", k=8)
        hT = HB[:, :].rearrange("p (k t) -> p k t", k=8)
        BIGb = BIG[:, :].bitcast(BF16)

        MOD = SM[:, 0:192].rearrange("p (i j c) -> p i j c", i=2, j=48)
        AM = SM[:, 192:448].rearrange("p (n k c) -> p n k c", n=4, k=8)
        SSQ = SM[:, 448:512]
        RSTD = SM[:, 512:576]
        NRM = SM[:, 576:608].rearrange("p (n k) -> p n k", n=4)
        ADB = SM[:, 608:704].rearrange("p (i j) -> p i j", i=2)
        CND = SM[:, 704:720]
        EPST = SM[:, 720:721]
        LB = SM[:, 736:768].rearrange("p (d l h) -> p d l h", d=2, l=2)
        LBV = SM[:, 768:784].rearrange("p (d h) -> p d h", d=2)
        OML = SM[:, 784:800].rearrange("p (d h) -> p d h", d=2)
        GN = SM[:, 800:801]
        SEL = SM[:, 801:803]
        ACOL = SM[:, 816:976].rearrange("p (d c) -> p d c", d=2)

        S.dma('sp', lambda e: e.dma_start(out=IDF[:, :], in_=identf[:, :]), [], ['IDF'])
        S.dma('sp', lambda e: e.dma_start(out=IDB[:, :], in_=identb[:, :]), [], ['IDB'])
        S.dma('sp', lambda e: e.dma_start(out=MSK[:, :, :], in_=maskAB[:, :, :]), [], ['MSK'])
        S.dma('sp', lambda e: e.dma_start(out=CST[:, :, :, :], in_=cs[:, :, :, :]), [], ['CST'])
        S.dma('sp', lambda e: e.dma_start(out=RM[:, :], in_=rmask[:, :]), [], ['RM'])
        S.dma('sp', lambda e: e.dma_start(out=CND, in_=cond[:, :]), [], ['CND'])
        S.dma('sp', lambda e: e.dma_start(out=ADB, in_=adab[:, :, :]), [], ['ADB'])
        S.dma('sp', lambda e: e.dma_start(out=NRM, in_=nrm[:, :, :]), [], ['NRM'])
        S.dma('sp', lambda e: e.dma_start(out=LB, in_=lbraw[:, :, :, :]), [], ['LB'])
        S.dma('sp', lambda e: e.dma_start(out=GN, in_=gnorm[:, :]), [], ['GN'])
        S.dma('sp', lambda e: e.dma_start(out=SEL, in_=selm[:, :]), [], ['SEL'])
        S.op('dve', lambda e: e.memset(ONES[:, :], 1.0), [], ['ONES'])
        S.op('dve', lambda e: e.memset(EPST, EPS), [], ['EPST'])

        import os
        _skip = os.environ.get('KSKIP', '').split(',')
        S.op('act', lambda e: e.activation(out=SCB[:, :], in_=CND, func=AF.Silu), ['CND'], ['SCB'])
        scb3 = SCB[:, :].rearrange("p (k c) -> p k c", c=2)
        AWB = [WB[:, 10240:14336].rearrange("p (k n) -> p k n", k=8), WB[:, 14336:18432].rearrange("p (k n) -> p k n", k=8)]
        ada_blocks = [(0, 0, 0), (0, 0, 1), (0, 1, 0), (0, 1, 1)] + [(0, j, hh) for j in range(2, 6) for hh in range(2)] \
            + [(1, j, hh) for j in range(6) for hh in range(2)]
        ada_cnt = [0]

        def ada_block():
            if ada_cnt[0] >= len(ada_blocks):
                return
            i, j, hh = ada_blocks[ada_cnt[0]]
            b = ada_cnt[0] % 2
            ada_cnt[0] += 1
            buf = AWB[b]; key = f"AWB{b}"
            src = ada_w[i, :, j * 1024 + hh * 512:j * 1024 + (hh + 1) * 512].rearrange("(k p) n -> p k n", p=128)
            S.dma('pool', lambda e, buf=buf, src=src: e.dma_start(out=buf, in_=src), [], [key])
            for o4 in range(4):
                oc = hh * 4 + o4
                col = (j * 8 + oc) * 2
                for kc in range(8):
                    S.op('pe', lambda e, buf=buf, o4=o4, kc=kc, col=col, i=i: e.matmul(
                        ps[i][:, col:col + 2], buf[:, kc, o4 * 128:(o4 + 1) * 128], scb3[:, kc, :],
                        start=(kc == 0), stop=(kc == 7)), [key, 'SCB'], [f"ps{i}"])

        def mod_evac(i, j0, j1):
            for c in range(2):
                S.op('dve', lambda e, i=i, c=c, j0=j0, j1=j1: e.tensor_tensor(
                    out=MOD[:, i, j0 * 8:j1 * 8, c], in0=ps[i][:, j0 * 16 + c:j1 * 16:2], in1=ADB[:, i, j0 * 8:j1 * 8], op=ALU.add),
                    [f"ps{i}", 'ADB'], [f"MOD{i}_{j0}"])

        def am_derive(n):
            i = n // 2; j = 1 if n % 2 == 0 else 4
            for c in range(2):
                S.op('dve', lambda e, n=n, i=i, j=j, c=c: e.scalar_tensor_tensor(
                    out=AM[:, n, :, c], in0=MOD[:, i, j * 8:(j + 1) * 8, c], scalar=1.0, in1=NRM[:, n, :],
                    op0=ALU.add, op1=ALU.mult), [f"MOD{i}_0", f"MOD{i}_2", 'NRM'], [f"AM{n}"])

        for _ in range(4):
            ada_block()
        mod_evac(0, 0, 2)
        am_derive(0)

        ALLX = [f"xT{k_}_{j_}" for k_ in range(8) for j_ in range(NT)]
        ALLH = [f"hT{k_}_{j_}" for k_ in range(8) for j_ in range(NT)]

        def modv(i, j, kc, c):
            return MOD[:, i, j * 8 + kc, c:c + 1]

        if 'fnet' not in _skip:
            ABC = WB[:, 6144:10240].rearrange("p (a n) -> p a n", a=4)
            DV = MISC[:, 4096:4608].bitcast(BF16).rearrange("p (k j) -> p k j", k=8)
            for v in range(4):
                c = v % 2
                vec = AM[:, 0, :, c] if v < 2 else MOD[:, 0, 0:8, c]
                S.op('dve', lambda e, vec=vec: e.tensor_tensor(
                    out=DV, in0=IDB[:, :].unsqueeze(1).to_broadcast([128, 8, 128]), in1=vec.unsqueeze(2).to_broadcast([128, 8, 128]), op=ALU.mult),
                    ['IDB', 'AM0', 'MOD0_0'], ['JUNK'])
                for hh in range(2):
                    S.op('pe', lambda e, hh=hh: e.matmul(ps[2 + hh][:, :], ONES[:, :], DV[:, hh * 4:(hh + 1) * 4, :], start=True, stop=True),
                         ['JUNK', 'ONES'], [f"ps{2 + hh}"])
                    if hh == 0:
                        S.op('act', lambda e, v=v, hh=hh: e.activation(out=ABC[:, v, hh * 512:(hh + 1) * 512], in_=ps[2 + hh][:, :], func=AF.Copy),
                             [f"ps{2 + hh}"], ['ABC'])
                    else:
                        S.op('dve', lambda e, v=v, hh=hh: e.tensor_copy(out=ABC[:, v, hh * 512:(hh + 1) * 512], in_=ps[2 + hh][:, :]),
                             [f"ps{2 + hh}"], ['ABC'])

            hn = BIGb[:, 0:36864].rearrange("p (c n) -> p c n", c=36)
            XIN = [MISC[:, 0:1024], MISC[:, 1024:2048]]
            PIN = [MISC[:, 2048:3072], MISC[:, 2048:3072]]
            TMPF = MISC[:, 3072:4096]
            JUNK = MISC[:, 4096:4608].bitcast(BF16)
            RB = MISC[:, 4608:5120]
            SQB = [MISC[:, 5120:5376].bitcast(BF16), MISC[:, 5376:5632].bitcast(BF16)]
            RL = [MISC[:, 5632:5888].bitcast(BF16), MISC[:, 5888:6144].bitcast(BF16)]
            XR = [XIN[0], XIN[1], WB[:, 0:2048].bitcast(F32)]
            PR = [PIN[0], WB[:, 2048:4096].bitcast(F32)]
            TR = [TMPF, WB[:, 4096:6144].bitcast(F32)]
            for ci in range(36):
                b = ci % 3; pb_ = ci % 2; tb_ = ci % 2
                c = 0 if ci < 32 else 1
                xk = f"XR{b}"; pk = f"PR{pb_}"; tk = f"TR{tb_}"
                if ci < 32:
                    for hf in range(2):
                        S.dma('sp', lambda e, ci=ci, b=b, hf=hf: e.dma_start(
                            out=XR[b][hf * 64:(hf + 1) * 64, :],
                            in_=x_all[0:4096, :].rearrange("(t1 r) c -> r t1 c", r=64)[32 * hf + ci]), [], [xk])
                        S.dma('sp', lambda e, ci=ci, pb_=pb_, hf=hf: e.dma_start(
                            out=PR[pb_][hf * 64:(hf + 1) * 64, :],
                            in_=pos_all[0:4096, :].rearrange("(t1 r) c -> r t1 c", r=64)[32 * hf + ci]), [], [pk])
                    S.op('pool', lambda e, b=b, pb_=pb_: e.tensor_tensor(out=XR[b][:, :], in0=XR[b][:, :], in1=PR[pb_][:, :], op=ALU.add),
                         [xk, pk], [xk])
                else:
                    S.dma('sp', lambda e, ci=ci, b=b: e.dma_start(out=XR[b][:, :], in_=x_all[ci * 128:(ci + 1) * 128, :]), [], [xk])
                S.op('act', lambda e, ci=ci, b=b: e.activation(out=JUNK[:, :], in_=XR[b][:, :], func=AF.Square,
                                                               accum_out=SSQ[:, ci:ci + 1]), [xk], ['JUNK', f"SSQ{ci}"])
                S.op('act', lambda e, ci=ci: e.activation(out=RSTD[:, ci:ci + 1], in_=SSQ[:, ci:ci + 1], func=AF.Sqrt,
                                                          bias=EPST, scale=1.0 / D), [f"SSQ{ci}", 'EPST'], [f"RSTD{ci}"])
                S.op('dve', lambda e, ci=ci: e.reciprocal(out=RSTD[:, ci:ci + 1], in_=RSTD[:, ci:ci + 1]),
                     [f"RSTD{ci}"], [f"RSTD{ci}"])
                S.op('dve', lambda e, ci=ci, b=b, c=c, tb_=tb_: e.scalar_tensor_tensor(
                    out=TR[tb_][:, :], in0=XR[b][:, :], scalar=RSTD[:, ci:ci + 1], in1=ABC[:, c, :],
                    op0=ALU.mult, op1=ALU.mult), [xk, f"RSTD{ci}", 'ABC'], [tk])
                S.op('dve', lambda e, ci=ci, c=c, tb_=tb_: e.tensor_tensor(out=hn[:, ci, :], in0=TR[tb_][:, :], in1=ABC[:, 2 + c, :], op=ALU.add),
                     [tk, 'ABC'], [f"hn{ci}"])
                ada_block()
            while ada_cnt[0] < len(ada_blocks):
                ada_block()
            mod_evac(0, 2, 6)
            mod_evac(1, 0, 2)
            mod_evac(1, 2, 6)
            am_derive(1); am_derive(2); am_derive(3)
            S.set_barrier()

            ZST = [WB[:, z * 2048:(z + 1) * 2048].rearrange("p (r c) -> p r c", r=2) for z in range(8)]
            MIXF = HB[:, 4096:20480].rearrange("p (k t) -> p k t", k=8)
            MIXP = HB[:, 0:4096].rearrange("p (k t) -> p k t", k=8)
            YTGS = [[WB[:, (2 * y_ + r_) * 4096:(2 * y_ + r_ + 1) * 4096].rearrange("p (k t) -> p k t", k=2) for r_ in range(2)] for y_ in range(2)]
            YTP = [WB[:, 0:4096].rearrange("p (k t) -> p k t", k=8), WB[:, 4096:8192].rearrange("p (k t) -> p k t", k=8)]
            FTP = [WB[:, 8192:9216].rearrange("p (c n) -> p c n", c=2), WB[:, 9216:10240].rearrange("p (c n) -> p c n", c=2)]
            WO = WB[:, 10240:18432].rearrange("p (k n) -> p k n", k=8)
            Z1T = [BIGb[:, 0:16384].rearrange("p (f c) -> p f c", f=64), BIGb[:, 16384:32768].rearrange("p (f c) -> p f c", f=64)]
            XTILES = [BIG[:, 0:4096].rearrange("p (k t) -> p k t", k=8), BIG[:, 4096:8192].rearrange("p (k t) -> p k t", k=8)]
            S.dma('sp', lambda e: e.dma_start(out=DAT[:, :, :], in_=da_in[:, :, :]), [], ['DAT'])
            S.dma('sp', lambda e: e.dma_start(out=MCT[:, :, :], in_=mc_in[:, :, :]), [], ['MCT'])
            for t2 in range(32):
                zb = t2 % 8
                for ri in range(2):
                    for ch in range(2):
                        bank = (t2 % 2) * 4 + ri * 2 + ch
                        S.op('pe', lambda e, t2=t2, ri=ri, ch=ch, bank=bank: e.matmul(
                            ps[bank][:, :], DAT[:, ri, :], hn[:, t2, ch * 512:(ch + 1) * 512], start=True, stop=True),
                            ['DAT', f"hn{t2}"], [f"ps{bank}"])
                        if (ri + ch) % 2 == 0:
                            S.op('act', lambda e, zb=zb, ri=ri, ch=ch, bank=bank: e.activation(
                                out=ZST[zb][:, ri, ch * 512:(ch + 1) * 512], in_=ps[bank][:, :], func=AF.Copy), [f"ps{bank}"], [f"ZST{zb}_{ri}_{ch}"])
                        else:
                            S.op('dve', lambda e, zb=zb, ri=ri, ch=ch, bank=bank: e.tensor_copy(
                                out=ZST[zb][:, ri, ch * 512:(ch + 1) * 512], in_=ps[bank][:, :]), [f"ps{bank}"], [f"ZST{zb}_{ri}_{ch}"])
                for hf in range(2):
                    S.dma('sp' if hf == 0 else 'pool', lambda e, t2=t2, zb=zb, hf=hf: e.dma_start(
                        out=zs[:, 32 * hf + t2, :, :], in_=ZST[zb][hf * 64:(hf + 1) * 64, :, :]),
                        [f"ZST{zb}_{r_}_{c_}" for r_ in range(2) for c_ in range(2)], [f"zs{t2}_{hf}"])
            S.set_barrier()
            for g in range(4):
                zb = g % 2
                YTG = YTGS[g % 2]; yk = f"YTG{g % 2}"
                for ri in range(2):
                    q_ = 'sp' if ri == 0 else 'act'
                    S.dma(q_, lambda e, g=g, zb=zb, ri=ri: e.dma_start(
                        out=Z1T[zb][ri * 64:(ri + 1) * 64, :, :],
                        in_=zs[:, :, ri, g * 256:(g + 1) * 256].rearrange("f t c -> t f c")), [], [f"Z1T{zb}"])
                for cl in range(2):
                    for a8 in range(8):
                        bank = a8
                        for f1l in range(8):
                            f1s = a8 * 8 + f1l
                            S.op('pe', lambda e, zb=zb, cl=cl, bank=bank, f1l=f1l, f1s=f1s: e.matmul(
                                ps[bank][:, f1l * 64:(f1l + 1) * 64], Z1T[zb][:, f1s, cl * 128:(cl + 1) * 128], MCT[:, f1s, :],
                                start=True, stop=True), [f"Z1T{zb}", 'MCT'], [f"ps{bank}"])
                        for ri in range(2):
                            src = ps[bank][:, :].rearrange("p (f r t) -> p f r t", f=8, r=2)[:, :, ri, :]
                            dst = YTG[ri][:, cl, :].rearrange("p (t f) -> p f t", f=64)[:, a8 * 8:(a8 + 1) * 8, :]
                            if a8 % 2 == 0:
                                S.op('act', lambda e, src=src, dst=dst: e.activation(out=dst, in_=src, func=AF.Copy), [f"ps{bank}"], [f"{yk}_{ri}_{cl}_{a8}"])
                            else:
                                S.op('dve', lambda e, src=src, dst=dst: e.tensor_copy(out=dst, in_=src), [f"ps{bank}"], [f"{yk}_{ri}_{cl}_{a8}"])
                for half in range(2):
                    oc = 2 * g + half
                    for jt in range(4):
                        bank = (half * 4 + jt) % 8
                        n = 0
                        for kc in range(2):
                            for ri in range(2):
                                S.op('pe', lambda e, half=half, kc=kc, ri=ri, jt=jt, n=n, bank=bank, YTG=YTG: e.matmul(
                                    ps[bank][:, :], CST[:, kc, ri, half * 128:(half + 1) * 128], YTG[ri][:, kc, jt * 512:(jt + 1) * 512],
                                    start=(n == 0), stop=(n == 3)), ['CST'] + [f"{yk}_{ri}_{kc}_{a}" for a in range(8)], [f"ps{bank}"])
                                n += 1
                        if jt % 2 == 0:
                            S.op('act', lambda e, oc=oc, jt=jt, bank=bank: e.activation(out=MIXF[:, oc, jt * 512:(jt + 1) * 512], in_=ps[bank][:, :], func=AF.Copy),
                                 [f"ps{bank}"], [f"MIXF{oc}_{jt}"])
                        else:
                            S.op('dve', lambda e, oc=oc, jt=jt, bank=bank: e.tensor_copy(out=MIXF[:, oc, jt * 512:(jt + 1) * 512], in_=ps[bank][:, :]),
                                 [f"ps{bank}"], [f"MIXF{oc}_{jt}"])
            S.set_barrier()
            S.dma('pool', lambda e: e.dma_start(out=WO, in_=fnet_wo[:, :].rearrange("(k p) n -> p k n", p=128)), [], ['WO'])
            ftc = [0]
            for sq in range(2):
                fb = ftc[0] % 2; ftc[0] += 1
                S.dma('sp', lambda e, fb=fb: e.dma_start(out=FTP[fb], in_=flp[:, :, :].rearrange("c p n -> p c n")), [], [f"FTP{fb}"])
                for tc in range(2):
                    for cc in range(8):
                        S.op('pe', lambda e, fb=fb, tc=tc, cc=cc, sq=sq: e.matmul(
                            ps[cc][:, :], hn[:, 32 + 2 * sq + tc, cc * 128:(cc + 1) * 128], FTP[fb][:, tc, :],
                            start=(tc == 0), stop=(tc == 1)), [f"FTP{fb}", f"hn{32 + 2 * sq + tc}"], [f"ps{cc}"])
                for cc in range(8):
                    for ri in range(2):
                        if cc % 2 == 0:
                            S.op('act', lambda e, cc=cc, ri=ri, sq=sq: e.activation(
                                out=YTP[ri][:, cc, sq * 256:(sq + 1) * 256], in_=ps[cc][:, ri * 256:(ri + 1) * 256], func=AF.Copy),
                                [f"ps{cc}"], [f"YTP{ri}"])
                        else:
                            S.op('dve', lambda e, cc=cc, ri=ri, sq=sq: e.tensor_copy(
                                out=YTP[ri][:, cc, sq * 256:(sq + 1) * 256], in_=ps[cc][:, ri * 256:(ri + 1) * 256]),
                                [f"ps{cc}"], [f"YTP{ri}"])
            for oc in range(8):
                g = oc // 2; half = oc % 2
                n = 0
                for kc in range(2):
                    for ri in range(2):
                        S.op('pe', lambda e, oc=oc, g=g, half=half, kc=kc, ri=ri, n=n: e.matmul(
                            ps[oc][:, :], CST[:, kc, ri, half * 128:(half + 1) * 128], YTP[ri][:, g * 2 + kc, :],
                            start=(n == 0), stop=(n == 3)), ['CST', f"YTP{ri}"], [f"ps{oc}"])
                        n += 1
                if oc % 2 == 0:
                    S.op('act', lambda e, oc=oc: e.activation(out=MIXP[:, oc, :], in_=ps[oc][:, :], func=AF.Copy), [f"ps{oc}"], [f"MIXF{oc}_4"])
                else:
                    S.op('dve', lambda e, oc=oc: e.tensor_copy(out=MIXP[:, oc, :], in_=ps[oc][:, :]), [f"ps{oc}"], [f"MIXF{oc}_4"])
            S.set_barrier()
            XQ = [XIN[0], XIN[1], WB[:, 0:2048].bitcast(F32), WB[:, 2048:4096].bitcast(F32)]
            PQ = [PIN[0], WB[:, 4096:6144].bitcast(F32), WB[:, 6144:8192].bitcast(F32)]
            xq_i = [0]
            for jt in range(NT):
                XT = XTILES[jt % 2]; xtk = f"XTILE{jt % 2}"
                for q4 in range(4):
                    b = xq_i[0] % 4; pb_ = xq_i[0] % 3; xq_i[0] += 1
                    r0 = jt * 512 + q4 * 128
                    S.dma('sp', lambda e, b=b, r0=r0: e.dma_start(out=XQ[b][:, :], in_=x_my[r0:r0 + 128, :]), [], [f"XQ{b}"])
                    if jt < 4:
                        S.dma('sp', lambda e, pb_=pb_, r0=r0: e.dma_start(out=PQ[pb_][:, :], in_=pos_my[r0:r0 + 128, :]), [], [f"PQ{pb_}"])
                        S.op('pool', lambda e, b=b, pb_=pb_: e.tensor_tensor(out=XQ[b][:, :], in0=XQ[b][:, :], in1=PQ[pb_][:, :], op=ALU.add),
                             [f"XQ{b}", f"PQ{pb_}"], [f"XQ{b}"])
                    for cc in range(8):
                        S.op('pe', lambda e, b=b, cc=cc, q4=q4: e.transpose(
                            ps[cc][:, q4 * 128:(q4 + 1) * 128], XQ[b][:, cc * 128:(cc + 1) * 128], IDF[:, :]),
                            [f"XQ{b}", 'IDF'], [f"ps{cc}"])
                for cc in range(8):
                    if cc % 2 == 0:
                        S.op('act', lambda e, cc=cc, XT=XT: e.activation(out=XT[:, cc, :], in_=ps[cc][:, :], func=AF.Copy), [f"ps{cc}"], [f"{xtk}_{cc}"])
                    else:
                        S.op('dve', lambda e, cc=cc, XT=XT: e.tensor_copy(out=XT[:, cc, :], in_=ps[cc][:, :]), [f"ps{cc}"], [f"{xtk}_{cc}"])
                c = 0 if jt < 4 else 1
                for oc in range(8):
                    for kc in range(8):
                        rhs = MIXF[:, kc, jt * 512:(jt + 1) * 512] if jt < 4 else MIXP[:, kc, :]
                        S.op('pe', lambda e, oc=oc, kc=kc, rhs=rhs: e.matmul(ps[oc][:, :], WO[:, kc, oc * 128:(oc + 1) * 128], rhs,
                                                                              start=(kc == 0), stop=(kc == 7)), ['WO', f"MIXF{kc}_{jt}"], [f"ps{oc}"])
                    S.op('dve', lambda e, oc=oc, c=c, XT=XT: e.scalar_tensor_tensor(
                        out=XT[:, oc, :], in0=ps[oc][:, :], scalar=modv(0, 2, oc, c), in1=XT[:, oc, :],
                        op0=ALU.mult, op1=ALU.add), [f"ps{oc}", f"{xtk}_{oc}", 'MOD0_2'], [f"{xtk}_{oc}"])
                S.dma('sp', lambda e, jt=jt, XT=XT: e.dma_start(
                    out=scr_x[:, :, jt * 512:(jt + 1) * 512].rearrange("k p t -> p k t"), in_=XT[:, :, :]),
                    [f"{xtk}_{c_}" for c_ in range(8)], [f"scrx{jt}"])

        S.set_barrier()
        S.dma('sp', lambda e: e.dma_start(out=xT, in_=scr_x[:, :, :].rearrange("k p t -> p k t")), [], ALLX)
        S.set_barrier()


        def norm_mod(n, i, jsh):
            TM2 = [TMPF[:, 0:512], TMPF[:, 512:1024]]
            for jt in range(NT):
                c = 0 if jt < 4 else 1
                for kc in range(8):
                    b = kc % 2
                    S.op('act', lambda e, kc=kc, jt=jt, b=b: e.activation(
                        out=SQB[b][:, :], in_=xT[:, kc, jt * 512:(jt + 1) * 512], func=AF.Square), [f"xT{kc}_{jt}"], [f"SQB{b}"])
                    S.op('pe', lambda e, kc=kc, b=b: e.matmul(ps[7][:, :], ONES[:, :], SQB[b][:, :], start=(kc == 0), stop=(kc == 7)),
                         [f"SQB{b}", 'ONES'], ['ps7'])
                S.op('act', lambda e: e.activation(out=RB[:, :], in_=ps[7][:, :], func=AF.Ln, bias=EPST, scale=1.0 / D),
                     ['ps7', 'EPST'], ['RB'])
                S.op('act', lambda e: e.activation(out=RB[:, :], in_=RB[:, :], func=AF.Exp, scale=-0.5), ['RB'], ['RB'])
                for kc in range(8):
                    tb_ = kc % 2
                    S.op('dve', lambda e, kc=kc, jt=jt, c=c, tb_=tb_: e.scalar_tensor_tensor(
                        out=TM2[tb_], in0=xT[:, kc, jt * 512:(jt + 1) * 512], scalar=AM[:, n, kc, c:c + 1], in1=RB[:, :],
                        op0=ALU.mult, op1=ALU.mult), [f"xT{kc}_{jt}", f"AM{n}", 'RB'], [f"TM2{tb_}"])
                    S.op('act', lambda e, kc=kc, jt=jt, c=c, tb_=tb_: e.activation(
                        out=hT[:, kc, jt * 512:(jt + 1) * 512], in_=TM2[tb_], func=AF.Identity,
                        bias=modv(i, jsh, kc, c), scale=1.0), [f"TM2{tb_}", f"MOD{i}_0", f"MOD{i}_2"], [f"hT{kc}_{jt}"])

        W1B = [WB[:, k * 4096:(k + 1) * 4096].rearrange("p (k n) -> p k n", k=8) for k in range(2)]
        W2B = [WB[:, 8192 + k * 4096:8192 + (k + 1) * 4096].rearrange("p (k n) -> p k n", k=4) for k in range(2)]
        UT = [MISC[:, 0:1024].bitcast(BF16).rearrange("p (k n) -> p k n", k=4), MISC[:, 1024:2048].bitcast(BF16).rearrange("p (k n) -> p k n", k=4)]
        wcnt = [0]

        def mlp(i):
            jg = 5
            its = [(hp, jt) for hp in range(8) for jt in range(NT)]
            wbs = {}

            def load_w(hp):
                wb = wcnt[0] % 2; wcnt[0] += 1
                wbs[hp] = wb
                S.dma('pool', lambda e, wb=wb, hp=hp: e.dma_start(
                    out=W1B[wb], in_=w1[i, :, hp * 512:(hp + 1) * 512].rearrange("(k p) n -> p k n", p=128)), [], [f"W1B{wb}"])
                S.dma('pool', lambda e, wb=wb, hp=hp: e.dma_start(
                    out=W2B[wb], in_=w2[i, hp * 512:(hp + 1) * 512, :].rearrange("(k p) n -> p k n", p=128)), [], [f"W2B{wb}"])

            def stage1(k):
                hp, jt = its[k]; wb = wbs[hp]; ub = k % 2
                for fc in range(4):
                    pb = fc % 4
                    for kc in range(8):
                        S.op('pe', lambda e, wb=wb, fc=fc, kc=kc, jt=jt, pb=pb: e.matmul(
                            ps[pb][:, :], W1B[wb][:, kc, fc * 128:(fc + 1) * 128], hT[:, kc, jt * 512:(jt + 1) * 512],
                            start=(kc == 0), stop=(kc == 7)), [f"W1B{wb}", f"hT{kc}_{jt}"], [f"ps{pb}"])
                    rb = fc % 2
                    S.op('act', lambda e, pb=pb, rb=rb: e.activation(out=RL[rb][:, :], in_=ps[pb][:, :], func=AF.Relu),
                         [f"ps{pb}"], [f"RL{rb}"])
                    S.op('pool', lambda e, rb=rb, ub=ub, fc=fc: e.tensor_tensor(
                        out=UT[ub][:, fc, :], in0=RL[rb][:, :], in1=RL[rb][:, :], op=ALU.mult), [f"RL{rb}"], [f"UT{ub}_{fc}"])

            def stage2(k):
                hp, jt = its[k]; wb = wbs[hp]; ub = k % 2
                c = 0 if jt < 4 else 1
                for oc in range(8):
                    pb = 4 + oc % 4
                    for fc in range(4):
                        S.op('pe', lambda e, wb=wb, fc=fc, oc=oc, ub=ub, pb=pb: e.matmul(
                            ps[pb][:, :], W2B[wb][:, fc, oc * 128:(oc + 1) * 128], UT[ub][:, fc, :],
                            start=(fc == 0), stop=(fc == 3)), [f"W2B{wb}", f"UT{ub}_{fc}"], [f"ps{pb}"])
                    S.op('dve', lambda e, oc=oc, jt=jt, c=c, pb=pb: e.scalar_tensor_tensor(
                        out=xT[:, oc, jt * 512:(jt + 1) * 512], in0=ps[pb][:, :], scalar=modv(i, jg, oc, c),
                        in1=xT[:, oc, jt * 512:(jt + 1) * 512], op0=ALU.mult, op1=ALU.add),
                        [f"ps{pb}", f"xT{oc}_{jt}", 'MOD0_2', 'MOD1_2'], [f"xT{oc}_{jt}"])

            load_w(0)
            for k in range(len(its)):
                hp, jt = its[k]
                if jt == 0 and hp + 1 < 8:
                    load_w(hp + 1)
                if k == 0:
                    stage1(0)
                if k + 1 < len(its):
                    stage1(k + 1)
                stage2(k)

        if stage >= 2:
            norm_mod(1, 0, 3)
            mlp(0)
            S.set_barrier()


        def hgrn():
            i = 1
            norm_mod(2, 1, 0)
            S.set_barrier()
            S.dma('sp', lambda e: e.dma_start(out=scr_x[:, :, :].rearrange("k p t -> p k t"), in_=xT), ALLX, ['scrx'])
            S.set_barrier()
            S.op('dve', lambda e: e.tensor_tensor(out=LBV, in0=LB[:, :, 1, :], in1=LB[:, :, 0, :], op=ALU.subtract), ['LB'], ['LBV'])
            S.op('act', lambda e: e.activation(out=LBV, in_=LBV, func=AF.Sigmoid), ['LBV'], ['LBV'])
            S.op('dve', lambda e: e.tensor_scalar(out=OML, in0=LBV, scalar1=-1.0, scalar2=1.0, op0=ALU.mult, op1=ALU.add), ['LBV'], ['OML'])
            S.dma('pool', lambda e: e.dma_start(out=WO, in_=hgrn_wo[:, :].rearrange("(k p) n -> p k n", p=128)), [], ['WO'])
            WH = [WB[:, w * 5120:(w + 1) * 5120].rearrange("p (k j n) -> p k j n", k=8, j=5) for w in range(2)]
            QE = [BIGb[:, 0:2560], BIGb[:, 2560:5120]]
            KE = [BIGb[:, 5120:7680], BIGb[:, 7680:10240]]
            KDT = [BIGb[:, 10240:12800].rearrange("p (b k) -> p b k", b=20), BIGb[:, 12800:15360].rearrange("p (b k) -> p b k", b=20)]
            VT = BIGb[:, 15360:17920].rearrange("p (b k) -> p b k", b=20)
            GS = BIGb[:, 17920:20480]
            OA = BIGb[:, 20480:23040]
            def wt(n):
                return BIG[:, 12288 + n * 512:12288 + (n + 1) * 512]
            QS = wt(0)
            SG = [wt(1), wt(2)]; FT_ = [wt(3), wt(4)]; LG = [wt(5), wt(6)]; KT = [wt(7), wt(8)]
            BB = [wt(9), wt(10)]; E1 = [wt(11), wt(12)]; E2 = [wt(13), wt(14)]
            SST = [MISC[:, 0:128], MISC[:, 3584:3712], MISC[:, 3712:3840], MISC[:, 3840:3968]]
            SR = MISC[:, 128:384].rearrange("p (r n) -> p r n", r=2)
            OS = MISC[:, 384:896]; Ro = MISC[:, 896:1408]; T1 = MISC[:, 1408:1920]
            OSB = [MISC[:, 384:896], MISC[:, 4096:4608]]
            mb = MISC[:, 1920:3584].bitcast(BF16)
            SBR = [mb[:, r * 128:(r + 1) * 128] for r in range(8)]
            SCb = [mb[:, 1024:1152], mb[:, 1152:1280]]
            SQo = mb[:, 1280:1792]; OGh = mb[:, 1792:2304]
            KDt = [mb[:, 2304:2816], mb[:, 2816:3328]]
            psb5 = ps[6][:, :].bitcast(BF16)
            sbc = [0]
            def load_wh(h_):
                wh_ = WH[h_ % 2]
                for j in range(5):
                    S.dma('pool', lambda e, wh_=wh_, j=j, h_=h_: e.dma_start(
                        out=wh_[:, :, j, :], in_=w_in[:, j * 1024 + h_ * 128:j * 1024 + (h_ + 1) * 128].rearrange("(k p) n -> p k n", p=128)),
                        [], [f"WH{h_ % 2}"])
            load_wh(0)
            for h in range(8):
                wh = WH[h % 2]; whk = f"WH{h % 2}"
                if h + 1 < 8:
                    load_wh(h + 1)
                def elementwise(jt):
                    tsl = slice(jt * 512, (jt + 1) * 512)
                    def proj(pj, bank):
                        for kc in range(8):
                            S.op('pe', lambda e, pj=pj, bank=bank, kc=kc, wh=wh, tsl=tsl: e.matmul(
                                ps[bank][:, :], wh[:, kc, pj, :], hT[:, kc, tsl], start=(kc == 0), stop=(kc == 7)),
                                [whk, f"hT{kc}_{jt}"], [f"ps{bank}"])
                    proj(0, 4)
                    S.op('act', lambda e: e.activation(out=QS, in_=ps[4][:, :], func=AF.Silu), ['ps4'], ['QS'])
                    proj(1, 6)
                    S.op('act', lambda e: e.activation(out=SG[0], in_=ps[6][:, :], func=AF.Sigmoid), ['ps6'], ['SG0'])
                    proj(2, 4)
                    S.op('act', lambda e: e.activation(out=SG[1], in_=ps[4][:, :], func=AF.Sigmoid), ['ps4'], ['SG1'])
                    proj(4, 6)
                    S.op('act', lambda e, tsl=tsl: e.activation(out=GS[:, tsl], in_=ps[6][:, :], func=AF.Silu), ['ps6'], [f"GS{jt}"])
                    for q4 in range(4):
                        blk = jt * 4 + q4
                        for kc in range(8):
                            S.op('pe', lambda e, kc=kc, q4=q4, blk=blk, wh=wh: e.matmul(
                                ps[4][:, q4 * 128:(q4 + 1) * 128], hT[:, kc, blk * 128:(blk + 1) * 128], wh[:, kc, 3, :],
                                start=(kc == 0), stop=(kc == 7)), [whk, f"hT{kc}_{jt}"], ['ps4'])
                    S.op('dve', lambda e, jt=jt: e.tensor_copy(out=VT[:, jt * 4:(jt + 1) * 4, :], in_=ps[4][:, :].rearrange("p (b k) -> p b k", b=4)),
                         ['ps4'], [f"VT{jt}"])
                    def estep(k, d):
                        lastcol = 31 if d == 0 else 0
                        if k == 0:
                            S.op('dve', lambda e, d=d, h=h: e.tensor_scalar(out=FT_[d], in0=SG[d], scalar1=OML[:, d, h:h + 1], scalar2=LBV[:, d, h:h + 1],
                                                                            op0=ALU.mult, op1=ALU.add), [f"SG{d}", 'OML', 'LBV'], [f"F{d}"])
                        elif k == 1:
                            S.op('act', lambda e, d=d: e.activation(out=LG[d], in_=FT_[d], func=AF.Ln), [f"F{d}"], [f"LG{d}"])
                            S.op('pool', lambda e, d=d: e.tensor_scalar(out=KT[d], in0=FT_[d], scalar1=-1.0, scalar2=1.0, op0=ALU.mult, op1=ALU.add),
                                 [f"F{d}"], [f"KT{d}"])
                        elif k == 2:
                            if d == 0:
                                S.op('dve', lambda e, d=d: e.tensor_tensor_scan(out=BB[d], data0=RM[:, :], data1=LG[d], initial=0.0, op0=ALU.mult, op1=ALU.add),
                                     ['RM', f"LG{d}"], [f"BB{d}"])
                            else:
                                S.op('dve', lambda e, d=d: e.tensor_tensor_scan(out=BB[d][:, ::-1], data0=RM[:, :], data1=LG[d][:, ::-1], initial=0.0,
                                                                                op0=ALU.mult, op1=ALU.add), ['RM', f"LG{d}"], [f"BB{d}"])
                        elif k == 3:
                            S.op('act', lambda e, d=d: e.activation(out=E1[d], in_=BB[d], func=AF.Exp), [f"BB{d}"], [f"E1{d}"])
                            S.op('act', lambda e, d=d: e.activation(out=E2[d], in_=BB[d], func=AF.Exp, scale=-1.0), [f"BB{d}"], [f"E2{d}"])
                        elif k == 4:
                            S.op('dve', lambda e, d=d, tsl=tsl: e.tensor_tensor(out=QE[d][:, tsl], in0=QS, in1=E1[d], op=ALU.mult),
                                 ['QS', f"E1{d}"], [f"QE{d}_{jt}"])
                            S.op('pool', lambda e, d=d, jt=jt, lastcol=lastcol: e.tensor_copy(out=ACOL[:, d, jt * 16:(jt + 1) * 16], in_=E1[d][:, lastcol::32]),
                                 [f"E1{d}"], [f"ACOL{d}_{jt}"])
                            S.op('dve', lambda e, d=d, tsl=tsl: e.tensor_tensor(out=KE[d][:, tsl], in0=KT[d], in1=E2[d], op=ALU.mult),
                                 [f"KT{d}", f"E2{d}"], [f"KE{d}_{jt}"])
                        elif k == 5:
                            S.op('pool', lambda e, d=d, tsl=tsl, jt=jt: e.tensor_tensor(
                                out=KDt[d].rearrange("p (c j) -> p c j", j=32), in0=KE[d][:, tsl].rearrange("p (c j) -> p c j", j=32),
                                in1=ACOL[:, d, jt * 16:(jt + 1) * 16].unsqueeze(2).to_broadcast([128, 16, 32]), op=ALU.mult),
                                [f"KE{d}_{jt}", f"ACOL{d}_{jt}"], [f"KDt{d}"])
                        elif k == 6:
                            for q4 in range(4):
                                S.op('pe', lambda e, d=d, q4=q4: e.transpose(psb5[:, q4 * 128:(q4 + 1) * 128], KDt[d][:, q4 * 128:(q4 + 1) * 128], IDB[:, :]),
                                     [f"KDt{d}", 'IDB'], ['ps6'])
                            S.op('dve', lambda e, d=d, jt=jt: e.tensor_copy(out=KDT[d][:, jt * 4:(jt + 1) * 4, :], in_=psb5[:, 0:512].rearrange("p (b k) -> p b k", b=4)),
                                 ['ps6'], [f"KDT{d}_{jt}"])
                    for k in range(7):
                        for d in range(2):
                            estep(k, d)

                def seg_start(d, init):
                    st = {'d': d, 'cs': 0}
                    S0 = SST[0]
                    if init[0] == 'dram':
                        S.dma('sp', lambda e: e.dma_start(out=S0, in_=init[1]), [], ['Sst0'])
                    elif init[0] == 'zero':
                        S.op('dve', lambda e: e.memset(S0, 0.0), [], ['Sst0'])
                    else:
                        S.op('dve', lambda e: e.tensor_scalar(out=S0, in0=SR[:, 0, :], scalar1=SEL[:, 0:1], scalar2=None, op0=ALU.mult),
                             ['SR', 'SEL'], ['Sst0'])
                        S.op('dve', lambda e: e.scalar_tensor_tensor(out=S0, in0=SR[:, 1, :], scalar=SEL[:, 1:2], in1=S0, op0=ALU.mult, op1=ALU.add),
                             ['SR', 'SEL', 'Sst0'], ['Sst0'])
                    cur = sbc[0] % 8; sbc[0] += 1
                    S.op('act', lambda e, cur=cur: e.activation(out=SBR[cur], in_=S0, func=AF.Copy), ['Sst0'], [f"SBR{cur}"])
                    st['cur'] = cur
                    return st

                def seg_pre(st, blk, parts):
                    d = st['d']
                    bsl = slice(blk * 128, (blk + 1) * 128)
                    sc = blk % 2
                    tj = blk // 4
                    for pt in parts:
                        if pt == 's':
                            S.op('pe', lambda e, d=d, bsl=bsl: e.matmul(ps[7][:, 0:128], KE[d][:, bsl], QE[d][:, bsl], start=True, stop=True),
                                 [f"KE{d}_{tj}", f"QE{d}_{tj}"], ['ps7'])
                        elif pt == 'm':
                            S.op('dve', lambda e, d=d, sc=sc: e.tensor_tensor(out=SCb[sc], in0=ps[7][:, 0:128], in1=MSK[:, d, :], op=ALU.mult),
                                 ['ps7', 'MSK'], [f"SCb{sc}"])
                        elif pt == 'o':
                            S.op('pe', lambda e, blk=blk, sc=sc: e.matmul(ps[5][:, 0:128], VT[:, blk, :], SCb[sc], start=True, stop=False),
                                 [f"VT{tj}", f"SCb{sc}"], ['ps5'])
                        else:
                            c = int(pt[1])
                            kw = dict(tile_position=(96, 0)) if c == 3 else {}
                            S.op('pe', lambda e, d=d, c=c, blk=blk, kw=kw: e.matmul(
                                ps[c][:, 0:128], KDT[d][c * 32:(c + 1) * 32, blk, :], VT[c * 32:(c + 1) * 32, blk, :], start=True, stop=True, **kw),
                                [f"KDT{d}_{tj}", f"VT{tj}"], [f"ps{c}"])

                def seg_block(st, blk, nb=None):
                    d = st['d']
                    bsl = slice(blk * 128, (blk + 1) * 128)
                    po = 5
                    tj = blk // 4
                    if st.get('pre') != blk:
                        seg_pre(st, blk, ['s', 'm', 'u0', 'u1', 'u2', 'u3', 'o'])
                    corder = range(4) if d == 0 else range(3, -1, -1)
                    for n, c in enumerate(corder):
                        cur = st['cur']; cs = st['cs']
                        S.op('pe', lambda e, d=d, c=c, blk=blk, po=po, cur=cur, n=n: e.matmul(
                            ps[po][:, c * 32:(c + 1) * 32], SBR[cur], QE[d][:, blk * 128 + c * 32:blk * 128 + (c + 1) * 32],
                            start=False, stop=(n == 3)), [f"SBR{cur}", f"QE{d}_{tj}"], [f"ps{po}"])
                        ns = (cs + 1) % 4
                        S.op('dve', lambda e, d=d, c=c, blk=blk, cs=cs, ns=ns: e.scalar_tensor_tensor(
                            out=SST[ns], in0=SST[cs], scalar=ACOL[:, d, blk * 4 + c:blk * 4 + c + 1], in1=ps[c][:, 0:128], op0=ALU.mult, op1=ALU.add),
                            [f"Sst{cs}", f"ACOL{d}_{tj}", f"ps{c}"], [f"Sst{ns}"])
                        st['cs'] = ns
                        cur = sbc[0] % 8; sbc[0] += 1
                        st['cur'] = cur
                        if n % 2 == 0:
                            S.op('act', lambda e, cur=cur, ns=ns: e.activation(out=SBR[cur], in_=SST[ns], func=AF.Copy), [f"Sst{ns}"], [f"SBR{cur}"])
                        else:
                            S.op('pool', lambda e, cur=cur, ns=ns: e.tensor_copy(out=SBR[cur], in_=SST[ns]), [f"Sst{ns}"], [f"SBR{cur}"])
                        if nb is not None:
                            seg_pre(st, nb, [f"u{c}"])
                            if n == 0:
                                seg_pre(st, nb, ['s'])
                            if n == 3:
                                seg_pre(st, nb, ['m'])
                    if d == 0:
                        S.op('act', lambda e, bsl=bsl, po=po: e.activation(out=OA[:, bsl], in_=ps[po][:, 0:128], func=AF.Copy), [f"ps{po}"], [f"OA{blk}"])
                    else:
                        q4 = blk % 4
                        ob = (blk // 4) % 2
                        OSx = OSB[ob]
                        S.op('dve', lambda e, bsl=bsl, po=po, q4=q4, OSx=OSx: e.tensor_tensor(
                            out=OSx[:, q4 * 128:(q4 + 1) * 128], in0=ps[po][:, 0:128], in1=OA[:, bsl], op=ALU.add), [f"ps{po}", f"OA{blk}"], [f"OS{ob}_{q4}"])
                    if nb is not None:
                        seg_pre(st, nb, ['o'])
                        st['pre'] = nb
                    if d == 1:
                        if st.get('pending') is not None and st.get('pend_at') == blk:
                            st['pending'](); st['pending'] = None
                        q4 = blk % 4
                        if q4 == 0:
                            jt = blk // 4
                            tsl = slice(jt * 512, (jt + 1) * 512)
                            ob = jt % 2
                            OSx = OSB[ob]
                            oskeys = [f"OS{ob}_{q_}" for q_ in range(4)]

                            def onorm(jt=jt, tsl=tsl, OSx=OSx, oskeys=oskeys):
                                S.op('act', lambda e: e.activation(out=SQo, in_=OSx, func=AF.Square), oskeys, ['SQo'])
                                S.op('pe', lambda e: e.matmul(ps[6][:, :], ONES[:, :], SQo, start=True, stop=True), ['SQo', 'ONES'], ['ps6'])
                                S.op('act', lambda e: e.activation(out=Ro, in_=ps[6][:, :], func=AF.Ln, bias=EPST, scale=1.0 / 128.0),
                                     ['ps6', 'EPST'], ['Ro'])
                                S.op('act', lambda e: e.activation(out=Ro, in_=Ro, func=AF.Exp, scale=-0.5), ['Ro'], ['Ro'])
                                S.op('dve', lambda e: e.scalar_tensor_tensor(out=T1, in0=OSx, scalar=GN, in1=Ro, op0=ALU.mult, op1=ALU.mult),
                                     oskeys + ['GN', 'Ro'], ['T1'])
                                S.op('pool', lambda e: e.tensor_tensor(out=OGh, in0=T1, in1=GS[:, tsl], op=ALU.mult), ['T1', f"GS{jt}"], ['OGh'])
                                S.dma('sp', lambda e, h=h: e.dma_start(out=scr_og[h, :, tsl], in_=OGh), ['OGh'], [f"scrog{h}_{jt}"])
                            if nb is not None and (nb // 4) != jt and (nb % 4) == 3:
                                st['pending'] = onorm
                                st['pend_at'] = nb - 1
                            else:
                                onorm()

                def seg_end(st, fin):
                    if st.get('pending') is not None:
                        st['pending'](); st['pending'] = None
                    cs = st['cs']
                    if fin[0] == 'state':
                        S.dma('sp', lambda e: e.dma_start(out=fin[1], in_=SST[cs]), [f"Sst{cs}"], [])
                    elif fin[0] == 'cc':
                        S.dma('pool', lambda e, h=h: e.dma_start(out=cc_in[h, :, :], in_=SST[cs]), [f"Sst{cs}"], [f"ccin{h}"])
                        S.special('pool', lambda e, h=h: e.collective_compute(
                            "AllGather", ALU.bypass, replica_groups=[[0, 1], [2, 3], [4, 5], [6, 7]],
                            ins=[cc_in[h, :, :]], outs=[cc_out[h, :, :]]), ('cc', h), [f"ccin{h}"], [f"ccout{h}"])
                        S.dma('sp', lambda e, h=h: e.dma_start(out=SR, in_=cc_out[h, :, :].rearrange("(r p) n -> p r n", p=128)),
                              [f"ccout{h}"], ['SR'])

                def run_segment(d, blocks, init, fin):
                    st = seg_start(d, init)
                    for i_, blk in enumerate(blocks):
                        seg_block(st, blk, blocks[i_ + 1] if i_ + 1 < len(blocks) else None)
                    seg_end(st, fin)

                elementwise(0)
                elementwise(1)
                stA = seg_start(0, ('dram', sA0[:, h, :]))
                for jt in range(4):
                    for q4 in range(4):
                        blk = jt * 4 + q4
                        seg_block(stA, blk, blk + 1 if blk + 1 < 16 else None)
                    if jt == 3:
                        seg_end(stA, ('cc',))
                    if jt + 2 < NT:
                        elementwise(jt + 2)
                run_segment(0, [16, 17], ('zero',), ('state', st_out[0, 0, h, :, :]))
                run_segment(0, [18, 19], ('zero',), ('state', st_out[1, 0, h, :, :]))
                run_segment(1, [19, 18], ('zero',), ('state', st_out[1, 1, h, :, :]))
                run_segment(1, [17, 16], ('zero',), ('state', st_out[0, 1, h, :, :]))
                run_segment(1, list(range(15, -1, -1)), ('sel',), ('none',))
            S.set_barrier()
            OGT = hT
            S.dma('sp', lambda e: e.dma_start(out=OGT, in_=scr_og[:, :, :].rearrange("h p t -> p h t")), [], ALLH)
            S.dma('act', lambda e: e.dma_start(out=xT, in_=scr_x[:, :, :].rearrange("k p t -> p k t")), ['scrx'], ALLX)
            S.set_barrier()
            for jt in range(NT):
                c = 0 if jt < 4 else 1
                tsl = slice(jt * 512, (jt + 1) * 512)
                for oc in range(8):
                    pb = oc % 4
                    for kc in range(8):
                        S.op('pe', lambda e, oc=oc, kc=kc, tsl=tsl, pb=pb: e.matmul(ps[pb][:, :], WO[:, kc, oc * 128:(oc + 1) * 128], OGT[:, kc, tsl],
                                                                                   start=(kc == 0), stop=(kc == 7)), ['WO', f"hT{kc}_{jt}"], [f"ps{pb}"])
                    S.op('dve', lambda e, oc=oc, tsl=tsl, c=c, pb=pb: e.scalar_tensor_tensor(
                        out=xT[:, oc, tsl], in0=ps[pb][:, :], scalar=modv(1, 2, oc, c), in1=xT[:, oc, tsl], op0=ALU.mult, op1=ALU.add),
                        [f"ps{pb}", f"xT{oc}_{jt}", 'MOD0_2', 'MOD1_2'], [f"xT{oc}_{jt}"])
            S.set_barrier()

        if stage >= 3:
            hgrn()
        if stage >= 4:
            norm_mod(3, 1, 3)
            mlp(1)
            S.set_barrier()

        OUTB = [MISC[:, 0:1024], MISC[:, 1024:2048]]
        NFB = WB[:, 0:2048].bitcast(F32)
        S.dma('sp', lambda e: e.dma_start(out=NFB[:, :], in_=nfin[:, :].partition_broadcast(128)), [], ['NFB'])
        final_norm = (stage >= 4)
        for bk in range(NBLK):
            ob = bk % 2
            pp = (bk % 2) * 4
            for kc in range(8):
                bank = pp + kc // 4
                S.op('pe', lambda e, kc=kc, bk=bk, bank=bank: e.transpose(
                    ps[bank][:, (kc % 4) * 128:(kc % 4 + 1) * 128], xT[:, kc, bk * 128:(bk + 1) * 128], IDF[:, :]),
                    [f"xT{kc}_{bk // 4}", 'IDF'], [f"ps{bank}"])
            for hh in range(2):
                bank = pp + hh
                if not final_norm:
                    S.op('act', lambda e, ob=ob, hh=hh, bank=bank: e.activation(
                        out=OUTB[ob][:, hh * 512:(hh + 1) * 512], in_=ps[bank][:, :], func=AF.Copy), [f"ps{bank}"], [f"OUTB{ob}"])
                else:
                    S.op('act', lambda e, hh=hh, bank=bank, bk=bk: e.activation(
                        out=JUNK[:, 0:512], in_=ps[bank][:, :], func=AF.Square, accum_out=SSQ[:, hh:hh + 1]),
                        [f"ps{bank}"], ['JUNK', f"SSQ{hh}"])
            if final_norm:
                S.op('dve', lambda e: e.tensor_tensor(out=SSQ[:, 2:3], in0=SSQ[:, 0:1], in1=SSQ[:, 1:2], op=ALU.add),
                     ['SSQ0', 'SSQ1'], ['SSQ2'])
                S.op('act', lambda e: e.activation(out=RSTD[:, 0:1], in_=SSQ[:, 2:3], func=AF.Sqrt, bias=EPST, scale=1.0 / D),
                     ['SSQ2', 'EPST'], ['RSTD0'])
                S.op('dve', lambda e: e.reciprocal(out=RSTD[:, 0:1], in_=RSTD[:, 0:1]), ['RSTD0'], ['RSTD0'])
                for hh in range(2):
                    bank = pp + hh
                    S.op('dve', lambda e, ob=ob, hh=hh, bank=bank: e.scalar_tensor_tensor(
                        out=OUTB[ob][:, hh * 512:(hh + 1) * 512], in0=ps[bank][:, :], scalar=RSTD[:, 0:1],
                        in1=NFB[:, hh * 512:(hh + 1) * 512], op0=ALU.mult, op1=ALU.mult),
                        [f"ps{bank}", 'RSTD0', 'NFB'], [f"OUTB{ob}"])
            S.dma('sp', lambda e, ob=ob, bk=bk: e.dma_start(out=y_my[bk * 128:(bk + 1) * 128, :], in_=OUTB[ob][:, :]),
                  [f"OUTB{ob}"], [])

        S.finalize()
        with nc.Block() as block:
            S.emit(nc, block, sems)
    return nc


GRID_W = 64


def _pos_embed(n_tok, d):
    rows = n_tok // GRID_W
    quarter = d // 4
    omega = (1.0 / (10000.0 ** (np.arange(quarter, dtype=np.float32) / quarter))).astype(np.float32)
    r = np.arange(rows, dtype=np.float32)[:, None] * omega[None, :]
    cl = np.arange(GRID_W, dtype=np.float32)[:, None] * omega[None, :]
    er = np.concatenate([np.sin(r), np.cos(r)], axis=-1)
    ec = np.concatenate([np.sin(cl), np.cos(cl)], axis=-1)
    emb = np.concatenate([np.broadcast_to(er[:, None, :], (rows, GRID_W, d // 2)),
                          np.broadcast_to(ec[None, :, :], (rows, GRID_W, d // 2))], axis=-1)
    return emb.reshape(n_tok, d).astype(np.float32)


def _tables(a):
    bf = ml_dtypes.bfloat16
    t1 = np.arange(64); f1s = np.arange(64)
    f1 = f1s if a == 0 else 63 - f1s
    angA = (2.0 * np.pi / 64.0) * ((t1[:, None] * f1[None, :]) % 64).astype(np.float64)
    da = np.zeros((128, 2, 128), np.float64)
    for hf in range(2):
        da[hf * 64:(hf + 1) * 64, 0, hf * 64:(hf + 1) * 64] = np.cos(angA) / 8.0
        da[hf * 64:(hf + 1) * 64, 1, hf * 64:(hf + 1) * 64] = -np.sin(angA) / 8.0
    t2 = np.arange(64); f2s = np.arange(32)
    f2 = f2s if a == 0 else 63 - f2s
    f = f1[:, None] + 64 * f2[None, :]
    th = (2.0 * np.pi / 4096.0) * ((t2[:, None, None].astype(np.int64) * f[None, :, :].astype(np.int64)) % 4096).astype(np.float64)
    co = np.cos(th) / 8.0; si = np.sin(th) / 8.0
    mc = np.zeros((2, 64, 64, 2, 32), np.float64)
    mc[0, :, :, 0, :] = co; mc[0, :, :, 1, :] = -si
    mc[1, :, :, 0, :] = si; mc[1, :, :, 1, :] = co
    mc = mc.reshape(128, 64, 64)
    tp = np.arange(256)
    fp = tp if a == 0 else 255 - tp
    angp = (2.0 * np.pi / 256.0) * ((tp[:, None] * fp[None, :]) % 256).astype(np.float64)
    flp = np.concatenate([np.cos(angp) / 16.0, -np.sin(angp) / 16.0], axis=1)
    flp = np.ascontiguousarray(flp.reshape(2, 128, 512)).astype(bf)
    c = np.arange(256)
    angc = (2.0 * np.pi / 256.0) * ((c[:, None] * c[None, :]) % 256).astype(np.float64)
    cc = np.cos(angc) / 16.0; sc = np.sin(angc) / 16.0
    cst = np.stack([cc, sc], axis=1)
    cst = cst.reshape(2, 128, 2, 256).transpose(1, 0, 2, 3)
    cst = np.ascontiguousarray(cst).astype(bf)
    return np.ascontiguousarray(da).astype(bf), np.ascontiguousarray(mc).astype(bf), flp, cst


_NC_CACHE = {}
STAGE = 4


def kernel(x_prompt, x_sample, state_hgrn, c, c_ctx, ada_w, ada_b, norm_mix, norm_mlp, fnet_wo,
           hgrn_w_in, hgrn_lb, hgrn_norm, hgrn_wo, mlp_w1, mlp_w2, norm_final, _stage=None):
    stage = STAGE if _stage is None else _stage
    f32 = np.float32
    bf = ml_dtypes.bfloat16
    x_prompt = np.asarray(x_prompt, f32); x_sample = np.asarray(x_sample, f32)
    state_hgrn = np.asarray(state_hgrn, f32); c = np.asarray(c, f32); c_ctx = np.asarray(c_ctx, f32)
    ada_w = np.ascontiguousarray(np.asarray(ada_w, f32)); ada_b = np.asarray(ada_b, f32)
    norm_mix = np.asarray(norm_mix, f32); norm_mlp = np.asarray(norm_mlp, f32)
    fnet_wo = np.ascontiguousarray(np.asarray(fnet_wo, f32)[0]); hgrn_w_in = np.asarray(hgrn_w_in, f32)[0]
    hgrn_lb = np.asarray(hgrn_lb, f32); hgrn_norm = np.asarray(hgrn_norm, f32)
    hgrn_wo = np.ascontiguousarray(np.asarray(hgrn_wo, f32)[0])
    mlp_w1 = np.ascontiguousarray(np.asarray(mlp_w1, f32)); mlp_w2 = np.ascontiguousarray(np.asarray(mlp_w2, f32))
    norm_final = np.asarray(norm_final, f32)

    if stage not in _NC_CACHE:
        _NC_CACHE[stage] = build(stage)
    nc = _NC_CACHE[stage]
    pos = _pos_embed(4096, D)
    tabs = [_tables(0), _tables(1)]
    adab = np.ascontiguousarray(ada_b.reshape(2, 48, 128).transpose(2, 0, 1))
    nrm = np.stack([norm_mix[0], norm_mlp[0], norm_mix[1], norm_mlp[1]], 0).reshape(4, 8, 128).transpose(2, 0, 1)
    nrm = np.ascontiguousarray(nrm)
    identf = np.eye(128, dtype=f32); identb = np.eye(128).astype(bf)
    s_i = np.arange(128)[:, None]; t_i = np.arange(128)[None, :]
    same = (s_i // 32) == (t_i // 32)
    maskAB = np.stack([(same & (s_i <= t_i)), (same & (s_i >= t_i))], axis=1).astype(f32).astype(bf)
    rmask = np.ones((128, 512), f32); rmask[:, ::32] = 0.0
    wq, wff, wfb, wi, wg = np.split(hgrn_w_in, 5, axis=1)
    in_maps = []
    for r in range(8):
        p, a = r // 2, r % 2
        xs = x_sample[p]
        if a == 0:
            loc = np.arange(2048)
        else:
            loc = 4095 - np.arange(2048)
        prs = [x_prompt[2 * r], x_prompt[2 * r + 1]]
        if a == 1:
            prs_loc = [pp[::-1] for pp in prs]
        else:
            prs_loc = prs
        x_my = np.concatenate([xs[loc]] + prs_loc, axis=0)
        x_all = np.concatenate([xs] + prs, axis=0)
        da, mc, flp, cst = tabs[a]
        cnd = np.stack([c[p], c_ctx], axis=-1).reshape(8, 128, 2).transpose(1, 0, 2).reshape(128, 16)
        if a == 0:
            w_in = np.concatenate([wq, wff, wfb, wi, wg], axis=1); lbr = hgrn_lb
        else:
            w_in = np.concatenate([wq, wfb, wff, wi, wg], axis=1); lbr = hgrn_lb[::-1]
        lbraw = np.ascontiguousarray(lbr.reshape(2, 2, 8, 128).transpose(3, 0, 1, 2))
        selm = np.zeros((128, 2), f32); selm[:, 1 - a] = 1.0
        sA0 = np.ascontiguousarray(state_hgrn[p, 0, a].transpose(1, 0, 2))
        in_maps.append({
            "x_my": np.ascontiguousarray(x_my), "pos_my": np.ascontiguousarray(pos[loc]),
            "x_all": np.ascontiguousarray(x_all), "pos_all": pos,
            "da_in": da, "mc_in": mc, "flp": flp, "cs": cst, "cond": np.ascontiguousarray(cnd),
            "ada_w": ada_w, "adab": adab, "nrm": nrm, "nfin": norm_final.reshape(1, D),
            "fnet_wo": fnet_wo, "w_in": np.ascontiguousarray(w_in), "lbraw": lbraw,
            "gnorm": np.ascontiguousarray(hgrn_norm.reshape(128, 1)), "hgrn_wo": hgrn_wo,
            "w1": mlp_w1, "w2": mlp_w2, "sA0": sA0, "selm": selm, "maskAB": np.ascontiguousarray(maskAB),
            "identf": identf, "identb": identb, "rmask": rmask,
        })
    res = run_bass_kernel_spmd(nc, in_maps, core_ids=list(range(8)))
    y_prompt = np.zeros((16, 256, D), f32); y_sample = np.zeros((4, 4096, D), f32)
    new_state = np.zeros((16, 1, 2, 8, 128, 128), f32)
    for r in range(8):
        p, a = r // 2, r % 2
        y = res.results[r]["y_my"]
        st = res.results[r]["st_out"]
        if a == 0:
            y_sample[p, 0:2048] = y[0:2048]
            y_prompt[2 * r] = y[2048:2304]; y_prompt[2 * r + 1] = y[2304:2560]
        else:
            y_sample[p, 2048:4096] = y[0:2048][::-1]
            y_prompt[2 * r] = y[2048:2304][::-1]; y_prompt[2 * r + 1] = y[2304:2560][::-1]
        for q in range(2):
            for dd in range(2):
                gd = dd if a == 0 else 1 - dd
                new_state[2 * r + q, 0, gd] = st[q, dd]
    return (y_prompt, y_sample, new_state)
```

```python
import numpy as np
import ml_dtypes
from contextlib import ExitStack
import concourse.bass as bass
import concourse.mybir as mybir
from concourse.bass_utils import run_bass_kernel_spmd

F32 = mybir.dt.float32
BF16 = mybir.dt.bfloat16
AF = mybir.ActivationFunctionType
ALU = mybir.AluOpType

D = 1024
T = 2560
TS = 2048
NT = 5
EPS = 1e-6
NBLK = 20


class _Op:
    pass


class Sched:
    ENG = ['pe', 'act', 'dve', 'pool', 'sp']

    def __init__(s):
        s.q = {e: [] for e in s.ENG}
        s.lastw = {}
        s.readers = {}
        s.barrier = []
        s.dma_ring = {'sp': 12, 'act': 6, 'pool': 12}
        s.dma_next = {k: 0 for k in s.dma_ring}
        s.slot_last = {}

    def _deps(s, reads, writes):
        d = []
        for b in reads:
            if b in s.lastw:
                d.append(s.lastw[b])
        for b in writes:
            if b in s.lastw:
                d.append(s.lastw[b])
            d.extend(s.readers.get(b, ()))
        d.extend(s.barrier)
        return d

    def _commit(s, o, reads, writes):
        for b in reads:
            s.readers.setdefault(b, []).append(o)
        for b in writes:
            s.lastw[b] = o
            s.readers[b] = []

    def op(s, eng, fn, reads=(), writes=()):
        psr = [b for b in reads if isinstance(b, str) and b.startswith('ps')]
        if psr:
            reads = [b for b in reads if b not in psr]
            writes = list(writes) + psr
        o = _Op()
        o.eng = eng; o.fn = fn; o.is_dma = False; o.sig = False; o.sigidx = 0
        o.deps = s._deps(reads, writes)
        s.q[eng].append(o)
        s._commit(o, reads, writes)
        return o

    def dma(s, q, fn, reads=(), writes=()):
        o = _Op()
        o.eng = q; o.fn = fn; o.is_dma = True; o.sig = False
        o.deps = s._deps(reads, writes)
        slot = s.dma_next[q]
        s.dma_next[q] = (slot + 1) % s.dma_ring[q]
        o.slot = (q, slot)
        prev = s.slot_last.get(o.slot)
        o.slotval = (prev.slotval if prev else 0) + 16
        if prev is not None:
            o.deps.append(prev)
        s.slot_last[o.slot] = o
        s.q[q].append(o)
        s._commit(o, reads, writes)
        return o

    def special(s, q, fn, key, reads=(), writes=()):
        o = _Op()
        o.eng = q; o.fn = fn; o.is_dma = True; o.sig = False
        o.deps = s._deps(reads, writes)
        o.slot = key
        o.slotval = 1
        o.incval = 1
        s.slot_last[o.slot] = o
        s.q[q].append(o)
        s._commit(o, reads, writes)
        return o

    def set_barrier(s):
        b = []
        for e in s.ENG:
            for o in reversed(s.q[e]):
                if not o.is_dma:
                    b.append(o)
                    break
        for o in s.slot_last.values():
            b.append(o)
        s.barrier = b

    def finalize(s):
        for e in s.ENG:
            for o in s.q[e]:
                for d in o.deps:
                    if d.is_dma:
                        continue
                    if d.eng == 'pe' and e == 'pe' and not o.is_dma:
                        continue
                    d.sig = True
        for e in s.ENG:
            c = 0
            for o in s.q[e]:
                if (not o.is_dma) and o.sig:
                    c += 1
                    o.sigidx = c
        for e in s.ENG:
            waited = {}
            for o in s.q[e]:
                w = {}
                for d in o.deps:
                    if d.is_dma:
                        key = ('dma',) + d.slot
                        val = d.slotval
                    else:
                        if d.eng == 'pe' and e == 'pe' and not o.is_dma:
                            continue
                        key = d.eng
                        val = d.sigidx
                    if waited.get(key, 0) >= val:
                        continue
                    if w.get(key, 0) < val:
                        w[key] = val
                for k, v in w.items():
                    waited[k] = v
                o.waits = list(w.items())

    def emit(s, nc, block, sems):
        def run(e, eng):
            for o in s.q[e]:
                for k, v in o.waits:
                    eng.wait_ge(sems[k], v)
                ins = o.fn(eng)
                if o.is_dma:
                    ins.then_inc(sems[('dma',) + o.slot], getattr(o, 'incval', 16))
                elif o.sig:
                    ins.then_inc(sems[e], 1)
            if e == 'sp':
                for slot, o in s.slot_last.items():
                    eng.wait_ge(sems[('dma',) + slot], o.slotval)

        @block.tensor
        def _(eng):
            run('pe', eng)

        @block.scalar
        def _(eng):
            run('act', eng)

        @block.vector
        def _(eng):
            run('dve', eng)

        @block.gpsimd
        def _(eng):
            run('pool', eng)

        @block.sync
        def _(eng):
            run('sp', eng)


def build(stage=4):
    nc = bass.Bass("TRN2", target_bir_lowering=False)
    S = Sched()

    def din(name, shape, dt=F32):
        return nc.dram_tensor(name, list(shape), dt, kind="ExternalInput").ap()

    x_my = din("x_my", [T, D]); pos_my = din("pos_my", [TS, D])
    x_all = din("x_all", [4608, D]); pos_all = din("pos_all", [4096, D])
    flp = din("flp", [2, 128, 512], BF16)
    cs = din("cs", [128, 2, 2, 256], BF16)
    cond = din("cond", [128, 16]); ada_w = din("ada_w", [2, D, 6 * D]); adab = din("adab", [128, 2, 48])
    nrm = din("nrm", [128, 4, 8]); nfin = din("nfin", [1, D])
    fnet_wo = din("fnet_wo", [D, D]); w_in = din("w_in", [D, 5 * D]); lbraw = din("lbraw", [128, 2, 2, 8])
    gnorm = din("gnorm", [128, 1]); hgrn_wo = din("hgrn_wo", [D, D])
    w1 = din("w1", [2, D, 4 * D]); w2 = din("w2", [2, 4 * D, D])
    sA0 = din("sA0", [128, 8, 128]); selm = din("selm", [128, 2])
    maskAB = din("maskAB", [128, 2, 128], BF16)
    identf = din("identf", [128, 128]); identb = din("identb", [128, 128], BF16)
    rmask = din("rmask", [128, 512])
    da_in = din("da_in", [128, 2, 128], BF16); mc_in = din("mc_in", [128, 64, 64], BF16)
    y_my = nc.dram_tensor("y_my", [T, D], F32, kind="ExternalOutput").ap()
    st_out = nc.dram_tensor("st_out", [2, 2, 8, 128, 128], F32, kind="ExternalOutput").ap()
    scr_x = nc.dram_tensor("scr_x", [8, 128, T], F32, kind="Internal").ap()
    scr_vec = nc.dram_tensor("scr_vec", [4, D], F32, kind="Internal").ap()
    zs = nc.dram_tensor("zs", [64, 64, 2, D], BF16, kind="Internal").ap()
    scr_og = nc.dram_tensor("scr_og", [8, 128, T], BF16, kind="Internal").ap()
    cc_in = nc.dram_tensor("cc_in", [8, 128, 128], F32, kind="Internal").ap()
    cc_out = nc.dram_tensor("cc_out", [8, 256, 128], F32, kind="Internal", addr_space="Local").ap()

    es = ExitStack()
    with es:
        def sb(name, shape, dt=F32):
            return es.enter_context(nc.sbuf_tensor(name, list(shape), dt))

        BIG = sb("BIG", [128, 20480], F32)
        HB = sb("HB", [128, 20480], BF16)
        WB = sb("WB", [128, 18432], BF16)
        MISC = sb("MISC", [128, 6144], F32)
        SM = sb("SM", [128, 1024], F32)
        IDF = sb("IDF", [128, 128], F32)
        IDB = sb("IDB", [128, 128], BF16)
        MSK = sb("MSK", [128, 2, 128], BF16)
        ONES = sb("ONES", [128, 128], BF16)
        SCB = sb("SCB", [128, 16], BF16)
        CST = sb("CST", [128, 2, 2, 256], BF16)
        RM = sb("RM", [128, 512], F32)
        DAT = sb("DAT", [128, 2, 128], BF16)
        MCT = sb("MCT", [128, 64, 64], BF16)
        ps = [es.enter_context(nc.psum_tensor(f"ps{i}", [128, 512], F32)) for i in range(8)]
        sems = {}
        for e in Sched.ENG:
            sems[e] = es.enter_context(nc.semaphore(f"sem_{e}"))
        for q, n in S.dma_ring.items():
            for i in range(n):
                sems[('dma', q, i)] = es.enter_context(nc.semaphore(f"dsem_{q}_{i}"))
        for hh_ in range(8):
            sems[('dma', 'cc', hh_)] = es.enter_context(nc.semaphore(f"ccsem_{hh_}"))

        xT = BIG[:, :].rearrange("p (k t) -> p k t", k=8)
        hT = HB[:, :].rearrange("p (k t) -> p k t", k=8)
        BIGb = BIG[:, :].bitcast(BF16)

        MOD = SM[:, 0:192].rearrange("p (i j c) -> p i j c", i=2, j=48)
        AM = SM[:, 192:448].rearrange("p (n k c) -> p n k c", n=4, k=8)
        SSQ = SM[:, 448:512]
        RSTD = SM[:, 512:576]
        NRM = SM[:, 576:608].rearrange("p (n k) -> p n k", n=4)
        ADB = SM[:, 608:704].rearrange("p (i j) -> p i j", i=2)
        CND = SM[:, 704:720]
        EPST = SM[:, 720:721]
        LB = SM[:, 736:768].rearrange("p (d l h) -> p d l h", d=2, l=2)
        LBV = SM[:, 768:784].rearrange("p (d h) -> p d h", d=2)
        OML = SM[:, 784:800].rearrange("p (d h) -> p d h", d=2)
        GN = SM[:, 800:801]
        SEL = SM[:, 801:803]
        ACOL = SM[:, 816:976].rearrange("p (d c) -> p d c", d=2)

        S.dma('sp', lambda e: e.dma_start(out=IDF[:, :], in_=identf[:, :]), [], ['IDF'])
        S.dma('sp', lambda e: e.dma_start(out=IDB[:, :], in_=identb[:, :]), [], ['IDB'])
        S.dma('sp', lambda e: e.dma_start(out=MSK[:, :, :], in_=maskAB[:, :, :]), [], ['MSK'])
        S.dma('sp', lambda e: e.dma_start(out=CST[:, :, :, :], in_=cs[:, :, :, :]), [], ['CST'])
        S.dma('sp', lambda e: e.dma_start(out=RM[:, :], in_=rmask[:, :]), [], ['RM'])
        S.dma('sp', lambda e: e.dma_start(out=CND, in_=cond[:, :]), [], ['CND'])
        S.dma('sp', lambda e: e.dma_start(out=ADB, in_=adab[:, :, :]), [], ['ADB'])
        S.dma('sp', lambda e: e.dma_start(out=NRM, in_=nrm[:, :, :]), [], ['NRM'])
        S.dma('sp', lambda e: e.dma_start(out=LB, in_=lbraw[:, :, :, :]), [], ['LB'])
        S.dma('sp', lambda e: e.dma_start(out=GN, in_=gnorm[:, :]), [], ['GN'])
        S.dma('sp', lambda e: e.dma_start(out=SEL, in_=selm[:, :]), [], ['SEL'])
        S.op('dve', lambda e: e.memset(ONES[:, :], 1.0), [], ['ONES'])
        S.op('dve', lambda e: e.memset(EPST, EPS), [], ['EPST'])

        import os
        _skip = os.environ.get('KSKIP', '').split(',')
        S.op('act', lambda e: e.activation(out=SCB[:, :], in_=CND, func=AF.Silu), ['CND'], ['SCB'])
        scb3 = SCB[:, :].rearrange("p (k c) -> p k c", c=2)
        AWB = [WB[:, 10240:14336].rearrange("p (k n) -> p k n", k=8), WB[:, 14336:18432].rearrange("p (k n) -> p k n", k=8)]
        ada_blocks = [(0, 0, 0), (0, 0, 1), (0, 1, 0), (0, 1, 1)] + [(0, j, hh) for j in range(2, 6) for hh in range(2)] \
            + [(1, j, hh) for j in range(6) for hh in range(2)]
        ada_cnt = [0]

        def ada_block():
            if ada_cnt[0] >= len(ada_blocks):
                return
            i, j, hh = ada_blocks[ada_cnt[0]]
            b = ada_cnt[0] % 2
            ada_cnt[0] += 1
            buf = AWB[b]; key = f"AWB{b}"
            src = ada_w[i, :, j * 1024 + hh * 512:j * 1024 + (hh + 1) * 512].rearrange("(k p) n -> p k n", p=128)
            S.dma('pool', lambda e, buf=buf, src=src: e.dma_start(out=buf, in_=src), [], [key])
            for o4 in range(4):
                oc = hh * 4 + o4
                col = (j * 8 + oc) * 2
                for kc in range(8):
                    S.op('pe', lambda e, buf=buf, o4=o4, kc=kc, col=col, i=i: e.matmul(
                        ps[i][:, col:col + 2], buf[:, kc, o4 * 128:(o4 + 1) * 128], scb3[:, kc, :],
                        start=(kc == 0), stop=(kc == 7)), [key, 'SCB'], [f"ps{i}"])

        def mod_evac(i, j0, j1):
            for c in range(2):
                S.op('dve', lambda e, i=i, c=c, j0=j0, j1=j1: e.tensor_tensor(
                    out=MOD[:, i, j0 * 8:j1 * 8, c], in0=ps[i][:, j0 * 16 + c:j1 * 16:2], in1=ADB[:, i, j0 * 8:j1 * 8], op=ALU.add),
                    [f"ps{i}", 'ADB'], [f"MOD{i}_{j0}"])

        def am_derive(n):
            i = n // 2; j = 1 if n % 2 == 0 else 4
            for c in range(2):
                S.op('dve', lambda e, n=n, i=i, j=j, c=c: e.scalar_tensor_tensor(
                    out=AM[:, n, :, c], in0=MOD[:, i, j * 8:(j + 1) * 8, c], scalar=1.0, in1=NRM[:, n, :],
                    op0=ALU.add, op1=ALU.mult), [f"MOD{i}_0", f"MOD{i}_2", 'NRM'], [f"AM{n}"])

        for _ in range(4):
            ada_block()
        mod_evac(0, 0, 2)
        am_derive(0)

        ALLX = [f"xT{k_}_{j_}" for k_ in range(8) for j_ in range(NT)]
        ALLH = [f"hT{k_}_{j_}" for k_ in range(8) for j_ in range(NT)]

        def modv(i, j, kc, c):
            return MOD[:, i, j * 8 + kc, c:c + 1]

        if 'fnet' not in _skip:
            ABC = WB[:, 6144:10240].rearrange("p (a n) -> p a n", a=4)
            DV = MISC[:, 4096:4608].bitcast(BF16).rearrange("p (k j) -> p k j", k=8)
            for v in range(4):
                c = v % 2
                vec = AM[:, 0, :, c] if v < 2 else MOD[:, 0, 0:8, c]
                S.op('dve', lambda e, vec=vec: e.tensor_tensor(
                    out=DV, in0=IDB[:, :].unsqueeze(1).to_broadcast([128, 8, 128]), in1=vec.unsqueeze(2).to_broadcast([128, 8, 128]), op=ALU.mult),
                    ['IDB', 'AM0', 'MOD0_0'], ['JUNK'])
                for hh in range(2):
                    S.op('pe', lambda e, hh=hh: e.matmul(ps[2 + hh][:, :], ONES[:, :], DV[:, hh * 4:(hh + 1) * 4, :], start=True, stop=True),
                         ['JUNK', 'ONES'], [f"ps{2 + hh}"])
                    if hh == 0:
                        S.op('act', lambda e, v=v, hh=hh: e.activation(out=ABC[:, v, hh * 512:(hh + 1) * 512], in_=ps[2 + hh][:, :], func=AF.Copy),
                             [f"ps{2 + hh}"], ['ABC'])
                    else:
                        S.op('dve', lambda e, v=v, hh=hh: e.tensor_copy(out=ABC[:, v, hh * 512:(hh + 1) * 512], in_=ps[2 + hh][:, :]),
                             [f"ps{2 + hh}"], ['ABC'])

            hn = BIGb[:, 0:36864].rearrange("p (c n) -> p c n", c=36)
            XIN = [MISC[:, 0:1024], MISC[:, 1024:2048]]
            PIN = [MISC[:, 2048:3072], MISC[:, 2048:3072]]
            TMPF = MISC[:, 3072:4096]
            JUNK = MISC[:, 4096:4608].bitcast(BF16)
            RB = MISC[:, 4608:5120]
            SQB = [MISC[:, 5120:5376].bitcast(BF16), MISC[:, 5376:5632].bitcast(BF16)]
            RL = [MISC[:, 5632:5888].bitcast(BF16), MISC[:, 5888:6144].bitcast(BF16)]
            XR = [XIN[0], XIN[1], WB[:, 0:2048].bitcast(F32)]
            PR = [PIN[0], WB[:, 2048:4096].bitcast(F32)]
            TR = [TMPF, WB[:, 4096:6144].bitcast(F32)]
            for ci in range(36):
                b = ci % 3; pb_ = ci % 2; tb_ = ci % 2
                c = 0 if ci < 32 else 1
                xk = f"XR{b}"; pk = f"PR{pb_}"; tk = f"TR{tb_}"
                if ci < 32:
                    for hf in range(2):
                        S.dma('sp', lambda e, ci=ci, b=b, hf=hf: e.dma_start(
                            out=XR[b][hf * 64:(hf + 1) * 64, :],
                            in_=x_all[0:4096, :].rearrange("(t1 r) c -> r t1 c", r=64)[32 * hf + ci]), [], [xk])
                        S.dma('sp', lambda e, ci=ci, pb_=pb_, hf=hf: e.dma_start(
                            out=PR[pb_][hf * 64:(hf + 1) * 64, :],
                            in_=pos_all[0:4096, :].rearrange("(t1 r) c -> r t1 c", r=64)[32 * hf + ci]), [], [pk])
                    S.op('pool', lambda e, b=b, pb_=pb_: e.tensor_tensor(out=XR[b][:, :], in0=XR[b][:, :], in1=PR[pb_][:, :], op=ALU.add),
                         [xk, pk], [xk])
                else:
                    S.dma('sp', lambda e, ci=ci, b=b: e.dma_start(out=XR[b][:, :], in_=x_all[ci * 128:(ci + 1) * 128, :]), [], [xk])
                S.op('act', lambda e, ci=ci, b=b: e.activation(out=JUNK[:, :], in_=XR[b][:, :], func=AF.Square,
                                                               accum_out=SSQ[:, ci:ci + 1]), [xk], ['JUNK', f"SSQ{ci}"])
                S.op('act', lambda e, ci=ci: e.activation(out=RSTD[:, ci:ci + 1], in_=SSQ[:, ci:ci + 1], func=AF.Sqrt,
                                                          bias=EPST, scale=1.0 / D), [f"SSQ{ci}", 'EPST'], [f"RSTD{ci}"])
                S.op('dve', lambda e, ci=ci: e.reciprocal(out=RSTD[:, ci:ci + 1], in_=RSTD[:, ci:ci + 1]),
                     [f"RSTD{ci}"], [f"RSTD{ci}"])
                S.op('dve', lambda e, ci=ci, b=b, c=c, tb_=tb_: e.scalar_tensor_tensor(
                    out=TR[tb_][:, :], in0=XR[b][:, :], scalar=RSTD[:, ci:ci + 1], in1=ABC[:, c, :],
                    op0=ALU.mult, op1=ALU.mult), [xk, f"RSTD{ci}", 'ABC'], [tk])
                S.op('dve', lambda e, ci=ci, c=c, tb_=tb_: e.tensor_tensor(out=hn[:, ci, :], in0=TR[tb_][:, :], in1=ABC[:, 2 + c, :], op=ALU.add),
                     [tk, 'ABC'], [f"hn{ci}"])
                ada_block()
            while ada_cnt[0] < len(ada_blocks):
                ada_block()
            mod_evac(0, 2, 6)
            mod_evac(1, 0, 2)
            mod_evac(1, 2, 6)
            am_derive(1); am_derive(2); am_derive(3)
            S.set_barrier()

            ZST = [WB[:, z * 2048:(z + 1) * 2048].rearrange("p (r c) -> p r c", r=2) for z in range(8)]
            MIXF = HB[:, 4096:20480].rearrange("p (k t) -> p k t", k=8)
            MIXP = HB[:, 0:4096].rearrange("p (k t) -> p k t", k=8)
            YTGS = [[WB[:, (2 * y_ + r_) * 4096:(2 * y_ + r_ + 1) * 4096].rearrange("p (k t) -> p k t", k=2) for r_ in range(2)] for y_ in range(2)]
            YTP = [WB[:, 0:4096].rearrange("p (k t) -> p k t", k=8), WB[:, 4096:8192].rearrange("p (k t) -> p k t", k=8)]
            FTP = [WB[:, 8192:9216].rearrange("p (c n) -> p c n", c=2), WB[:, 9216:10240].rearrange("p (c n) -> p c n", c=2)]
            WO = WB[:, 10240:18432].rearrange("p (k n) -> p k n", k=8)
            Z1T = [BIGb[:, 0:16384].rearrange("p (f c) -> p f c", f=64), BIGb[:, 16384:32768].rearrange("p (f c) -> p f c", f=64)]
            XTILES = [BIG[:, 0:4096].rearrange("p (k t) -> p k t", k=8), BIG[:, 4096:8192].rearrange("p (k t) -> p k t", k=8)]
            S.dma('sp', lambda e: e.dma_start(out=DAT[:, :, :], in_=da_in[:, :, :]), [], ['DAT'])
            S.dma('sp', lambda e: e.dma_start(out=MCT[:, :, :], in_=mc_in[:, :, :]), [], ['MCT'])
            for t2 in range(32):
                zb = t2 % 8
                for ri in range(2):
                    for ch in range(2):
                        bank = (t2 % 2) * 4 + ri * 2 + ch
                        S.op('pe', lambda e, t2=t2, ri=ri, ch=ch, bank=bank: e.matmul(
                            ps[bank][:, :], DAT[:, ri, :], hn[:, t2, ch * 512:(ch + 1) * 512], start=True, stop=True),
                            ['DAT', f"hn{t2}"], [f"ps{bank}"])
                        if (ri + ch) % 2 == 0:
                            S.op('act', lambda e, zb=zb, ri=ri, ch=ch, bank=bank: e.activation(
                                out=ZST[zb][:, ri, ch * 512:(ch + 1) * 512], in_=ps[bank][:, :], func=AF.Copy), [f"ps{bank}"], [f"ZST{zb}_{ri}_{ch}"])
                        else:
                            S.op('dve', lambda e, zb=zb, ri=ri, ch=ch, bank=bank: e.tensor_copy(
                                out=ZST[zb][:, ri, ch * 512:(ch + 1) * 512], in_=ps[bank][:, :]), [f"ps{bank}"], [f"ZST{zb}_{ri}_{ch}"])
                for hf in range(2):
                    S.dma('sp' if hf == 0 else 'pool', lambda e, t2=t2, zb=zb, hf=hf: e.dma_start(
                        out=zs[:, 32 * hf + t2, :, :], in_=ZST[zb][hf * 64:(hf + 1) * 64, :, :]),
                        [f"ZST{zb}_{r_}_{c_}" for r_ in range(2) for c_ in range(2)], [f"zs{t2}_{hf}"])
            S.set_barrier()
            for g in range(4):
                zb = g % 2
                YTG = YTGS[g % 2]; yk = f"YTG{g % 2}"
                for ri in range(2):
                    q_ = 'sp' if ri == 0 else 'act'
                    S.dma(q_, lambda e, g=g, zb=zb, ri=ri: e.dma_start(
                        out=Z1T[zb][ri * 64:(ri + 1) * 64, :, :],
                        in_=zs[:, :, ri, g * 256:(g + 1) * 256].rearrange("f t c -> t f c")), [], [f"Z1T{zb}"])
                for cl in range(2):
                    for a8 in range(8):
                        bank = a8
                        for f1l in range(8):
                            f1s = a8 * 8 + f1l
                            S.op('pe', lambda e, zb=zb, cl=cl, bank=bank, f1l=f1l, f1s=f1s: e.matmul(
                                ps[bank][:, f1l * 64:(f1l + 1) * 64], Z1T[zb][:, f1s, cl * 128:(cl + 1) * 128], MCT[:, f1s, :],
                                start=True, stop=True), [f"Z1T{zb}", 'MCT'], [f"ps{bank}"])
                        for ri in range(2):
                            src = ps[bank][:, :].rearrange("p (f r t) -> p f r t", f=8, r=2)[:, :, ri, :]
                            dst = YTG[ri][:, cl, :].rearrange("p (t f) -> p f t", f=64)[:, a8 * 8:(a8 + 1) * 8, :]
                            if a8 % 2 == 0:
                                S.op('act', lambda e, src=src, dst=dst: e.activation(out=dst, in_=src, func=AF.Copy), [f"ps{bank}"], [f"{yk}_{ri}_{cl}_{a8}"])
                            else:
                                S.op('dve', lambda e, src=src, dst=dst: e.tensor_copy(out=dst, in_=src), [f"ps{bank}"], [f"{yk}_{ri}_{cl}_{a8}"])
                for half in range(2):
                    oc = 2 * g + half
                    for jt in range(4):
                        bank = (half * 4 + jt) % 8
                        n = 0
                        for kc in range(2):
                            for ri in range(2):
                                S.op('pe', lambda e, half=half, kc=kc, ri=ri, jt=jt, n=n, bank=bank, YTG=YTG: e.matmul(
                                    ps[bank][:, :], CST[:, kc, ri, half * 128:(half + 1) * 128], YTG[ri][:, kc, jt * 512:(jt + 1) * 512],
                                    start=(n == 0), stop=(n == 3)), ['CST'] + [f"{yk}_{ri}_{kc}_{a}" for a in range(8)], [f"ps{bank}"])
                                n += 1
                        if jt % 2 == 0:
                            S.op('act', lambda e, oc=oc, jt=jt, bank=bank: e.activation(out=MIXF[:, oc, jt * 512:(jt + 1) * 512], in_=ps[bank][:, :], func=AF.Copy),
                                 [f"ps{bank}"], [f"MIXF{oc}_{jt}"])
                        else:
                            S.op('dve', lambda e, oc=oc, jt=jt, bank=bank: e.tensor_copy(out=MIXF[:, oc, jt * 512:(jt + 1) * 512], in_=ps[bank][:, :]),
                                 [f"ps{bank}"], [f"MIXF{oc}_{jt}"])
            S.set_barrier()
            S.dma('pool', lambda e: e.dma_start(out=WO, in_=fnet_wo[:, :].rearrange("(k p) n -> p k n", p=128)), [], ['WO'])
            ftc = [0]
            for sq in range(2):
                fb = ftc[0] % 2; ftc[0] += 1
                S.dma('sp', lambda e, fb=fb: e.dma_start(out=FTP[fb], in_=flp[:, :, :].rearrange("c p n -> p c n")), [], [f"FTP{fb}"])
                for tc in range(2):
                    for cc in range(8):
                        S.op('pe', lambda e, fb=fb, tc=tc, cc=cc, sq=sq: e.matmul(
                            ps[cc][:, :], hn[:, 32 + 2 * sq + tc, cc * 128:(cc + 1) * 128], FTP[fb][:, tc, :],
                            start=(tc == 0), stop=(tc == 1)), [f"FTP{fb}", f"hn{32 + 2 * sq + tc}"], [f"ps{cc}"])
                for cc in range(8):
                    for ri in range(2):
                        if cc % 2 == 0:
                            S.op('act', lambda e, cc=cc, ri=ri, sq=sq: e.activation(
                                out=YTP[ri][:, cc, sq * 256:(sq + 1) * 256], in_=ps[cc][:, ri * 256:(ri + 1) * 256], func=AF.Copy),
                                [f"ps{cc}"], [f"YTP{ri}"])
                        else:
                            S.op('dve', lambda e, cc=cc, ri=ri, sq=sq: e.tensor_copy(
                                out=YTP[ri][:, cc, sq * 256:(sq + 1) * 256], in_=ps[cc][:, ri * 256:(ri + 1) * 256]),
                                [f"ps{cc}"], [f"YTP{ri}"])
            for oc in range(8):
                g = oc // 2; half = oc % 2
                n = 0
                for kc in range(2):
                    for ri in range(2):
                        S.op('pe', lambda e, oc=oc, g=g, half=half, kc=kc, ri=ri, n=n: e.matmul(
                            ps[oc][:, :], CST[:, kc, ri, half * 128:(half + 1) * 128], YTP[ri][:, g * 2 + kc, :],
                            start=(n == 0), stop=(n == 3)), ['CST', f"YTP{ri}"], [f"ps{oc}"])
                        n += 1
                if oc % 2 == 0:
                    S.op('act', lambda e, oc=oc: e.activation(out=MIXP[:, oc, :], in_=ps[oc][:, :], func=AF.Copy), [f"ps{oc}"], [f"MIXF{oc}_4"])
                else:
                    S.op('dve', lambda e, oc=oc: e.tensor_copy(out=MIXP[:, oc, :], in_=ps[oc][:, :]), [f"ps{oc}"], [f"MIXF{oc}_4"])
            S.set_barrier()
            XQ = [XIN[0], XIN[1], WB[:, 0:2048].bitcast(F32), WB[:, 2048:4096].bitcast(F32)]
            PQ = [PIN[0], WB[:, 4096:6144].bitcast(F32), WB[:, 6144:8192].bitcast(F32)]
            xq_i = [0]
            for jt in range(NT):
                XT = XTILES[jt % 2]; xtk = f"XTILE{jt % 2}"
                for q4 in range(4):
                    b = xq_i[0] % 4; pb_ = xq_i[0] % 3; xq_i[0] += 1
                    r0 = jt * 512 + q4 * 128
                    S.dma('sp', lambda e, b=b, r0=r0: e.dma_start(out=XQ[b][:, :], in_=x_my[r0:r0 + 128, :]), [], [f"XQ{b}"])
                    if jt < 4:
                        S.dma('sp', lambda e, pb_=pb_, r0=r0: e.dma_start(out=PQ[pb_][:, :], in_=pos_my[r0:r0 + 128, :]), [], [f"PQ{pb_}"])
                        S.op('pool', lambda e, b=b, pb_=pb_: e.tensor_tensor(out=XQ[b][:, :], in0=XQ[b][:, :], in1=PQ[pb_][:, :], op=ALU.add),
                             [f"XQ{b}", f"PQ{pb_}"], [f"XQ{b}"])
                    for cc in range(8):
                        S.op('pe', lambda e, b=b, cc=cc, q4=q4: e.transpose(
                            ps[cc][:, q4 * 128:(q4 + 1) * 128], XQ[b][:, cc * 128:(cc + 1) * 128], IDF[:, :]),
                            [f"XQ{b}", 'IDF'], [f"ps{cc}"])
                for cc in range(8):
                    if cc % 2 == 0:
                        S.op('act', lambda e, cc=cc, XT=XT: e.activation(out=XT[:, cc, :], in_=ps[cc][:, :], func=AF.Copy), [f"ps{cc}"], [f"{xtk}_{cc}"])
                    else:
                        S.op('dve', lambda e, cc=cc, XT=XT: e.tensor_copy(out=XT[:, cc, :], in_=ps[cc][:, :]), [f"ps{cc}"], [f"{xtk}_{cc}"])
                c = 0 if jt < 4 else 1
                for oc in range(8):
                    for kc in range(8):
                        rhs = MIXF[:, kc, jt * 512:(jt + 1) * 512] if jt < 4 else MIXP[:, kc, :]
                        S.op('pe', lambda e, oc=oc, kc=kc, rhs=rhs: e.matmul(ps[oc][:, :], WO[:, kc, oc * 128:(oc + 1) * 128], rhs,
                                                                              start=(kc == 0), stop=(kc == 7)), ['WO', f"MIXF{kc}_{jt}"], [f"ps{oc}"])
                    S.op('dve', lambda e, oc=oc, c=c, XT=XT: e.scalar_tensor_tensor(
                        out=XT[:, oc, :], in0=ps[oc][:, :], scalar=modv(0, 2, oc, c), in1=XT[:, oc, :],
                        op0=ALU.mult, op1=ALU.add), [f"ps{oc}", f"{xtk}_{oc}", 'MOD0_2'], [f"{xtk}_{oc}"])
                S.dma('sp', lambda e, jt=jt, XT=XT: e.dma_start(
                    out=scr_x[:, :, jt * 512:(jt + 1) * 512].rearrange("k p t -> p k t"), in_=XT[:, :, :]),
                    [f"{xtk}_{c_}" for c_ in range(8)], [f"scrx{jt}"])

        S.set_barrier()
        S.dma('sp', lambda e: e.dma_start(out=xT, in_=scr_x[:, :, :].rearrange("k p t -> p k t")), [], ALLX)
        S.set_barrier()


        def norm_mod(n, i, jsh):
            TM2 = [TMPF[:, 0:512], TMPF[:, 512:1024]]
            for jt in range(NT):
                c = 0 if jt < 4 else 1
                for kc in range(8):
                    b = kc % 2
                    S.op('act', lambda e, kc=kc, jt=jt, b=b: e.activation(
                        out=SQB[b][:, :], in_=xT[:, kc, jt * 512:(jt + 1) * 512], func=AF.Square), [f"xT{kc}_{jt}"], [f"SQB{b}"])
                    S.op('pe', lambda e, kc=kc, b=b: e.matmul(ps[7][:, :], ONES[:, :], SQB[b][:, :], start=(kc == 0), stop=(kc == 7)),
                         [f"SQB{b}", 'ONES'], ['ps7'])
                S.op('act', lambda e: e.activation(out=RB[:, :], in_=ps[7][:, :], func=AF.Ln, bias=EPST, scale=1.0 / D),
                     ['ps7', 'EPST'], ['RB'])
                S.op('act', lambda e: e.activation(out=RB[:, :], in_=RB[:, :], func=AF.Exp, scale=-0.5), ['RB'], ['RB'])
                for kc in range(8):
                    tb_ = kc % 2
                    S.op('dve', lambda e, kc=kc, jt=jt, c=c, tb_=tb_: e.scalar_tensor_tensor(
                        out=TM2[tb_], in0=xT[:, kc, jt * 512:(jt + 1) * 512], scalar=AM[:, n, kc, c:c + 1], in1=RB[:, :],
                        op0=ALU.mult, op1=ALU.mult), [f"xT{kc}_{jt}", f"AM{n}", 'RB'], [f"TM2{tb_}"])
                    S.op('act', lambda e, kc=kc, jt=jt, c=c, tb_=tb_: e.activation(
                        out=hT[:, kc, jt * 512:(jt + 1) * 512], in_=TM2[tb_], func=AF.Identity,
                        bias=modv(i, jsh, kc, c), scale=1.0), [f"TM2{tb_}", f"MOD{i}_0", f"MOD{i}_2"], [f"hT{kc}_{jt}"])

        W1B = [WB[:, k * 4096:(k + 1) * 4096].rearrange("p (k n) -> p k n", k=8) for k in range(2)]
        W2B = [WB[:, 8192 + k * 4096:8192 + (k + 1) * 4096].rearrange("p (k n) -> p k n", k=4) for k in range(2)]
        UT = [MISC[:, 0:1024].bitcast(BF16).rearrange("p (k n) -> p k n", k=4), MISC[:, 1024:2048].bitcast(BF16).rearrange("p (k n) -> p k n", k=4)]
        wcnt = [0]

        def mlp(i):
            jg = 5
            its = [(hp, jt) for hp in range(8) for jt in range(NT)]
            wbs = {}

            def load_w(hp):
                wb = wcnt[0] % 2; wcnt[0] += 1
                wbs[hp] = wb
                S.dma('pool', lambda e, wb=wb, hp=hp: e.dma_start(
                    out=W1B[wb], in_=w1[i, :, hp * 512:(hp + 1) * 512].rearrange("(k p) n -> p k n", p=128)), [], [f"W1B{wb}"])
                S.dma('pool', lambda e, wb=wb, hp=hp: e.dma_start(
                    out=W2B[wb], in_=w2[i, hp * 512:(hp + 1) * 512, :].rearrange("(k p) n -> p k n", p=128)), [], [f"W2B{wb}"])

            def stage1(k):
                hp, jt = its[k]; wb = wbs[hp]; ub = k % 2
                for fc in range(4):
                    pb = fc % 4
                    for kc in range(8):
                        S.op('pe', lambda e, wb=wb, fc=fc, kc=kc, jt=jt, pb=pb: e.matmul(
                            ps[pb][:, :], W1B[wb][:, kc, fc * 128:(fc + 1) * 128], hT[:, kc, jt * 512:(jt + 1) * 512],
                            start=(kc == 0), stop=(kc == 7)), [f"W1B{wb}", f"hT{kc}_{jt}"], [f"ps{pb}"])
                    rb = fc % 2
                    S.op('act', lambda e, pb=pb, rb=rb: e.activation(out=RL[rb][:, :], in_=ps[pb][:, :], func=AF.Relu),
                         [f"ps{pb}"], [f"RL{rb}"])
                    S.op('pool', lambda e, rb=rb, ub=ub, fc=fc: e.tensor_tensor(
                        out=UT[ub][:, fc, :], in0=RL[rb][:, :], in1=RL[rb][:, :], op=ALU.mult), [f"RL{rb}"], [f"UT{ub}_{fc}"])

            def stage2(k):
                hp, jt = its[k]; wb = wbs[hp]; ub = k % 2
                c = 0 if jt < 4 else 1
                for oc in range(8):
                    pb = 4 + oc % 4
                    for fc in range(4):
                        S.op('pe', lambda e, wb=wb, fc=fc, oc=oc, ub=ub, pb=pb: e.matmul(
                            ps[pb][:, :], W2B[wb][:, fc, oc * 128:(oc + 1) * 128], UT[ub][:, fc, :],
                            start=(fc == 0), stop=(fc == 3)), [f"W2B{wb}", f"UT{ub}_{fc}"], [f"ps{pb}"])
                    S.op('dve', lambda e, oc=oc, jt=jt, c=c, pb=pb: e.scalar_tensor_tensor(
                        out=xT[:, oc, jt * 512:(jt + 1) * 512], in0=ps[pb][:, :], scalar=modv(i, jg, oc, c),
                        in1=xT[:, oc, jt * 512:(jt + 1) * 512], op0=ALU.mult, op1=ALU.add),
                        [f"ps{pb}", f"xT{oc}_{jt}", 'MOD0_2', 'MOD1_2'], [f"xT{oc}_{jt}"])

            load_w(0)
            for k in range(len(its)):
                hp, jt = its[k]
                if jt == 0 and hp + 1 < 8:
                    load_w(hp + 1)
                if k == 0:
                    stage1(0)
                if k + 1 < len(its):
                    stage1(k + 1)
                stage2(k)

        if stage >= 2:
            norm_mod(1, 0, 3)
            mlp(0)
            if stage < 3:
                S.set_barrier()


        def hgrn():
            i = 1
            norm_mod(2, 1, 0)
            for jt_ in range(NT):
                S.dma('sp', lambda e, jt_=jt_: e.dma_start(
                    out=scr_x[:, :, jt_ * 512:(jt_ + 1) * 512].rearrange("k p t -> p k t"), in_=xT[:, :, jt_ * 512:(jt_ + 1) * 512]),
                    [f"xT{k_}_{jt_}" for k_ in range(8)], [f"scrx_h{jt_}"])
            S.set_barrier()
            S.op('dve', lambda e: e.tensor_tensor(out=LBV, in0=LB[:, :, 1, :], in1=LB[:, :, 0, :], op=ALU.subtract), ['LB'], ['LBV'])
            S.op('act', lambda e: e.activation(out=LBV, in_=LBV, func=AF.Sigmoid), ['LBV'], ['LBV'])
            S.op('dve', lambda e: e.tensor_scalar(out=OML, in0=LBV, scalar1=-1.0, scalar2=1.0, op0=ALU.mult, op1=ALU.add), ['LBV'], ['OML'])
            S.dma('pool', lambda e: e.dma_start(out=WO, in_=hgrn_wo[:, :].rearrange("(k p) n -> p k n", p=128)), [], ['WO'])
            WH = [WB[:, w * 5120:(w + 1) * 5120].rearrange("p (k j n) -> p k j n", k=8, j=5) for w in range(2)]
            QE = [BIGb[:, 0:2560], BIGb[:, 2560:5120]]
            KE = [BIGb[:, 5120:7680], BIGb[:, 7680:10240]]
            KDT = [BIGb[:, 10240:12800].rearrange("p (b k) -> p b k", b=20), BIGb[:, 12800:15360].rearrange("p (b k) -> p b k", b=20)]
            VT = BIGb[:, 15360:17920].rearrange("p (b k) -> p b k", b=20)
            GS = BIGb[:, 17920:20480]
            OA = BIGb[:, 20480:23040]
            def wt(n):
                return BIG[:, 12288 + n * 512:12288 + (n + 1) * 512]
            QS = wt(0)
            SG = [wt(1), wt(2)]; FT_ = [wt(3), wt(4)]; LG = [wt(5), wt(6)]; KT = [wt(7), wt(8)]
            BB = [wt(9), wt(10)]; E1 = [wt(11), wt(12)]; E2 = [wt(13), wt(14)]
            SST = [MISC[:, 0:128], MISC[:, 3584:3712], MISC[:, 3712:3840], MISC[:, 3840:3968]]
            SR = MISC[:, 128:384].rearrange("p (r n) -> p r n", r=2)
            OS = MISC[:, 384:896]; Ro = MISC[:, 896:1408]; T1 = MISC[:, 1408:1920]
            OSB = [MISC[:, 384:896], MISC[:, 4096:4608]]
            mb = MISC[:, 1920:3584].bitcast(BF16)
            SBR = [mb[:, r * 128:(r + 1) * 128] for r in range(8)]
            SCb = [mb[:, 1024:1152], mb[:, 1152:1280]]
            SQo = mb[:, 1280:1792]; OGh = mb[:, 1792:2304]
            KDt = [mb[:, 2304:2816], mb[:, 2816:3328]]
            psb5 = ps[6][:, :].bitcast(BF16)
            sbc = [0]
            def load_wh(h_):
                wh_ = WH[h_ % 2]
                for j in range(5):
                    S.dma('pool', lambda e, wh_=wh_, j=j, h_=h_: e.dma_start(
                        out=wh_[:, :, j, :], in_=w_in[:, j * 1024 + h_ * 128:j * 1024 + (h_ + 1) * 128].rearrange("(k p) n -> p k n", p=128)),
                        [], [f"WH{h_ % 2}"])
            load_wh(0)
            for h in range(8):
                wh = WH[h % 2]; whk = f"WH{h % 2}"
                if h + 1 < 8:
                    load_wh(h + 1)
                def elementwise(jt):
                    tsl = slice(jt * 512, (jt + 1) * 512)
                    def proj(pj, bank):
                        for kc in range(8):
                            S.op('pe', lambda e, pj=pj, bank=bank, kc=kc, wh=wh, tsl=tsl: e.matmul(
                                ps[bank][:, :], wh[:, kc, pj, :], hT[:, kc, tsl], start=(kc == 0), stop=(kc == 7)),
                                [whk, f"hT{kc}_{jt}"], [f"ps{bank}"])
                    proj(0, 4)
                    S.op('act', lambda e: e.activation(out=QS, in_=ps[4][:, :], func=AF.Silu), ['ps4'], ['QS'])
                    proj(1, 6)
                    S.op('act', lambda e: e.activation(out=SG[0], in_=ps[6][:, :], func=AF.Sigmoid), ['ps6'], ['SG0'])
                    proj(2, 4)
                    S.op('act', lambda e: e.activation(out=SG[1], in_=ps[4][:, :], func=AF.Sigmoid), ['ps4'], ['SG1'])
                    proj(4, 6)
                    S.op('act', lambda e, tsl=tsl: e.activation(out=GS[:, tsl], in_=ps[6][:, :], func=AF.Silu), ['ps6'], [f"GS{jt}"])
                    for q4 in range(4):
                        blk = jt * 4 + q4
                        for kc in range(8):
                            S.op('pe', lambda e, kc=kc, q4=q4, blk=blk, wh=wh: e.matmul(
                                ps[4][:, q4 * 128:(q4 + 1) * 128], hT[:, kc, blk * 128:(blk + 1) * 128], wh[:, kc, 3, :],
                                start=(kc == 0), stop=(kc == 7)), [whk, f"hT{kc}_{jt}"], ['ps4'])
                    S.op('dve', lambda e, jt=jt: e.tensor_copy(out=VT[:, jt * 4:(jt + 1) * 4, :], in_=ps[4][:, :].rearrange("p (b k) -> p b k", b=4)),
                         ['ps4'], [f"VT{jt}"])
                    def estep(k, d):
                        lastcol = 31 if d == 0 else 0
                        if k == 0:
                            S.op('dve', lambda e, d=d, h=h: e.tensor_scalar(out=FT_[d], in0=SG[d], scalar1=OML[:, d, h:h + 1], scalar2=LBV[:, d, h:h + 1],
                                                                            op0=ALU.mult, op1=ALU.add), [f"SG{d}", 'OML', 'LBV'], [f"F{d}"])
                        elif k == 1:
                            S.op('act', lambda e, d=d: e.activation(out=LG[d], in_=FT_[d], func=AF.Ln), [f"F{d}"], [f"LG{d}"])
                            S.op('pool', lambda e, d=d: e.tensor_scalar(out=KT[d], in0=FT_[d], scalar1=-1.0, scalar2=1.0, op0=ALU.mult, op1=ALU.add),
                                 [f"F{d}"], [f"KT{d}"])
                        elif k == 2:
                            if d == 0:
                                S.op('dve', lambda e, d=d: e.tensor_tensor_scan(out=BB[d], data0=RM[:, :], data1=LG[d], initial=0.0, op0=ALU.mult, op1=ALU.add),
                                     ['RM', f"LG{d}"], [f"BB{d}"])
                            else:
                                S.op('dve', lambda e, d=d: e.tensor_tensor_scan(out=BB[d][:, ::-1], data0=RM[:, :], data1=LG[d][:, ::-1], initial=0.0,
                                                                                op0=ALU.mult, op1=ALU.add), ['RM', f"LG{d}"], [f"BB{d}"])
                        elif k == 3:
                            S.op('act', lambda e, d=d: e.activation(out=E1[d], in_=BB[d], func=AF.Exp), [f"BB{d}"], [f"E1{d}"])
                            S.op('act', lambda e, d=d: e.activation(out=E2[d], in_=BB[d], func=AF.Exp, scale=-1.0), [f"BB{d}"], [f"E2{d}"])
                        elif k == 4:
                            S.op('dve', lambda e, d=d, tsl=tsl: e.tensor_tensor(out=QE[d][:, tsl], in0=QS, in1=E1[d], op=ALU.mult),
                                 ['QS', f"E1{d}"], [f"QE{d}_{jt}"])
                            S.op('pool', lambda e, d=d, jt=jt, lastcol=lastcol: e.tensor_copy(out=ACOL[:, d, jt * 16:(jt + 1) * 16], in_=E1[d][:, lastcol::32]),
                                 [f"E1{d}"], [f"ACOL{d}_{jt}"])
                            S.op('dve', lambda e, d=d, tsl=tsl: e.tensor_tensor(out=KE[d][:, tsl], in0=KT[d], in1=E2[d], op=ALU.mult),
                                 [f"KT{d}", f"E2{d}"], [f"KE{d}_{jt}"])
                        elif k == 5:
                            S.op('pool', lambda e, d=d, tsl=tsl, jt=jt: e.tensor_tensor(
                                out=KDt[d].rearrange("p (c j) -> p c j", j=32), in0=KE[d][:, tsl].rearrange("p (c j) -> p c j", j=32),
                                in1=ACOL[:, d, jt * 16:(jt + 1) * 16].unsqueeze(2).to_broadcast([128, 16, 32]), op=ALU.mult),
                                [f"KE{d}_{jt}", f"ACOL{d}_{jt}"], [f"KDt{d}"])
                        elif k == 6:
                            for q4 in range(4):
                                S.op('pe', lambda e, d=d, q4=q4: e.transpose(psb5[:, q4 * 128:(q4 + 1) * 128], KDt[d][:, q4 * 128:(q4 + 1) * 128], IDB[:, :]),
                                     [f"KDt{d}", 'IDB'], ['ps6'])
                            S.op('dve', lambda e, d=d, jt=jt: e.tensor_copy(out=KDT[d][:, jt * 4:(jt + 1) * 4, :], in_=psb5[:, 0:512].rearrange("p (b k) -> p b k", b=4)),
                                 ['ps6'], [f"KDT{d}_{jt}"])
                    for k in range(7):
                        for d in range(2):
                            estep(k, d)

                def seg_start(d, init):
                    st = {'d': d, 'cs': 0}
                    S0 = SST[0]
                    if init[0] == 'dram':
                        S.dma('sp', lambda e: e.dma_start(out=S0, in_=init[1]), [], ['Sst0'])
                    elif init[0] == 'zero':
                        S.op('dve', lambda e: e.memset(S0, 0.0), [], ['Sst0'])
                    else:
                        S.op('dve', lambda e: e.tensor_scalar(out=S0, in0=SR[:, 0, :], scalar1=SEL[:, 0:1], scalar2=None, op0=ALU.mult),
                             ['SR', 'SEL'], ['Sst0'])
                        S.op('dve', lambda e: e.scalar_tensor_tensor(out=S0, in0=SR[:, 1, :], scalar=SEL[:, 1:2], in1=S0, op0=ALU.mult, op1=ALU.add),
                             ['SR', 'SEL', 'Sst0'], ['Sst0'])
                    cur = sbc[0] % 8; sbc[0] += 1
                    S.op('act', lambda e, cur=cur: e.activation(out=SBR[cur], in_=S0, func=AF.Copy), ['Sst0'], [f"SBR{cur}"])
                    st['cur'] = cur
                    return st

                def seg_pre(st, blk, parts):
                    d = st['d']
                    bsl = slice(blk * 128, (blk + 1) * 128)
                    sc = blk % 2
                    tj = blk // 4
                    for pt in parts:
                        if pt == 's':
                            S.op('pe', lambda e, d=d, bsl=bsl: e.matmul(ps[7][:, 0:128], KE[d][:, bsl], QE[d][:, bsl], start=True, stop=True),
                                 [f"KE{d}_{tj}", f"QE{d}_{tj}"], ['ps7'])
                        elif pt == 'm':
                            S.op('dve', lambda e, d=d, sc=sc: e.tensor_tensor(out=SCb[sc], in0=ps[7][:, 0:128], in1=MSK[:, d, :], op=ALU.mult),
                                 ['ps7', 'MSK'], [f"SCb{sc}"])
                        elif pt == 'o':
                            S.op('pe', lambda e, blk=blk, sc=sc: e.matmul(ps[5][:, 0:128], VT[:, blk, :], SCb[sc], start=True, stop=False),
                                 [f"VT{tj}", f"SCb{sc}"], ['ps5'])
                        else:
                            c = int(pt[1])
                            kw = dict(tile_position=(96, 0)) if c == 3 else {}
                            S.op('pe', lambda e, d=d, c=c, blk=blk, kw=kw: e.matmul(
                                ps[c][:, 0:128], KDT[d][c * 32:(c + 1) * 32, blk, :], VT[c * 32:(c + 1) * 32, blk, :], start=True, stop=True, **kw),
                                [f"KDT{d}_{tj}", f"VT{tj}"], [f"ps{c}"])

                def seg_block(st, blk, nb=None):
                    d = st['d']
                    bsl = slice(blk * 128, (blk + 1) * 128)
                    po = 5
                    tj = blk // 4
                    if st.get('pre') != blk:
                        seg_pre(st, blk, ['s', 'm', 'u0', 'u1', 'u2', 'u3', 'o'])
                    corder = range(4) if d == 0 else range(3, -1, -1)
                    for n, c in enumerate(corder):
                        cur = st['cur']; cs = st['cs']
                        S.op('pe', lambda e, d=d, c=c, blk=blk, po=po, cur=cur, n=n: e.matmul(
                            ps[po][:, c * 32:(c + 1) * 32], SBR[cur], QE[d][:, blk * 128 + c * 32:blk * 128 + (c + 1) * 32],
                            start=False, stop=(n == 3)), [f"SBR{cur}", f"QE{d}_{tj}"], [f"ps{po}"])
                        ns = (cs + 1) % 4
                        S.op('dve', lambda e, d=d, c=c, blk=blk, cs=cs, ns=ns: e.scalar_tensor_tensor(
                            out=SST[ns], in0=SST[cs], scalar=ACOL[:, d, blk * 4 + c:blk * 4 + c + 1], in1=ps[c][:, 0:128], op0=ALU.mult, op1=ALU.add),
                            [f"Sst{cs}", f"ACOL{d}_{tj}", f"ps{c}"], [f"Sst{ns}"])
                        st['cs'] = ns
                        cur = sbc[0] % 8; sbc[0] += 1
                        st['cur'] = cur
                        if n % 2 == 0:
                            S.op('act', lambda e, cur=cur, ns=ns: e.activation(out=SBR[cur], in_=SST[ns], func=AF.Copy), [f"Sst{ns}"], [f"SBR{cur}"])
                        else:
                            S.op('pool', lambda e, cur=cur, ns=ns: e.tensor_copy(out=SBR[cur], in_=SST[ns]), [f"Sst{ns}"], [f"SBR{cur}"])
                        if nb is not None:
                            seg_pre(st, nb, [f"u{c}"])
                            if n == 0:
                                seg_pre(st, nb, ['s'])
                            if n == 3:
                                seg_pre(st, nb, ['m'])
                    if d == 0:
                        S.op('act', lambda e, bsl=bsl, po=po: e.activation(out=OA[:, bsl], in_=ps[po][:, 0:128], func=AF.Copy), [f"ps{po}"], [f"OA{blk}"])
                    else:
                        q4 = blk % 4
                        ob = (blk // 4) % 2
                        OSx = OSB[ob]
                        S.op('dve', lambda e, bsl=bsl, po=po, q4=q4, OSx=OSx: e.tensor_tensor(
                            out=OSx[:, q4 * 128:(q4 + 1) * 128], in0=ps[po][:, 0:128], in1=OA[:, bsl], op=ALU.add), [f"ps{po}", f"OA{blk}"], [f"OS{ob}_{q4}"])
                    if nb is not None:
                        seg_pre(st, nb, ['o'])
                        st['pre'] = nb
                    if d == 1:
                        if st.get('pending') is not None and st.get('pend_at') == blk:
                            st['pending'](); st['pending'] = None
                        q4 = blk % 4
                        if q4 == 0:
                            jt = blk // 4
                            tsl = slice(jt * 512, (jt + 1) * 512)
                            ob = jt % 2
                            OSx = OSB[ob]
                            oskeys = [f"OS{ob}_{q_}" for q_ in range(4)]

                            def onorm(jt=jt, tsl=tsl, OSx=OSx, oskeys=oskeys):
                                S.op('act', lambda e: e.activation(out=SQo, in_=OSx, func=AF.Square), oskeys, ['SQo'])
                                S.op('pe', lambda e: e.matmul(ps[6][:, :], ONES[:, :], SQo, start=True, stop=True), ['SQo', 'ONES'], ['ps6'])
                                S.op('act', lambda e: e.activation(out=Ro, in_=ps[6][:, :], func=AF.Ln, bias=EPST, scale=1.0 / 128.0),
                                     ['ps6', 'EPST'], ['Ro'])
                                S.op('act', lambda e: e.activation(out=Ro, in_=Ro, func=AF.Exp, scale=-0.5), ['Ro'], ['Ro'])
                                S.op('dve', lambda e: e.scalar_tensor_tensor(out=T1, in0=OSx, scalar=GN, in1=Ro, op0=ALU.mult, op1=ALU.mult),
                                     oskeys + ['GN', 'Ro'], ['T1'])
                                S.op('pool', lambda e: e.tensor_tensor(out=OGh, in0=T1, in1=GS[:, tsl], op=ALU.mult), ['T1', f"GS{jt}"], ['OGh'])
                                S.dma('sp', lambda e, h=h: e.dma_start(out=scr_og[h, :, tsl], in_=OGh), ['OGh'], [f"scrog{h}_{jt}"])
                            if nb is not None and (nb // 4) != jt and (nb % 4) == 3:
                                st['pending'] = onorm
                                st['pend_at'] = nb - 1
                            else:
                                onorm()

                def seg_end(st, fin):
                    if st.get('pending') is not None:
                        st['pending'](); st['pending'] = None
                    cs = st['cs']
                    if fin[0] == 'state':
                        S.dma('sp', lambda e: e.dma_start(out=fin[1], in_=SST[cs]), [f"Sst{cs}"], [])
                    elif fin[0] == 'cc':
                        S.dma('pool', lambda e, h=h: e.dma_start(out=cc_in[h, :, :], in_=SST[cs]), [f"Sst{cs}"], [f"ccin{h}"])
                        S.special('pool', lambda e, h=h: e.collective_compute(
                            "AllGather", ALU.bypass, replica_groups=[[0, 1], [2, 3], [4, 5], [6, 7]],
                            ins=[cc_in[h, :, :]], outs=[cc_out[h, :, :]]), ('cc', h), [f"ccin{h}"], [f"ccout{h}"])
                        S.dma('sp', lambda e, h=h: e.dma_start(out=SR, in_=cc_out[h, :, :].rearrange("(r p) n -> p r n", p=128)),
                              [f"ccout{h}"], ['SR'])

                def run_segment(d, blocks, init, fin):
                    st = seg_start(d, init)
                    for i_, blk in enumerate(blocks):
                        seg_block(st, blk, blocks[i_ + 1] if i_ + 1 < len(blocks) else None)
                    seg_end(st, fin)

                elementwise(0)
                elementwise(1)
                stA = seg_start(0, ('dram', sA0[:, h, :]))
                for jt in range(4):
                    for q4 in range(4):
                        blk = jt * 4 + q4
                        seg_block(stA, blk, blk + 1 if blk + 1 < 16 else None)
                    if jt == 3:
                        seg_end(stA, ('cc',))
                    if jt + 2 < NT:
                        elementwise(jt + 2)
                run_segment(0, [16, 17], ('zero',), ('state', st_out[0, 0, h, :, :]))
                run_segment(0, [18, 19], ('zero',), ('state', st_out[1, 0, h, :, :]))
                run_segment(1, [19, 18], ('zero',), ('state', st_out[1, 1, h, :, :]))
                run_segment(1, [17, 16], ('zero',), ('state', st_out[0, 1, h, :, :]))
                run_segment(1, list(range(15, -1, -1)), ('sel',), ('none',))
            S.set_barrier()
            OGT = hT
            S.dma('sp', lambda e: e.dma_start(out=OGT, in_=scr_og[:, :, :].rearrange("h p t -> p h t")), [], ALLH)
            S.dma('act', lambda e: e.dma_start(out=xT, in_=scr_x[:, :, :].rearrange("k p t -> p k t")), [], ALLX)
            S.set_barrier()
            for jt in range(NT):
                c = 0 if jt < 4 else 1
                tsl = slice(jt * 512, (jt + 1) * 512)
                for oc in range(8):
                    pb = oc % 4
                    for kc in range(8):
                        S.op('pe', lambda e, oc=oc, kc=kc, tsl=tsl, pb=pb: e.matmul(ps[pb][:, :], WO[:, kc, oc * 128:(oc + 1) * 128], OGT[:, kc, tsl],
                                                                                   start=(kc == 0), stop=(kc == 7)), ['WO', f"hT{kc}_{jt}"], [f"ps{pb}"])
                    S.op('dve', lambda e, oc=oc, tsl=tsl, c=c, pb=pb: e.scalar_tensor_tensor(
                        out=xT[:, oc, tsl], in0=ps[pb][:, :], scalar=modv(1, 2, oc, c), in1=xT[:, oc, tsl], op0=ALU.mult, op1=ALU.add),
                        [f"ps{pb}", f"xT{oc}_{jt}", 'MOD0_2', 'MOD1_2'], [f"xT{oc}_{jt}"])
            S.set_barrier()

        if stage >= 3:
            hgrn()
        if stage >= 4:
            norm_mod(3, 1, 3)
            mlp(1)
            S.set_barrier()

        OUTB = [MISC[:, 0:1024], MISC[:, 1024:2048]]
        NFB = WB[:, 0:2048].bitcast(F32)
        S.dma('sp', lambda e: e.dma_start(out=NFB[:, :], in_=nfin[:, :].partition_broadcast(128)), [], ['NFB'])
        final_norm = (stage >= 4)
        for bk in range(NBLK):
            ob = bk % 2
            pp = (bk % 2) * 4
            for kc in range(8):
                bank = pp + kc // 4
                S.op('pe', lambda e, kc=kc, bk=bk, bank=bank: e.transpose(
                    ps[bank][:, (kc % 4) * 128:(kc % 4 + 1) * 128], xT[:, kc, bk * 128:(bk + 1) * 128], IDF[:, :]),
                    [f"xT{kc}_{bk // 4}", 'IDF'], [f"ps{bank}"])
            for hh in range(2):
                bank = pp + hh
                if not final_norm:
                    S.op('act', lambda e, ob=ob, hh=hh, bank=bank: e.activation(
                        out=OUTB[ob][:, hh * 512:(hh + 1) * 512], in_=ps[bank][:, :], func=AF.Copy), [f"ps{bank}"], [f"OUTB{ob}"])
                else:
                    S.op('act', lambda e, hh=hh, bank=bank, bk=bk: e.activation(
                        out=JUNK[:, 0:512], in_=ps[bank][:, :], func=AF.Square, accum_out=SSQ[:, hh:hh + 1]),
                        [f"ps{bank}"], ['JUNK', f"SSQ{hh}"])
            if final_norm:
                S.op('dve', lambda e: e.tensor_tensor(out=SSQ[:, 2:3], in0=SSQ[:, 0:1], in1=SSQ[:, 1:2], op=ALU.add),
                     ['SSQ0', 'SSQ1'], ['SSQ2'])
                S.op('act', lambda e: e.activation(out=RSTD[:, 0:1], in_=SSQ[:, 2:3], func=AF.Sqrt, bias=EPST, scale=1.0 / D),
                     ['SSQ2', 'EPST'], ['RSTD0'])
                S.op('dve', lambda e: e.reciprocal(out=RSTD[:, 0:1], in_=RSTD[:, 0:1]), ['RSTD0'], ['RSTD0'])
                for hh in range(2):
                    bank = pp + hh
                    S.op('dve', lambda e, ob=ob, hh=hh, bank=bank: e.scalar_tensor_tensor(
                        out=OUTB[ob][:, hh * 512:(hh + 1) * 512], in0=ps[bank][:, :], scalar=RSTD[:, 0:1],
                        in1=NFB[:, hh * 512:(hh + 1) * 512], op0=ALU.mult, op1=ALU.mult),
                        [f"ps{bank}", 'RSTD0', 'NFB'], [f"OUTB{ob}"])
            S.dma('sp', lambda e, ob=ob, bk=bk: e.dma_start(out=y_my[bk * 128:(bk + 1) * 128, :], in_=OUTB[ob][:, :]),
                  [f"OUTB{ob}"], [])

        S.finalize()
        with nc.Block() as block:
            S.emit(nc, block, sems)
    return nc


GRID_W = 64


def _pos_embed(n_tok, d):
    rows = n_tok // GRID_W
    quarter = d // 4
    omega = (1.0 / (10000.0 ** (np.arange(quarter, dtype=np.float32) / quarter))).astype(np.float32)
    r = np.arange(rows, dtype=np.float32)[:, None] * omega[None, :]
    cl = np.arange(GRID_W, dtype=np.float32)[:, None] * omega[None, :]
    er = np.concatenate([np.sin(r), np.cos(r)], axis=-1)
    ec = np.concatenate([np.sin(cl), np.cos(cl)], axis=-1)
    emb = np.concatenate([np.broadcast_to(er[:, None, :], (rows, GRID_W, d // 2)),
                          np.broadcast_to(ec[None, :, :], (rows, GRID_W, d // 2))], axis=-1)
    return emb.reshape(n_tok, d).astype(np.float32)


def _tables(a):
    bf = ml_dtypes.bfloat16
    t1 = np.arange(64); f1s = np.arange(64)
    f1 = f1s if a == 0 else 63 - f1s
    angA = (2.0 * np.pi / 64.0) * ((t1[:, None] * f1[None, :]) % 64).astype(np.float64)
    da = np.zeros((128, 2, 128), np.float64)
    for hf in range(2):
        da[hf * 64:(hf + 1) * 64, 0, hf * 64:(hf + 1) * 64] = np.cos(angA) / 8.0
        da[hf * 64:(hf + 1) * 64, 1, hf * 64:(hf + 1) * 64] = -np.sin(angA) / 8.0
    t2 = np.arange(64); f2s = np.arange(32)
    f2 = f2s if a == 0 else 63 - f2s
    f = f1[:, None] + 64 * f2[None, :]
    th = (2.0 * np.pi / 4096.0) * ((t2[:, None, None].astype(np.int64) * f[None, :, :].astype(np.int64)) % 4096).astype(np.float64)
    co = np.cos(th) / 8.0; si = np.sin(th) / 8.0
    mc = np.zeros((2, 64, 64, 2, 32), np.float64)
    mc[0, :, :, 0, :] = co; mc[0, :, :, 1, :] = -si
    mc[1, :, :, 0, :] = si; mc[1, :, :, 1, :] = co
    mc = mc.reshape(128, 64, 64)
    tp = np.arange(256)
    fp = tp if a == 0 else 255 - tp
    angp = (2.0 * np.pi / 256.0) * ((tp[:, None] * fp[None, :]) % 256).astype(np.float64)
    flp = np.concatenate([np.cos(angp) / 16.0, -np.sin(angp) / 16.0], axis=1)
    flp = np.ascontiguousarray(flp.reshape(2, 128, 512)).astype(bf)
    c = np.arange(256)
    angc = (2.0 * np.pi / 256.0) * ((c[:, None] * c[None, :]) % 256).astype(np.float64)
    cc = np.cos(angc) / 16.0; sc = np.sin(angc) / 16.0
    cst = np.stack([cc, sc], axis=1)
    cst = cst.reshape(2, 128, 2, 256).transpose(1, 0, 2, 3)
    cst = np.ascontiguousarray(cst).astype(bf)
    return np.ascontiguousarray(da).astype(bf), np.ascontiguousarray(mc).astype(bf), flp, cst


_NC_CACHE = {}
STAGE = 4


def kernel(x_prompt, x_sample, state_hgrn, c, c_ctx, ada_w, ada_b, norm_mix, norm_mlp, fnet_wo,
           hgrn_w_in, hgrn_lb, hgrn_norm, hgrn_wo, mlp_w1, mlp_w2, norm_final, _stage=None):
    stage = STAGE if _stage is None else _stage
    f32 = np.float32
    bf = ml_dtypes.bfloat16
    x_prompt = np.asarray(x_prompt, f32); x_sample = np.asarray(x_sample, f32)
    state_hgrn = np.asarray(state_hgrn, f32); c = np.asarray(c, f32); c_ctx = np.asarray(c_ctx, f32)
    ada_w = np.ascontiguousarray(np.asarray(ada_w, f32)); ada_b = np.asarray(ada_b, f32)
    norm_mix = np.asarray(norm_mix, f32); norm_mlp = np.asarray(norm_mlp, f32)
    fnet_wo = np.ascontiguousarray(np.asarray(fnet_wo, f32)[0]); hgrn_w_in = np.asarray(hgrn_w_in, f32)[0]
    hgrn_lb = np.asarray(hgrn_lb, f32); hgrn_norm = np.asarray(hgrn_norm, f32)
    hgrn_wo = np.ascontiguousarray(np.asarray(hgrn_wo, f32)[0])
    mlp_w1 = np.ascontiguousarray(np.asarray(mlp_w1, f32)); mlp_w2 = np.ascontiguousarray(np.asarray(mlp_w2, f32))
    norm_final = np.asarray(norm_final, f32)

    if stage not in _NC_CACHE:
        _NC_CACHE[stage] = build(stage)
    nc = _NC_CACHE[stage]
    pos = _pos_embed(4096, D)
    tabs = [_tables(0), _tables(1)]
    adab = np.ascontiguousarray(ada_b.reshape(2, 48, 128).transpose(2, 0, 1))
    nrm = np.stack([norm_mix[0], norm_mlp[0], norm_mix[1], norm_mlp[1]], 0).reshape(4, 8, 128).transpose(2, 0, 1)
    nrm = np.ascontiguousarray(nrm)
    identf = np.eye(128, dtype=f32); identb = np.eye(128).astype(bf)
    s_i = np.arange(128)[:, None]; t_i = np.arange(128)[None, :]
    same = (s_i // 32) == (t_i // 32)
    maskAB = np.stack([(same & (s_i <= t_i)), (same & (s_i >= t_i))], axis=1).astype(f32).astype(bf)
    rmask = np.ones((128, 512), f32); rmask[:, ::32] = 0.0
    wq, wff, wfb, wi, wg = np.split(hgrn_w_in, 5, axis=1)
    in_maps = []
    for r in range(8):
        p, a = r // 2, r % 2
        xs = x_sample[p]
        if a == 0:
            loc = np.arange(2048)
        else:
            loc = 4095 - np.arange(2048)
        prs = [x_prompt[2 * r], x_prompt[2 * r + 1]]
        if a == 1:
            prs_loc = [pp[::-1] for pp in prs]
        else:
            prs_loc = prs
        x_my = np.concatenate([xs[loc]] + prs_loc, axis=0)
        x_all = np.concatenate([xs] + prs, axis=0)
        da, mc, flp, cst = tabs[a]
        cnd = np.stack([c[p], c_ctx], axis=-1).reshape(8, 128, 2).transpose(1, 0, 2).reshape(128, 16)
        if a == 0:
            w_in = np.concatenate([wq, wff, wfb, wi, wg], axis=1); lbr = hgrn_lb
        else:
            w_in = np.concatenate([wq, wfb, wff, wi, wg], axis=1); lbr = hgrn_lb[::-1]
        lbraw = np.ascontiguousarray(lbr.reshape(2, 2, 8, 128).transpose(3, 0, 1, 2))
        selm = np.zeros((128, 2), f32); selm[:, 1 - a] = 1.0
        sA0 = np.ascontiguousarray(state_hgrn[p, 0, a].transpose(1, 0, 2))
        in_maps.append({
            "x_my": np.ascontiguousarray(x_my), "pos_my": np.ascontiguousarray(pos[loc]),
            "x_all": np.ascontiguousarray(x_all), "pos_all": pos,
            "da_in": da, "mc_in": mc, "flp": flp, "cs": cst, "cond": np.ascontiguousarray(cnd),
            "ada_w": ada_w, "adab": adab, "nrm": nrm, "nfin": norm_final.reshape(1, D),
            "fnet_wo": fnet_wo, "w_in": np.ascontiguousarray(w_in), "lbraw": lbraw,
            "gnorm": np.ascontiguousarray(hgrn_norm.reshape(128, 1)), "hgrn_wo": hgrn_wo,
            "w1": mlp_w1, "w2": mlp_w2, "sA0": sA0, "selm": selm, "maskAB": np.ascontiguousarray(maskAB),
            "identf": identf, "identb": identb, "rmask": rmask,
        })
    res = run_bass_kernel_spmd(nc, in_maps, core_ids=list(range(8)))
    y_prompt = np.zeros((16, 256, D), f32); y_sample = np.zeros((4, 4096, D), f32)
    new_state = np.zeros((16, 1, 2, 8, 128, 128), f32)
    for r in range(8):
        p, a = r // 2, r % 2
        y = res.results[r]["y_my"]
        st = res.results[r]["st_out"]
        if a == 0:
            y_sample[p, 0:2048] = y[0:2048]
            y_prompt[2 * r] = y[2048:2304]; y_prompt[2 * r + 1] = y[2304:2560]
        else:
            y_sample[p, 2048:4096] = y[0:2048][::-1]
            y_prompt[2 * r] = y[2048:2304][::-1]; y_prompt[2 * r + 1] = y[2304:2560][::-1]
        for q in range(2):
            for dd in range(2):
                gd = dd if a == 0 else 1 - dd
                new_state[2 * r + q, 0, gd] = st[q, dd]
    return (y_prompt, y_sample, new_state)
```

```python
import numpy as np
import ml_dtypes
from contextlib import ExitStack
import concourse.bass as bass
import concourse.mybir as mybir
from concourse.bass_utils import run_bass_kernel_spmd

F32 = mybir.dt.float32
BF16 = mybir.dt.bfloat16
AF = mybir.ActivationFunctionType
ALU = mybir.AluOpType

D = 1024
T = 2560
TS = 2048
NT = 5
EPS = 1e-6
NBLK = 20


class _Op:
    pass


class Sched:
    ENG = ['pe', 'act', 'dve', 'pool', 'sp']

    def __init__(s):
        s.q = {e: [] for e in s.ENG}
        s.lastw = {}
        s.readers = {}
        s.barrier = []
        s.dma_ring = {'sp': 12, 'act': 6, 'pool': 12}
        s.dma_next = {k: 0 for k in s.dma_ring}
        s.slot_last = {}

    def _deps(s, reads, writes):
        d = []
        for b in reads:
            if b in s.lastw:
                d.append(s.lastw[b])
        for b in writes:
            if b in s.lastw:
                d.append(s.lastw[b])
            d.extend(s.readers.get(b, ()))
        d.extend(s.barrier)
        return d

    def _commit(s, o, reads, writes):
        for b in reads:
            s.readers.setdefault(b, []).append(o)
        for b in writes:
            s.lastw[b] = o
            s.readers[b] = []

    def op(s, eng, fn, reads=(), writes=()):
        psr = [b for b in reads if isinstance(b, str) and b.startswith('ps')]
        if psr:
            reads = [b for b in reads if b not in psr]
            writes = list(writes) + psr
        o = _Op()
        o.eng = eng; o.fn = fn; o.is_dma = False; o.sig = False; o.sigidx = 0
        o.deps = s._deps(reads, writes)
        s.q[eng].append(o)
        s._commit(o, reads, writes)
        return o

    def dma(s, q, fn, reads=(), writes=()):
        o = _Op()
        o.eng = q; o.fn = fn; o.is_dma = True; o.sig = False
        o.deps = s._deps(reads, writes)
        slot = s.dma_next[q]
        s.dma_next[q] = (slot + 1) % s.dma_ring[q]
        o.slot = (q, slot)
        prev = s.slot_last.get(o.slot)
        o.slotval = (prev.slotval if prev else 0) + 16
        if prev is not None:
            o.deps.append(prev)
        s.slot_last[o.slot] = o
        s.q[q].append(o)
        s._commit(o, reads, writes)
        return o

    def special(s, q, fn, key, reads=(), writes=()):
        o = _Op()
        o.eng = q; o.fn = fn; o.is_dma = True; o.sig = False
        o.deps = s._deps(reads, writes)
        o.slot = key
        o.slotval = 1
        o.incval = 1
        s.slot_last[o.slot] = o
        s.q[q].append(o)
        s._commit(o, reads, writes)
        return o

    def set_barrier(s):
        b = []
        for e in s.ENG:
            for o in reversed(s.q[e]):
                if not o.is_dma:
                    b.append(o)
                    break
        for o in s.slot_last.values():
            b.append(o)
        s.barrier = b

    def finalize(s):
        for e in s.ENG:
            for o in s.q[e]:
                for d in o.deps:
                    if d.is_dma:
                        continue
                    if d.eng == 'pe' and e == 'pe' and not o.is_dma:
                        continue
                    d.sig = True
        for e in s.ENG:
            c = 0
            for o in s.q[e]:
                if (not o.is_dma) and o.sig:
                    c += 1
                    o.sigidx = c
        for e in s.ENG:
            waited = {}
            for o in s.q[e]:
                w = {}
                for d in o.deps:
                    if d.is_dma:
                        key = ('dma',) + d.slot
                        val = d.slotval
                    else:
                        if d.eng == 'pe' and e == 'pe' and not o.is_dma:
                            continue
                        key = d.eng
                        val = d.sigidx
                    if waited.get(key, 0) >= val:
                        continue
                    if w.get(key, 0) < val:
                        w[key] = val
                for k, v in w.items():
                    waited[k] = v
                o.waits = list(w.items())

    def emit(s, nc, block, sems):
        def run(e, eng):
            for o in s.q[e]:
                for k, v in o.waits:
                    eng.wait_ge(sems[k], v)
                ins = o.fn(eng)
                if o.is_dma:
                    ins.then_inc(sems[('dma',) + o.slot], getattr(o, 'incval', 16))
                elif o.sig:
                    ins.then_inc(sems[e], 1)
            if e == 'sp':
                for slot, o in s.slot_last.items():
                    eng.wait_ge(sems[('dma',) + slot], o.slotval)

        @block.tensor
        def _(eng):
            run('pe', eng)

        @block.scalar
        def _(eng):
            run('act', eng)

        @block.vector
        def _(eng):
            run('dve', eng)

        @block.gpsimd
        def _(eng):
            run('pool', eng)

        @block.sync
        def _(eng):
            run('sp', eng)


def build(stage=4):
    nc = bass.Bass("TRN2", target_bir_lowering=False)
    S = Sched()

    def din(name, shape, dt=F32):
        return nc.dram_tensor(name, list(shape), dt, kind="ExternalInput").ap()

    x_my = din("x_my", [T, D]); pos_my = din("pos_my", [TS, D])
    x_all = din("x_all", [4608, D]); pos_all = din("pos_all", [4096, D])
    flp = din("flp", [2, 128, 512], BF16)
    cs = din("cs", [128, 2, 2, 256], BF16)
    cond = din("cond", [128, 16]); ada_w = din("ada_w", [2, D, 6 * D]); adab = din("adab", [128, 2, 48])
    nrm = din("nrm", [128, 4, 8]); nfin = din("nfin", [1, D])
    fnet_wo = din("fnet_wo", [D, D]); w_in = din("w_in", [D, 5 * D]); lbraw = din("lbraw", [128, 2, 2, 8])
    gnorm = din("gnorm", [128, 1]); hgrn_wo = din("hgrn_wo", [D, D])
    w1 = din("w1", [2, D, 4 * D]); w2 = din("w2", [2, 4 * D, D])
    sA0 = din("sA0", [128, 8, 128]); selm = din("selm", [128, 2])
    maskAB = din("maskAB", [128, 2, 128], BF16)
    identf = din("identf", [128, 128]); identb = din("identb", [128, 128], BF16)
    rmask = din("rmask", [128, 512])
    da_in = din("da_in", [128, 2, 128], BF16); mc_in = din("mc_in", [128, 64, 64], BF16)
    y_my = nc.dram_tensor("y_my", [T, D], F32, kind="ExternalOutput").ap()
    st_out = nc.dram_tensor("st_out", [2, 2, 8, 128, 128], F32, kind="ExternalOutput").ap()
    scr_x = nc.dram_tensor("scr_x", [8, 128, T], F32, kind="Internal").ap()
    scr_vec = nc.dram_tensor("scr_vec", [4, D], F32, kind="Internal").ap()
    zs = nc.dram_tensor("zs", [64, 64, 2, D], BF16, kind="Internal").ap()
    scr_og = nc.dram_tensor("scr_og", [8, 128, T], BF16, kind="Internal").ap()
    cc_in = nc.dram_tensor("cc_in", [8, 128, 128], F32, kind="Internal").ap()
    cc_out = nc.dram_tensor("cc_out", [8, 256, 128], F32, kind="Internal", addr_space="Local").ap()

    es = ExitStack()
    with es:
        def sb(name, shape, dt=F32):
            return es.enter_context(nc.sbuf_tensor(name, list(shape), dt))

        BIG = sb("BIG", [128, 20480], F32)
        HB = sb("HB", [128, 20480], BF16)
        WB = sb("WB", [128, 18432], BF16)
        MISC = sb("MISC", [128, 6144], F32)
        SM = sb("SM", [128, 1024], F32)
        IDF = sb("IDF", [128, 128], F32)
        IDB = sb("IDB", [128, 128], BF16)
        MSK = sb("MSK", [128, 2, 128], BF16)
        ONES = sb("ONES", [128, 128], BF16)
        SCB = sb("SCB", [128, 16], BF16)
        CST = sb("CST", [128, 2, 2, 256], BF16)
        RM = sb("RM", [128, 512], F32)
        DAT = sb("DAT", [128, 2, 128], BF16)
        MCT = sb("MCT", [128, 64, 64], BF16)
        ps = [es.enter_context(nc.psum_tensor(f"ps{i}", [128, 512], F32)) for i in range(8)]
        sems = {}
        for e in Sched.ENG:
            sems[e] = es.enter_context(nc.semaphore(f"sem_{e}"))
        for q, n in S.dma_ring.items():
            for i in range(n):
                sems[('dma', q, i)] = es.enter_context(nc.semaphore(f"dsem_{q}_{i}"))
        for hh_ in range(8):
            sems[('dma', 'cc', hh_)] = es.enter_context(nc.semaphore(f"ccsem_{hh_}"))

        xT = BIG[:, :].rearrange("p (k t) -> p k t", k=8)
        hT = HB[:, :].rearrange("p (k t) -> p k t", k=8)
        BIGb = BIG[:, :].bitcast(BF16)

        MOD = SM[:, 0:192].rearrange("p (i j c) -> p i j c", i=2, j=48)
        AM = SM[:, 192:448].rearrange("p (n k c) -> p n k c", n=4, k=8)
        SSQ = SM[:, 448:512]
        RSTD = SM[:, 512:576]
        NRM = SM[:, 576:608].rearrange("p (n k) -> p n k", n=4)
        ADB = SM[:, 608:704].rearrange("p (i j) -> p i j", i=2)
        CND = SM[:, 704:720]
        EPST = SM[:, 720:721]
        LB = SM[:, 736:768].rearrange("p (d l h) -> p d l h", d=2, l=2)
        LBV = SM[:, 768:784].rearrange("p (d h) -> p d h", d=2)
        OML = SM[:, 784:800].rearrange("p (d h) -> p d h", d=2)
        GN = SM[:, 800:801]
        SEL = SM[:, 801:803]
        ACOL = SM[:, 816:976].rearrange("p (d c) -> p d c", d=2)

        S.dma('sp', lambda e: e.dma_start(out=IDF[:, :], in_=identf[:, :]), [], ['IDF'])
        S.dma('sp', lambda e: e.dma_start(out=IDB[:, :], in_=identb[:, :]), [], ['IDB'])
        S.dma('sp', lambda e: e.dma_start(out=MSK[:, :, :], in_=maskAB[:, :, :]), [], ['MSK'])
        S.dma('sp', lambda e: e.dma_start(out=CST[:, :, :, :], in_=cs[:, :, :, :]), [], ['CST'])
        S.dma('sp', lambda e: e.dma_start(out=RM[:, :], in_=rmask[:, :]), [], ['RM'])
        S.dma('sp', lambda e: e.dma_start(out=CND, in_=cond[:, :]), [], ['CND'])
        S.dma('sp', lambda e: e.dma_start(out=ADB, in_=adab[:, :, :]), [], ['ADB'])
        S.dma('sp', lambda e: e.dma_start(out=NRM, in_=nrm[:, :, :]), [], ['NRM'])
        S.dma('sp', lambda e: e.dma_start(out=LB, in_=lbraw[:, :, :, :]), [], ['LB'])
        S.dma('sp', lambda e: e.dma_start(out=GN, in_=gnorm[:, :]), [], ['GN'])
        S.dma('sp', lambda e: e.dma_start(out=SEL, in_=selm[:, :]), [], ['SEL'])
        S.op('dve', lambda e: e.memset(ONES[:, :], 1.0), [], ['ONES'])
        S.op('dve', lambda e: e.memset(EPST, EPS), [], ['EPST'])

        import os
        _skip = os.environ.get('KSKIP', '').split(',')
        S.op('act', lambda e: e.activation(out=SCB[:, :], in_=CND, func=AF.Silu), ['CND'], ['SCB'])
        scb3 = SCB[:, :].rearrange("p (k c) -> p k c", c=2)
        AWB = [WB[:, 10240:14336].rearrange("p (k n) -> p k n", k=8), WB[:, 14336:18432].rearrange("p (k n) -> p k n", k=8)]
        ada_blocks = [(0, 0, 0), (0, 0, 1), (0, 1, 0), (0, 1, 1)] + [(0, j, hh) for j in range(2, 6) for hh in range(2)] \
            + [(1, j, hh) for j in range(6) for hh in range(2)]
        ada_cnt = [0]

        def ada_block():
            if ada_cnt[0] >= len(ada_blocks):
                return
            i, j, hh = ada_blocks[ada_cnt[0]]
            b = ada_cnt[0] % 2
            ada_cnt[0] += 1
            buf = AWB[b]; key = f"AWB{b}"
            src = ada_w[i, :, j * 1024 + hh * 512:j * 1024 + (hh + 1) * 512].rearrange("(k p) n -> p k n", p=128)
            S.dma('pool', lambda e, buf=buf, src=src: e.dma_start(out=buf, in_=src), [], [key])
            for o4 in range(4):
                oc = hh * 4 + o4
                col = (j * 8 + oc) * 2
                for kc in range(8):
                    S.op('pe', lambda e, buf=buf, o4=o4, kc=kc, col=col, i=i: e.matmul(
                        ps[i][:, col:col + 2], buf[:, kc, o4 * 128:(o4 + 1) * 128], scb3[:, kc, :],
                        start=(kc == 0), stop=(kc == 7)), [key, 'SCB'], [f"ps{i}"])

        def mod_evac(i, j0, j1):
            for c in range(2):
                S.op('dve', lambda e, i=i, c=c, j0=j0, j1=j1: e.tensor_tensor(
                    out=MOD[:, i, j0 * 8:j1 * 8, c], in0=ps[i][:, j0 * 16 + c:j1 * 16:2], in1=ADB[:, i, j0 * 8:j1 * 8], op=ALU.add),
                    [f"ps{i}", 'ADB'], [f"MOD{i}_{j0}"])

        def am_derive(n):
            i = n // 2; j = 1 if n % 2 == 0 else 4
            for c in range(2):
                S.op('dve', lambda e, n=n, i=i, j=j, c=c: e.scalar_tensor_tensor(
                    out=AM[:, n, :, c], in0=MOD[:, i, j * 8:(j + 1) * 8, c], scalar=1.0, in1=NRM[:, n, :],
                    op0=ALU.add, op1=ALU.mult), [f"MOD{i}_0", f"MOD{i}_2", 'NRM'], [f"AM{n}"])

        for _ in range(4):
            ada_block()
        mod_evac(0, 0, 2)
        am_derive(0)

        ALLX = [f"xT{k_}_{j_}" for k_ in range(8) for j_ in range(NT)]
        ALLH = [f"hT{k_}_{j_}" for k_ in range(8) for j_ in range(NT)]

        def modv(i, j, kc, c):
            return MOD[:, i, j * 8 + kc, c:c + 1]

        if 'fnet' not in _skip:
            ABC = WB[:, 6144:10240].rearrange("p (a n) -> p a n", a=4)
            DV = MISC[:, 4096:4608].bitcast(BF16).rearrange("p (k j) -> p k j", k=8)
            for v in range(4):
                c = v % 2
                vec = AM[:, 0, :, c] if v < 2 else MOD[:, 0, 0:8, c]
                S.op('dve', lambda e, vec=vec: e.tensor_tensor(
                    out=DV, in0=IDB[:, :].unsqueeze(1).to_broadcast([128, 8, 128]), in1=vec.unsqueeze(2).to_broadcast([128, 8, 128]), op=ALU.mult),
                    ['IDB', 'AM0', 'MOD0_0'], ['JUNK'])
                for hh in range(2):
                    S.op('pe', lambda e, hh=hh: e.matmul(ps[2 + hh][:, :], ONES[:, :], DV[:, hh * 4:(hh + 1) * 4, :], start=True, stop=True),
                         ['JUNK', 'ONES'], [f"ps{2 + hh}"])
                    if hh == 0:
                        S.op('act', lambda e, v=v, hh=hh: e.activation(out=ABC[:, v, hh * 512:(hh + 1) * 512], in_=ps[2 + hh][:, :], func=AF.Copy),
                             [f"ps{2 + hh}"], ['ABC'])
                    else:
                        S.op('dve', lambda e, v=v, hh=hh: e.tensor_copy(out=ABC[:, v, hh * 512:(hh + 1) * 512], in_=ps[2 + hh][:, :]),
                             [f"ps{2 + hh}"], ['ABC'])

            hn = BIGb[:, 0:36864].rearrange("p (c n) -> p c n", c=36)
            XIN = [MISC[:, 0:1024], MISC[:, 1024:2048]]
            PIN = [MISC[:, 2048:3072], MISC[:, 2048:3072]]
            TMPF = MISC[:, 3072:4096]
            JUNK = MISC[:, 4096:4608].bitcast(BF16)
            RB = MISC[:, 4608:5120]
            SQB = [MISC[:, 5120:5376].bitcast(BF16), MISC[:, 5376:5632].bitcast(BF16)]
            RL = [MISC[:, 5632:5888].bitcast(BF16), MISC[:, 5888:6144].bitcast(BF16)]
            XR = [XIN[0], XIN[1], WB[:, 0:2048].bitcast(F32)]
            PR = [PIN[0], WB[:, 2048:4096].bitcast(F32)]
            TR = [TMPF, WB[:, 4096:6144].bitcast(F32)]
            for ci in range(36):
                b = ci % 3; pb_ = ci % 2; tb_ = ci % 2
                c = 0 if ci < 32 else 1
                xk = f"XR{b}"; pk = f"PR{pb_}"; tk = f"TR{tb_}"
                if ci < 32:
                    for hf in range(2):
                        S.dma('sp', lambda e, ci=ci, b=b, hf=hf: e.dma_start(
                            out=XR[b][hf * 64:(hf + 1) * 64, :],
                            in_=x_all[0:4096, :].rearrange("(t1 r) c -> r t1 c", r=64)[32 * hf + ci]), [], [xk])
                        S.dma('sp', lambda e, ci=ci, pb_=pb_, hf=hf: e.dma_start(
                            out=PR[pb_][hf * 64:(hf + 1) * 64, :],
                            in_=pos_all[0:4096, :].rearrange("(t1 r) c -> r t1 c", r=64)[32 * hf + ci]), [], [pk])
                    S.op('pool', lambda e, b=b, pb_=pb_: e.tensor_tensor(out=XR[b][:, :], in0=XR[b][:, :], in1=PR[pb_][:, :], op=ALU.add),
                         [xk, pk], [xk])
                else:
                    S.dma('sp', lambda e, ci=ci, b=b: e.dma_start(out=XR[b][:, :], in_=x_all[ci * 128:(ci + 1) * 128, :]), [], [xk])
                S.op('act', lambda e, ci=ci, b=b: e.activation(out=JUNK[:, :], in_=XR[b][:, :], func=AF.Square,
                                                               accum_out=SSQ[:, ci:ci + 1]), [xk], ['JUNK', f"SSQ{ci}"])
                S.op('act', lambda e, ci=ci: e.activation(out=RSTD[:, ci:ci + 1], in_=SSQ[:, ci:ci + 1], func=AF.Sqrt,
                                                          bias=EPST, scale=1.0 / D), [f"SSQ{ci}", 'EPST'], [f"RSTD{ci}"])
                S.op('dve', lambda e, ci=ci: e.reciprocal(out=RSTD[:, ci:ci + 1], in_=RSTD[:, ci:ci + 1]),
                     [f"RSTD{ci}"], [f"RSTD{ci}"])
                S.op('dve', lambda e, ci=ci, b=b, c=c, tb_=tb_: e.scalar_tensor_tensor(
                    out=TR[tb_][:, :], in0=XR[b][:, :], scalar=RSTD[:, ci:ci + 1], in1=ABC[:, c, :],
                    op0=ALU.mult, op1=ALU.mult), [xk, f"RSTD{ci}", 'ABC'], [tk])
                S.op('dve', lambda e, ci=ci, c=c, tb_=tb_: e.tensor_tensor(out=hn[:, ci, :], in0=TR[tb_][:, :], in1=ABC[:, 2 + c, :], op=ALU.add),
                     [tk, 'ABC'], [f"hn{ci}"])
                ada_block()
            while ada_cnt[0] < len(ada_blocks):
                ada_block()
            mod_evac(0, 2, 6)
            mod_evac(1, 0, 2)
            mod_evac(1, 2, 6)
            am_derive(1); am_derive(2); am_derive(3)
            S.set_barrier()

            ZST = [WB[:, z * 2048:(z + 1) * 2048].rearrange("p (r c) -> p r c", r=2) for z in range(8)]
            MIXF = HB[:, 4096:20480].rearrange("p (k t) -> p k t", k=8)
            MIXP = HB[:, 0:4096].rearrange("p (k t) -> p k t", k=8)
            YTGS = [[WB[:, (2 * y_ + r_) * 4096:(2 * y_ + r_ + 1) * 4096].rearrange("p (k t) -> p k t", k=2) for r_ in range(2)] for y_ in range(2)]
            YTP = [WB[:, 0:4096].rearrange("p (k t) -> p k t", k=8), WB[:, 4096:8192].rearrange("p (k t) -> p k t", k=8)]
            FTP = [WB[:, 8192:9216].rearrange("p (c n) -> p c n", c=2), WB[:, 9216:10240].rearrange("p (c n) -> p c n", c=2)]
            WO = WB[:, 10240:18432].rearrange("p (k n) -> p k n", k=8)
            Z1T = [BIGb[:, 0:16384].rearrange("p (f c) -> p f c", f=64), BIGb[:, 16384:32768].rearrange("p (f c) -> p f c", f=64)]
            XTILES = [BIG[:, 0:4096].rearrange("p (k t) -> p k t", k=8), BIG[:, 4096:8192].rearrange("p (k t) -> p k t", k=8)]
            S.dma('sp', lambda e: e.dma_start(out=DAT[:, :, :], in_=da_in[:, :, :]), [], ['DAT'])
            S.dma('sp', lambda e: e.dma_start(out=MCT[:, :, :], in_=mc_in[:, :, :]), [], ['MCT'])
            for t2 in range(32):
                zb = t2 % 8
                for ri in range(2):
                    for ch in range(2):
                        bank = (t2 % 2) * 4 + ri * 2 + ch
                        S.op('pe', lambda e, t2=t2, ri=ri, ch=ch, bank=bank: e.matmul(
                            ps[bank][:, :], DAT[:, ri, :], hn[:, t2, ch * 512:(ch + 1) * 512], start=True, stop=True),
                            ['DAT', f"hn{t2}"], [f"ps{bank}"])
                        if (ri + ch) % 2 == 0:
                            S.op('act', lambda e, zb=zb, ri=ri, ch=ch, bank=bank: e.activation(
                                out=ZST[zb][:, ri, ch * 512:(ch + 1) * 512], in_=ps[bank][:, :], func=AF.Copy), [f"ps{bank}"], [f"ZST{zb}_{ri}_{ch}"])
                        else:
                            S.op('dve', lambda e, zb=zb, ri=ri, ch=ch, bank=bank: e.tensor_copy(
                                out=ZST[zb][:, ri, ch * 512:(ch + 1) * 512], in_=ps[bank][:, :]), [f"ps{bank}"], [f"ZST{zb}_{ri}_{ch}"])
                for hf in range(2):
                    S.dma('sp' if hf == 0 else 'pool', lambda e, t2=t2, zb=zb, hf=hf: e.dma_start(
                        out=zs[:, 32 * hf + t2, :, :], in_=ZST[zb][hf * 64:(hf + 1) * 64, :, :]),
                        [f"ZST{zb}_{r_}_{c_}" for r_ in range(2) for c_ in range(2)], [f"zs{t2}_{hf}"])
            S.set_barrier()
            for g in range(4):
                zb = g % 2
                YTG = YTGS[g % 2]; yk = f"YTG{g % 2}"
                for ri in range(2):
                    q_ = 'sp' if ri == 0 else 'act'
                    S.dma(q_, lambda e, g=g, zb=zb, ri=ri: e.dma_start(
                        out=Z1T[zb][ri * 64:(ri + 1) * 64, :, :],
                        in_=zs[:, :, ri, g * 256:(g + 1) * 256].rearrange("f t c -> t f c")), [], [f"Z1T{zb}"])
                for cl in range(2):
                    for a8 in range(8):
                        bank = a8
                        for f1l in range(8):
                            f1s = a8 * 8 + f1l
                            S.op('pe', lambda e, zb=zb, cl=cl, bank=bank, f1l=f1l, f1s=f1s: e.matmul(
                                ps[bank][:, f1l * 64:(f1l + 1) * 64], Z1T[zb][:, f1s, cl * 128:(cl + 1) * 128], MCT[:, f1s, :],
                                start=True, stop=True), [f"Z1T{zb}", 'MCT'], [f"ps{bank}"])
                        for ri in range(2):
                            src = ps[bank][:, :].rearrange("p (f r t) -> p f r t", f=8, r=2)[:, :, ri, :]
                            dst = YTG[ri][:, cl, :].rearrange("p (t f) -> p f t", f=64)[:, a8 * 8:(a8 + 1) * 8, :]
                            if a8 % 2 == 0:
                                S.op('act', lambda e, src=src, dst=dst: e.activation(out=dst, in_=src, func=AF.Copy), [f"ps{bank}"], [f"{yk}_{ri}_{cl}_{a8}"])
                            else:
                                S.op('dve', lambda e, src=src, dst=dst: e.tensor_copy(out=dst, in_=src), [f"ps{bank}"], [f"{yk}_{ri}_{cl}_{a8}"])
                for half in range(2):
                    oc = 2 * g + half
                    for jt in range(4):
                        bank = (half * 4 + jt) % 8
                        n = 0
                        for kc in range(2):
                            for ri in range(2):
                                S.op('pe', lambda e, half=half, kc=kc, ri=ri, jt=jt, n=n, bank=bank, YTG=YTG: e.matmul(
                                    ps[bank][:, :], CST[:, kc, ri, half * 128:(half + 1) * 128], YTG[ri][:, kc, jt * 512:(jt + 1) * 512],
                                    start=(n == 0), stop=(n == 3)), ['CST'] + [f"{yk}_{ri}_{kc}_{a}" for a in range(8)], [f"ps{bank}"])
                                n += 1
                        if jt % 2 == 0:
                            S.op('act', lambda e, oc=oc, jt=jt, bank=bank: e.activation(out=MIXF[:, oc, jt * 512:(jt + 1) * 512], in_=ps[bank][:, :], func=AF.Copy),
                                 [f"ps{bank}"], [f"MIXF{oc}_{jt}"])
                        else:
                            S.op('dve', lambda e, oc=oc, jt=jt, bank=bank: e.tensor_copy(out=MIXF[:, oc, jt * 512:(jt + 1) * 512], in_=ps[bank][:, :]),
                                 [f"ps{bank}"], [f"MIXF{oc}_{jt}"])
            S.set_barrier()
            S.dma('pool', lambda e: e.dma_start(out=WO, in_=fnet_wo[:, :].rearrange("(k p) n -> p k n", p=128)), [], ['WO'])
            ftc = [0]
            for sq in range(2):
                fb = ftc[0] % 2; ftc[0] += 1
                S.dma('sp', lambda e, fb=fb: e.dma_start(out=FTP[fb], in_=flp[:, :, :].rearrange("c p n -> p c n")), [], [f"FTP{fb}"])
                for tc in range(2):
                    for cc in range(8):
                        S.op('pe', lambda e, fb=fb, tc=tc, cc=cc, sq=sq: e.matmul(
                            ps[cc][:, :], hn[:, 32 + 2 * sq + tc, cc * 128:(cc + 1) * 128], FTP[fb][:, tc, :],
                            start=(tc == 0), stop=(tc == 1)), [f"FTP{fb}", f"hn{32 + 2 * sq + tc}"], [f"ps{cc}"])
                for cc in range(8):
                    for ri in range(2):
                        if cc % 2 == 0:
                            S.op('act', lambda e, cc=cc, ri=ri, sq=sq: e.activation(
                                out=YTP[ri][:, cc, sq * 256:(sq + 1) * 256], in_=ps[cc][:, ri * 256:(ri + 1) * 256], func=AF.Copy),
                                [f"ps{cc}"], [f"YTP{ri}"])
                        else:
                            S.op('dve', lambda e, cc=cc, ri=ri, sq=sq: e.tensor_copy(
                                out=YTP[ri][:, cc, sq * 256:(sq + 1) * 256], in_=ps[cc][:, ri * 256:(ri + 1) * 256]),
                                [f"ps{cc}"], [f"YTP{ri}"])
            for oc in range(8):
                g = oc // 2; half = oc % 2
                n = 0
                for kc in range(2):
                    for ri in range(2):
                        S.op('pe', lambda e, oc=oc, g=g, half=half, kc=kc, ri=ri, n=n: e.matmul(
                            ps[oc][:, :], CST[:, kc, ri, half * 128:(half + 1) * 128], YTP[ri][:, g * 2 + kc, :],
                            start=(n == 0), stop=(n == 3)), ['CST', f"YTP{ri}"], [f"ps{oc}"])
                        n += 1
                if oc % 2 == 0:
                    S.op('act', lambda e, oc=oc: e.activation(out=MIXP[:, oc, :], in_=ps[oc][:, :], func=AF.Copy), [f"ps{oc}"], [f"MIXF{oc}_4"])
                else:
                    S.op('dve', lambda e, oc=oc: e.tensor_copy(out=MIXP[:, oc, :], in_=ps[oc][:, :]), [f"ps{oc}"], [f"MIXF{oc}_4"])
            S.set_barrier()
            XQ = [XIN[0], XIN[1], WB[:, 0:2048].bitcast(F32), WB[:, 2048:4096].bitcast(F32)]
            PQ = [PIN[0], WB[:, 4096:6144].bitcast(F32), WB[:, 6144:8192].bitcast(F32)]
            xq_i = [0]
            for jt in range(NT):
                XT = XTILES[jt % 2]; xtk = f"XTILE{jt % 2}"
                for q4 in range(4):
                    b = xq_i[0] % 4; pb_ = xq_i[0] % 3; xq_i[0] += 1
                    r0 = jt * 512 + q4 * 128
                    S.dma('sp', lambda e, b=b, r0=r0: e.dma_start(out=XQ[b][:, :], in_=x_my[r0:r0 + 128, :]), [], [f"XQ{b}"])
                    if jt < 4:
                        S.dma('sp', lambda e, pb_=pb_, r0=r0: e.dma_start(out=PQ[pb_][:, :], in_=pos_my[r0:r0 + 128, :]), [], [f"PQ{pb_}"])
                        S.op('pool', lambda e, b=b, pb_=pb_: e.tensor_tensor(out=XQ[b][:, :], in0=XQ[b][:, :], in1=PQ[pb_][:, :], op=ALU.add),
                             [f"XQ{b}", f"PQ{pb_}"], [f"XQ{b}"])
                    for cc in range(8):
                        S.op('pe', lambda e, b=b, cc=cc, q4=q4: e.transpose(
                            ps[cc][:, q4 * 128:(q4 + 1) * 128], XQ[b][:, cc * 128:(cc + 1) * 128], IDF[:, :]),
                            [f"XQ{b}", 'IDF'], [f"ps{cc}"])
                for cc in range(8):
                    if cc % 2 == 0:
                        S.op('act', lambda e, cc=cc, XT=XT: e.activation(out=XT[:, cc, :], in_=ps[cc][:, :], func=AF.Copy), [f"ps{cc}"], [f"{xtk}_{cc}"])
                    else:
                        S.op('dve', lambda e, cc=cc, XT=XT: e.tensor_copy(out=XT[:, cc, :], in_=ps[cc][:, :]), [f"ps{cc}"], [f"{xtk}_{cc}"])
                c = 0 if jt < 4 else 1
                for oc in range(8):
                    for kc in range(8):
                        rhs = MIXF[:, kc, jt * 512:(jt + 1) * 512] if jt < 4 else MIXP[:, kc, :]
                        S.op('pe', lambda e, oc=oc, kc=kc, rhs=rhs: e.matmul(ps[oc][:, :], WO[:, kc, oc * 128:(oc + 1) * 128], rhs,
                                                                              start=(kc == 0), stop=(kc == 7)), ['WO', f"MIXF{kc}_{jt}"], [f"ps{oc}"])
                    S.op('dve', lambda e, oc=oc, c=c, XT=XT: e.scalar_tensor_tensor(
                        out=XT[:, oc, :], in0=ps[oc][:, :], scalar=modv(0, 2, oc, c), in1=XT[:, oc, :],
                        op0=ALU.mult, op1=ALU.add), [f"ps{oc}", f"{xtk}_{oc}", 'MOD0_2'], [f"{xtk}_{oc}"])
                S.dma('sp', lambda e, jt=jt, XT=XT: e.dma_start(
                    out=scr_x[:, :, jt * 512:(jt + 1) * 512].rearrange("k p t -> p k t"), in_=XT[:, :, :]),
                    [f"{xtk}_{c_}" for c_ in range(8)], [f"scrx{jt}"])

        S.set_barrier()
        S.dma('sp', lambda e: e.dma_start(out=xT, in_=scr_x[:, :, :].rearrange("k p t -> p k t")), [], ALLX)
        S.set_barrier()


        def norm_mod(n, i, jsh):
            TM2 = [TMPF[:, 0:512], TMPF[:, 512:1024]]
            for jt in range(NT):
                c = 0 if jt < 4 else 1
                for kc in range(8):
                    b = kc % 2
                    S.op('act', lambda e, kc=kc, jt=jt, b=b: e.activation(
                        out=SQB[b][:, :], in_=xT[:, kc, jt * 512:(jt + 1) * 512], func=AF.Square), [f"xT{kc}_{jt}"], [f"SQB{b}"])
                    S.op('pe', lambda e, kc=kc, b=b: e.matmul(ps[7][:, :], ONES[:, :], SQB[b][:, :], start=(kc == 0), stop=(kc == 7)),
                         [f"SQB{b}", 'ONES'], ['ps7'])
                S.op('act', lambda e: e.activation(out=RB[:, :], in_=ps[7][:, :], func=AF.Ln, bias=EPST, scale=1.0 / D),
                     ['ps7', 'EPST'], ['RB'])
                S.op('act', lambda e: e.activation(out=RB[:, :], in_=RB[:, :], func=AF.Exp, scale=-0.5), ['RB'], ['RB'])
                for kc in range(8):
                    tb_ = kc % 2
                    S.op('dve', lambda e, kc=kc, jt=jt, c=c, tb_=tb_: e.scalar_tensor_tensor(
                        out=TM2[tb_], in0=xT[:, kc, jt * 512:(jt + 1) * 512], scalar=AM[:, n, kc, c:c + 1], in1=RB[:, :],
                        op0=ALU.mult, op1=ALU.mult), [f"xT{kc}_{jt}", f"AM{n}", 'RB'], [f"TM2{tb_}"])
                    S.op('act', lambda e, kc=kc, jt=jt, c=c, tb_=tb_: e.activation(
                        out=hT[:, kc, jt * 512:(jt + 1) * 512], in_=TM2[tb_], func=AF.Identity,
                        bias=modv(i, jsh, kc, c), scale=1.0), [f"TM2{tb_}", f"MOD{i}_0", f"MOD{i}_2"], [f"hT{kc}_{jt}"])

        W1B = [WB[:, k * 4096:(k + 1) * 4096].rearrange("p (k n) -> p k n", k=8) for k in range(2)]
        W2B = [WB[:, 8192 + k * 4096:8192 + (k + 1) * 4096].rearrange("p (k n) -> p k n", k=4) for k in range(2)]
        UT = [MISC[:, 0:1024].bitcast(BF16).rearrange("p (k n) -> p k n", k=4), MISC[:, 1024:2048].bitcast(BF16).rearrange("p (k n) -> p k n", k=4)]
        wcnt = [0]

        def mlp(i):
            jg = 5
            its = [(hp, jt) for hp in range(8) for jt in range(NT)]
            wbs = {}

            def load_w(hp):
                wb = wcnt[0] % 2; wcnt[0] += 1
                wbs[hp] = wb
                S.dma('pool', lambda e, wb=wb, hp=hp: e.dma_start(
                    out=W1B[wb], in_=w1[i, :, hp * 512:(hp + 1) * 512].rearrange("(k p) n -> p k n", p=128)), [], [f"W1B{wb}"])
                S.dma('pool', lambda e, wb=wb, hp=hp: e.dma_start(
                    out=W2B[wb], in_=w2[i, hp * 512:(hp + 1) * 512, :].rearrange("(k p) n -> p k n", p=128)), [], [f"W2B{wb}", 'WO'])

            def stage1(k):
                hp, jt = its[k]; wb = wbs[hp]; ub = k % 2
                for fc in range(4):
                    pb = fc % 4
                    for kc in range(8):
                        S.op('pe', lambda e, wb=wb, fc=fc, kc=kc, jt=jt, pb=pb: e.matmul(
                            ps[pb][:, :], W1B[wb][:, kc, fc * 128:(fc + 1) * 128], hT[:, kc, jt * 512:(jt + 1) * 512],
                            start=(kc == 0), stop=(kc == 7)), [f"W1B{wb}", f"hT{kc}_{jt}"], [f"ps{pb}"])
                    rb = fc % 2
                    S.op('act', lambda e, pb=pb, rb=rb: e.activation(out=RL[rb][:, :], in_=ps[pb][:, :], func=AF.Relu),
                         [f"ps{pb}"], [f"RL{rb}"])
                    S.op('pool', lambda e, rb=rb, ub=ub, fc=fc: e.tensor_tensor(
                        out=UT[ub][:, fc, :], in0=RL[rb][:, :], in1=RL[rb][:, :], op=ALU.mult), [f"RL{rb}"], [f"UT{ub}_{fc}"])

            def stage2(k):
                hp, jt = its[k]; wb = wbs[hp]; ub = k % 2
                c = 0 if jt < 4 else 1
                for oc in range(8):
                    pb = 4 + oc % 4
                    for fc in range(4):
                        S.op('pe', lambda e, wb=wb, fc=fc, oc=oc, ub=ub, pb=pb: e.matmul(
                            ps[pb][:, :], W2B[wb][:, fc, oc * 128:(oc + 1) * 128], UT[ub][:, fc, :],
                            start=(fc == 0), stop=(fc == 3)), [f"W2B{wb}", f"UT{ub}_{fc}"], [f"ps{pb}"])
                    S.op('dve', lambda e, oc=oc, jt=jt, c=c, pb=pb: e.scalar_tensor_tensor(
                        out=xT[:, oc, jt * 512:(jt + 1) * 512], in0=ps[pb][:, :], scalar=modv(i, jg, oc, c),
                        in1=xT[:, oc, jt * 512:(jt + 1) * 512], op0=ALU.mult, op1=ALU.add),
                        [f"ps{pb}", f"xT{oc}_{jt}", 'MOD0_2', 'MOD1_2'], [f"xT{oc}_{jt}"])

            load_w(0)
            for k in range(len(its)):
                hp, jt = its[k]
                if jt == 0 and hp + 1 < 8:
                    load_w(hp + 1)
                if k == 0:
                    stage1(0)
                if k + 1 < len(its):
                    stage1(k + 1)
                stage2(k)

        if stage >= 2:
            norm_mod(1, 0, 3)
            mlp(0)
            if stage < 3:
                S.set_barrier()


        def hgrn():
            i = 1
            norm_mod(2, 1, 0)
            for jt_ in range(NT):
                S.dma('sp', lambda e, jt_=jt_: e.dma_start(
                    out=scr_x[:, :, jt_ * 512:(jt_ + 1) * 512].rearrange("k p t -> p k t"), in_=xT[:, :, jt_ * 512:(jt_ + 1) * 512]),
                    [f"xT{k_}_{jt_}" for k_ in range(8)], [f"scrx_h{jt_}"])
            S.set_barrier()
            S.op('dve', lambda e: e.tensor_tensor(out=LBV, in0=LB[:, :, 1, :], in1=LB[:, :, 0, :], op=ALU.subtract), ['LB'], ['LBV'])
            S.op('act', lambda e: e.activation(out=LBV, in_=LBV, func=AF.Sigmoid), ['LBV'], ['LBV'])
            S.op('dve', lambda e: e.tensor_scalar(out=OML, in0=LBV, scalar1=-1.0, scalar2=1.0, op0=ALU.mult, op1=ALU.add), ['LBV'], ['OML'])
            S.dma('pool', lambda e: e.dma_start(out=WO, in_=hgrn_wo[:, :].rearrange("(k p) n -> p k n", p=128)), [], ['WO'])
            WH = [WB[:, w * 5120:(w + 1) * 5120].rearrange("p (k j n) -> p k j n", k=8, j=5) for w in range(2)]
            QE = [BIGb[:, 0:2560], BIGb[:, 2560:5120]]
            KE = [BIGb[:, 5120:7680], BIGb[:, 7680:10240]]
            KDT = [BIGb[:, 10240:12800].rearrange("p (b k) -> p b k", b=20), BIGb[:, 12800:15360].rearrange("p (b k) -> p b k", b=20)]
            VT = BIGb[:, 15360:17920].rearrange("p (b k) -> p b k", b=20)
            GS = BIGb[:, 17920:20480]
            OA = BIGb[:, 20480:23040]
            def wt(n):
                return BIG[:, 12288 + n * 512:12288 + (n + 1) * 512]
            QS = wt(0)
            SG = [wt(1), wt(2)]; FT_ = [wt(3), wt(4)]; LG = [wt(5), wt(6)]; KT = [wt(7), wt(8)]
            BB = [wt(9), wt(10)]; E1 = [wt(11), wt(12)]; E2 = [wt(13), wt(14)]
            SST = [MISC[:, 0:128], MISC[:, 3584:3712], MISC[:, 3712:3840], MISC[:, 3840:3968]]
            SR = MISC[:, 128:384].rearrange("p (r n) -> p r n", r=2)
            OS = MISC[:, 384:896]; Ro = MISC[:, 896:1408]; T1 = MISC[:, 1408:1920]
            OSB = [MISC[:, 384:896], MISC[:, 4096:4608]]
            mb = MISC[:, 1920:3584].bitcast(BF16)
            SBR = [mb[:, r * 128:(r + 1) * 128] for r in range(8)]
            SCb = [mb[:, 1024:1152], mb[:, 1152:1280]]
            SQo = mb[:, 1280:1792]; OGh = mb[:, 1792:2304]
            KDt = [mb[:, 2304:2816], mb[:, 2816:3328]]
            psb5 = ps[6][:, :].bitcast(BF16)
            sbc = [0]
            def load_wh(h_):
                wh_ = WH[h_ % 2]
                for j in range(5):
                    S.dma('pool', lambda e, wh_=wh_, j=j, h_=h_: e.dma_start(
                        out=wh_[:, :, j, :], in_=w_in[:, j * 1024 + h_ * 128:j * 1024 + (h_ + 1) * 128].rearrange("(k p) n -> p k n", p=128)),
                        [], [f"WH{h_ % 2}"])
            load_wh(0)
            for h in range(8):
                wh = WH[h % 2]; whk = f"WH{h % 2}"
                if h + 1 < 8:
                    load_wh(h + 1)
                def elementwise(jt):
                    tsl = slice(jt * 512, (jt + 1) * 512)
                    def proj(pj, bank):
                        for kc in range(8):
                            S.op('pe', lambda e, pj=pj, bank=bank, kc=kc, wh=wh, tsl=tsl: e.matmul(
                                ps[bank][:, :], wh[:, kc, pj, :], hT[:, kc, tsl], start=(kc == 0), stop=(kc == 7)),
                                [whk, f"hT{kc}_{jt}"], [f"ps{bank}"])
                    proj(0, 4)
                    S.op('act', lambda e: e.activation(out=QS, in_=ps[4][:, :], func=AF.Silu), ['ps4'], ['QS'])
                    proj(1, 6)
                    S.op('act', lambda e: e.activation(out=SG[0], in_=ps[6][:, :], func=AF.Sigmoid), ['ps6'], ['SG0'])
                    proj(2, 4)
                    S.op('act', lambda e: e.activation(out=SG[1], in_=ps[4][:, :], func=AF.Sigmoid), ['ps4'], ['SG1'])
                    proj(4, 6)
                    S.op('act', lambda e, tsl=tsl: e.activation(out=GS[:, tsl], in_=ps[6][:, :], func=AF.Silu), ['ps6'], [f"GS{jt}"])
                    for q4 in range(4):
                        blk = jt * 4 + q4
                        for kc in range(8):
                            S.op('pe', lambda e, kc=kc, q4=q4, blk=blk, wh=wh: e.matmul(
                                ps[4][:, q4 * 128:(q4 + 1) * 128], hT[:, kc, blk * 128:(blk + 1) * 128], wh[:, kc, 3, :],
                                start=(kc == 0), stop=(kc == 7)), [whk, f"hT{kc}_{jt}"], ['ps4'])
                    S.op('dve', lambda e, jt=jt: e.tensor_copy(out=VT[:, jt * 4:(jt + 1) * 4, :], in_=ps[4][:, :].rearrange("p (b k) -> p b k", b=4)),
                         ['ps4'], [f"VT{jt}"])
                    def estep(k, d):
                        lastcol = 31 if d == 0 else 0
                        if k == 0:
                            S.op('dve', lambda e, d=d, h=h: e.tensor_scalar(out=FT_[d], in0=SG[d], scalar1=OML[:, d, h:h + 1], scalar2=LBV[:, d, h:h + 1],
                                                                            op0=ALU.mult, op1=ALU.add), [f"SG{d}", 'OML', 'LBV'], [f"F{d}"])
                        elif k == 1:
                            S.op('act', lambda e, d=d: e.activation(out=LG[d], in_=FT_[d], func=AF.Ln), [f"F{d}"], [f"LG{d}"])
                            S.op('pool', lambda e, d=d: e.tensor_scalar(out=KT[d], in0=FT_[d], scalar1=-1.0, scalar2=1.0, op0=ALU.mult, op1=ALU.add),
                                 [f"F{d}"], [f"KT{d}"])
                        elif k == 2:
                            if d == 0:
                                S.op('dve', lambda e, d=d: e.tensor_tensor_scan(out=BB[d], data0=RM[:, :], data1=LG[d], initial=0.0, op0=ALU.mult, op1=ALU.add),
                                     ['RM', f"LG{d}"], [f"BB{d}"])
                            else:
                                S.op('dve', lambda e, d=d: e.tensor_tensor_scan(out=BB[d][:, ::-1], data0=RM[:, :], data1=LG[d][:, ::-1], initial=0.0,
                                                                                op0=ALU.mult, op1=ALU.add), ['RM', f"LG{d}"], [f"BB{d}"])
                        elif k == 3:
                            S.op('act', lambda e, d=d: e.activation(out=E1[d], in_=BB[d], func=AF.Exp), [f"BB{d}"], [f"E1{d}"])
                            S.op('act', lambda e, d=d: e.activation(out=E2[d], in_=BB[d], func=AF.Exp, scale=-1.0), [f"BB{d}"], [f"E2{d}"])
                        elif k == 4:
                            S.op('dve', lambda e, d=d, tsl=tsl: e.tensor_tensor(out=QE[d][:, tsl], in0=QS, in1=E1[d], op=ALU.mult),
                                 ['QS', f"E1{d}"], [f"QE{d}_{jt}"])
                            S.op('pool', lambda e, d=d, jt=jt, lastcol=lastcol: e.tensor_copy(out=ACOL[:, d, jt * 16:(jt + 1) * 16], in_=E1[d][:, lastcol::32]),
                                 [f"E1{d}"], [f"ACOL{d}_{jt}"])
                            S.op('dve', lambda e, d=d, tsl=tsl: e.tensor_tensor(out=KE[d][:, tsl], in0=KT[d], in1=E2[d], op=ALU.mult),
                                 [f"KT{d}", f"E2{d}"], [f"KE{d}_{jt}"])
                        elif k == 5:
                            S.op('pool', lambda e, d=d, tsl=tsl, jt=jt: e.tensor_tensor(
                                out=KDt[d].rearrange("p (c j) -> p c j", j=32), in0=KE[d][:, tsl].rearrange("p (c j) -> p c j", j=32),
                                in1=ACOL[:, d, jt * 16:(jt + 1) * 16].unsqueeze(2).to_broadcast([128, 16, 32]), op=ALU.mult),
                                [f"KE{d}_{jt}", f"ACOL{d}_{jt}"], [f"KDt{d}"])
                        elif k == 6:
                            for q4 in range(4):
                                S.op('pe', lambda e, d=d, q4=q4: e.transpose(psb5[:, q4 * 128:(q4 + 1) * 128], KDt[d][:, q4 * 128:(q4 + 1) * 128], IDB[:, :]),
                                     [f"KDt{d}", 'IDB'], ['ps6'])
                            S.op('dve', lambda e, d=d, jt=jt: e.tensor_copy(out=KDT[d][:, jt * 4:(jt + 1) * 4, :], in_=psb5[:, 0:512].rearrange("p (b k) -> p b k", b=4)),
                                 ['ps6'], [f"KDT{d}_{jt}"])
                    for k in range(7):
                        for d in range(2):
                            estep(k, d)

                def seg_start(d, init):
                    st = {'d': d, 'cs': 0}
                    S0 = SST[0]
                    if init[0] == 'dram':
                        S.dma('sp', lambda e: e.dma_start(out=S0, in_=init[1]), [], ['Sst0'])
                    elif init[0] == 'zero':
                        S.op('dve', lambda e: e.memset(S0, 0.0), [], ['Sst0'])
                    else:
                        S.op('dve', lambda e: e.tensor_scalar(out=S0, in0=SR[:, 0, :], scalar1=SEL[:, 0:1], scalar2=None, op0=ALU.mult),
                             ['SR', 'SEL'], ['Sst0'])
                        S.op('dve', lambda e: e.scalar_tensor_tensor(out=S0, in0=SR[:, 1, :], scalar=SEL[:, 1:2], in1=S0, op0=ALU.mult, op1=ALU.add),
                             ['SR', 'SEL', 'Sst0'], ['Sst0'])
                    cur = sbc[0] % 8; sbc[0] += 1
                    S.op('act', lambda e, cur=cur: e.activation(out=SBR[cur], in_=S0, func=AF.Copy), ['Sst0'], [f"SBR{cur}"])
                    st['cur'] = cur
                    return st

                def seg_pre(st, blk, parts):
                    d = st['d']
                    bsl = slice(blk * 128, (blk + 1) * 128)
                    sc = blk % 2
                    tj = blk // 4
                    for pt in parts:
                        if pt == 's':
                            S.op('pe', lambda e, d=d, bsl=bsl: e.matmul(ps[7][:, 0:128], KE[d][:, bsl], QE[d][:, bsl], start=True, stop=True),
                                 [f"KE{d}_{tj}", f"QE{d}_{tj}"], ['ps7'])
                        elif pt == 'm':
                            S.op('dve', lambda e, d=d, sc=sc: e.tensor_tensor(out=SCb[sc], in0=ps[7][:, 0:128], in1=MSK[:, d, :], op=ALU.mult),
                                 ['ps7', 'MSK'], [f"SCb{sc}"])
                        elif pt == 'o':
                            S.op('pe', lambda e, blk=blk, sc=sc: e.matmul(ps[5][:, 0:128], VT[:, blk, :], SCb[sc], start=True, stop=False),
                                 [f"VT{tj}", f"SCb{sc}"], ['ps5'])
                        else:
                            c = int(pt[1])
                            kw = dict(tile_position=(96, 0)) if c == 3 else {}
                            S.op('pe', lambda e, d=d, c=c, blk=blk, kw=kw: e.matmul(
                                ps[c][:, 0:128], KDT[d][c * 32:(c + 1) * 32, blk, :], VT[c * 32:(c + 1) * 32, blk, :], start=True, stop=True, **kw),
                                [f"KDT{d}_{tj}", f"VT{tj}"], [f"ps{c}"])

                def seg_block(st, blk, nb=None):
                    d = st['d']
                    bsl = slice(blk * 128, (blk + 1) * 128)
                    po = 5
                    tj = blk // 4
                    if st.get('pre') != blk:
                        seg_pre(st, blk, ['s', 'm', 'u0', 'u1', 'u2', 'u3', 'o'])
                    corder = range(4) if d == 0 else range(3, -1, -1)
                    for n, c in enumerate(corder):
                        cur = st['cur']; cs = st['cs']
                        S.op('pe', lambda e, d=d, c=c, blk=blk, po=po, cur=cur, n=n: e.matmul(
                            ps[po][:, c * 32:(c + 1) * 32], SBR[cur], QE[d][:, blk * 128 + c * 32:blk * 128 + (c + 1) * 32],
                            start=False, stop=(n == 3)), [f"SBR{cur}", f"QE{d}_{tj}"], [f"ps{po}"])
                        ns = (cs + 1) % 4
                        S.op('dve', lambda e, d=d, c=c, blk=blk, cs=cs, ns=ns: e.scalar_tensor_tensor(
                            out=SST[ns], in0=SST[cs], scalar=ACOL[:, d, blk * 4 + c:blk * 4 + c + 1], in1=ps[c][:, 0:128], op0=ALU.mult, op1=ALU.add),
                            [f"Sst{cs}", f"ACOL{d}_{tj}", f"ps{c}"], [f"Sst{ns}"])
                        st['cs'] = ns
                        cur = sbc[0] % 8; sbc[0] += 1
                        st['cur'] = cur
                        if n % 2 == 0:
                            S.op('act', lambda e, cur=cur, ns=ns: e.activation(out=SBR[cur], in_=SST[ns], func=AF.Copy), [f"Sst{ns}"], [f"SBR{cur}"])
                        else:
                            S.op('pool', lambda e, cur=cur, ns=ns: e.tensor_copy(out=SBR[cur], in_=SST[ns]), [f"Sst{ns}"], [f"SBR{cur}"])
                        if nb is not None:
                            seg_pre(st, nb, [f"u{c}"])
                            if n == 0:
                                seg_pre(st, nb, ['s'])
                            if n == 3:
                                seg_pre(st, nb, ['m'])
                    if d == 0:
                        S.op('act', lambda e, bsl=bsl, po=po: e.activation(out=OA[:, bsl], in_=ps[po][:, 0:128], func=AF.Copy), [f"ps{po}"], [f"OA{blk}"])
                    else:
                        q4 = blk % 4
                        ob = (blk // 4) % 2
                        OSx = OSB[ob]
                        S.op('dve', lambda e, bsl=bsl, po=po, q4=q4, OSx=OSx: e.tensor_tensor(
                            out=OSx[:, q4 * 128:(q4 + 1) * 128], in0=ps[po][:, 0:128], in1=OA[:, bsl], op=ALU.add), [f"ps{po}", f"OA{blk}"], [f"OS{ob}_{q4}"])
                    if nb is not None:
                        seg_pre(st, nb, ['o'])
                        st['pre'] = nb
                    if d == 1:
                        if st.get('pending') is not None and st.get('pend_at') == blk:
                            st['pending'](); st['pending'] = None
                        q4 = blk % 4
                        if q4 == 0:
                            jt = blk // 4
                            tsl = slice(jt * 512, (jt + 1) * 512)
                            ob = jt % 2
                            OSx = OSB[ob]
                            oskeys = [f"OS{ob}_{q_}" for q_ in range(4)]

                            def onorm(jt=jt, tsl=tsl, OSx=OSx, oskeys=oskeys):
                                S.op('act', lambda e: e.activation(out=SQo, in_=OSx, func=AF.Square), oskeys, ['SQo'])
                                S.op('pe', lambda e: e.matmul(ps[6][:, :], ONES[:, :], SQo, start=True, stop=True), ['SQo', 'ONES'], ['ps6'])
                                S.op('act', lambda e: e.activation(out=Ro, in_=ps[6][:, :], func=AF.Ln, bias=EPST, scale=1.0 / 128.0),
                                     ['ps6', 'EPST'], ['Ro'])
                                S.op('act', lambda e: e.activation(out=Ro, in_=Ro, func=AF.Exp, scale=-0.5), ['Ro'], ['Ro'])
                                S.op('dve', lambda e: e.scalar_tensor_tensor(out=T1, in0=OSx, scalar=GN, in1=Ro, op0=ALU.mult, op1=ALU.mult),
                                     oskeys + ['GN', 'Ro'], ['T1'])
                                S.op('pool', lambda e: e.tensor_tensor(out=OGh, in0=T1, in1=GS[:, tsl], op=ALU.mult), ['T1', f"GS{jt}"], ['OGh'])
                                S.dma('sp', lambda e, h=h: e.dma_start(out=scr_og[h, :, tsl], in_=OGh), ['OGh'], [f"scrog{h}_{jt}"])
                            if nb is not None and (nb // 4) != jt and (nb % 4) == 3:
                                st['pending'] = onorm
                                st['pend_at'] = nb - 1
                            else:
                                onorm()

                def seg_end(st, fin):
                    if st.get('pending') is not None:
                        st['pending'](); st['pending'] = None
                    cs = st['cs']
                    if fin[0] == 'state':
                        S.dma('sp', lambda e: e.dma_start(out=fin[1], in_=SST[cs]), [f"Sst{cs}"], [])
                    elif fin[0] == 'cc':
                        S.dma('pool', lambda e, h=h: e.dma_start(out=cc_in[h, :, :], in_=SST[cs]), [f"Sst{cs}"], [f"ccin{h}"])
                        S.special('pool', lambda e, h=h: e.collective_compute(
                            "AllGather", ALU.bypass, replica_groups=[[0, 1], [2, 3], [4, 5], [6, 7]],
                            ins=[cc_in[h, :, :]], outs=[cc_out[h, :, :]]), ('cc', h), [f"ccin{h}"], [f"ccout{h}"])
                        S.dma('sp', lambda e, h=h: e.dma_start(out=SR, in_=cc_out[h, :, :].rearrange("(r p) n -> p r n", p=128)),
                              [f"ccout{h}"], ['SR'])

                def run_segment(d, blocks, init, fin):
                    st = seg_start(d, init)
                    for i_, blk in enumerate(blocks):
                        seg_block(st, blk, blocks[i_ + 1] if i_ + 1 < len(blocks) else None)
                    seg_end(st, fin)

                elementwise(0)
                elementwise(1)
                stA = seg_start(0, ('dram', sA0[:, h, :]))
                for jt in range(4):
                    for q4 in range(4):
                        blk = jt * 4 + q4
                        seg_block(stA, blk, blk + 1 if blk + 1 < 16 else None)
                    if jt == 3:
                        seg_end(stA, ('cc',))
                    if jt + 2 < NT:
                        elementwise(jt + 2)
                run_segment(0, [16, 17], ('zero',), ('state', st_out[0, 0, h, :, :]))
                run_segment(0, [18, 19], ('zero',), ('state', st_out[1, 0, h, :, :]))
                run_segment(1, [19, 18], ('zero',), ('state', st_out[1, 1, h, :, :]))
                run_segment(1, [17, 16], ('zero',), ('state', st_out[0, 1, h, :, :]))
                run_segment(1, list(range(15, -1, -1)), ('sel',), ('none',))
            S.set_barrier()
            OGT = hT
            for jt_ in range(NT):
                S.dma('sp', lambda e, jt_=jt_: e.dma_start(
                    out=OGT[:, :, jt_ * 512:(jt_ + 1) * 512], in_=scr_og[:, :, jt_ * 512:(jt_ + 1) * 512].rearrange("h p t -> p h t")),
                    [], [f"hT{k_}_{jt_}" for k_ in range(8)])
                S.dma('act', lambda e, jt_=jt_: e.dma_start(
                    out=xT[:, :, jt_ * 512:(jt_ + 1) * 512], in_=scr_x[:, :, jt_ * 512:(jt_ + 1) * 512].rearrange("k p t -> p k t")),
                    [], [f"xT{k_}_{jt_}" for k_ in range(8)])
            for jt in range(NT):
                c = 0 if jt < 4 else 1
                tsl = slice(jt * 512, (jt + 1) * 512)
                for oc in range(8):
                    pb = oc % 4
                    for kc in range(8):
                        S.op('pe', lambda e, oc=oc, kc=kc, tsl=tsl, pb=pb: e.matmul(ps[pb][:, :], WO[:, kc, oc * 128:(oc + 1) * 128], OGT[:, kc, tsl],
                                                                                   start=(kc == 0), stop=(kc == 7)), ['WO', f"hT{kc}_{jt}"], [f"ps{pb}"])
                    S.op('dve', lambda e, oc=oc, tsl=tsl, c=c, pb=pb: e.scalar_tensor_tensor(
                        out=xT[:, oc, tsl], in0=ps[pb][:, :], scalar=modv(1, 2, oc, c), in1=xT[:, oc, tsl], op0=ALU.mult, op1=ALU.add),
                        [f"ps{pb}", f"xT{oc}_{jt}", 'MOD0_2', 'MOD1_2'], [f"xT{oc}_{jt}"])
            if stage < 4:
                S.set_barrier()

        if stage >= 3:
            hgrn()
        if stage >= 4:
            norm_mod(3, 1, 3)
            mlp(1)

        OUTB = [WB[:, 16384:18432].bitcast(F32), MISC[:, 3072:4096]]
        NFB = MISC[:, 2048:3072]
        S.dma('sp', lambda e: e.dma_start(out=NFB[:, :], in_=nfin[:, :].partition_broadcast(128)), [], ['NFB'])
        final_norm = (stage >= 4)
        for bk in range(NBLK):
            ob = bk % 2
            pp = (bk % 2) * 4
            for kc in range(8):
                bank = pp + kc // 4
                S.op('pe', lambda e, kc=kc, bk=bk, bank=bank: e.transpose(
                    ps[bank][:, (kc % 4) * 128:(kc % 4 + 1) * 128], xT[:, kc, bk * 128:(bk + 1) * 128], IDF[:, :]),
                    [f"xT{kc}_{bk // 4}", 'IDF'], [f"ps{bank}"])
            for hh in range(2):
                bank = pp + hh
                if not final_norm:
                    S.op('act', lambda e, ob=ob, hh=hh, bank=bank: e.activation(
                        out=OUTB[ob][:, hh * 512:(hh + 1) * 512], in_=ps[bank][:, :], func=AF.Copy), [f"ps{bank}"], [f"OUTB{ob}"])
                else:
                    S.op('act', lambda e, hh=hh, bank=bank, bk=bk: e.activation(
                        out=JUNK[:, 0:512], in_=ps[bank][:, :], func=AF.Square, accum_out=SSQ[:, hh:hh + 1]),
                        [f"ps{bank}"], ['JUNK', f"SSQ{hh}"])
            if final_norm:
                S.op('dve', lambda e: e.tensor_tensor(out=SSQ[:, 2:3], in0=SSQ[:, 0:1], in1=SSQ[:, 1:2], op=ALU.add),
                     ['SSQ0', 'SSQ1'], ['SSQ2'])
                S.op('act', lambda e: e.activation(out=RSTD[:, 0:1], in_=SSQ[:, 2:3], func=AF.Sqrt, bias=EPST, scale=1.0 / D),
                     ['SSQ2', 'EPST'], ['RSTD0'])
                S.op('dve', lambda e: e.reciprocal(out=RSTD[:, 0:1], in_=RSTD[:, 0:1]), ['RSTD0'], ['RSTD0'])
                for hh in range(2):
                    bank = pp + hh
                    S.op('dve', lambda e, ob=ob, hh=hh, bank=bank: e.scalar_tensor_tensor(
                        out=OUTB[ob][:, hh * 512:(hh + 1) * 512], in0=ps[bank][:, :], scalar=RSTD[:, 0:1],
                        in1=NFB[:, hh * 512:(hh + 1) * 512], op0=ALU.mult, op1=ALU.mult),
                        [f"ps{bank}", 'RSTD0', 'NFB'], [f"OUTB{ob}"])
            S.dma('sp', lambda e, ob=ob, bk=bk: e.dma_start(out=y_my[bk * 128:(bk + 1) * 128, :], in_=OUTB[ob][:, :]),
                  [f"OUTB{ob}"], [])

        S.finalize()
        with nc.Block() as block:
            S.emit(nc, block, sems)
    return nc


GRID_W = 64


def _pos_embed(n_tok, d):
    rows = n_tok // GRID_W
    quarter = d // 4
    omega = (1.0 / (10000.0 ** (np.arange(quarter, dtype=np.float32) / quarter))).astype(np.float32)
    r = np.arange(rows, dtype=np.float32)[:, None] * omega[None, :]
    cl = np.arange(GRID_W, dtype=np.float32)[:, None] * omega[None, :]
    er = np.concatenate([np.sin(r), np.cos(r)], axis=-1)
    ec = np.concatenate([np.sin(cl), np.cos(cl)], axis=-1)
    emb = np.concatenate([np.broadcast_to(er[:, None, :], (rows, GRID_W, d // 2)),
                          np.broadcast_to(ec[None, :, :], (rows, GRID_W, d // 2))], axis=-1)
    return emb.reshape(n_tok, d).astype(np.float32)


def _tables(a):
    bf = ml_dtypes.bfloat16
    t1 = np.arange(64); f1s = np.arange(64)
    f1 = f1s if a == 0 else 63 - f1s
    angA = (2.0 * np.pi / 64.0) * ((t1[:, None] * f1[None, :]) % 64).astype(np.float64)
    da = np.zeros((128, 2, 128), np.float64)
    for hf in range(2):
        da[hf * 64:(hf + 1) * 64, 0, hf * 64:(hf + 1) * 64] = np.cos(angA) / 8.0
        da[hf * 64:(hf + 1) * 64, 1, hf * 64:(hf + 1) * 64] = -np.sin(angA) / 8.0
    t2 = np.arange(64); f2s = np.arange(32)
    f2 = f2s if a == 0 else 63 - f2s
    f = f1[:, None] + 64 * f2[None, :]
    th = (2.0 * np.pi / 4096.0) * ((t2[:, None, None].astype(np.int64) * f[None, :, :].astype(np.int64)) % 4096).astype(np.float64)
    co = np.cos(th) / 8.0; si = np.sin(th) / 8.0
    mc = np.zeros((2, 64, 64, 2, 32), np.float64)
    mc[0, :, :, 0, :] = co; mc[0, :, :, 1, :] = -si
    mc[1, :, :, 0, :] = si; mc[1, :, :, 1, :] = co
    mc = mc.reshape(128, 64, 64)
    tp = np.arange(256)
    fp = tp if a == 0 else 255 - tp
    angp = (2.0 * np.pi / 256.0) * ((tp[:, None] * fp[None, :]) % 256).astype(np.float64)
    flp = np.concatenate([np.cos(angp) / 16.0, -np.sin(angp) / 16.0], axis=1)
    flp = np.ascontiguousarray(flp.reshape(2, 128, 512)).astype(bf)
    c = np.arange(256)
    angc = (2.0 * np.pi / 256.0) * ((c[:, None] * c[None, :]) % 256).astype(np.float64)
    cc = np.cos(angc) / 16.0; sc = np.sin(angc) / 16.0
    cst = np.stack([cc, sc], axis=1)
    cst = cst.reshape(2, 128, 2, 256).transpose(1, 0, 2, 3)
    cst = np.ascontiguousarray(cst).astype(bf)
    return np.ascontiguousarray(da).astype(bf), np.ascontiguousarray(mc).astype(bf), flp, cst


_NC_CACHE = {}
STAGE = 4


def kernel(x_prompt, x_sample, state_hgrn, c, c_ctx, ada_w, ada_b, norm_mix, norm_mlp, fnet_wo,
           hgrn_w_in, hgrn_lb, hgrn_norm, hgrn_wo, mlp_w1, mlp_w2, norm_final, _stage=None):
    stage = STAGE if _stage is None else _stage
    f32 = np.float32
    bf = ml_dtypes.bfloat16
    x_prompt = np.asarray(x_prompt, f32); x_sample = np.asarray(x_sample, f32)
    state_hgrn = np.asarray(state_hgrn, f32); c = np.asarray(c, f32); c_ctx = np.asarray(c_ctx, f32)
    ada_w = np.ascontiguousarray(np.asarray(ada_w, f32)); ada_b = np.asarray(ada_b, f32)
    norm_mix = np.asarray(norm_mix, f32); norm_mlp = np.asarray(norm_mlp, f32)
    fnet_wo = np.ascontiguousarray(np.asarray(fnet_wo, f32)[0]); hgrn_w_in = np.asarray(hgrn_w_in, f32)[0]
    hgrn_lb = np.asarray(hgrn_lb, f32); hgrn_norm = np.asarray(hgrn_norm, f32)
    hgrn_wo = np.ascontiguousarray(np.asarray(hgrn_wo, f32)[0])
    mlp_w1 = np.ascontiguousarray(np.asarray(mlp_w1, f32)); mlp_w2 = np.ascontiguousarray(np.asarray(mlp_w2, f32))
    norm_final = np.asarray(norm_final, f32)

    if stage not in _NC_CACHE:
        _NC_CACHE[stage] = build(stage)
    nc = _NC_CACHE[stage]
    pos = _pos_embed(4096, D)
    tabs = [_tables(0), _tables(1)]
    adab = np.ascontiguousarray(ada_b.reshape(2, 48, 128).transpose(2, 0, 1))
    nrm = np.stack([norm_mix[0], norm_mlp[0], norm_mix[1], norm_mlp[1]], 0).reshape(4, 8, 128).transpose(2, 0, 1)
    nrm = np.ascontiguousarray(nrm)
    identf = np.eye(128, dtype=f32); identb = np.eye(128).astype(bf)
    s_i = np.arange(128)[:, None]; t_i = np.arange(128)[None, :]
    same = (s_i // 32) == (t_i // 32)
    maskAB = np.stack([(same & (s_i <= t_i)), (same & (s_i >= t_i))], axis=1).astype(f32).astype(bf)
    rmask = np.ones((128, 512), f32); rmask[:, ::32] = 0.0
    wq, wff, wfb, wi, wg = np.split(hgrn_w_in, 5, axis=1)
    in_maps = []
    for r in range(8):
        p, a = r // 2, r % 2
        xs = x_sample[p]
        if a == 0:
            loc = np.arange(2048)
        else:
            loc = 4095 - np.arange(2048)
        prs = [x_prompt[2 * r], x_prompt[2 * r + 1]]
        if a == 1:
            prs_loc = [pp[::-1] for pp in prs]
        else:
            prs_loc = prs
        x_my = np.concatenate([xs[loc]] + prs_loc, axis=0)
        x_all = np.concatenate([xs] + prs, axis=0)
        da, mc, flp, cst = tabs[a]
        cnd = np.stack([c[p], c_ctx], axis=-1).reshape(8, 128, 2).transpose(1, 0, 2).reshape(128, 16)
        if a == 0:
            w_in = np.concatenate([wq, wff, wfb, wi, wg], axis=1); lbr = hgrn_lb
        else:
            w_in = np.concatenate([wq, wfb, wff, wi, wg], axis=1); lbr = hgrn_lb[::-1]
        lbraw = np.ascontiguousarray(lbr.reshape(2, 2, 8, 128).transpose(3, 0, 1, 2))
        selm = np.zeros((128, 2), f32); selm[:, 1 - a] = 1.0
        sA0 = np.ascontiguousarray(state_hgrn[p, 0, a].transpose(1, 0, 2))
        in_maps.append({
            "x_my": np.ascontiguousarray(x_my), "pos_my": np.ascontiguousarray(pos[loc]),
            "x_all": np.ascontiguousarray(x_all), "pos_all": pos,
            "da_in": da, "mc_in": mc, "flp": flp, "cs": cst, "cond": np.ascontiguousarray(cnd),
            "ada_w": ada_w, "adab": adab, "nrm": nrm, "nfin": norm_final.reshape(1, D),
            "fnet_wo": fnet_wo, "w_in": np.ascontiguousarray(w_in), "lbraw": lbraw,
            "gnorm": np.ascontiguousarray(hgrn_norm.reshape(128, 1)), "hgrn_wo": hgrn_wo,
            "w1": mlp_w1, "w2": mlp_w2, "sA0": sA0, "selm": selm, "maskAB": np.ascontiguousarray(maskAB),
            "identf": identf, "identb": identb, "rmask": rmask,
        })
    res = run_bass_kernel_spmd(nc, in_maps, core_ids=list(range(8)))
    y_prompt = np.zeros((16, 256, D), f32); y_sample = np.zeros((4, 4096, D), f32)
    new_state = np.zeros((16, 1, 2, 8, 128, 128), f32)
    for r in range(8):
        p, a = r // 2, r % 2
        y = res.results[r]["y_my"]
        st = res.results[r]["st_out"]
        if a == 0:
            y_sample[p, 0:2048] = y[0:2048]
            y_prompt[2 * r] = y[2048:2304]; y_prompt[2 * r + 1] = y[2304:2560]
        else:
            y_sample[p, 2048:4096] = y[0:2048][::-1]
            y_prompt[2 * r] = y[2048:2304][::-1]; y_prompt[2 * r + 1] = y[2304:2560][::-1]
        for q in range(2):
            for dd in range(2):
                gd = dd if a == 0 else 1 - dd
                new_state[2 * r + q, 0, gd] = st[q, dd]
    return (y_prompt, y_sample, new_state)
```

```python
import numpy as np
import ml_dtypes
from contextlib import ExitStack
import concourse.bass as bass
import concourse.mybir as mybir
from concourse.bass_utils import run_bass_kernel_spmd

F32 = mybir.dt.float32
BF16 = mybir.dt.bfloat16
AF = mybir.ActivationFunctionType
ALU = mybir.AluOpType

D = 1024
T = 2560
TS = 2048
NT = 5
EPS = 1e-6
NBLK = 20


class _Op:
    pass


class Sched:
    ENG = ['pe', 'act', 'dve', 'pool', 'sp']

    def __init__(s):
        s.q = {e: [] for e in s.ENG}
        s.lastw = {}
        s.readers = {}
        s.barrier = []
        s.dma_ring = {'sp': 12, 'act': 6, 'pool': 12}
        s.dma_next = {k: 0 for k in s.dma_ring}
        s.slot_last = {}

    def _deps(s, reads, writes):
        d = []
        for b in reads:
            if b in s.lastw:
                d.append(s.lastw[b])
        for b in writes:
            if b in s.lastw:
                d.append(s.lastw[b])
            d.extend(s.readers.get(b, ()))
        d.extend(s.barrier)
        return d

    def _commit(s, o, reads, writes):
        for b in reads:
            s.readers.setdefault(b, []).append(o)
        for b in writes:
            s.lastw[b] = o
            s.readers[b] = []

    def op(s, eng, fn, reads=(), writes=()):
        psr = [b for b in reads if isinstance(b, str) and b.startswith('ps')]
        if psr:
            reads = [b for b in reads if b not in psr]
            writes = list(writes) + psr
        o = _Op()
        o.eng = eng; o.fn = fn; o.is_dma = False; o.sig = False; o.sigidx = 0
        o.deps = s._deps(reads, writes)
        s.q[eng].append(o)
        s._commit(o, reads, writes)
        return o

    def dma(s, q, fn, reads=(), writes=()):
        o = _Op()
        o.eng = q; o.fn = fn; o.is_dma = True; o.sig = False
        o.deps = s._deps(reads, writes)
        slot = s.dma_next[q]
        s.dma_next[q] = (slot + 1) % s.dma_ring[q]
        o.slot = (q, slot)
        prev = s.slot_last.get(o.slot)
        o.slotval = (prev.slotval if prev else 0) + 16
        if prev is not None:
            o.deps.append(prev)
        s.slot_last[o.slot] = o
        s.q[q].append(o)
        s._commit(o, reads, writes)
        return o

    def special(s, q, fn, key, reads=(), writes=()):
        o = _Op()
        o.eng = q; o.fn = fn; o.is_dma = True; o.sig = False
        o.deps = s._deps(reads, writes)
        o.slot = key
        o.slotval = 1
        o.incval = 1
        s.slot_last[o.slot] = o
        s.q[q].append(o)
        s._commit(o, reads, writes)
        return o

    def set_barrier(s):
        b = []
        for e in s.ENG:
            for o in reversed(s.q[e]):
                if not o.is_dma:
                    b.append(o)
                    break
        for o in s.slot_last.values():
            b.append(o)
        s.barrier = b

    def finalize(s):
        for e in s.ENG:
            for o in s.q[e]:
                for d in o.deps:
                    if d.is_dma:
                        continue
                    if d.eng == 'pe' and e == 'pe' and not o.is_dma:
                        continue
                    d.sig = True
        for e in s.ENG:
            c = 0
            for o in s.q[e]:
                if (not o.is_dma) and o.sig:
                    c += 1
                    o.sigidx = c
        for e in s.ENG:
            waited = {}
            for o in s.q[e]:
                w = {}
                for d in o.deps:
                    if d.is_dma:
                        key = ('dma',) + d.slot
                        val = d.slotval
                    else:
                        if d.eng == 'pe' and e == 'pe' and not o.is_dma:
                            continue
                        key = d.eng
                        val = d.sigidx
                    if waited.get(key, 0) >= val:
                        continue
                    if w.get(key, 0) < val:
                        w[key] = val
                for k, v in w.items():
                    waited[k] = v
                o.waits = list(w.items())

    def emit(s, nc, block, sems):
        def run(e, eng):
            for o in s.q[e]:
                for k, v in o.waits:
                    eng.wait_ge(sems[k], v)
                ins = o.fn(eng)
                if o.is_dma:
                    ins.then_inc(sems[('dma',) + o.slot], getattr(o, 'incval', 16))
                elif o.sig:
                    ins.then_inc(sems[e], 1)
            if e == 'sp':
                for slot, o in s.slot_last.items():
                    eng.wait_ge(sems[('dma',) + slot], o.slotval)

        @block.tensor
        def _(eng):
            run('pe', eng)

        @block.scalar
        def _(eng):
            run('act', eng)

        @block.vector
        def _(eng):
            run('dve', eng)

        @block.gpsimd
        def _(eng):
            run('pool', eng)

        @block.sync
        def _(eng):
            run('sp', eng)


def build(stage=4):
    nc = bass.Bass("TRN2", target_bir_lowering=False)
    S = Sched()

    def din(name, shape, dt=F32):
        return nc.dram_tensor(name, list(shape), dt, kind="ExternalInput").ap()

    x_my = din("x_my", [T, D]); pos_my = din("pos_my", [TS, D])
    x_all = din("x_all", [4608, D]); pos_all = din("pos_all", [4096, D])
    flp = din("flp", [2, 128, 512], BF16)
    cs = din("cs", [128, 2, 2, 256], BF16)
    cond = din("cond", [128, 16]); ada_w = din("ada_w", [2, D, 6 * D]); adab = din("adab", [128, 2, 48])
    nrm = din("nrm", [128, 4, 8]); nfin = din("nfin", [1, D])
    fnet_wo = din("fnet_wo", [D, D]); w_in = din("w_in", [D, 5 * D]); lbraw = din("lbraw", [128, 2, 2, 8])
    gnorm = din("gnorm", [128, 1]); hgrn_wo = din("hgrn_wo", [D, D])
    w1 = din("w1", [2, D, 4 * D]); w2 = din("w2", [2, 4 * D, D])
    sA0 = din("sA0", [128, 8, 128]); selm = din("selm", [128, 2])
    maskAB = din("maskAB", [128, 2, 128], BF16)
    identf = din("identf", [128, 128]); identb = din("identb", [128, 128], BF16)
    rmask = din("rmask", [128, 512])
    da_in = din("da_in", [128, 2, 128], BF16); mc_in = din("mc_in", [128, 64, 64], BF16)
    y_my = nc.dram_tensor("y_my", [T, D], F32, kind="ExternalOutput").ap()
    st_out = nc.dram_tensor("st_out", [2, 2, 8, 128, 128], F32, kind="ExternalOutput").ap()
    scr_x = nc.dram_tensor("scr_x", [8, 128, T], F32, kind="Internal").ap()
    scr_vec = nc.dram_tensor("scr_vec", [4, D], F32, kind="Internal").ap()
    zs = nc.dram_tensor("zs", [64, 64, 2, D], BF16, kind="Internal").ap()
    scr_og = nc.dram_tensor("scr_og", [8, 128, T], BF16, kind="Internal").ap()
    cc_in = nc.dram_tensor("cc_in", [8, 128, 128], F32, kind="Internal").ap()
    cc_out = nc.dram_tensor("cc_out", [8, 256, 128], F32, kind="Internal", addr_space="Local").ap()

    es = ExitStack()
    with es:
        def sb(name, shape, dt=F32):
            return es.enter_context(nc.sbuf_tensor(name, list(shape), dt))

        BIG = sb("BIG", [128, 20480], F32)
        HB = sb("HB", [128, 20480], BF16)
        WB = sb("WB", [128, 18432], BF16)
        MISC = sb("MISC", [128, 6144], F32)
        SM = sb("SM", [128, 1024], F32)
        IDF = sb("IDF", [128, 128], F32)
        IDB = sb("IDB", [128, 128], BF16)
        MSK = sb("MSK", [128, 2, 128], BF16)
        ONES = sb("ONES", [128, 128], BF16)
        SCB = sb("SCB", [128, 16], BF16)
        CST = sb("CST", [128, 2, 2, 256], BF16)
        RM = sb("RM", [128, 512], F32)
        DAT = sb("DAT", [128, 2, 128], BF16)
        MCT = sb("MCT", [128, 64, 64], BF16)
        ps = [es.enter_context(nc.psum_tensor(f"ps{i}", [128, 512], F32)) for i in range(8)]
        sems = {}
        for e in Sched.ENG:
            sems[e] = es.enter_context(nc.semaphore(f"sem_{e}"))
        for q, n in S.dma_ring.items():
            for i in range(n):
                sems[('dma', q, i)] = es.enter_context(nc.semaphore(f"dsem_{q}_{i}"))
        for hh_ in range(8):
            sems[('dma', 'cc', hh_)] = es.enter_context(nc.semaphore(f"ccsem_{hh_}"))

        xT = BIG[:, :].rearrange("p (k t) -> p k t", k=8)
        hT = HB[:, :].rearrange("p (k t) -> p k t", k=8)
        BIGb = BIG[:, :].bitcast(BF16)

        MOD = SM[:, 0:192].rearrange("p (i j c) -> p i j c", i=2, j=48)
        AM = SM[:, 192:448].rearrange("p (n k c) -> p n k c", n=4, k=8)
        SSQ = SM[:, 448:512]
        RSTD = SM[:, 512:576]
        NRM = SM[:, 576:608].rearrange("p (n k) -> p n k", n=4)
        ADB = SM[:, 608:704].rearrange("p (i j) -> p i j", i=2)
        CND = SM[:, 704:720]
        EPST = SM[:, 720:721]
        LB = SM[:, 736:768].rearrange("p (d l h) -> p d l h", d=2, l=2)
        LBV = SM[:, 768:784].rearrange("p (d h) -> p d h", d=2)
        OML = SM[:, 784:800].rearrange("p (d h) -> p d h", d=2)
        GN = SM[:, 800:801]
        SEL = SM[:, 801:803]
        ACOL = SM[:, 816:976].rearrange("p (d c) -> p d c", d=2)

        S.dma('sp', lambda e: e.dma_start(out=IDF[:, :], in_=identf[:, :]), [], ['IDF'])
        S.dma('sp', lambda e: e.dma_start(out=IDB[:, :], in_=identb[:, :]), [], ['IDB'])
        S.dma('sp', lambda e: e.dma_start(out=MSK[:, :, :], in_=maskAB[:, :, :]), [], ['MSK'])
        S.dma('sp', lambda e: e.dma_start(out=CST[:, :, :, :], in_=cs[:, :, :, :]), [], ['CST'])
        S.dma('sp', lambda e: e.dma_start(out=RM[:, :], in_=rmask[:, :]), [], ['RM'])
        S.dma('sp', lambda e: e.dma_start(out=CND, in_=cond[:, :]), [], ['CND'])
        S.dma('sp', lambda e: e.dma_start(out=ADB, in_=adab[:, :, :]), [], ['ADB'])
        S.dma('sp', lambda e: e.dma_start(out=NRM, in_=nrm[:, :, :]), [], ['NRM'])
        S.dma('sp', lambda e: e.dma_start(out=LB, in_=lbraw[:, :, :, :]), [], ['LB'])
        S.dma('sp', lambda e: e.dma_start(out=GN, in_=gnorm[:, :]), [], ['GN'])
        S.dma('sp', lambda e: e.dma_start(out=SEL, in_=selm[:, :]), [], ['SEL'])
        S.op('dve', lambda e: e.memset(ONES[:, :], 1.0), [], ['ONES'])
        S.op('dve', lambda e: e.memset(EPST, EPS), [], ['EPST'])

        import os
        _skip = os.environ.get('KSKIP', '').split(',')
        S.op('act', lambda e: e.activation(out=SCB[:, :], in_=CND, func=AF.Silu), ['CND'], ['SCB'])
        scb3 = SCB[:, :].rearrange("p (k c) -> p k c", c=2)
        AWB = [WB[:, 10240:14336].rearrange("p (k n) -> p k n", k=8), WB[:, 14336:18432].rearrange("p (k n) -> p k n", k=8)]
        ada_blocks = [(0, 0, 0), (0, 0, 1), (0, 1, 0), (0, 1, 1)] + [(0, j, hh) for j in range(2, 6) for hh in range(2)] \
            + [(1, j, hh) for j in range(6) for hh in range(2)]
        ada_cnt = [0]

        def ada_block():
            if ada_cnt[0] >= len(ada_blocks):
                return
            i, j, hh = ada_blocks[ada_cnt[0]]
            b = ada_cnt[0] % 2
            ada_cnt[0] += 1
            buf = AWB[b]; key = f"AWB{b}"
            src = ada_w[i, :, j * 1024 + hh * 512:j * 1024 + (hh + 1) * 512].rearrange("(k p) n -> p k n", p=128)
            S.dma('pool', lambda e, buf=buf, src=src: e.dma_start(out=buf, in_=src), [], [key])
            for o4 in range(4):
                oc = hh * 4 + o4
                col = (j * 8 + oc) * 2
                for kc in range(8):
                    S.op('pe', lambda e, buf=buf, o4=o4, kc=kc, col=col, i=i: e.matmul(
                        ps[i][:, col:col + 2], buf[:, kc, o4 * 128:(o4 + 1) * 128], scb3[:, kc, :],
                        start=(kc == 0), stop=(kc == 7)), [key, 'SCB'], [f"ps{i}"])

        def mod_evac(i, j0, j1):
            for c in range(2):
                S.op('dve', lambda e, i=i, c=c, j0=j0, j1=j1: e.tensor_tensor(
                    out=MOD[:, i, j0 * 8:j1 * 8, c], in0=ps[i][:, j0 * 16 + c:j1 * 16:2], in1=ADB[:, i, j0 * 8:j1 * 8], op=ALU.add),
                    [f"ps{i}", 'ADB'], [f"MOD{i}_{j0}"])

        def am_derive(n):
            i = n // 2; j = 1 if n % 2 == 0 else 4
            for c in range(2):
                S.op('dve', lambda e, n=n, i=i, j=j, c=c: e.scalar_tensor_tensor(
                    out=AM[:, n, :, c], in0=MOD[:, i, j * 8:(j + 1) * 8, c], scalar=1.0, in1=NRM[:, n, :],
                    op0=ALU.add, op1=ALU.mult), [f"MOD{i}_0", f"MOD{i}_2", 'NRM'], [f"AM{n}"])

        for _ in range(4):
            ada_block()
        mod_evac(0, 0, 2)
        am_derive(0)

        ALLX = [f"xT{k_}_{j_}" for k_ in range(8) for j_ in range(NT)]
        ALLH = [f"hT{k_}_{j_}" for k_ in range(8) for j_ in range(NT)]

        def modv(i, j, kc, c):
            return MOD[:, i, j * 8 + kc, c:c + 1]

        if 'fnet' not in _skip:
            ABC = WB[:, 6144:10240].rearrange("p (a n) -> p a n", a=4)
            DV = MISC[:, 4096:4608].bitcast(BF16).rearrange("p (k j) -> p k j", k=8)
            for v in range(4):
                c = v % 2
                vec = AM[:, 0, :, c] if v < 2 else MOD[:, 0, 0:8, c]
                S.op('dve', lambda e, vec=vec: e.tensor_tensor(
                    out=DV, in0=IDB[:, :].unsqueeze(1).to_broadcast([128, 8, 128]), in1=vec.unsqueeze(2).to_broadcast([128, 8, 128]), op=ALU.mult),
                    ['IDB', 'AM0', 'MOD0_0'], ['JUNK'])
                for hh in range(2):
                    S.op('pe', lambda e, hh=hh: e.matmul(ps[2 + hh][:, :], ONES[:, :], DV[:, hh * 4:(hh + 1) * 4, :], start=True, stop=True),
                         ['JUNK', 'ONES'], [f"ps{2 + hh}"])
                    if hh == 0:
                        S.op('act', lambda e, v=v, hh=hh: e.activation(out=ABC[:, v, hh * 512:(hh + 1) * 512], in_=ps[2 + hh][:, :], func=AF.Copy),
                             [f"ps{2 + hh}"], ['ABC'])
                    else:
                        S.op('dve', lambda e, v=v, hh=hh: e.tensor_copy(out=ABC[:, v, hh * 512:(hh + 1) * 512], in_=ps[2 + hh][:, :]),
                             [f"ps{2 + hh}"], ['ABC'])

            hn = BIGb[:, 0:36864].rearrange("p (c n) -> p c n", c=36)
            XIN = [MISC[:, 0:1024], MISC[:, 1024:2048]]
            PIN = [MISC[:, 2048:3072], MISC[:, 2048:3072]]
            TMPF = MISC[:, 3072:4096]
            JUNK = MISC[:, 4096:4608].bitcast(BF16)
            RB = MISC[:, 4608:5120]
            SQB = [MISC[:, 5120:5376].bitcast(BF16), MISC[:, 5376:5632].bitcast(BF16)]
            RL = [MISC[:, 5632:5888].bitcast(BF16), MISC[:, 5888:6144].bitcast(BF16)]
            XR = [XIN[0], XIN[1], WB[:, 0:2048].bitcast(F32)]
            PR = [PIN[0], WB[:, 2048:4096].bitcast(F32)]
            TR = [TMPF, WB[:, 4096:6144].bitcast(F32)]
            for ci in range(36):
                b = ci % 3; pb_ = ci % 2; tb_ = ci % 2
                c = 0 if ci < 32 else 1
                xk = f"XR{b}"; pk = f"PR{pb_}"; tk = f"TR{tb_}"
                if ci < 32:
                    for hf in range(2):
                        S.dma('sp', lambda e, ci=ci, b=b, hf=hf: e.dma_start(
                            out=XR[b][hf * 64:(hf + 1) * 64, :],
                            in_=x_all[0:4096, :].rearrange("(t1 r) c -> r t1 c", r=64)[32 * hf + ci]), [], [xk])
                        S.dma('sp', lambda e, ci=ci, pb_=pb_, hf=hf: e.dma_start(
                            out=PR[pb_][hf * 64:(hf + 1) * 64, :],
                            in_=pos_all[0:4096, :].rearrange("(t1 r) c -> r t1 c", r=64)[32 * hf + ci]), [], [pk])
                    S.op('pool', lambda e, b=b, pb_=pb_: e.tensor_tensor(out=XR[b][:, :], in0=XR[b][:, :], in1=PR[pb_][:, :], op=ALU.add),
                         [xk, pk], [xk])
                else:
                    S.dma('sp', lambda e, ci=ci, b=b: e.dma_start(out=XR[b][:, :], in_=x_all[ci * 128:(ci + 1) * 128, :]), [], [xk])
                S.op('act', lambda e, ci=ci, b=b: e.activation(out=JUNK[:, :], in_=XR[b][:, :], func=AF.Square,
                                                               accum_out=SSQ[:, ci:ci + 1]), [xk], ['JUNK', f"SSQ{ci}"])
                S.op('act', lambda e, ci=ci: e.activation(out=RSTD[:, ci:ci + 1], in_=SSQ[:, ci:ci + 1], func=AF.Sqrt,
                                                          bias=EPST, scale=1.0 / D), [f"SSQ{ci}", 'EPST'], [f"RSTD{ci}"])
                S.op('dve', lambda e, ci=ci: e.reciprocal(out=RSTD[:, ci:ci + 1], in_=RSTD[:, ci:ci + 1]),
                     [f"RSTD{ci}"], [f"RSTD{ci}"])
                S.op('dve', lambda e, ci=ci, b=b, c=c, tb_=tb_: e.scalar_tensor_tensor(
                    out=TR[tb_][:, :], in0=XR[b][:, :], scalar=RSTD[:, ci:ci + 1], in1=ABC[:, c, :],
                    op0=ALU.mult, op1=ALU.mult), [xk, f"RSTD{ci}", 'ABC'], [tk])
                S.op('dve', lambda e, ci=ci, c=c, tb_=tb_: e.tensor_tensor(out=hn[:, ci, :], in0=TR[tb_][:, :], in1=ABC[:, 2 + c, :], op=ALU.add),
                     [tk, 'ABC'], [f"hn{ci}"])
                ada_block()
            while ada_cnt[0] < len(ada_blocks):
                ada_block()
            mod_evac(0, 2, 6)
            mod_evac(1, 0, 2)
            mod_evac(1, 2, 6)
            am_derive(1); am_derive(2); am_derive(3)
            S.set_barrier()

            ZST = [WB[:, z * 2048:(z + 1) * 2048].rearrange("p (r c) -> p r c", r=2) for z in range(8)]
            MIXF = HB[:, 4096:20480].rearrange("p (k t) -> p k t", k=8)
            MIXP = HB[:, 0:4096].rearrange("p (k t) -> p k t", k=8)
            YTGS = [[WB[:, (2 * y_ + r_) * 4096:(2 * y_ + r_ + 1) * 4096].rearrange("p (k t) -> p k t", k=2) for r_ in range(2)] for y_ in range(2)]
            YTP = [WB[:, 0:4096].rearrange("p (k t) -> p k t", k=8), WB[:, 4096:8192].rearrange("p (k t) -> p k t", k=8)]
            FTP = [WB[:, 8192:9216].rearrange("p (c n) -> p c n", c=2), WB[:, 9216:10240].rearrange("p (c n) -> p c n", c=2)]
            WO = WB[:, 10240:18432].rearrange("p (k n) -> p k n", k=8)
            Z1T = [BIGb[:, 0:16384].rearrange("p (f c) -> p f c", f=64), BIGb[:, 16384:32768].rearrange("p (f c) -> p f c", f=64)]
            XTILES = [BIG[:, 0:4096].rearrange("p (k t) -> p k t", k=8), BIG[:, 4096:8192].rearrange("p (k t) -> p k t", k=8)]
            S.dma('sp', lambda e: e.dma_start(out=DAT[:, :, :], in_=da_in[:, :, :]), [], ['DAT'])
            S.dma('sp', lambda e: e.dma_start(out=MCT[:, :, :], in_=mc_in[:, :, :]), [], ['MCT'])
            for t2 in range(32):
                zb = t2 % 8
                for ri in range(2):
                    for ch in range(2):
                        bank = (t2 % 2) * 4 + ri * 2 + ch
                        S.op('pe', lambda e, t2=t2, ri=ri, ch=ch, bank=bank: e.matmul(
                            ps[bank][:, :], DAT[:, ri, :], hn[:, t2, ch * 512:(ch + 1) * 512], start=True, stop=True),
                            ['DAT', f"hn{t2}"], [f"ps{bank}"])
                        if (ri + ch) % 2 == 0:
                            S.op('act', lambda e, zb=zb, ri=ri, ch=ch, bank=bank: e.activation(
                                out=ZST[zb][:, ri, ch * 512:(ch + 1) * 512], in_=ps[bank][:, :], func=AF.Copy), [f"ps{bank}"], [f"ZST{zb}_{ri}_{ch}"])
                        else:
                            S.op('dve', lambda e, zb=zb, ri=ri, ch=ch, bank=bank: e.tensor_copy(
                                out=ZST[zb][:, ri, ch * 512:(ch + 1) * 512], in_=ps[bank][:, :]), [f"ps{bank}"], [f"ZST{zb}_{ri}_{ch}"])
                for hf in range(2):
                    S.dma('sp' if hf == 0 else 'pool', lambda e, t2=t2, zb=zb, hf=hf: e.dma_start(
                        out=zs[:, 32 * hf + t2, :, :], in_=ZST[zb][hf * 64:(hf + 1) * 64, :, :]),
                        [f"ZST{zb}_{r_}_{c_}" for r_ in range(2) for c_ in range(2)], [f"zs{t2}_{hf}"])
            S.set_barrier()
            for g in range(4):
                zb = g % 2
                YTG = YTGS[g % 2]; yk = f"YTG{g % 2}"
                for ri in range(2):
                    q_ = 'sp' if ri == 0 else 'act'
                    S.dma(q_, lambda e, g=g, zb=zb, ri=ri: e.dma_start(
                        out=Z1T[zb][ri * 64:(ri + 1) * 64, :, :],
                        in_=zs[:, :, ri, g * 256:(g + 1) * 256].rearrange("f t c -> t f c")), [], [f"Z1T{zb}"])
                for cl in range(2):
                    for a8 in range(8):
                        bank = a8
                        for f1l in range(8):
                            f1s = a8 * 8 + f1l
                            S.op('pe', lambda e, zb=zb, cl=cl, bank=bank, f1l=f1l, f1s=f1s: e.matmul(
                                ps[bank][:, f1l * 64:(f1l + 1) * 64], Z1T[zb][:, f1s, cl * 128:(cl + 1) * 128], MCT[:, f1s, :],
                                start=True, stop=True), [f"Z1T{zb}", 'MCT'], [f"ps{bank}"])
                        for ri in range(2):
                            src = ps[bank][:, :].rearrange("p (f r t) -> p f r t", f=8, r=2)[:, :, ri, :]
                            dst = YTG[ri][:, cl, :].rearrange("p (t f) -> p f t", f=64)[:, a8 * 8:(a8 + 1) * 8, :]
                            if a8 % 2 == 0:
                                S.op('act', lambda e, src=src, dst=dst: e.activation(out=dst, in_=src, func=AF.Copy), [f"ps{bank}"], [f"{yk}_{ri}_{cl}_{a8}"])
                            else:
                                S.op('dve', lambda e, src=src, dst=dst: e.tensor_copy(out=dst, in_=src), [f"ps{bank}"], [f"{yk}_{ri}_{cl}_{a8}"])
                for half in range(2):
                    oc = 2 * g + half
                    for jt in range(4):
                        bank = (half * 4 + jt) % 8
                        n = 0
                        for kc in range(2):
                            for ri in range(2):
                                S.op('pe', lambda e, half=half, kc=kc, ri=ri, jt=jt, n=n, bank=bank, YTG=YTG: e.matmul(
                                    ps[bank][:, :], CST[:, kc, ri, half * 128:(half + 1) * 128], YTG[ri][:, kc, jt * 512:(jt + 1) * 512],
                                    start=(n == 0), stop=(n == 3)), ['CST'] + [f"{yk}_{ri}_{kc}_{a}" for a in range(8)], [f"ps{bank}"])
                                n += 1
                        if jt % 2 == 0:
                            S.op('act', lambda e, oc=oc, jt=jt, bank=bank: e.activation(out=MIXF[:, oc, jt * 512:(jt + 1) * 512], in_=ps[bank][:, :], func=AF.Copy),
                                 [f"ps{bank}"], [f"MIXF{oc}_{jt}"])
                        else:
                            S.op('dve', lambda e, oc=oc, jt=jt, bank=bank: e.tensor_copy(out=MIXF[:, oc, jt * 512:(jt + 1) * 512], in_=ps[bank][:, :]),
                                 [f"ps{bank}"], [f"MIXF{oc}_{jt}"])
            S.set_barrier()
            S.dma('pool', lambda e: e.dma_start(out=WO, in_=fnet_wo[:, :].rearrange("(k p) n -> p k n", p=128)), [], ['WO'])
            ftc = [0]
            for sq in range(2):
                fb = ftc[0] % 2; ftc[0] += 1
                S.dma('sp', lambda e, fb=fb: e.dma_start(out=FTP[fb], in_=flp[:, :, :].rearrange("c p n -> p c n")), [], [f"FTP{fb}"])
                for tc in range(2):
                    for cc in range(8):
                        S.op('pe', lambda e, fb=fb, tc=tc, cc=cc, sq=sq: e.matmul(
                            ps[cc][:, :], hn[:, 32 + 2 * sq + tc, cc * 128:(cc + 1) * 128], FTP[fb][:, tc, :],
                            start=(tc == 0), stop=(tc == 1)), [f"FTP{fb}", f"hn{32 + 2 * sq + tc}"], [f"ps{cc}"])
                for cc in range(8):
                    for ri in range(2):
                        if cc % 2 == 0:
                            S.op('act', lambda e, cc=cc, ri=ri, sq=sq: e.activation(
                                out=YTP[ri][:, cc, sq * 256:(sq + 1) * 256], in_=ps[cc][:, ri * 256:(ri + 1) * 256], func=AF.Copy),
                                [f"ps{cc}"], [f"YTP{ri}"])
                        else:
                            S.op('dve', lambda e, cc=cc, ri=ri, sq=sq: e.tensor_copy(
                                out=YTP[ri][:, cc, sq * 256:(sq + 1) * 256], in_=ps[cc][:, ri * 256:(ri + 1) * 256]),
                                [f"ps{cc}"], [f"YTP{ri}"])
            for oc in range(8):
                g = oc // 2; half = oc % 2
                n = 0
                for kc in range(2):
                    for ri in range(2):
                        S.op('pe', lambda e, oc=oc, g=g, half=half, kc=kc, ri=ri, n=n: e.matmul(
                            ps[oc][:, :], CST[:, kc, ri, half * 128:(half + 1) * 128], YTP[ri][:, g * 2 + kc, :],
                            start=(n == 0), stop=(n == 3)), ['CST', f"YTP{ri}"], [f"ps{oc}"])
                        n += 1
                if oc % 2 == 0:
                    S.op('act', lambda e, oc=oc: e.activation(out=MIXP[:, oc, :], in_=ps[oc][:, :], func=AF.Copy), [f"ps{oc}"], [f"MIXF{oc}_4"])
                else:
                    S.op('dve', lambda e, oc=oc: e.tensor_copy(out=MIXP[:, oc, :], in_=ps[oc][:, :]), [f"ps{oc}"], [f"MIXF{oc}_4"])
            S.set_barrier()
            XQ = [XIN[0], XIN[1], WB[:, 0:2048].bitcast(F32), WB[:, 2048:4096].bitcast(F32)]
            PQ = [PIN[0], WB[:, 4096:6144].bitcast(F32), WB[:, 6144:8192].bitcast(F32)]
            xq_i = [0]
            for jt in range(NT):
                XT = XTILES[jt % 2]; xtk = f"XTILE{jt % 2}"
                for q4 in range(4):
                    b = xq_i[0] % 4; pb_ = xq_i[0] % 3; xq_i[0] += 1
                    r0 = jt * 512 + q4 * 128
                    S.dma('sp', lambda e, b=b, r0=r0: e.dma_start(out=XQ[b][:, :], in_=x_my[r0:r0 + 128, :]), [], [f"XQ{b}"])
                    if jt < 4:
                        S.dma('sp', lambda e, pb_=pb_, r0=r0: e.dma_start(out=PQ[pb_][:, :], in_=pos_my[r0:r0 + 128, :]), [], [f"PQ{pb_}"])
                        S.op('pool', lambda e, b=b, pb_=pb_: e.tensor_tensor(out=XQ[b][:, :], in0=XQ[b][:, :], in1=PQ[pb_][:, :], op=ALU.add),
                             [f"XQ{b}", f"PQ{pb_}"], [f"XQ{b}"])
                    for cc in range(8):
                        S.op('pe', lambda e, b=b, cc=cc, q4=q4: e.transpose(
                            ps[cc][:, q4 * 128:(q4 + 1) * 128], XQ[b][:, cc * 128:(cc + 1) * 128], IDF[:, :]),
                            [f"XQ{b}", 'IDF'], [f"ps{cc}"])
                for cc in range(8):
                    if cc % 2 == 0:
                        S.op('act', lambda e, cc=cc, XT=XT: e.activation(out=XT[:, cc, :], in_=ps[cc][:, :], func=AF.Copy), [f"ps{cc}"], [f"{xtk}_{cc}"])
                    else:
                        S.op('dve', lambda e, cc=cc, XT=XT: e.tensor_copy(out=XT[:, cc, :], in_=ps[cc][:, :]), [f"ps{cc}"], [f"{xtk}_{cc}"])
                c = 0 if jt < 4 else 1
                for oc in range(8):
                    for kc in range(8):
                        rhs = MIXF[:, kc, jt * 512:(jt + 1) * 512] if jt < 4 else MIXP[:, kc, :]
                        S.op('pe', lambda e, oc=oc, kc=kc, rhs=rhs: e.matmul(ps[oc][:, :], WO[:, kc, oc * 128:(oc + 1) * 128], rhs,
                                                                              start=(kc == 0), stop=(kc == 7)), ['WO', f"MIXF{kc}_{jt}"], [f"ps{oc}"])
                    S.op('dve', lambda e, oc=oc, c=c, XT=XT: e.scalar_tensor_tensor(
                        out=XT[:, oc, :], in0=ps[oc][:, :], scalar=modv(0, 2, oc, c), in1=XT[:, oc, :],
                        op0=ALU.mult, op1=ALU.add), [f"ps{oc}", f"{xtk}_{oc}", 'MOD0_2'], [f"{xtk}_{oc}"])
                S.dma('sp', lambda e, jt=jt, XT=XT: e.dma_start(
                    out=scr_x[:, :, jt * 512:(jt + 1) * 512].rearrange("k p t -> p k t"), in_=XT[:, :, :]),
                    [f"{xtk}_{c_}" for c_ in range(8)], [f"scrx{jt}"])

        S.set_barrier()
        for jt_ in range(NT):
            S.dma('sp' if jt_ % 2 == 0 else 'act', lambda e, jt_=jt_: e.dma_start(
                out=xT[:, :, jt_ * 512:(jt_ + 1) * 512], in_=scr_x[:, :, jt_ * 512:(jt_ + 1) * 512].rearrange("k p t -> p k t")),
                [], [f"xT{k_}_{jt_}" for k_ in range(8)])
        if stage < 2:
            S.set_barrier()


        def norm_mod(n, i, jsh):
            TM2 = [TMPF[:, 0:512], TMPF[:, 512:1024]]
            for jt in range(NT):
                c = 0 if jt < 4 else 1
                for kc in range(8):
                    b = kc % 2
                    S.op('act', lambda e, kc=kc, jt=jt, b=b: e.activation(
                        out=SQB[b][:, :], in_=xT[:, kc, jt * 512:(jt + 1) * 512], func=AF.Square), [f"xT{kc}_{jt}"], [f"SQB{b}"])
                    S.op('pe', lambda e, kc=kc, b=b: e.matmul(ps[7][:, :], ONES[:, :], SQB[b][:, :], start=(kc == 0), stop=(kc == 7)),
                         [f"SQB{b}", 'ONES'], ['ps7'])
                S.op('act', lambda e: e.activation(out=RB[:, :], in_=ps[7][:, :], func=AF.Ln, bias=EPST, scale=1.0 / D),
                     ['ps7', 'EPST'], ['RB'])
                S.op('act', lambda e: e.activation(out=RB[:, :], in_=RB[:, :], func=AF.Exp, scale=-0.5), ['RB'], ['RB'])
                for kc in range(8):
                    tb_ = kc % 2
                    S.op('dve', lambda e, kc=kc, jt=jt, c=c, tb_=tb_: e.scalar_tensor_tensor(
                        out=TM2[tb_], in0=xT[:, kc, jt * 512:(jt + 1) * 512], scalar=AM[:, n, kc, c:c + 1], in1=RB[:, :],
                        op0=ALU.mult, op1=ALU.mult), [f"xT{kc}_{jt}", f"AM{n}", 'RB'], [f"TM2{tb_}"])
                    S.op('act', lambda e, kc=kc, jt=jt, c=c, tb_=tb_: e.activation(
                        out=hT[:, kc, jt * 512:(jt + 1) * 512], in_=TM2[tb_], func=AF.Identity,
                        bias=modv(i, jsh, kc, c), scale=1.0), [f"TM2{tb_}", f"MOD{i}_0", f"MOD{i}_2"], [f"hT{kc}_{jt}"])

        W1B = [WB[:, k * 4096:(k + 1) * 4096].rearrange("p (k n) -> p k n", k=8) for k in range(2)]
        W2B = [WB[:, 8192 + k * 4096:8192 + (k + 1) * 4096].rearrange("p (k n) -> p k n", k=4) for k in range(2)]
        UT = [MISC[:, 0:1024].bitcast(BF16).rearrange("p (k n) -> p k n", k=4), MISC[:, 1024:2048].bitcast(BF16).rearrange("p (k n) -> p k n", k=4)]
        wcnt = [0]

        def mlp(i):
            jg = 5
            its = [(hp, jt) for hp in range(8) for jt in range(NT)]
            wbs = {}

            def load_w(hp):
                wb = wcnt[0] % 2; wcnt[0] += 1
                wbs[hp] = wb
                S.dma('pool', lambda e, wb=wb, hp=hp: e.dma_start(
                    out=W1B[wb], in_=w1[i, :, hp * 512:(hp + 1) * 512].rearrange("(k p) n -> p k n", p=128)), [], [f"W1B{wb}"])
                S.dma('pool', lambda e, wb=wb, hp=hp: e.dma_start(
                    out=W2B[wb], in_=w2[i, hp * 512:(hp + 1) * 512, :].rearrange("(k p) n -> p k n", p=128)), [], [f"W2B{wb}", 'WO'])

            def stage1(k):
                hp, jt = its[k]; wb = wbs[hp]; ub = k % 2
                for fc in range(4):
                    pb = fc % 4
                    for kc in range(8):
                        S.op('pe', lambda e, wb=wb, fc=fc, kc=kc, jt=jt, pb=pb: e.matmul(
                            ps[pb][:, :], W1B[wb][:, kc, fc * 128:(fc + 1) * 128], hT[:, kc, jt * 512:(jt + 1) * 512],
                            start=(kc == 0), stop=(kc == 7)), [f"W1B{wb}", f"hT{kc}_{jt}"], [f"ps{pb}"])
                    rb = fc % 2
                    S.op('act', lambda e, pb=pb, rb=rb: e.activation(out=RL[rb][:, :], in_=ps[pb][:, :], func=AF.Relu),
                         [f"ps{pb}"], [f"RL{rb}"])
                    S.op('pool', lambda e, rb=rb, ub=ub, fc=fc: e.tensor_tensor(
                        out=UT[ub][:, fc, :], in0=RL[rb][:, :], in1=RL[rb][:, :], op=ALU.mult), [f"RL{rb}"], [f"UT{ub}_{fc}"])

            def stage2(k):
                hp, jt = its[k]; wb = wbs[hp]; ub = k % 2
                c = 0 if jt < 4 else 1
                for oc in range(8):
                    pb = 4 + oc % 4
                    for fc in range(4):
                        S.op('pe', lambda e, wb=wb, fc=fc, oc=oc, ub=ub, pb=pb: e.matmul(
                            ps[pb][:, :], W2B[wb][:, fc, oc * 128:(oc + 1) * 128], UT[ub][:, fc, :],
                            start=(fc == 0), stop=(fc == 3)), [f"W2B{wb}", f"UT{ub}_{fc}"], [f"ps{pb}"])
                    S.op('dve', lambda e, oc=oc, jt=jt, c=c, pb=pb: e.scalar_tensor_tensor(
                        out=xT[:, oc, jt * 512:(jt + 1) * 512], in0=ps[pb][:, :], scalar=modv(i, jg, oc, c),
                        in1=xT[:, oc, jt * 512:(jt + 1) * 512], op0=ALU.mult, op1=ALU.add),
                        [f"ps{pb}", f"xT{oc}_{jt}", 'MOD0_2', 'MOD1_2'], [f"xT{oc}_{jt}"])

            load_w(0)
            for k in range(len(its)):
                hp, jt = its[k]
                if jt == 0 and hp + 1 < 8:
                    load_w(hp + 1)
                if k == 0:
                    stage1(0)
                if k + 1 < len(its):
                    stage1(k + 1)
                stage2(k)

        if stage >= 2:
            norm_mod(1, 0, 3)
            mlp(0)
            if stage < 3:
                S.set_barrier()


        def hgrn():
            i = 1
            norm_mod(2, 1, 0)
            for jt_ in range(NT):
                S.dma('sp', lambda e, jt_=jt_: e.dma_start(
                    out=scr_x[:, :, jt_ * 512:(jt_ + 1) * 512].rearrange("k p t -> p k t"), in_=xT[:, :, jt_ * 512:(jt_ + 1) * 512]),
                    [f"xT{k_}_{jt_}" for k_ in range(8)], [f"scrx_h{jt_}"])
            S.set_barrier()
            S.op('dve', lambda e: e.tensor_tensor(out=LBV, in0=LB[:, :, 1, :], in1=LB[:, :, 0, :], op=ALU.subtract), ['LB'], ['LBV'])
            S.op('act', lambda e: e.activation(out=LBV, in_=LBV, func=AF.Sigmoid), ['LBV'], ['LBV'])
            S.op('dve', lambda e: e.tensor_scalar(out=OML, in0=LBV, scalar1=-1.0, scalar2=1.0, op0=ALU.mult, op1=ALU.add), ['LBV'], ['OML'])
            S.dma('pool', lambda e: e.dma_start(out=WO, in_=hgrn_wo[:, :].rearrange("(k p) n -> p k n", p=128)), [], ['WO'])
            WH = [WB[:, w * 5120:(w + 1) * 5120].rearrange("p (k j n) -> p k j n", k=8, j=5) for w in range(2)]
            QE = [BIGb[:, 0:2560], BIGb[:, 2560:5120]]
            KE = [BIGb[:, 5120:7680], BIGb[:, 7680:10240]]
            KDT = [BIGb[:, 10240:12800].rearrange("p (b k) -> p b k", b=20), BIGb[:, 12800:15360].rearrange("p (b k) -> p b k", b=20)]
            VT = BIGb[:, 15360:17920].rearrange("p (b k) -> p b k", b=20)
            GS = BIGb[:, 17920:20480]
            OA = BIGb[:, 20480:23040]
            def wt(n):
                return BIG[:, 12288 + n * 512:12288 + (n + 1) * 512]
            QS = wt(0)
            SG = [wt(1), wt(2)]; FT_ = [wt(3), wt(4)]; LG = [wt(5), wt(6)]; KT = [wt(7), wt(8)]
            BB = [wt(9), wt(10)]; E1 = [wt(11), wt(12)]; E2 = [wt(13), wt(14)]
            SST = [MISC[:, 0:128], MISC[:, 3584:3712], MISC[:, 3712:3840], MISC[:, 3840:3968]]
            SR = MISC[:, 128:384].rearrange("p (r n) -> p r n", r=2)
            OS = MISC[:, 384:896]; Ro = MISC[:, 896:1408]; T1 = MISC[:, 1408:1920]
            OSB = [MISC[:, 384:896], MISC[:, 4096:4608]]
            mb = MISC[:, 1920:3584].bitcast(BF16)
            SBR = [mb[:, r * 128:(r + 1) * 128] for r in range(8)]
            SCb = [mb[:, 1024:1152], mb[:, 1152:1280]]
            SQo = mb[:, 1280:1792]; OGh = mb[:, 1792:2304]
            KDt = [mb[:, 2304:2816], mb[:, 2816:3328]]
            psb5 = ps[6][:, :].bitcast(BF16)
            sbc = [0]
            def load_wh(h_):
                wh_ = WH[h_ % 2]
                for j in range(5):
                    S.dma('pool', lambda e, wh_=wh_, j=j, h_=h_: e.dma_start(
                        out=wh_[:, :, j, :], in_=w_in[:, j * 1024 + h_ * 128:j * 1024 + (h_ + 1) * 128].rearrange("(k p) n -> p k n", p=128)),
                        [], [f"WH{h_ % 2}"])
            load_wh(0)
            for h in range(8):
                wh = WH[h % 2]; whk = f"WH{h % 2}"
                if h + 1 < 8:
                    load_wh(h + 1)
                def elementwise(jt):
                    tsl = slice(jt * 512, (jt + 1) * 512)
                    def proj(pj, bank):
                        for kc in range(8):
                            S.op('pe', lambda e, pj=pj, bank=bank, kc=kc, wh=wh, tsl=tsl: e.matmul(
                                ps[bank][:, :], wh[:, kc, pj, :], hT[:, kc, tsl], start=(kc == 0), stop=(kc == 7)),
                                [whk, f"hT{kc}_{jt}"], [f"ps{bank}"])
                    proj(0, 4)
                    S.op('act', lambda e: e.activation(out=QS, in_=ps[4][:, :], func=AF.Silu), ['ps4'], ['QS'])
                    proj(1, 6)
                    S.op('act', lambda e: e.activation(out=SG[0], in_=ps[6][:, :], func=AF.Sigmoid), ['ps6'], ['SG0'])
                    proj(2, 4)
                    S.op('act', lambda e: e.activation(out=SG[1], in_=ps[4][:, :], func=AF.Sigmoid), ['ps4'], ['SG1'])
                    proj(4, 6)
                    S.op('act', lambda e, tsl=tsl: e.activation(out=GS[:, tsl], in_=ps[6][:, :], func=AF.Silu), ['ps6'], [f"GS{jt}"])
                    for q4 in range(4):
                        blk = jt * 4 + q4
                        for kc in range(8):
                            S.op('pe', lambda e, kc=kc, q4=q4, blk=blk, wh=wh: e.matmul(
                                ps[4][:, q4 * 128:(q4 + 1) * 128], hT[:, kc, blk * 128:(blk + 1) * 128], wh[:, kc, 3, :],
                                start=(kc == 0), stop=(kc == 7)), [whk, f"hT{kc}_{jt}"], ['ps4'])
                    S.op('dve', lambda e, jt=jt: e.tensor_copy(out=VT[:, jt * 4:(jt + 1) * 4, :], in_=ps[4][:, :].rearrange("p (b k) -> p b k", b=4)),
                         ['ps4'], [f"VT{jt}"])
                    def estep(k, d):
                        lastcol = 31 if d == 0 else 0
                        if k == 0:
                            S.op('dve', lambda e, d=d, h=h: e.tensor_scalar(out=FT_[d], in0=SG[d], scalar1=OML[:, d, h:h + 1], scalar2=LBV[:, d, h:h + 1],
                                                                            op0=ALU.mult, op1=ALU.add), [f"SG{d}", 'OML', 'LBV'], [f"F{d}"])
                        elif k == 1:
                            S.op('act', lambda e, d=d: e.activation(out=LG[d], in_=FT_[d], func=AF.Ln), [f"F{d}"], [f"LG{d}"])
                            S.op('pool', lambda e, d=d: e.tensor_scalar(out=KT[d], in0=FT_[d], scalar1=-1.0, scalar2=1.0, op0=ALU.mult, op1=ALU.add),
                                 [f"F{d}"], [f"KT{d}"])
                        elif k == 2:
                            if d == 0:
                                S.op('dve', lambda e, d=d: e.tensor_tensor_scan(out=BB[d], data0=RM[:, :], data1=LG[d], initial=0.0, op0=ALU.mult, op1=ALU.add),
                                     ['RM', f"LG{d}"], [f"BB{d}"])
                            else:
                                S.op('dve', lambda e, d=d: e.tensor_tensor_scan(out=BB[d][:, ::-1], data0=RM[:, :], data1=LG[d][:, ::-1], initial=0.0,
                                                                                op0=ALU.mult, op1=ALU.add), ['RM', f"LG{d}"], [f"BB{d}"])
                        elif k == 3:
                            S.op('act', lambda e, d=d: e.activation(out=E1[d], in_=BB[d], func=AF.Exp), [f"BB{d}"], [f"E1{d}"])
                            S.op('act', lambda e, d=d: e.activation(out=E2[d], in_=BB[d], func=AF.Exp, scale=-1.0), [f"BB{d}"], [f"E2{d}"])
                        elif k == 4:
                            S.op('dve', lambda e, d=d, tsl=tsl: e.tensor_tensor(out=QE[d][:, tsl], in0=QS, in1=E1[d], op=ALU.mult),
                                 ['QS', f"E1{d}"], [f"QE{d}_{jt}"])
                            S.op('pool', lambda e, d=d, jt=jt, lastcol=lastcol: e.tensor_copy(out=ACOL[:, d, jt * 16:(jt + 1) * 16], in_=E1[d][:, lastcol::32]),
                                 [f"E1{d}"], [f"ACOL{d}_{jt}"])
                            S.op('dve', lambda e, d=d, tsl=tsl: e.tensor_tensor(out=KE[d][:, tsl], in0=KT[d], in1=E2[d], op=ALU.mult),
                                 [f"KT{d}", f"E2{d}"], [f"KE{d}_{jt}"])
                        elif k == 5:
                            S.op('pool', lambda e, d=d, tsl=tsl, jt=jt: e.tensor_tensor(
                                out=KDt[d].rearrange("p (c j) -> p c j", j=32), in0=KE[d][:, tsl].rearrange("p (c j) -> p c j", j=32),
                                in1=ACOL[:, d, jt * 16:(jt + 1) * 16].unsqueeze(2).to_broadcast([128, 16, 32]), op=ALU.mult),
                                [f"KE{d}_{jt}", f"ACOL{d}_{jt}"], [f"KDt{d}"])
                        elif k == 6:
                            for q4 in range(4):
                                S.op('pe', lambda e, d=d, q4=q4: e.transpose(psb5[:, q4 * 128:(q4 + 1) * 128], KDt[d][:, q4 * 128:(q4 + 1) * 128], IDB[:, :]),
                                     [f"KDt{d}", 'IDB'], ['ps6'])
                            S.op('dve', lambda e, d=d, jt=jt: e.tensor_copy(out=KDT[d][:, jt * 4:(jt + 1) * 4, :], in_=psb5[:, 0:512].rearrange("p (b k) -> p b k", b=4)),
                                 ['ps6'], [f"KDT{d}_{jt}"])
                    for k in range(7):
                        for d in range(2):
                            estep(k, d)

                def seg_start(d, init):
                    st = {'d': d, 'cs': 0}
                    S0 = SST[0]
                    if init[0] == 'dram':
                        S.dma('sp', lambda e: e.dma_start(out=S0, in_=init[1]), [], ['Sst0'])
                    elif init[0] == 'zero':
                        S.op('dve', lambda e: e.memset(S0, 0.0), [], ['Sst0'])
                    else:
                        S.op('dve', lambda e: e.tensor_scalar(out=S0, in0=SR[:, 0, :], scalar1=SEL[:, 0:1], scalar2=None, op0=ALU.mult),
                             ['SR', 'SEL'], ['Sst0'])
                        S.op('dve', lambda e: e.scalar_tensor_tensor(out=S0, in0=SR[:, 1, :], scalar=SEL[:, 1:2], in1=S0, op0=ALU.mult, op1=ALU.add),
                             ['SR', 'SEL', 'Sst0'], ['Sst0'])
                    cur = sbc[0] % 8; sbc[0] += 1
                    S.op('act', lambda e, cur=cur: e.activation(out=SBR[cur], in_=S0, func=AF.Copy), ['Sst0'], [f"SBR{cur}"])
                    st['cur'] = cur
                    return st

                def seg_pre(st, blk, parts):
                    d = st['d']
                    bsl = slice(blk * 128, (blk + 1) * 128)
                    sc = blk % 2
                    tj = blk // 4
                    for pt in parts:
                        if pt == 's':
                            S.op('pe', lambda e, d=d, bsl=bsl: e.matmul(ps[7][:, 0:128], KE[d][:, bsl], QE[d][:, bsl], start=True, stop=True),
                                 [f"KE{d}_{tj}", f"QE{d}_{tj}"], ['ps7'])
                        elif pt == 'm':
                            S.op('dve', lambda e, d=d, sc=sc: e.tensor_tensor(out=SCb[sc], in0=ps[7][:, 0:128], in1=MSK[:, d, :], op=ALU.mult),
                                 ['ps7', 'MSK'], [f"SCb{sc}"])
                        elif pt == 'o':
                            S.op('pe', lambda e, blk=blk, sc=sc: e.matmul(ps[5][:, 0:128], VT[:, blk, :], SCb[sc], start=True, stop=False),
                                 [f"VT{tj}", f"SCb{sc}"], ['ps5'])
                        else:
                            c = int(pt[1])
                            kw = dict(tile_position=(96, 0)) if c == 3 else {}
                            S.op('pe', lambda e, d=d, c=c, blk=blk, kw=kw: e.matmul(
                                ps[c][:, 0:128], KDT[d][c * 32:(c + 1) * 32, blk, :], VT[c * 32:(c + 1) * 32, blk, :], start=True, stop=True, **kw),
                                [f"KDT{d}_{tj}", f"VT{tj}"], [f"ps{c}"])

                def seg_block(st, blk, nb=None):
                    d = st['d']
                    bsl = slice(blk * 128, (blk + 1) * 128)
                    po = 5
                    tj = blk // 4
                    if st.get('pre') != blk:
                        seg_pre(st, blk, ['s', 'm', 'u0', 'u1', 'u2', 'u3', 'o'])
                    corder = range(4) if d == 0 else range(3, -1, -1)
                    for n, c in enumerate(corder):
                        cur = st['cur']; cs = st['cs']
                        S.op('pe', lambda e, d=d, c=c, blk=blk, po=po, cur=cur, n=n: e.matmul(
                            ps[po][:, c * 32:(c + 1) * 32], SBR[cur], QE[d][:, blk * 128 + c * 32:blk * 128 + (c + 1) * 32],
                            start=False, stop=(n == 3)), [f"SBR{cur}", f"QE{d}_{tj}"], [f"ps{po}"])
                        ns = (cs + 1) % 4
                        S.op('dve', lambda e, d=d, c=c, blk=blk, cs=cs, ns=ns: e.scalar_tensor_tensor(
                            out=SST[ns], in0=SST[cs], scalar=ACOL[:, d, blk * 4 + c:blk * 4 + c + 1], in1=ps[c][:, 0:128], op0=ALU.mult, op1=ALU.add),
                            [f"Sst{cs}", f"ACOL{d}_{tj}", f"ps{c}"], [f"Sst{ns}"])
                        st['cs'] = ns
                        cur = sbc[0] % 8; sbc[0] += 1
                        st['cur'] = cur
                        if n % 2 == 0:
                            S.op('act', lambda e, cur=cur, ns=ns: e.activation(out=SBR[cur], in_=SST[ns], func=AF.Copy), [f"Sst{ns}"], [f"SBR{cur}"])
                        else:
                            S.op('pool', lambda e, cur=cur, ns=ns: e.tensor_copy(out=SBR[cur], in_=SST[ns]), [f"Sst{ns}"], [f"SBR{cur}"])
                        if nb is not None:
                            seg_pre(st, nb, [f"u{c}"])
                            if n == 0:
                                seg_pre(st, nb, ['s'])
                            if n == 3:
                                seg_pre(st, nb, ['m'])
                    if d == 0:
                        S.op('act', lambda e, bsl=bsl, po=po: e.activation(out=OA[:, bsl], in_=ps[po][:, 0:128], func=AF.Copy), [f"ps{po}"], [f"OA{blk}"])
                    else:
                        q4 = blk % 4
                        ob = (blk // 4) % 2
                        OSx = OSB[ob]
                        S.op('dve', lambda e, bsl=bsl, po=po, q4=q4, OSx=OSx: e.tensor_tensor(
                            out=OSx[:, q4 * 128:(q4 + 1) * 128], in0=ps[po][:, 0:128], in1=OA[:, bsl], op=ALU.add), [f"ps{po}", f"OA{blk}"], [f"OS{ob}_{q4}"])
                    if nb is not None:
                        seg_pre(st, nb, ['o'])
                        st['pre'] = nb
                    if d == 1:
                        if st.get('pending') is not None and st.get('pend_at') == blk:
                            st['pending'](); st['pending'] = None
                        q4 = blk % 4
                        if q4 == 0:
                            jt = blk // 4
                            tsl = slice(jt * 512, (jt + 1) * 512)
                            ob = jt % 2
                            OSx = OSB[ob]
                            oskeys = [f"OS{ob}_{q_}" for q_ in range(4)]

                            def onorm(jt=jt, tsl=tsl, OSx=OSx, oskeys=oskeys):
                                S.op('act', lambda e: e.activation(out=SQo, in_=OSx, func=AF.Square), oskeys, ['SQo'])
                                S.op('pe', lambda e: e.matmul(ps[6][:, :], ONES[:, :], SQo, start=True, stop=True), ['SQo', 'ONES'], ['ps6'])
                                S.op('act', lambda e: e.activation(out=Ro, in_=ps[6][:, :], func=AF.Ln, bias=EPST, scale=1.0 / 128.0),
                                     ['ps6', 'EPST'], ['Ro'])
                                S.op('act', lambda e: e.activation(out=Ro, in_=Ro, func=AF.Exp, scale=-0.5), ['Ro'], ['Ro'])
                                S.op('dve', lambda e: e.scalar_tensor_tensor(out=T1, in0=OSx, scalar=GN, in1=Ro, op0=ALU.mult, op1=ALU.mult),
                                     oskeys + ['GN', 'Ro'], ['T1'])
                                S.op('pool', lambda e: e.tensor_tensor(out=OGh, in0=T1, in1=GS[:, tsl], op=ALU.mult), ['T1', f"GS{jt}"], ['OGh'])
                                S.dma('sp', lambda e, h=h: e.dma_start(out=scr_og[h, :, tsl], in_=OGh), ['OGh'], [f"scrog{h}_{jt}"])
                            if nb is not None and (nb // 4) != jt and (nb % 4) == 3:
                                st['pending'] = onorm
                                st['pend_at'] = nb - 1
                            else:
                                onorm()

                def seg_end(st, fin):
                    if st.get('pending') is not None:
                        st['pending'](); st['pending'] = None
                    cs = st['cs']
                    if fin[0] == 'state':
                        S.dma('sp', lambda e: e.dma_start(out=fin[1], in_=SST[cs]), [f"Sst{cs}"], [])
                    elif fin[0] == 'cc':
                        S.dma('pool', lambda e, h=h: e.dma_start(out=cc_in[h, :, :], in_=SST[cs]), [f"Sst{cs}"], [f"ccin{h}"])
                        S.special('pool', lambda e, h=h: e.collective_compute(
                            "AllGather", ALU.bypass, replica_groups=[[0, 1], [2, 3], [4, 5], [6, 7]],
                            ins=[cc_in[h, :, :]], outs=[cc_out[h, :, :]]), ('cc', h), [f"ccin{h}"], [f"ccout{h}"])
                        S.dma('sp', lambda e, h=h: e.dma_start(out=SR, in_=cc_out[h, :, :].rearrange("(r p) n -> p r n", p=128)),
                              [f"ccout{h}"], ['SR'])

                def run_segment(d, blocks, init, fin):
                    st = seg_start(d, init)
                    for i_, blk in enumerate(blocks):
                        seg_block(st, blk, blocks[i_ + 1] if i_ + 1 < len(blocks) else None)
                    seg_end(st, fin)

                elementwise(0)
                elementwise(1)
                stA = seg_start(0, ('dram', sA0[:, h, :]))
                for jt in range(4):
                    for q4 in range(4):
                        blk = jt * 4 + q4
                        seg_block(stA, blk, blk + 1 if blk + 1 < 16 else None)
                    if jt == 3:
                        seg_end(stA, ('cc',))
                    if jt + 2 < NT:
                        elementwise(jt + 2)
                run_segment(0, [16, 17], ('zero',), ('state', st_out[0, 0, h, :, :]))
                run_segment(0, [18, 19], ('zero',), ('state', st_out[1, 0, h, :, :]))
                run_segment(1, [19, 18], ('zero',), ('state', st_out[1, 1, h, :, :]))
                run_segment(1, [17, 16], ('zero',), ('state', st_out[0, 1, h, :, :]))
                run_segment(1, list(range(15, -1, -1)), ('sel',), ('none',))
            S.set_barrier()
            OGT = hT
            for jt_ in range(NT):
                S.dma('sp', lambda e, jt_=jt_: e.dma_start(
                    out=OGT[:, :, jt_ * 512:(jt_ + 1) * 512], in_=scr_og[:, :, jt_ * 512:(jt_ + 1) * 512].rearrange("h p t -> p h t")),
                    [], [f"hT{k_}_{jt_}" for k_ in range(8)])
                S.dma('act', lambda e, jt_=jt_: e.dma_start(
                    out=xT[:, :, jt_ * 512:(jt_ + 1) * 512], in_=scr_x[:, :, jt_ * 512:(jt_ + 1) * 512].rearrange("k p t -> p k t")),
                    [], [f"xT{k_}_{jt_}" for k_ in range(8)])
            for jt in range(NT):
                c = 0 if jt < 4 else 1
                tsl = slice(jt * 512, (jt + 1) * 512)
                for oc in range(8):
                    pb = oc % 4
                    for kc in range(8):
                        S.op('pe', lambda e, oc=oc, kc=kc, tsl=tsl, pb=pb: e.matmul(ps[pb][:, :], WO[:, kc, oc * 128:(oc + 1) * 128], OGT[:, kc, tsl],
                                                                                   start=(kc == 0), stop=(kc == 7)), ['WO', f"hT{kc}_{jt}"], [f"ps{pb}"])
                    S.op('dve', lambda e, oc=oc, tsl=tsl, c=c, pb=pb: e.scalar_tensor_tensor(
                        out=xT[:, oc, tsl], in0=ps[pb][:, :], scalar=modv(1, 2, oc, c), in1=xT[:, oc, tsl], op0=ALU.mult, op1=ALU.add),
                        [f"ps{pb}", f"xT{oc}_{jt}", 'MOD0_2', 'MOD1_2'], [f"xT{oc}_{jt}"])
            if stage < 4:
                S.set_barrier()

        if stage >= 3:
            hgrn()
        if stage >= 4:
            norm_mod(3, 1, 3)
            mlp(1)

        OUTB = [WB[:, 16384:18432].bitcast(F32), MISC[:, 3072:4096]]
        NFB = MISC[:, 2048:3072]
        S.dma('sp', lambda e: e.dma_start(out=NFB[:, :], in_=nfin[:, :].partition_broadcast(128)), [], ['NFB'])
        final_norm = (stage >= 4)
        for bk in range(NBLK):
            ob = bk % 2
            pp = (bk % 2) * 4
            for kc in range(8):
                bank = pp + kc // 4
                S.op('pe', lambda e, kc=kc, bk=bk, bank=bank: e.transpose(
                    ps[bank][:, (kc % 4) * 128:(kc % 4 + 1) * 128], xT[:, kc, bk * 128:(bk + 1) * 128], IDF[:, :]),
                    [f"xT{kc}_{bk // 4}", 'IDF'], [f"ps{bank}"])
            for hh in range(2):
                bank = pp + hh
                if not final_norm:
                    S.op('act', lambda e, ob=ob, hh=hh, bank=bank: e.activation(
                        out=OUTB[ob][:, hh * 512:(hh + 1) * 512], in_=ps[bank][:, :], func=AF.Copy), [f"ps{bank}"], [f"OUTB{ob}"])
                else:
                    S.op('act', lambda e, hh=hh, bank=bank, bk=bk: e.activation(
                        out=JUNK[:, 0:512], in_=ps[bank][:, :], func=AF.Square, accum_out=SSQ[:, hh:hh + 1]),
                        [f"ps{bank}"], ['JUNK', f"SSQ{hh}"])
            if final_norm:
                S.op('dve', lambda e: e.tensor_tensor(out=SSQ[:, 2:3], in0=SSQ[:, 0:1], in1=SSQ[:, 1:2], op=ALU.add),
                     ['SSQ0', 'SSQ1'], ['SSQ2'])
                S.op('act', lambda e: e.activation(out=RSTD[:, 0:1], in_=SSQ[:, 2:3], func=AF.Sqrt, bias=EPST, scale=1.0 / D),
                     ['SSQ2', 'EPST'], ['RSTD0'])
                S.op('dve', lambda e: e.reciprocal(out=RSTD[:, 0:1], in_=RSTD[:, 0:1]), ['RSTD0'], ['RSTD0'])
                for hh in range(2):
                    bank = pp + hh
                    S.op('dve', lambda e, ob=ob, hh=hh, bank=bank: e.scalar_tensor_tensor(
                        out=OUTB[ob][:, hh * 512:(hh + 1) * 512], in0=ps[bank][:, :], scalar=RSTD[:, 0:1],
                        in1=NFB[:, hh * 512:(hh + 1) * 512], op0=ALU.mult, op1=ALU.mult),
                        [f"ps{bank}", 'RSTD0', 'NFB'], [f"OUTB{ob}"])
            S.dma('sp', lambda e, ob=ob, bk=bk: e.dma_start(out=y_my[bk * 128:(bk + 1) * 128, :], in_=OUTB[ob][:, :]),
                  [f"OUTB{ob}"], [])

        S.finalize()
        with nc.Block() as block:
            S.emit(nc, block, sems)
    return nc


GRID_W = 64


def _pos_embed(n_tok, d):
    rows = n_tok // GRID_W
    quarter = d // 4
    omega = (1.0 / (10000.0 ** (np.arange(quarter, dtype=np.float32) / quarter))).astype(np.float32)
    r = np.arange(rows, dtype=np.float32)[:, None] * omega[None, :]
    cl = np.arange(GRID_W, dtype=np.float32)[:, None] * omega[None, :]
    er = np.concatenate([np.sin(r), np.cos(r)], axis=-1)
    ec = np.concatenate([np.sin(cl), np.cos(cl)], axis=-1)
    emb = np.concatenate([np.broadcast_to(er[:, None, :], (rows, GRID_W, d // 2)),
                          np.broadcast_to(ec[None, :, :], (rows, GRID_W, d // 2))], axis=-1)
    return emb.reshape(n_tok, d).astype(np.float32)


def _tables(a):
    bf = ml_dtypes.bfloat16
    t1 = np.arange(64); f1s = np.arange(64)
    f1 = f1s if a == 0 else 63 - f1s
    angA = (2.0 * np.pi / 64.0) * ((t1[:, None] * f1[None, :]) % 64).astype(np.float64)
    da = np.zeros((128, 2, 128), np.float64)
    for hf in range(2):
        da[hf * 64:(hf + 1) * 64, 0, hf * 64:(hf + 1) * 64] = np.cos(angA) / 8.0
        da[hf * 64:(hf + 1) * 64, 1, hf * 64:(hf + 1) * 64] = -np.sin(angA) / 8.0
    t2 = np.arange(64); f2s = np.arange(32)
    f2 = f2s if a == 0 else 63 - f2s
    f = f1[:, None] + 64 * f2[None, :]
    th = (2.0 * np.pi / 4096.0) * ((t2[:, None, None].astype(np.int64) * f[None, :, :].astype(np.int64)) % 4096).astype(np.float64)
    co = np.cos(th) / 8.0; si = np.sin(th) / 8.0
    mc = np.zeros((2, 64, 64, 2, 32), np.float64)
    mc[0, :, :, 0, :] = co; mc[0, :, :, 1, :] = -si
    mc[1, :, :, 0, :] = si; mc[1, :, :, 1, :] = co
    mc = mc.reshape(128, 64, 64)
    tp = np.arange(256)
    fp = tp if a == 0 else 255 - tp
    angp = (2.0 * np.pi / 256.0) * ((tp[:, None] * fp[None, :]) % 256).astype(np.float64)
    flp = np.concatenate([np.cos(angp) / 16.0, -np.sin(angp) / 16.0], axis=1)
    flp = np.ascontiguousarray(flp.reshape(2, 128, 512)).astype(bf)
    c = np.arange(256)
    angc = (2.0 * np.pi / 256.0) * ((c[:, None] * c[None, :]) % 256).astype(np.float64)
    cc = np.cos(angc) / 16.0; sc = np.sin(angc) / 16.0
    cst = np.stack([cc, sc], axis=1)
    cst = cst.reshape(2, 128, 2, 256).transpose(1, 0, 2, 3)
    cst = np.ascontiguousarray(cst).astype(bf)
    return np.ascontiguousarray(da).astype(bf), np.ascontiguousarray(mc).astype(bf), flp, cst


_NC_CACHE = {}
STAGE = 4


def kernel(x_prompt, x_sample, state_hgrn, c, c_ctx, ada_w, ada_b, norm_mix, norm_mlp, fnet_wo,
           hgrn_w_in, hgrn_lb, hgrn_norm, hgrn_wo, mlp_w1, mlp_w2, norm_final, _stage=None):
    stage = STAGE if _stage is None else _stage
    f32 = np.float32
    bf = ml_dtypes.bfloat16
    x_prompt = np.asarray(x_prompt, f32); x_sample = np.asarray(x_sample, f32)
    state_hgrn = np.asarray(state_hgrn, f32); c = np.asarray(c, f32); c_ctx = np.asarray(c_ctx, f32)
    ada_w = np.ascontiguousarray(np.asarray(ada_w, f32)); ada_b = np.asarray(ada_b, f32)
    norm_mix = np.asarray(norm_mix, f32); norm_mlp = np.asarray(norm_mlp, f32)
    fnet_wo = np.ascontiguousarray(np.asarray(fnet_wo, f32)[0]); hgrn_w_in = np.asarray(hgrn_w_in, f32)[0]
    hgrn_lb = np.asarray(hgrn_lb, f32); hgrn_norm = np.asarray(hgrn_norm, f32)
    hgrn_wo = np.ascontiguousarray(np.asarray(hgrn_wo, f32)[0])
    mlp_w1 = np.ascontiguousarray(np.asarray(mlp_w1, f32)); mlp_w2 = np.ascontiguousarray(np.asarray(mlp_w2, f32))
    norm_final = np.asarray(norm_final, f32)

    if stage not in _NC_CACHE:
        _NC_CACHE[stage] = build(stage)
    nc = _NC_CACHE[stage]
    pos = _pos_embed(4096, D)
    tabs = [_tables(0), _tables(1)]
    adab = np.ascontiguousarray(ada_b.reshape(2, 48, 128).transpose(2, 0, 1))
    nrm = np.stack([norm_mix[0], norm_mlp[0], norm_mix[1], norm_mlp[1]], 0).reshape(4, 8, 128).transpose(2, 0, 1)
    nrm = np.ascontiguousarray(nrm)
    identf = np.eye(128, dtype=f32); identb = np.eye(128).astype(bf)
    s_i = np.arange(128)[:, None]; t_i = np.arange(128)[None, :]
    same = (s_i // 32) == (t_i // 32)
    maskAB = np.stack([(same & (s_i <= t_i)), (same & (s_i >= t_i))], axis=1).astype(f32).astype(bf)
    rmask = np.ones((128, 512), f32); rmask[:, ::32] = 0.0
    wq, wff, wfb, wi, wg = np.split(hgrn_w_in, 5, axis=1)
    in_maps = []
    for r in range(8):
        p, a = r // 2, r % 2
        xs = x_sample[p]
        if a == 0:
            loc = np.arange(2048)
        else:
            loc = 4095 - np.arange(2048)
        prs = [x_prompt[2 * r], x_prompt[2 * r + 1]]
        if a == 1:
            prs_loc = [pp[::-1] for pp in prs]
        else:
            prs_loc = prs
        x_my = np.concatenate([xs[loc]] + prs_loc, axis=0)
        x_all = np.concatenate([xs] + prs, axis=0)
        da, mc, flp, cst = tabs[a]
        cnd = np.stack([c[p], c_ctx], axis=-1).reshape(8, 128, 2).transpose(1, 0, 2).reshape(128, 16)
        if a == 0:
            w_in = np.concatenate([wq, wff, wfb, wi, wg], axis=1); lbr = hgrn_lb
        else:
            w_in = np.concatenate([wq, wfb, wff, wi, wg], axis=1); lbr = hgrn_lb[::-1]
        lbraw = np.ascontiguousarray(lbr.reshape(2, 2, 8, 128).transpose(3, 0, 1, 2))
        selm = np.zeros((128, 2), f32); selm[:, 1 - a] = 1.0
        sA0 = np.ascontiguousarray(state_hgrn[p, 0, a].transpose(1, 0, 2))
        in_maps.append({
            "x_my": np.ascontiguousarray(x_my), "pos_my": np.ascontiguousarray(pos[loc]),
            "x_all": np.ascontiguousarray(x_all), "pos_all": pos,
            "da_in": da, "mc_in": mc, "flp": flp, "cs": cst, "cond": np.ascontiguousarray(cnd),
            "ada_w": ada_w, "adab": adab, "nrm": nrm, "nfin": norm_final.reshape(1, D),
            "fnet_wo": fnet_wo, "w_in": np.ascontiguousarray(w_in), "lbraw": lbraw,
            "gnorm": np.ascontiguousarray(hgrn_norm.reshape(128, 1)), "hgrn_wo": hgrn_wo,
            "w1": mlp_w1, "w2": mlp_w2, "sA0": sA0, "selm": selm, "maskAB": np.ascontiguousarray(maskAB),
            "identf": identf, "identb": identb, "rmask": rmask,
        })
    res = run_bass_kernel_spmd(nc, in_maps, core_ids=list(range(8)))
    y_prompt = np.zeros((16, 256, D), f32); y_sample = np.zeros((4, 4096, D), f32)
    new_state = np.zeros((16, 1, 2, 8, 128, 128), f32)
    for r in range(8):
        p, a = r // 2, r % 2
        y = res.results[r]["y_my"]
        st = res.results[r]["st_out"]
        if a == 0:
            y_sample[p, 0:2048] = y[0:2048]
            y_prompt[2 * r] = y[2048:2304]; y_prompt[2 * r + 1] = y[2304:2560]
        else:
            y_sample[p, 2048:4096] = y[0:2048][::-1]
            y_prompt[2 * r] = y[2048:2304][::-1]; y_prompt[2 * r + 1] = y[2304:2560][::-1]
        for q in range(2):
            for dd in range(2):
                gd = dd if a == 0 else 1 - dd
                new_state[2 * r + q, 0, gd] = st[q, dd]
    return (y_prompt, y_sample, new_state)
```

```python
import numpy as np
import ml_dtypes
from contextlib import ExitStack
import concourse.bass as bass
import concourse.mybir as mybir
from concourse.bass_utils import run_bass_kernel_spmd

F32 = mybir.dt.float32
BF16 = mybir.dt.bfloat16
AF = mybir.ActivationFunctionType
ALU = mybir.AluOpType

D = 1024
T = 2560
TS = 2048
NT = 5
EPS = 1e-6
NBLK = 20


class _Op:
    pass


class Sched:
    ENG = ['pe', 'act', 'dve', 'pool', 'sp']

    def __init__(s):
        s.q = {e: [] for e in s.ENG}
        s.lastw = {}
        s.readers = {}
        s.barrier = []
        s.dma_ring = {'sp': 12, 'act': 6, 'pool': 12}
        s.dma_next = {k: 0 for k in s.dma_ring}
        s.slot_last = {}

    def _deps(s, reads, writes):
        d = []
        for b in reads:
            if b in s.lastw:
                d.append(s.lastw[b])
        for b in writes:
            if b in s.lastw:
                d.append(s.lastw[b])
            d.extend(s.readers.get(b, ()))
        d.extend(s.barrier)
        return d

    def _commit(s, o, reads, writes):
        for b in reads:
            s.readers.setdefault(b, []).append(o)
        for b in writes:
            s.lastw[b] = o
            s.readers[b] = []

    def op(s, eng, fn, reads=(), writes=()):
        psr = [b for b in reads if isinstance(b, str) and b.startswith('ps')]
        if psr:
            reads = [b for b in reads if b not in psr]
            writes = list(writes) + psr
        o = _Op()
        o.eng = eng; o.fn = fn; o.is_dma = False; o.sig = False; o.sigidx = 0
        o.deps = s._deps(reads, writes)
        s.q[eng].append(o)
        s._commit(o, reads, writes)
        return o

    def dma(s, q, fn, reads=(), writes=()):
        o = _Op()
        o.eng = q; o.fn = fn; o.is_dma = True; o.sig = False
        o.deps = s._deps(reads, writes)
        slot = s.dma_next[q]
        s.dma_next[q] = (slot + 1) % s.dma_ring[q]
        o.slot = (q, slot)
        prev = s.slot_last.get(o.slot)
        o.slotval = (prev.slotval if prev else 0) + 16
        if prev is not None:
            o.deps.append(prev)
        s.slot_last[o.slot] = o
        s.q[q].append(o)
        s._commit(o, reads, writes)
        return o

    def special(s, q, fn, key, reads=(), writes=()):
        o = _Op()
        o.eng = q; o.fn = fn; o.is_dma = True; o.sig = False
        o.deps = s._deps(reads, writes)
        o.slot = key
        o.slotval = 1
        o.incval = 1
        s.slot_last[o.slot] = o
        s.q[q].append(o)
        s._commit(o, reads, writes)
        return o

    def set_barrier(s):
        b = []
        for e in s.ENG:
            for o in reversed(s.q[e]):
                if not o.is_dma:
                    b.append(o)
                    break
        for o in s.slot_last.values():
            b.append(o)
        s.barrier = b

    def finalize(s):
        for e in s.ENG:
            for o in s.q[e]:
                for d in o.deps:
                    if d.is_dma:
                        continue
                    if d.eng == 'pe' and e == 'pe' and not o.is_dma:
                        continue
                    d.sig = True
        for e in s.ENG:
            c = 0
            for o in s.q[e]:
                if (not o.is_dma) and o.sig:
                    c += 1
                    o.sigidx = c
        for e in s.ENG:
            waited = {}
            for o in s.q[e]:
                w = {}
                for d in o.deps:
                    if d.is_dma:
                        key = ('dma',) + d.slot
                        val = d.slotval
                    else:
                        if d.eng == 'pe' and e == 'pe' and not o.is_dma:
                            continue
                        key = d.eng
                        val = d.sigidx
                    if waited.get(key, 0) >= val:
                        continue
                    if w.get(key, 0) < val:
                        w[key] = val
                for k, v in w.items():
                    waited[k] = v
                o.waits = list(w.items())

    def emit(s, nc, block, sems):
        def run(e, eng):
            for o in s.q[e]:
                for k, v in o.waits:
                    eng.wait_ge(sems[k], v)
                ins = o.fn(eng)
                if o.is_dma:
                    ins.then_inc(sems[('dma',) + o.slot], getattr(o, 'incval', 16))
                elif o.sig:
                    ins.then_inc(sems[e], 1)
            if e == 'sp':
                for slot, o in s.slot_last.items():
                    eng.wait_ge(sems[('dma',) + slot], o.slotval)

        @block.tensor
        def _(eng):
            run('pe', eng)

        @block.scalar
        def _(eng):
            run('act', eng)

        @block.vector
        def _(eng):
            run('dve', eng)

        @block.gpsimd
        def _(eng):
            run('pool', eng)

        @block.sync
        def _(eng):
            run('sp', eng)


def build(stage=4):
    nc = bass.Bass("TRN2", target_bir_lowering=False)
    S = Sched()

    def din(name, shape, dt=F32):
        return nc.dram_tensor(name, list(shape), dt, kind="ExternalInput").ap()

    x_my = din("x_my", [T, D]); pos_my = din("pos_my", [TS, D])
    x_all = din("x_all", [4608, D]); pos_all = din("pos_all", [4096, D])
    flp = din("flp", [2, 128, 512], BF16)
    cs = din("cs", [128, 2, 2, 256], BF16)
    cond = din("cond", [128, 16]); ada_w = din("ada_w", [2, D, 6 * D]); adab = din("adab", [128, 2, 48])
    nrm = din("nrm", [128, 4, 8]); nfin = din("nfin", [1, D])
    fnet_wo = din("fnet_wo", [D, D]); w_in = din("w_in", [D, 5 * D]); lbraw = din("lbraw", [128, 2, 2, 8])
    gnorm = din("gnorm", [128, 1]); hgrn_wo = din("hgrn_wo", [D, D])
    w1 = din("w1", [2, D, 4 * D]); w2 = din("w2", [2, 4 * D, D])
    sA0 = din("sA0", [128, 8, 128]); selm = din("selm", [128, 2])
    maskAB = din("maskAB", [128, 2, 128], BF16)
    identf = din("identf", [128, 128]); identb = din("identb", [128, 128], BF16)
    rmask = din("rmask", [128, 512])
    da_in = din("da_in", [128, 2, 128], BF16); mc_in = din("mc_in", [128, 64, 64], BF16)
    y_my = nc.dram_tensor("y_my", [T, D], F32, kind="ExternalOutput").ap()
    st_out = nc.dram_tensor("st_out", [2, 2, 8, 128, 128], F32, kind="ExternalOutput").ap()
    scr_x = nc.dram_tensor("scr_x", [8, 128, T], F32, kind="Internal").ap()
    scr_vec = nc.dram_tensor("scr_vec", [4, D], F32, kind="Internal").ap()
    zs = nc.dram_tensor("zs", [64, 64, 2, D], BF16, kind="Internal").ap()
    scr_og = nc.dram_tensor("scr_og", [8, 128, T], BF16, kind="Internal").ap()
    cc_in = nc.dram_tensor("cc_in", [8, 128, 128], F32, kind="Internal").ap()
    cc_out = nc.dram_tensor("cc_out", [8, 256, 128], F32, kind="Internal", addr_space="Local").ap()

    es = ExitStack()
    with es:
        def sb(name, shape, dt=F32):
            return es.enter_context(nc.sbuf_tensor(name, list(shape), dt))

        BIG = sb("BIG", [128, 20480], F32)
        HB = sb("HB", [128, 20480], BF16)
        WB = sb("WB", [128, 18432], BF16)
        MISC = sb("MISC", [128, 6144], F32)
        SM = sb("SM", [128, 1024], F32)
        IDF = sb("IDF", [128, 128], F32)
        IDB = sb("IDB", [128, 128], BF16)
        MSK = sb("MSK", [128, 2, 128], BF16)
        ONES = sb("ONES", [128, 128], BF16)
        SCB = sb("SCB", [128, 16], BF16)
        CST = sb("CST", [128, 2, 2, 256], BF16)
        RM = sb("RM", [128, 512], F32)
        DAT = sb("DAT", [128, 2, 128], BF16)
        MCT = sb("MCT", [128, 64, 64], BF16)
        ps = [es.enter_context(nc.psum_tensor(f"ps{i}", [128, 512], F32)) for i in range(8)]
        sems = {}
        for e in Sched.ENG:
            sems[e] = es.enter_context(nc.semaphore(f"sem_{e}"))
        for q, n in S.dma_ring.items():
            for i in range(n):
                sems[('dma', q, i)] = es.enter_context(nc.semaphore(f"dsem_{q}_{i}"))
        for hh_ in range(8):
            sems[('dma', 'cc', hh_)] = es.enter_context(nc.semaphore(f"ccsem_{hh_}"))

        xT = BIG[:, :].rearrange("p (k t) -> p k t", k=8)
        hT = HB[:, :].rearrange("p (k t) -> p k t", k=8)
        BIGb = BIG[:, :].bitcast(BF16)

        MOD = SM[:, 0:192].rearrange("p (i j c) -> p i j c", i=2, j=48)
        AM = SM[:, 192:448].rearrange("p (n k c) -> p n k c", n=4, k=8)
        SSQ = SM[:, 448:512]
        RSTD = SM[:, 512:576]
        NRM = SM[:, 576:608].rearrange("p (n k) -> p n k", n=4)
        ADB = SM[:, 608:704].rearrange("p (i j) -> p i j", i=2)
        CND = SM[:, 704:720]
        EPST = SM[:, 720:721]
        LB = SM[:, 736:768].rearrange("p (d l h) -> p d l h", d=2, l=2)
        LBV = SM[:, 768:784].rearrange("p (d h) -> p d h", d=2)
        OML = SM[:, 784:800].rearrange("p (d h) -> p d h", d=2)
        GN = SM[:, 800:801]
        SEL = SM[:, 801:803]
        ACOL = SM[:, 816:976].rearrange("p (d c) -> p d c", d=2)

        S.dma('sp', lambda e: e.dma_start(out=IDF[:, :], in_=identf[:, :]), [], ['IDF'])
        S.dma('sp', lambda e: e.dma_start(out=IDB[:, :], in_=identb[:, :]), [], ['IDB'])
        S.dma('sp', lambda e: e.dma_start(out=MSK[:, :, :], in_=maskAB[:, :, :]), [], ['MSK'])
        S.dma('sp', lambda e: e.dma_start(out=CST[:, :, :, :], in_=cs[:, :, :, :]), [], ['CST'])
        S.dma('sp', lambda e: e.dma_start(out=RM[:, :], in_=rmask[:, :]), [], ['RM'])
        S.dma('sp', lambda e: e.dma_start(out=CND, in_=cond[:, :]), [], ['CND'])
        S.dma('sp', lambda e: e.dma_start(out=ADB, in_=adab[:, :, :]), [], ['ADB'])
        S.dma('sp', lambda e: e.dma_start(out=NRM, in_=nrm[:, :, :]), [], ['NRM'])
        S.dma('sp', lambda e: e.dma_start(out=LB, in_=lbraw[:, :, :, :]), [], ['LB'])
        S.dma('sp', lambda e: e.dma_start(out=GN, in_=gnorm[:, :]), [], ['GN'])
        S.dma('sp', lambda e: e.dma_start(out=SEL, in_=selm[:, :]), [], ['SEL'])
        S.op('dve', lambda e: e.memset(ONES[:, :], 1.0), [], ['ONES'])
        S.op('dve', lambda e: e.memset(EPST, EPS), [], ['EPST'])

        import os
        _skip = os.environ.get('KSKIP', '').split(',')
        S.op('act', lambda e: e.activation(out=SCB[:, :], in_=CND, func=AF.Silu), ['CND'], ['SCB'])
        scb3 = SCB[:, :].rearrange("p (k c) -> p k c", c=2)
        AWB = [WB[:, 10240:14336].rearrange("p (k n) -> p k n", k=8), WB[:, 14336:18432].rearrange("p (k n) -> p k n", k=8)]
        ada_blocks = [(0, 0, 0), (0, 0, 1), (0, 1, 0), (0, 1, 1)] + [(0, j, hh) for j in range(2, 6) for hh in range(2)] \
            + [(1, j, hh) for j in range(6) for hh in range(2)]
        ada_cnt = [0]

        def ada_block():
            if ada_cnt[0] >= len(ada_blocks):
                return
            i, j, hh = ada_blocks[ada_cnt[0]]
            b = ada_cnt[0] % 2
            ada_cnt[0] += 1
            buf = AWB[b]; key = f"AWB{b}"
            src = ada_w[i, :, j * 1024 + hh * 512:j * 1024 + (hh + 1) * 512].rearrange("(k p) n -> p k n", p=128)
            S.dma('pool', lambda e, buf=buf, src=src: e.dma_start(out=buf, in_=src), [], [key])
            for o4 in range(4):
                oc = hh * 4 + o4
                col = (j * 8 + oc) * 2
                for kc in range(8):
                    S.op('pe', lambda e, buf=buf, o4=o4, kc=kc, col=col, i=i: e.matmul(
                        ps[i][:, col:col + 2], buf[:, kc, o4 * 128:(o4 + 1) * 128], scb3[:, kc, :],
                        start=(kc == 0), stop=(kc == 7)), [key, 'SCB'], [f"ps{i}"])

        def mod_evac(i, j0, j1):
            for c in range(2):
                S.op('dve', lambda e, i=i, c=c, j0=j0, j1=j1: e.tensor_tensor(
                    out=MOD[:, i, j0 * 8:j1 * 8, c], in0=ps[i][:, j0 * 16 + c:j1 * 16:2], in1=ADB[:, i, j0 * 8:j1 * 8], op=ALU.add),
                    [f"ps{i}", 'ADB'], [f"MOD{i}_{j0}"])

        def am_derive(n):
            i = n // 2; j = 1 if n % 2 == 0 else 4
            for c in range(2):
                S.op('dve', lambda e, n=n, i=i, j=j, c=c: e.scalar_tensor_tensor(
                    out=AM[:, n, :, c], in0=MOD[:, i, j * 8:(j + 1) * 8, c], scalar=1.0, in1=NRM[:, n, :],
                    op0=ALU.add, op1=ALU.mult), [f"MOD{i}_0", f"MOD{i}_2", 'NRM'], [f"AM{n}"])

        for _ in range(4):
            ada_block()
        mod_evac(0, 0, 2)
        am_derive(0)

        ALLX = [f"xT{k_}_{j_}" for k_ in range(8) for j_ in range(NT)]
        ALLH = [f"hT{k_}_{j_}" for k_ in range(8) for j_ in range(NT)]

        def modv(i, j, kc, c):
            return MOD[:, i, j * 8 + kc, c:c + 1]

        if 'fnet' not in _skip:
            ABC = WB[:, 6144:10240].rearrange("p (a n) -> p a n", a=4)
            DV = MISC[:, 4096:4608].bitcast(BF16).rearrange("p (k j) -> p k j", k=8)
            for v in range(4):
                c = v % 2
                vec = AM[:, 0, :, c] if v < 2 else MOD[:, 0, 0:8, c]
                S.op('dve', lambda e, vec=vec: e.tensor_tensor(
                    out=DV, in0=IDB[:, :].unsqueeze(1).to_broadcast([128, 8, 128]), in1=vec.unsqueeze(2).to_broadcast([128, 8, 128]), op=ALU.mult),
                    ['IDB', 'AM0', 'MOD0_0'], ['JUNK'])
                for hh in range(2):
                    S.op('pe', lambda e, hh=hh: e.matmul(ps[2 + hh][:, :], ONES[:, :], DV[:, hh * 4:(hh + 1) * 4, :], start=True, stop=True),
                         ['JUNK', 'ONES'], [f"ps{2 + hh}"])
                    if hh == 0:
                        S.op('act', lambda e, v=v, hh=hh: e.activation(out=ABC[:, v, hh * 512:(hh + 1) * 512], in_=ps[2 + hh][:, :], func=AF.Copy),
                             [f"ps{2 + hh}"], ['ABC'])
                    else:
                        S.op('dve', lambda e, v=v, hh=hh: e.tensor_copy(out=ABC[:, v, hh * 512:(hh + 1) * 512], in_=ps[2 + hh][:, :]),
                             [f"ps{2 + hh}"], ['ABC'])

            hn = BIGb[:, 0:36864].rearrange("p (c n) -> p c n", c=36)
            XIN = [MISC[:, 0:1024], MISC[:, 1024:2048]]
            PIN = [MISC[:, 2048:3072], MISC[:, 2048:3072]]
            TMPF = MISC[:, 3072:4096]
            JUNK = MISC[:, 4096:4608].bitcast(BF16)
            RB = MISC[:, 4608:5120]
            SQB = [MISC[:, 5120:5376].bitcast(BF16), MISC[:, 5376:5632].bitcast(BF16)]
            RL = [MISC[:, 5632:5888].bitcast(BF16), MISC[:, 5888:6144].bitcast(BF16)]
            XR = [XIN[0], XIN[1], WB[:, 0:2048].bitcast(F32)]
            PR = [PIN[0], WB[:, 2048:4096].bitcast(F32)]
            TR = [TMPF, WB[:, 4096:6144].bitcast(F32)]
            for ci in range(36):
                b = ci % 3; pb_ = ci % 2; tb_ = ci % 2
                c = 0 if ci < 32 else 1
                xk = f"XR{b}"; pk = f"PR{pb_}"; tk = f"TR{tb_}"
                if ci < 32:
                    for hf in range(2):
                        S.dma('sp', lambda e, ci=ci, b=b, hf=hf: e.dma_start(
                            out=XR[b][hf * 64:(hf + 1) * 64, :],
                            in_=x_all[0:4096, :].rearrange("(t1 r) c -> r t1 c", r=64)[32 * hf + ci]), [], [xk])
                        S.dma('sp', lambda e, ci=ci, pb_=pb_, hf=hf: e.dma_start(
                            out=PR[pb_][hf * 64:(hf + 1) * 64, :],
                            in_=pos_all[0:4096, :].rearrange("(t1 r) c -> r t1 c", r=64)[32 * hf + ci]), [], [pk])
                    S.op('pool', lambda e, b=b, pb_=pb_: e.tensor_tensor(out=XR[b][:, :], in0=XR[b][:, :], in1=PR[pb_][:, :], op=ALU.add),
                         [xk, pk], [xk])
                else:
                    S.dma('sp', lambda e, ci=ci, b=b: e.dma_start(out=XR[b][:, :], in_=x_all[ci * 128:(ci + 1) * 128, :]), [], [xk])
                S.op('act', lambda e, ci=ci, b=b: e.activation(out=JUNK[:, :], in_=XR[b][:, :], func=AF.Square,
                                                               accum_out=SSQ[:, ci:ci + 1]), [xk], ['JUNK', f"SSQ{ci}"])
                S.op('act', lambda e, ci=ci: e.activation(out=RSTD[:, ci:ci + 1], in_=SSQ[:, ci:ci + 1], func=AF.Sqrt,
                                                          bias=EPST, scale=1.0 / D), [f"SSQ{ci}", 'EPST'], [f"RSTD{ci}"])
                S.op('dve', lambda e, ci=ci: e.reciprocal(out=RSTD[:, ci:ci + 1], in_=RSTD[:, ci:ci + 1]),
                     [f"RSTD{ci}"], [f"RSTD{ci}"])
                S.op('dve', lambda e, ci=ci, b=b, c=c, tb_=tb_: e.scalar_tensor_tensor(
                    out=TR[tb_][:, :], in0=XR[b][:, :], scalar=RSTD[:, ci:ci + 1], in1=ABC[:, c, :],
                    op0=ALU.mult, op1=ALU.mult), [xk, f"RSTD{ci}", 'ABC'], [tk])
                S.op('dve', lambda e, ci=ci, c=c, tb_=tb_: e.tensor_tensor(out=hn[:, ci, :], in0=TR[tb_][:, :], in1=ABC[:, 2 + c, :], op=ALU.add),
                     [tk, 'ABC'], [f"hn{ci}"])
                ada_block()
            while ada_cnt[0] < len(ada_blocks):
                ada_block()
            mod_evac(0, 2, 6)
            mod_evac(1, 0, 2)
            mod_evac(1, 2, 6)
            am_derive(1); am_derive(2); am_derive(3)
            S.set_barrier()

            ZST = [WB[:, z * 2048:(z + 1) * 2048].rearrange("p (r c) -> p r c", r=2) for z in range(8)]
            MIXF = HB[:, 4096:20480].rearrange("p (k t) -> p k t", k=8)
            MIXP = HB[:, 0:4096].rearrange("p (k t) -> p k t", k=8)
            YTGS = [[WB[:, (2 * y_ + r_) * 4096:(2 * y_ + r_ + 1) * 4096].rearrange("p (k t) -> p k t", k=2) for r_ in range(2)] for y_ in range(2)]
            YTP = [WB[:, 0:4096].rearrange("p (k t) -> p k t", k=8), WB[:, 4096:8192].rearrange("p (k t) -> p k t", k=8)]
            FTP = [WB[:, 8192:9216].rearrange("p (c n) -> p c n", c=2), WB[:, 9216:10240].rearrange("p (c n) -> p c n", c=2)]
            WO = WB[:, 10240:18432].rearrange("p (k n) -> p k n", k=8)
            Z1T = [BIGb[:, 0:16384].rearrange("p (f c) -> p f c", f=64), BIGb[:, 16384:32768].rearrange("p (f c) -> p f c", f=64)]
            XTILES = [BIG[:, 0:4096].rearrange("p (k t) -> p k t", k=8), BIG[:, 4096:8192].rearrange("p (k t) -> p k t", k=8)]
            S.dma('sp', lambda e: e.dma_start(out=DAT[:, :, :], in_=da_in[:, :, :]), [], ['DAT'])
            S.dma('sp', lambda e: e.dma_start(out=MCT[:, :, :], in_=mc_in[:, :, :]), [], ['MCT'])
            for t2 in range(32):
                zb = t2 % 8
                for ri in range(2):
                    for ch in range(2):
                        bank = (t2 % 2) * 4 + ri * 2 + ch
                        S.op('pe', lambda e, t2=t2, ri=ri, ch=ch, bank=bank: e.matmul(
                            ps[bank][:, :], DAT[:, ri, :], hn[:, t2, ch * 512:(ch + 1) * 512], start=True, stop=True),
                            ['DAT', f"hn{t2}"], [f"ps{bank}"])
                        if (ri + ch) % 2 == 0:
                            S.op('act', lambda e, zb=zb, ri=ri, ch=ch, bank=bank: e.activation(
                                out=ZST[zb][:, ri, ch * 512:(ch + 1) * 512], in_=ps[bank][:, :], func=AF.Copy), [f"ps{bank}"], [f"ZST{zb}_{ri}_{ch}"])
                        else:
                            S.op('dve', lambda e, zb=zb, ri=ri, ch=ch, bank=bank: e.tensor_copy(
                                out=ZST[zb][:, ri, ch * 512:(ch + 1) * 512], in_=ps[bank][:, :]), [f"ps{bank}"], [f"ZST{zb}_{ri}_{ch}"])
                for hf in range(2):
                    S.dma('sp' if hf == 0 else 'pool', lambda e, t2=t2, zb=zb, hf=hf: e.dma_start(
                        out=zs[:, 32 * hf + t2, :, :], in_=ZST[zb][hf * 64:(hf + 1) * 64, :, :]),
                        [f"ZST{zb}_{r_}_{c_}" for r_ in range(2) for c_ in range(2)], [f"zs{t2}_{hf}"])
            S.set_barrier()
            for g in range(4):
                zb = g % 2
                YTG = YTGS[g % 2]; yk = f"YTG{g % 2}"
                for ri in range(2):
                    q_ = 'sp' if ri == 0 else 'act'
                    S.dma(q_, lambda e, g=g, zb=zb, ri=ri: e.dma_start(
                        out=Z1T[zb][ri * 64:(ri + 1) * 64, :, :],
                        in_=zs[:, :, ri, g * 256:(g + 1) * 256].rearrange("f t c -> t f c")), [], [f"Z1T{zb}"])
                for cl in range(2):
                    for a8 in range(8):
                        bank = a8
                        for f1l in range(8):
                            f1s = a8 * 8 + f1l
                            S.op('pe', lambda e, zb=zb, cl=cl, bank=bank, f1l=f1l, f1s=f1s: e.matmul(
                                ps[bank][:, f1l * 64:(f1l + 1) * 64], Z1T[zb][:, f1s, cl * 128:(cl + 1) * 128], MCT[:, f1s, :],
                                start=True, stop=True), [f"Z1T{zb}", 'MCT'], [f"ps{bank}"])
                        for ri in range(2):
                            src = ps[bank][:, :].rearrange("p (f r t) -> p f r t", f=8, r=2)[:, :, ri, :]
                            dst = YTG[ri][:, cl, :].rearrange("p (t f) -> p f t", f=64)[:, a8 * 8:(a8 + 1) * 8, :]
                            if a8 % 2 == 0:
                                S.op('act', lambda e, src=src, dst=dst: e.activation(out=dst, in_=src, func=AF.Copy), [f"ps{bank}"], [f"{yk}_{ri}_{cl}_{a8}"])
                            else:
                                S.op('dve', lambda e, src=src, dst=dst: e.tensor_copy(out=dst, in_=src), [f"ps{bank}"], [f"{yk}_{ri}_{cl}_{a8}"])
                for half in range(2):
                    oc = 2 * g + half
                    for jt in range(4):
                        bank = (half * 4 + jt) % 8
                        n = 0
                        for kc in range(2):
                            for ri in range(2):
                                S.op('pe', lambda e, half=half, kc=kc, ri=ri, jt=jt, n=n, bank=bank, YTG=YTG: e.matmul(
                                    ps[bank][:, :], CST[:, kc, ri, half * 128:(half + 1) * 128], YTG[ri][:, kc, jt * 512:(jt + 1) * 512],
                                    start=(n == 0), stop=(n == 3)), ['CST'] + [f"{yk}_{ri}_{kc}_{a}" for a in range(8)], [f"ps{bank}"])
                                n += 1
                        if jt % 2 == 0:
                            S.op('act', lambda e, oc=oc, jt=jt, bank=bank: e.activation(out=MIXF[:, oc, jt * 512:(jt + 1) * 512], in_=ps[bank][:, :], func=AF.Copy),
                                 [f"ps{bank}"], [f"MIXF{oc}_{jt}"])
                        else:
                            S.op('dve', lambda e, oc=oc, jt=jt, bank=bank: e.tensor_copy(out=MIXF[:, oc, jt * 512:(jt + 1) * 512], in_=ps[bank][:, :]),
                                 [f"ps{bank}"], [f"MIXF{oc}_{jt}"])
            S.set_barrier()
            S.dma('pool', lambda e: e.dma_start(out=WO, in_=fnet_wo[:, :].rearrange("(k p) n -> p k n", p=128)), [], ['WO'])
            ftc = [0]
            for sq in range(2):
                fb = ftc[0] % 2; ftc[0] += 1
                S.dma('sp', lambda e, fb=fb: e.dma_start(out=FTP[fb], in_=flp[:, :, :].rearrange("c p n -> p c n")), [], [f"FTP{fb}"])
                for tc in range(2):
                    for cc in range(8):
                        S.op('pe', lambda e, fb=fb, tc=tc, cc=cc, sq=sq: e.matmul(
                            ps[cc][:, :], hn[:, 32 + 2 * sq + tc, cc * 128:(cc + 1) * 128], FTP[fb][:, tc, :],
                            start=(tc == 0), stop=(tc == 1)), [f"FTP{fb}", f"hn{32 + 2 * sq + tc}"], [f"ps{cc}"])
                for cc in range(8):
                    for ri in range(2):
                        if cc % 2 == 0:
                            S.op('act', lambda e, cc=cc, ri=ri, sq=sq: e.activation(
                                out=YTP[ri][:, cc, sq * 256:(sq + 1) * 256], in_=ps[cc][:, ri * 256:(ri + 1) * 256], func=AF.Copy),
                                [f"ps{cc}"], [f"YTP{ri}"])
                        else:
                            S.op('dve', lambda e, cc=cc, ri=ri, sq=sq: e.tensor_copy(
                                out=YTP[ri][:, cc, sq * 256:(sq + 1) * 256], in_=ps[cc][:, ri * 256:(ri + 1) * 256]),
                                [f"ps{cc}"], [f"YTP{ri}"])
            for oc in range(8):
                g = oc // 2; half = oc % 2
                n = 0
                for kc in range(2):
                    for ri in range(2):
                        S.op('pe', lambda e, oc=oc, g=g, half=half, kc=kc, ri=ri, n=n: e.matmul(
                            ps[oc][:, :], CST[:, kc, ri, half * 128:(half + 1) * 128], YTP[ri][:, g * 2 + kc, :],
                            start=(n == 0), stop=(n == 3)), ['CST', f"YTP{ri}"], [f"ps{oc}"])
                        n += 1
                if oc % 2 == 0:
                    S.op('act', lambda e, oc=oc: e.activation(out=MIXP[:, oc, :], in_=ps[oc][:, :], func=AF.Copy), [f"ps{oc}"], [f"MIXF{oc}_4"])
                else:
                    S.op('dve', lambda e, oc=oc: e.tensor_copy(out=MIXP[:, oc, :], in_=ps[oc][:, :]), [f"ps{oc}"], [f"MIXF{oc}_4"])
            S.set_barrier()
            XQ = [XIN[0], XIN[1], WB[:, 0:2048].bitcast(F32), WB[:, 2048:4096].bitcast(F32)]
            PQ = [PIN[0], WB[:, 4096:6144].bitcast(F32), WB[:, 6144:8192].bitcast(F32)]
            xq_i = [0]
            for jt in range(NT):
                XT = XTILES[jt % 2]; xtk = f"XTILE{jt % 2}"
                for q4 in range(4):
                    b = xq_i[0] % 4; pb_ = xq_i[0] % 3; xq_i[0] += 1
                    r0 = jt * 512 + q4 * 128
                    S.dma('sp', lambda e, b=b, r0=r0: e.dma_start(out=XQ[b][:, :], in_=x_my[r0:r0 + 128, :]), [], [f"XQ{b}"])
                    if jt < 4:
                        S.dma('sp', lambda e, pb_=pb_, r0=r0: e.dma_start(out=PQ[pb_][:, :], in_=pos_my[r0:r0 + 128, :]), [], [f"PQ{pb_}"])
                        S.op('pool', lambda e, b=b, pb_=pb_: e.tensor_tensor(out=XQ[b][:, :], in0=XQ[b][:, :], in1=PQ[pb_][:, :], op=ALU.add),
                             [f"XQ{b}", f"PQ{pb_}"], [f"XQ{b}"])
                    for cc in range(8):
                        S.op('pe', lambda e, b=b, cc=cc, q4=q4: e.transpose(
                            ps[cc][:, q4 * 128:(q4 + 1) * 128], XQ[b][:, cc * 128:(cc + 1) * 128], IDF[:, :]),
                            [f"XQ{b}", 'IDF'], [f"ps{cc}"])
                for cc in range(8):
                    if cc % 2 == 0:
                        S.op('act', lambda e, cc=cc, XT=XT: e.activation(out=XT[:, cc, :], in_=ps[cc][:, :], func=AF.Copy), [f"ps{cc}"], [f"{xtk}_{cc}"])
                    else:
                        S.op('dve', lambda e, cc=cc, XT=XT: e.tensor_copy(out=XT[:, cc, :], in_=ps[cc][:, :]), [f"ps{cc}"], [f"{xtk}_{cc}"])
                c = 0 if jt < 4 else 1
                for oc in range(8):
                    for kc in range(8):
                        rhs = MIXF[:, kc, jt * 512:(jt + 1) * 512] if jt < 4 else MIXP[:, kc, :]
                        S.op('pe', lambda e, oc=oc, kc=kc, rhs=rhs: e.matmul(ps[oc][:, :], WO[:, kc, oc * 128:(oc + 1) * 128], rhs,
                                                                              start=(kc == 0), stop=(kc == 7)), ['WO', f"MIXF{kc}_{jt}"], [f"ps{oc}"])
                    S.op('dve', lambda e, oc=oc, c=c, XT=XT: e.scalar_tensor_tensor(
                        out=XT[:, oc, :], in0=ps[oc][:, :], scalar=modv(0, 2, oc, c), in1=XT[:, oc, :],
                        op0=ALU.mult, op1=ALU.add), [f"ps{oc}", f"{xtk}_{oc}", 'MOD0_2'], [f"{xtk}_{oc}"])
                S.dma('sp', lambda e, jt=jt, XT=XT: e.dma_start(
                    out=scr_x[:, :, jt * 512:(jt + 1) * 512].rearrange("k p t -> p k t"), in_=XT[:, :, :]),
                    [f"{xtk}_{c_}" for c_ in range(8)], [f"scrx{jt}"])

        S.set_barrier()
        for jt_ in range(NT):
            S.dma('sp' if jt_ % 2 == 0 else 'act', lambda e, jt_=jt_: e.dma_start(
                out=xT[:, :, jt_ * 512:(jt_ + 1) * 512], in_=scr_x[:, :, jt_ * 512:(jt_ + 1) * 512].rearrange("k p t -> p k t")),
                [], [f"xT{k_}_{jt_}" for k_ in range(8)])
        if stage < 2:
            S.set_barrier()


        def norm_mod(n, i, jsh):
            TM2 = [TMPF[:, 0:512], TMPF[:, 512:1024]]
            for jt in range(NT):
                c = 0 if jt < 4 else 1
                for kc in range(8):
                    b = kc % 2
                    S.op('act', lambda e, kc=kc, jt=jt, b=b: e.activation(
                        out=SQB[b][:, :], in_=xT[:, kc, jt * 512:(jt + 1) * 512], func=AF.Square), [f"xT{kc}_{jt}"], [f"SQB{b}"])
                    S.op('pe', lambda e, kc=kc, b=b: e.matmul(ps[7][:, :], ONES[:, :], SQB[b][:, :], start=(kc == 0), stop=(kc == 7)),
                         [f"SQB{b}", 'ONES'], ['ps7'])
                S.op('act', lambda e: e.activation(out=RB[:, :], in_=ps[7][:, :], func=AF.Ln, bias=EPST, scale=1.0 / D),
                     ['ps7', 'EPST'], ['RB'])
                S.op('act', lambda e: e.activation(out=RB[:, :], in_=RB[:, :], func=AF.Exp, scale=-0.5), ['RB'], ['RB'])
                for kc in range(8):
                    tb_ = kc % 2
                    S.op('dve', lambda e, kc=kc, jt=jt, c=c, tb_=tb_: e.scalar_tensor_tensor(
                        out=TM2[tb_], in0=xT[:, kc, jt * 512:(jt + 1) * 512], scalar=AM[:, n, kc, c:c + 1], in1=RB[:, :],
                        op0=ALU.mult, op1=ALU.mult), [f"xT{kc}_{jt}", f"AM{n}", 'RB'], [f"TM2{tb_}"])
                    S.op('act', lambda e, kc=kc, jt=jt, c=c, tb_=tb_: e.activation(
                        out=hT[:, kc, jt * 512:(jt + 1) * 512], in_=TM2[tb_], func=AF.Identity,
                        bias=modv(i, jsh, kc, c), scale=1.0), [f"TM2{tb_}", f"MOD{i}_0", f"MOD{i}_2"], [f"hT{kc}_{jt}"])

        W1B = [WB[:, k * 4096:(k + 1) * 4096].rearrange("p (k n) -> p k n", k=8) for k in range(2)]
        W2B = [WB[:, 8192 + k * 4096:8192 + (k + 1) * 4096].rearrange("p (k n) -> p k n", k=4) for k in range(2)]
        UT = [MISC[:, 0:1024].bitcast(BF16).rearrange("p (k n) -> p k n", k=4), MISC[:, 1024:2048].bitcast(BF16).rearrange("p (k n) -> p k n", k=4)]
        wcnt = [0]

        def mlp(i):
            jg = 5
            its = [(hp, jt) for hp in range(8) for jt in range(NT)]
            wbs = {}

            def load_w(hp):
                wb = wcnt[0] % 2; wcnt[0] += 1
                wbs[hp] = wb
                S.dma('pool', lambda e, wb=wb, hp=hp: e.dma_start(
                    out=W1B[wb], in_=w1[i, :, hp * 512:(hp + 1) * 512].rearrange("(k p) n -> p k n", p=128)), [], [f"W1B{wb}"])
                S.dma('pool', lambda e, wb=wb, hp=hp: e.dma_start(
                    out=W2B[wb], in_=w2[i, hp * 512:(hp + 1) * 512, :].rearrange("(k p) n -> p k n", p=128)), [], [f"W2B{wb}", 'WO'])

            def stage1(k):
                hp, jt = its[k]; wb = wbs[hp]; ub = k % 2
                for fc in range(4):
                    pb = fc % 4
                    for kc in range(8):
                        S.op('pe', lambda e, wb=wb, fc=fc, kc=kc, jt=jt, pb=pb: e.matmul(
                            ps[pb][:, :], W1B[wb][:, kc, fc * 128:(fc + 1) * 128], hT[:, kc, jt * 512:(jt + 1) * 512],
                            start=(kc == 0), stop=(kc == 7)), [f"W1B{wb}", f"hT{kc}_{jt}"], [f"ps{pb}"])
                    rb = fc % 2
                    S.op('act', lambda e, pb=pb, rb=rb: e.activation(out=RL[rb][:, :], in_=ps[pb][:, :], func=AF.Relu),
                         [f"ps{pb}"], [f"RL{rb}"])
                    S.op('pool', lambda e, rb=rb, ub=ub, fc=fc: e.tensor_tensor(
                        out=UT[ub][:, fc, :], in0=RL[rb][:, :], in1=RL[rb][:, :], op=ALU.mult), [f"RL{rb}"], [f"UT{ub}_{fc}"])

            def stage2(k):
                hp, jt = its[k]; wb = wbs[hp]; ub = k % 2
                c = 0 if jt < 4 else 1
                for oc in range(8):
                    pb = 4 + oc % 4
                    for fc in range(4):
                        S.op('pe', lambda e, wb=wb, fc=fc, oc=oc, ub=ub, pb=pb: e.matmul(
                            ps[pb][:, :], W2B[wb][:, fc, oc * 128:(oc + 1) * 128], UT[ub][:, fc, :],
                            start=(fc == 0), stop=(fc == 3)), [f"W2B{wb}", f"UT{ub}_{fc}"], [f"ps{pb}"])
                    S.op('dve', lambda e, oc=oc, jt=jt, c=c, pb=pb: e.scalar_tensor_tensor(
                        out=xT[:, oc, jt * 512:(jt + 1) * 512], in0=ps[pb][:, :], scalar=modv(i, jg, oc, c),
                        in1=xT[:, oc, jt * 512:(jt + 1) * 512], op0=ALU.mult, op1=ALU.add),
                        [f"ps{pb}", f"xT{oc}_{jt}", 'MOD0_2', 'MOD1_2'], [f"xT{oc}_{jt}"])

            load_w(0)
            for k in range(len(its)):
                hp, jt = its[k]
                if jt == 0 and hp + 1 < 8:
                    load_w(hp + 1)
                if k == 0:
                    stage1(0)
                if k + 1 < len(its):
                    stage1(k + 1)
                stage2(k)

        if stage >= 2:
            norm_mod(1, 0, 3)
            mlp(0)
            if stage < 3:
                S.set_barrier()


        def hgrn():
            i = 1
            norm_mod(2, 1, 0)
            for jt_ in range(NT):
                S.dma('sp', lambda e, jt_=jt_: e.dma_start(
                    out=scr_x[:, :, jt_ * 512:(jt_ + 1) * 512].rearrange("k p t -> p k t"), in_=xT[:, :, jt_ * 512:(jt_ + 1) * 512]),
                    [f"xT{k_}_{jt_}" for k_ in range(8)], [f"scrx_h{jt_}"])
            S.set_barrier()
            S.op('dve', lambda e: e.tensor_tensor(out=LBV, in0=LB[:, :, 1, :], in1=LB[:, :, 0, :], op=ALU.subtract), ['LB'], ['LBV'])
            S.op('act', lambda e: e.activation(out=LBV, in_=LBV, func=AF.Sigmoid), ['LBV'], ['LBV'])
            S.op('dve', lambda e: e.tensor_scalar(out=OML, in0=LBV, scalar1=-1.0, scalar2=1.0, op0=ALU.mult, op1=ALU.add), ['LBV'], ['OML'])
            S.dma('pool', lambda e: e.dma_start(out=WO, in_=hgrn_wo[:, :].rearrange("(k p) n -> p k n", p=128)), [], ['WO'])
            WH = [WB[:, w * 5120:(w + 1) * 5120].rearrange("p (k j n) -> p k j n", k=8, j=5) for w in range(2)]
            QE = [BIGb[:, 0:2560], BIGb[:, 2560:5120]]
            KE = [BIGb[:, 5120:7680], BIGb[:, 7680:10240]]
            KDT = [BIGb[:, 10240:12800].rearrange("p (b k) -> p b k", b=20), BIGb[:, 12800:15360].rearrange("p (b k) -> p b k", b=20)]
            VT = BIGb[:, 15360:17920].rearrange("p (b k) -> p b k", b=20)
            GS = BIGb[:, 17920:20480]
            OA = BIGb[:, 20480:23040]
            def wt(n):
                return BIG[:, 12288 + n * 512:12288 + (n + 1) * 512]
            QS = wt(0)
            SG = [wt(1), wt(2)]; FT_ = [wt(3), wt(4)]; LG = [wt(5), wt(6)]; KT = [wt(7), wt(8)]
            BB = [wt(9), wt(10)]; E1 = [wt(11), wt(12)]; E2 = [wt(13), wt(14)]
            SST = [MISC[:, 0:128], MISC[:, 3584:3712], MISC[:, 3712:3840], MISC[:, 3840:3968]]
            SR = MISC[:, 128:384].rearrange("p (r n) -> p r n", r=2)
            OS = MISC[:, 384:896]; Ro = MISC[:, 896:1408]; T1 = MISC[:, 1408:1920]
            OSB = [MISC[:, 384:896], MISC[:, 4096:4608]]
            mb = MISC[:, 1920:3584].bitcast(BF16)
            SBR = [mb[:, r * 128:(r + 1) * 128] for r in range(8)]
            SCb = [mb[:, 1024:1152], mb[:, 1152:1280]]
            SQo = mb[:, 1280:1792]; OGh = mb[:, 1792:2304]
            KDt = [mb[:, 2304:2816], mb[:, 2816:3328]]
            psb5 = ps[6][:, :].bitcast(BF16)
            sbc = [0]
            def load_wh(h_):
                wh_ = WH[h_ % 2]
                for j in range(5):
                    S.dma('pool', lambda e, wh_=wh_, j=j, h_=h_: e.dma_start(
                        out=wh_[:, :, j, :], in_=w_in[:, j * 1024 + h_ * 128:j * 1024 + (h_ + 1) * 128].rearrange("(k p) n -> p k n", p=128)),
                        [], [f"WH{h_ % 2}"])
            load_wh(0)
            for h in range(8):
                wh = WH[h % 2]; whk = f"WH{h % 2}"
                if h + 1 < 8:
                    load_wh(h + 1)
                def elementwise(jt):
                    tsl = slice(jt * 512, (jt + 1) * 512)
                    def proj(pj, bank):
                        for kc in range(8):
                            S.op('pe', lambda e, pj=pj, bank=bank, kc=kc, wh=wh, tsl=tsl: e.matmul(
                                ps[bank][:, :], wh[:, kc, pj, :], hT[:, kc, tsl], start=(kc == 0), stop=(kc == 7)),
                                [whk, f"hT{kc}_{jt}"], [f"ps{bank}"])
                    proj(0, 4)
                    S.op('act', lambda e: e.activation(out=QS, in_=ps[4][:, :], func=AF.Silu), ['ps4'], ['QS'])
                    proj(1, 6)
                    S.op('act', lambda e: e.activation(out=SG[0], in_=ps[6][:, :], func=AF.Sigmoid), ['ps6'], ['SG0'])
                    proj(2, 4)
                    S.op('act', lambda e: e.activation(out=SG[1], in_=ps[4][:, :], func=AF.Sigmoid), ['ps4'], ['SG1'])
                    proj(4, 6)
                    S.op('act', lambda e, tsl=tsl: e.activation(out=GS[:, tsl], in_=ps[6][:, :], func=AF.Silu), ['ps6'], [f"GS{jt}"])
                    for q4 in range(4):
                        blk = jt * 4 + q4
                        for kc in range(8):
                            S.op('pe', lambda e, kc=kc, q4=q4, blk=blk, wh=wh: e.matmul(
                                ps[4][:, q4 * 128:(q4 + 1) * 128], hT[:, kc, blk * 128:(blk + 1) * 128], wh[:, kc, 3, :],
                                start=(kc == 0), stop=(kc == 7)), [whk, f"hT{kc}_{jt}"], ['ps4'])
                    S.op('dve', lambda e, jt=jt: e.tensor_copy(out=VT[:, jt * 4:(jt + 1) * 4, :], in_=ps[4][:, :].rearrange("p (b k) -> p b k", b=4)),
                         ['ps4'], [f"VT{jt}"])
                    def estep(k, d):
                        lastcol = 31 if d == 0 else 0
                        if k == 0:
                            S.op('dve', lambda e, d=d, h=h: e.tensor_scalar(out=FT_[d], in0=SG[d], scalar1=OML[:, d, h:h + 1], scalar2=LBV[:, d, h:h + 1],
                                                                            op0=ALU.mult, op1=ALU.add), [f"SG{d}", 'OML', 'LBV'], [f"F{d}"])
                        elif k == 1:
                            S.op('act', lambda e, d=d: e.activation(out=LG[d], in_=FT_[d], func=AF.Ln), [f"F{d}"], [f"LG{d}"])
                            S.op('pool', lambda e, d=d: e.tensor_scalar(out=KT[d], in0=FT_[d], scalar1=-1.0, scalar2=1.0, op0=ALU.mult, op1=ALU.add),
                                 [f"F{d}"], [f"KT{d}"])
                        elif k == 2:
                            if d == 0:
                                S.op('dve', lambda e, d=d: e.tensor_tensor_scan(out=BB[d], data0=RM[:, :], data1=LG[d], initial=0.0, op0=ALU.mult, op1=ALU.add),
                                     ['RM', f"LG{d}"], [f"BB{d}"])
                            else:
                                S.op('dve', lambda e, d=d: e.tensor_tensor_scan(out=BB[d][:, ::-1], data0=RM[:, :], data1=LG[d][:, ::-1], initial=0.0,
                                                                                op0=ALU.mult, op1=ALU.add), ['RM', f"LG{d}"], [f"BB{d}"])
                        elif k == 3:
                            S.op('act', lambda e, d=d: e.activation(out=E1[d], in_=BB[d], func=AF.Exp), [f"BB{d}"], [f"E1{d}"])
                            S.op('act', lambda e, d=d: e.activation(out=E2[d], in_=BB[d], func=AF.Exp, scale=-1.0), [f"BB{d}"], [f"E2{d}"])
                        elif k == 4:
                            S.op('dve', lambda e, d=d, tsl=tsl: e.tensor_tensor(out=QE[d][:, tsl], in0=QS, in1=E1[d], op=ALU.mult),
                                 ['QS', f"E1{d}"], [f"QE{d}_{jt}"])
                            S.op('pool', lambda e, d=d, jt=jt, lastcol=lastcol: e.tensor_copy(out=ACOL[:, d, jt * 16:(jt + 1) * 16], in_=E1[d][:, lastcol::32]),
                                 [f"E1{d}"], [f"ACOL{d}_{jt}"])
                            S.op('dve', lambda e, d=d, tsl=tsl: e.tensor_tensor(out=KE[d][:, tsl], in0=KT[d], in1=E2[d], op=ALU.mult),
                                 [f"KT{d}", f"E2{d}"], [f"KE{d}_{jt}"])
                        elif k == 5:
                            S.op('pool', lambda e, d=d, tsl=tsl, jt=jt: e.tensor_tensor(
                                out=KDt[d].rearrange("p (c j) -> p c j", j=32), in0=KE[d][:, tsl].rearrange("p (c j) -> p c j", j=32),
                                in1=ACOL[:, d, jt * 16:(jt + 1) * 16].unsqueeze(2).to_broadcast([128, 16, 32]), op=ALU.mult),
                                [f"KE{d}_{jt}", f"ACOL{d}_{jt}"], [f"KDt{d}"])
                        elif k == 6:
                            for q4 in range(4):
                                S.op('pe', lambda e, d=d, q4=q4: e.transpose(psb5[:, q4 * 128:(q4 + 1) * 128], KDt[d][:, q4 * 128:(q4 + 1) * 128], IDB[:, :]),
                                     [f"KDt{d}", 'IDB'], ['ps6'])
                            S.op('dve', lambda e, d=d, jt=jt: e.tensor_copy(out=KDT[d][:, jt * 4:(jt + 1) * 4, :], in_=psb5[:, 0:512].rearrange("p (b k) -> p b k", b=4)),
                                 ['ps6'], [f"KDT{d}_{jt}"])
                    for k in range(7):
                        for d in range(2):
                            estep(k, d)

                def seg_start(d, init):
                    st = {'d': d, 'cs': 0}
                    S0 = SST[0]
                    if init[0] == 'dram':
                        S.dma('sp', lambda e: e.dma_start(out=S0, in_=init[1]), [], ['Sst0'])
                    elif init[0] == 'zero':
                        S.op('dve', lambda e: e.memset(S0, 0.0), [], ['Sst0'])
                    else:
                        S.op('dve', lambda e: e.tensor_scalar(out=S0, in0=SR[:, 0, :], scalar1=SEL[:, 0:1], scalar2=None, op0=ALU.mult),
                             ['SR', 'SEL'], ['Sst0'])
                        S.op('dve', lambda e: e.scalar_tensor_tensor(out=S0, in0=SR[:, 1, :], scalar=SEL[:, 1:2], in1=S0, op0=ALU.mult, op1=ALU.add),
                             ['SR', 'SEL', 'Sst0'], ['Sst0'])
                    cur = sbc[0] % 8; sbc[0] += 1
                    S.op('act', lambda e, cur=cur: e.activation(out=SBR[cur], in_=S0, func=AF.Copy), ['Sst0'], [f"SBR{cur}"])
                    st['cur'] = cur
                    return st

                def seg_pre(st, blk, parts):
                    d = st['d']
                    bsl = slice(blk * 128, (blk + 1) * 128)
                    sc = blk % 2
                    tj = blk // 4
                    for pt in parts:
                        if pt == 's':
                            S.op('pe', lambda e, d=d, bsl=bsl: e.matmul(ps[7][:, 0:128], KE[d][:, bsl], QE[d][:, bsl], start=True, stop=True),
                                 [f"KE{d}_{tj}", f"QE{d}_{tj}"], ['ps7'])
                        elif pt == 'm':
                            S.op('dve', lambda e, d=d, sc=sc: e.tensor_tensor(out=SCb[sc], in0=ps[7][:, 0:128], in1=MSK[:, d, :], op=ALU.mult),
                                 ['ps7', 'MSK'], [f"SCb{sc}"])
                        elif pt == 'o':
                            S.op('pe', lambda e, blk=blk, sc=sc: e.matmul(ps[5][:, 0:128], VT[:, blk, :], SCb[sc], start=True, stop=False),
                                 [f"VT{tj}", f"SCb{sc}"], ['ps5'])
                        else:
                            c = int(pt[1])
                            kw = dict(tile_position=(96, 0)) if c == 3 else {}
                            S.op('pe', lambda e, d=d, c=c, blk=blk, kw=kw: e.matmul(
                                ps[c][:, 0:128], KDT[d][c * 32:(c + 1) * 32, blk, :], VT[c * 32:(c + 1) * 32, blk, :], start=True, stop=True, **kw),
                                [f"KDT{d}_{tj}", f"VT{tj}"], [f"ps{c}"])

                def seg_block(st, blk, nb=None):
                    d = st['d']
                    bsl = slice(blk * 128, (blk + 1) * 128)
                    po = 5
                    tj = blk // 4
                    if st.get('pre') != blk:
                        seg_pre(st, blk, ['s', 'm', 'u0', 'u1', 'u2', 'u3', 'o'])
                    corder = range(4) if d == 0 else range(3, -1, -1)
                    for n, c in enumerate(corder):
                        cur = st['cur']; cs = st['cs']
                        S.op('pe', lambda e, d=d, c=c, blk=blk, po=po, cur=cur, n=n: e.matmul(
                            ps[po][:, c * 32:(c + 1) * 32], SBR[cur], QE[d][:, blk * 128 + c * 32:blk * 128 + (c + 1) * 32],
                            start=False, stop=(n == 3)), [f"SBR{cur}", f"QE{d}_{tj}"], [f"ps{po}"])
                        ns = (cs + 1) % 4
                        S.op('dve', lambda e, d=d, c=c, blk=blk, cs=cs, ns=ns: e.scalar_tensor_tensor(
                            out=SST[ns], in0=SST[cs], scalar=ACOL[:, d, blk * 4 + c:blk * 4 + c + 1], in1=ps[c][:, 0:128], op0=ALU.mult, op1=ALU.add),
                            [f"Sst{cs}", f"ACOL{d}_{tj}", f"ps{c}"], [f"Sst{ns}"])
                        st['cs'] = ns
                        cur = sbc[0] % 8; sbc[0] += 1
                        st['cur'] = cur
                        if True:
                            S.op('act', lambda e, cur=cur, ns=ns: e.activation(out=SBR[cur], in_=SST[ns], func=AF.Copy), [f"Sst{ns}"], [f"SBR{cur}"])
                        else:
                            S.op('pool', lambda e, cur=cur, ns=ns: e.tensor_copy(out=SBR[cur], in_=SST[ns]), [f"Sst{ns}"], [f"SBR{cur}"])
                        if nb is not None:
                            seg_pre(st, nb, [f"u{c}"])
                            if n == 0:
                                seg_pre(st, nb, ['s'])
                            if n == 3:
                                seg_pre(st, nb, ['m'])
                    if d == 0:
                        S.op('act', lambda e, bsl=bsl, po=po: e.activation(out=OA[:, bsl], in_=ps[po][:, 0:128], func=AF.Copy), [f"ps{po}"], [f"OA{blk}"])
                    else:
                        q4 = blk % 4
                        ob = (blk // 4) % 2
                        OSx = OSB[ob]
                        S.op('dve', lambda e, bsl=bsl, po=po, q4=q4, OSx=OSx: e.tensor_tensor(
                            out=OSx[:, q4 * 128:(q4 + 1) * 128], in0=ps[po][:, 0:128], in1=OA[:, bsl], op=ALU.add), [f"ps{po}", f"OA{blk}"], [f"OS{ob}_{q4}"])
                    if nb is not None:
                        seg_pre(st, nb, ['o'])
                        st['pre'] = nb
                    if d == 1:
                        if st.get('pending') is not None and st.get('pend_at') == blk:
                            st['pending'](); st['pending'] = None
                        q4 = blk % 4
                        if q4 == 0:
                            jt = blk // 4
                            tsl = slice(jt * 512, (jt + 1) * 512)
                            ob = jt % 2
                            OSx = OSB[ob]
                            oskeys = [f"OS{ob}_{q_}" for q_ in range(4)]

                            def onorm(jt=jt, tsl=tsl, OSx=OSx, oskeys=oskeys):
                                S.op('act', lambda e: e.activation(out=SQo, in_=OSx, func=AF.Square), oskeys, ['SQo'])
                                S.op('pe', lambda e: e.matmul(ps[6][:, :], ONES[:, :], SQo, start=True, stop=True), ['SQo', 'ONES'], ['ps6'])
                                S.op('act', lambda e: e.activation(out=Ro, in_=ps[6][:, :], func=AF.Ln, bias=EPST, scale=1.0 / 128.0),
                                     ['ps6', 'EPST'], ['Ro'])
                                S.op('act', lambda e: e.activation(out=Ro, in_=Ro, func=AF.Exp, scale=-0.5), ['Ro'], ['Ro'])
                                S.op('dve', lambda e: e.scalar_tensor_tensor(out=T1, in0=OSx, scalar=GN, in1=Ro, op0=ALU.mult, op1=ALU.mult),
                                     oskeys + ['GN', 'Ro'], ['T1'])
                                S.op('pool', lambda e: e.tensor_tensor(out=OGh, in0=T1, in1=GS[:, tsl], op=ALU.mult), ['T1', f"GS{jt}"], ['OGh'])
                                S.dma('sp', lambda e, h=h: e.dma_start(out=scr_og[h, :, tsl], in_=OGh), ['OGh'], [f"scrog{h}_{jt}"])
                            if nb is not None and (nb // 4) != jt and (nb % 4) == 3:
                                st['pending'] = onorm
                                st['pend_at'] = nb - 1
                            else:
                                onorm()

                def seg_end(st, fin):
                    if st.get('pending') is not None:
                        st['pending'](); st['pending'] = None
                    cs = st['cs']
                    if fin[0] == 'state':
                        S.dma('sp', lambda e: e.dma_start(out=fin[1], in_=SST[cs]), [f"Sst{cs}"], [])
                    elif fin[0] == 'cc':
                        S.dma('pool', lambda e, h=h: e.dma_start(out=cc_in[h, :, :], in_=SST[cs]), [f"Sst{cs}"], [f"ccin{h}"])
                        S.special('pool', lambda e, h=h: e.collective_compute(
                            "AllGather", ALU.bypass, replica_groups=[[0, 1], [2, 3], [4, 5], [6, 7]],
                            ins=[cc_in[h, :, :]], outs=[cc_out[h, :, :]]), ('cc', h), [f"ccin{h}"], [f"ccout{h}"])
                        S.dma('sp', lambda e, h=h: e.dma_start(out=SR, in_=cc_out[h, :, :].rearrange("(r p) n -> p r n", p=128)),
                              [f"ccout{h}"], ['SR'])

                def run_segment(d, blocks, init, fin):
                    st = seg_start(d, init)
                    for i_, blk in enumerate(blocks):
                        seg_block(st, blk, blocks[i_ + 1] if i_ + 1 < len(blocks) else None)
                    seg_end(st, fin)

                elementwise(0)
                elementwise(1)
                stA = seg_start(0, ('dram', sA0[:, h, :]))
                for jt in range(4):
                    for q4 in range(4):
                        blk = jt * 4 + q4
                        seg_block(stA, blk, blk + 1 if blk + 1 < 16 else None)
                    if jt == 3:
                        seg_end(stA, ('cc',))
                    if jt + 2 < NT:
                        elementwise(jt + 2)
                run_segment(0, [16, 17], ('zero',), ('state', st_out[0, 0, h, :, :]))
                run_segment(0, [18, 19], ('zero',), ('state', st_out[1, 0, h, :, :]))
                run_segment(1, [19, 18], ('zero',), ('state', st_out[1, 1, h, :, :]))
                run_segment(1, [17, 16], ('zero',), ('state', st_out[0, 1, h, :, :]))
                run_segment(1, list(range(15, -1, -1)), ('sel',), ('none',))
            S.set_barrier()
            OGT = hT
            for jt_ in range(NT):
                S.dma('sp', lambda e, jt_=jt_: e.dma_start(
                    out=OGT[:, :, jt_ * 512:(jt_ + 1) * 512], in_=scr_og[:, :, jt_ * 512:(jt_ + 1) * 512].rearrange("h p t -> p h t")),
                    [], [f"hT{k_}_{jt_}" for k_ in range(8)])
                S.dma('act', lambda e, jt_=jt_: e.dma_start(
                    out=xT[:, :, jt_ * 512:(jt_ + 1) * 512], in_=scr_x[:, :, jt_ * 512:(jt_ + 1) * 512].rearrange("k p t -> p k t")),
                    [], [f"xT{k_}_{jt_}" for k_ in range(8)])
            for jt in range(NT):
                c = 0 if jt < 4 else 1
                tsl = slice(jt * 512, (jt + 1) * 512)
                for oc in range(8):
                    pb = oc % 4
                    for kc in range(8):
                        S.op('pe', lambda e, oc=oc, kc=kc, tsl=tsl, pb=pb: e.matmul(ps[pb][:, :], WO[:, kc, oc * 128:(oc + 1) * 128], OGT[:, kc, tsl],
                                                                                   start=(kc == 0), stop=(kc == 7)), ['WO', f"hT{kc}_{jt}"], [f"ps{pb}"])
                    S.op('dve', lambda e, oc=oc, tsl=tsl, c=c, pb=pb: e.scalar_tensor_tensor(
                        out=xT[:, oc, tsl], in0=ps[pb][:, :], scalar=modv(1, 2, oc, c), in1=xT[:, oc, tsl], op0=ALU.mult, op1=ALU.add),
                        [f"ps{pb}", f"xT{oc}_{jt}", 'MOD0_2', 'MOD1_2'], [f"xT{oc}_{jt}"])
            if stage < 4:
                S.set_barrier()

        if stage >= 3:
            hgrn()
        if stage >= 4:
            norm_mod(3, 1, 3)
            mlp(1)

        OUTB = [WB[:, 16384:18432].bitcast(F32), MISC[:, 3072:4096]]
        NFB = MISC[:, 2048:3072]
        S.dma('sp', lambda e: e.dma_start(out=NFB[:, :], in_=nfin[:, :].partition_broadcast(128)), [], ['NFB'])
        final_norm = (stage >= 4)
        for bk in range(NBLK):
            ob = bk % 2
            pp = (bk % 2) * 4
            for kc in range(8):
                bank = pp + kc // 4
                S.op('pe', lambda e, kc=kc, bk=bk, bank=bank: e.transpose(
                    ps[bank][:, (kc % 4) * 128:(kc % 4 + 1) * 128], xT[:, kc, bk * 128:(bk + 1) * 128], IDF[:, :]),
                    [f"xT{kc}_{bk // 4}", 'IDF'], [f"ps{bank}"])
            for hh in range(2):
                bank = pp + hh
                if not final_norm:
                    S.op('act', lambda e, ob=ob, hh=hh, bank=bank: e.activation(
                        out=OUTB[ob][:, hh * 512:(hh + 1) * 512], in_=ps[bank][:, :], func=AF.Copy), [f"ps{bank}"], [f"OUTB{ob}"])
                else:
                    S.op('act', lambda e, hh=hh, bank=bank, bk=bk: e.activation(
                        out=JUNK[:, 0:512], in_=ps[bank][:, :], func=AF.Square, accum_out=SSQ[:, hh:hh + 1]),
                        [f"ps{bank}"], ['JUNK', f"SSQ{hh}"])
            if final_norm:
                S.op('dve', lambda e: e.tensor_tensor(out=SSQ[:, 2:3], in0=SSQ[:, 0:1], in1=SSQ[:, 1:2], op=ALU.add),
                     ['SSQ0', 'SSQ1'], ['SSQ2'])
                S.op('act', lambda e: e.activation(out=RSTD[:, 0:1], in_=SSQ[:, 2:3], func=AF.Sqrt, bias=EPST, scale=1.0 / D),
                     ['SSQ2', 'EPST'], ['RSTD0'])
                S.op('dve', lambda e: e.reciprocal(out=RSTD[:, 0:1], in_=RSTD[:, 0:1]), ['RSTD0'], ['RSTD0'])
                for hh in range(2):
                    bank = pp + hh
                    S.op('dve', lambda e, ob=ob, hh=hh, bank=bank: e.scalar_tensor_tensor(
                        out=OUTB[ob][:, hh * 512:(hh + 1) * 512], in0=ps[bank][:, :], scalar=RSTD[:, 0:1],
                        in1=NFB[:, hh * 512:(hh + 1) * 512], op0=ALU.mult, op1=ALU.mult),
                        [f"ps{bank}", 'RSTD0', 'NFB'], [f"OUTB{ob}"])
            S.dma('sp', lambda e, ob=ob, bk=bk: e.dma_start(out=y_my[bk * 128:(bk + 1) * 128, :], in_=OUTB[ob][:, :]),
                  [f"OUTB{ob}"], [])

        S.finalize()
        with nc.Block() as block:
            S.emit(nc, block, sems)
    return nc


GRID_W = 64


def _pos_embed(n_tok, d):
    rows = n_tok // GRID_W
    quarter = d // 4
    omega = (1.0 / (10000.0 ** (np.arange(quarter, dtype=np.float32) / quarter))).astype(np.float32)
    r = np.arange(rows, dtype=np.float32)[:, None] * omega[None, :]
    cl = np.arange(GRID_W, dtype=np.float32)[:, None] * omega[None, :]
    er = np.concatenate([np.sin(r), np.cos(r)], axis=-1)
    ec = np.concatenate([np.sin(cl), np.cos(cl)], axis=-1)
    emb = np.concatenate([np.broadcast_to(er[:, None, :], (rows, GRID_W, d // 2)),
                          np.broadcast_to(ec[None, :, :], (rows, GRID_W, d // 2))], axis=-1)
    return emb.reshape(n_tok, d).astype(np.float32)


def _tables(a):
    bf = ml_dtypes.bfloat16
    t1 = np.arange(64); f1s = np.arange(64)
    f1 = f1s if a == 0 else 63 - f1s
    angA = (2.0 * np.pi / 64.0) * ((t1[:, None] * f1[None, :]) % 64).astype(np.float64)
    da = np.zeros((128, 2, 128), np.float64)
    for hf in range(2):
        da[hf * 64:(hf + 1) * 64, 0, hf * 64:(hf + 1) * 64] = np.cos(angA) / 8.0
        da[hf * 64:(hf + 1) * 64, 1, hf * 64:(hf + 1) * 64] = -np.sin(angA) / 8.0
    t2 = np.arange(64); f2s = np.arange(32)
    f2 = f2s if a == 0 else 63 - f2s
    f = f1[:, None] + 64 * f2[None, :]
    th = (2.0 * np.pi / 4096.0) * ((t2[:, None, None].astype(np.int64) * f[None, :, :].astype(np.int64)) % 4096).astype(np.float64)
    co = np.cos(th) / 8.0; si = np.sin(th) / 8.0
    mc = np.zeros((2, 64, 64, 2, 32), np.float64)
    mc[0, :, :, 0, :] = co; mc[0, :, :, 1, :] = -si
    mc[1, :, :, 0, :] = si; mc[1, :, :, 1, :] = co
    mc = mc.reshape(128, 64, 64)
    tp = np.arange(256)
    fp = tp if a == 0 else 255 - tp
    angp = (2.0 * np.pi / 256.0) * ((tp[:, None] * fp[None, :]) % 256).astype(np.float64)
    flp = np.concatenate([np.cos(angp) / 16.0, -np.sin(angp) / 16.0], axis=1)
    flp = np.ascontiguousarray(flp.reshape(2, 128, 512)).astype(bf)
    c = np.arange(256)
    angc = (2.0 * np.pi / 256.0) * ((c[:, None] * c[None, :]) % 256).astype(np.float64)
    cc = np.cos(angc) / 16.0; sc = np.sin(angc) / 16.0
    cst = np.stack([cc, sc], axis=1)
    cst = cst.reshape(2, 128, 2, 256).transpose(1, 0, 2, 3)
    cst = np.ascontiguousarray(cst).astype(bf)
    return np.ascontiguousarray(da).astype(bf), np.ascontiguousarray(mc).astype(bf), flp, cst


_NC_CACHE = {}
STAGE = 4


def kernel(x_prompt, x_sample, state_hgrn, c, c_ctx, ada_w, ada_b, norm_mix, norm_mlp, fnet_wo,
           hgrn_w_in, hgrn_lb, hgrn_norm, hgrn_wo, mlp_w1, mlp_w2, norm_final, _stage=None):
    stage = STAGE if _stage is None else _stage
    f32 = np.float32
    bf = ml_dtypes.bfloat16
    x_prompt = np.asarray(x_prompt, f32); x_sample = np.asarray(x_sample, f32)
    state_hgrn = np.asarray(state_hgrn, f32); c = np.asarray(c, f32); c_ctx = np.asarray(c_ctx, f32)
    ada_w = np.ascontiguousarray(np.asarray(ada_w, f32)); ada_b = np.asarray(ada_b, f32)
    norm_mix = np.asarray(norm_mix, f32); norm_mlp = np.asarray(norm_mlp, f32)
    fnet_wo = np.ascontiguousarray(np.asarray(fnet_wo, f32)[0]); hgrn_w_in = np.asarray(hgrn_w_in, f32)[0]
    hgrn_lb = np.asarray(hgrn_lb, f32); hgrn_norm = np.asarray(hgrn_norm, f32)
    hgrn_wo = np.ascontiguousarray(np.asarray(hgrn_wo, f32)[0])
    mlp_w1 = np.ascontiguousarray(np.asarray(mlp_w1, f32)); mlp_w2 = np.ascontiguousarray(np.asarray(mlp_w2, f32))
    norm_final = np.asarray(norm_final, f32)

    if stage not in _NC_CACHE:
        _NC_CACHE[stage] = build(stage)
    nc = _NC_CACHE[stage]
    pos = _pos_embed(4096, D)
    tabs = [_tables(0), _tables(1)]
    adab = np.ascontiguousarray(ada_b.reshape(2, 48, 128).transpose(2, 0, 1))
    nrm = np.stack([norm_mix[0], norm_mlp[0], norm_mix[1], norm_mlp[1]], 0).reshape(4, 8, 128).transpose(2, 0, 1)
    nrm = np.ascontiguousarray(nrm)
    identf = np.eye(128, dtype=f32); identb = np.eye(128).astype(bf)
    s_i = np.arange(128)[:, None]; t_i = np.arange(128)[None, :]
    same = (s_i // 32) == (t_i // 32)
    maskAB = np.stack([(same & (s_i <= t_i)), (same & (s_i >= t_i))], axis=1).astype(f32).astype(bf)
    rmask = np.ones((128, 512), f32); rmask[:, ::32] = 0.0
    wq, wff, wfb, wi, wg = np.split(hgrn_w_in, 5, axis=1)
    in_maps = []
    for r in range(8):
        p, a = r // 2, r % 2
        xs = x_sample[p]
        if a == 0:
            loc = np.arange(2048)
        else:
            loc = 4095 - np.arange(2048)
        prs = [x_prompt[2 * r], x_prompt[2 * r + 1]]
        if a == 1:
            prs_loc = [pp[::-1] for pp in prs]
        else:
            prs_loc = prs
        x_my = np.concatenate([xs[loc]] + prs_loc, axis=0)
        x_all = np.concatenate([xs] + prs, axis=0)
        da, mc, flp, cst = tabs[a]
        cnd = np.stack([c[p], c_ctx], axis=-1).reshape(8, 128, 2).transpose(1, 0, 2).reshape(128, 16)
        if a == 0:
            w_in = np.concatenate([wq, wff, wfb, wi, wg], axis=1); lbr = hgrn_lb
        else:
            w_in = np.concatenate([wq, wfb, wff, wi, wg], axis=1); lbr = hgrn_lb[::-1]
        lbraw = np.ascontiguousarray(lbr.reshape(2, 2, 8, 128).transpose(3, 0, 1, 2))
        selm = np.zeros((128, 2), f32); selm[:, 1 - a] = 1.0
        sA0 = np.ascontiguousarray(state_hgrn[p, 0, a].transpose(1, 0, 2))
        in_maps.append({
            "x_my": np.ascontiguousarray(x_my), "pos_my": np.ascontiguousarray(pos[loc]),
            "x_all": np.ascontiguousarray(x_all), "pos_all": pos,
            "da_in": da, "mc_in": mc, "flp": flp, "cs": cst, "cond": np.ascontiguousarray(cnd),
            "ada_w": ada_w, "adab": adab, "nrm": nrm, "nfin": norm_final.reshape(1, D),
            "fnet_wo": fnet_wo, "w_in": np.ascontiguousarray(w_in), "lbraw": lbraw,
            "gnorm": np.ascontiguousarray(hgrn_norm.reshape(128, 1)), "hgrn_wo": hgrn_wo,
            "w1": mlp_w1, "w2": mlp_w2, "sA0": sA0, "selm": selm, "maskAB": np.ascontiguousarray(maskAB),
            "identf": identf, "identb": identb, "rmask": rmask,
        })
    res = run_bass_kernel_spmd(nc, in_maps, core_ids=list(range(8)))
    y_prompt = np.zeros((16, 256, D), f32); y_sample = np.zeros((4, 4096, D), f32)
    new_state = np.zeros((16, 1, 2, 8, 128, 128), f32)
    for r in range(8):
        p, a = r // 2, r % 2
        y = res.results[r]["y_my"]
        st = res.results[r]["st_out"]
        if a == 0:
            y_sample[p, 0:2048] = y[0:2048]
            y_prompt[2 * r] = y[2048:2304]; y_prompt[2 * r + 1] = y[2304:2560]
        else:
            y_sample[p, 2048:4096] = y[0:2048][::-1]
            y_prompt[2 * r] = y[2048:2304][::-1]; y_prompt[2 * r + 1] = y[2304:2560][::-1]
        for q in range(2):
            for dd in range(2):
                gd = dd if a == 0 else 1 - dd
                new_state[2 * r + q, 0, gd] = st[q, dd]
    return (y_prompt, y_sample, new_state)
```
